# Optimizing a Trainium2 kernel written in Bass

```python
import math
import jax, jax.numpy as jnp
from jax import lax
import numpy as np

D_MODEL = 1024
BATCH = 4
SEQ = 8192
DEPTH = 2

HEAD_DIM = 64
SB_HEADS = D_MODEL // (4 * HEAD_DIM)
SWA_HEADS = D_MODEL // (2 * HEAD_DIM)
SWA_KV_HEADS = SWA_HEADS // 4
ML_HEADS = D_MODEL // (4 * HEAD_DIM)
SB_WIDTH = SB_HEADS * HEAD_DIM
SWA_WIDTH = SWA_HEADS * HEAD_DIM
SWA_KV_WIDTH = SWA_KV_HEADS * HEAD_DIM
ML_WIDTH = ML_HEADS * HEAD_DIM
MIX_WIDTH = SB_WIDTH + SWA_WIDTH + ML_WIDTH
IN_SPLITS = (SB_WIDTH, SB_WIDTH, SB_WIDTH, SWA_WIDTH, SWA_KV_WIDTH, SWA_KV_WIDTH,
             2 * ML_WIDTH, ML_WIDTH, ML_HEADS, ML_HEADS, ML_WIDTH)
N_IN = sum(IN_SPLITS)
BLOCK = 128
WINDOW = 128
ML_CHUNK = 128
CONV_WIDTH = 4
NUM_BUCKETS = 32
MAX_DISTANCE = 128
MEM_TOKENS = 256
XATTN_HEADS = 4
XATTN_HEAD_DIM = D_MODEL // XATTN_HEADS
D_FF = ((8 * D_MODEL // 3 + 255) // 256) * 256
ALPHA = (2 * DEPTH) ** 0.25
BETA = (8 * DEPTH) ** -0.25
LN_EPS = 1e-5

kernel_name = "hybrid_sb_swa_mlstm_macaron_deepnorm"


def layer_norm(x, g, b):
    xf = x.astype(jnp.float32)
    mu = xf.mean(-1, keepdims=True)
    var = jnp.square(xf - mu).mean(-1, keepdims=True)
    return ((xf - mu) * lax.rsqrt(var + LN_EPS) * g + b).astype(x.dtype)


def swiglu(x, w_in, w_out):
    a, b = jnp.split(x @ w_in, 2, axis=-1)
    return (jax.nn.silu(a) * b) @ w_out


def t5_bucket(dist):
    max_exact = NUM_BUCKETS // 2
    d = np.maximum(dist, 1)
    large = max_exact + (np.log(d / max_exact) / np.log(MAX_DISTANCE / max_exact)
                         * (NUM_BUCKETS - max_exact)).astype(np.int32)
    large = np.minimum(large, NUM_BUCKETS - 1)
    return np.where(dist < max_exact, dist, large).astype(np.int32)


def causal_conv(x, w, b):
    out = lax.conv_general_dilated(x, w[:, None, :], window_strides=(1,),
                                   padding=[(CONV_WIDTH - 1, 0)],
                                   dimension_numbers=('NWC', 'WIO', 'NWC'),
                                   feature_group_count=x.shape[-1])
    return out + b


def stick_breaking(q, k, v):
    bsz, s_len, h, d = q.shape
    nb = s_len // BLOCK
    kf = k.astype(jnp.float32).transpose(0, 2, 1, 3)
    vf = v.astype(jnp.float32).transpose(0, 2, 1, 3)
    qb = (q.astype(jnp.float32) * d ** -0.5).reshape(bsz, nb, BLOCK, h, d).transpose(1, 0, 3, 2, 4)
    key_pos = jnp.arange(s_len)

    def block(args):
        q_blk, blk = args
        t = blk * BLOCK + jnp.arange(BLOCK)
        causal = key_pos[None, :] < t[:, None]
        z = jnp.einsum('bhqd,bhsd->bhqs', q_blk, kf)
        log_one_minus = jnp.where(causal, jax.nn.log_sigmoid(-z), 0.0)
        after = lax.cumsum(log_one_minus, axis=3, reverse=True) - log_one_minus
        weight = jnp.where(causal, jnp.exp(jax.nn.log_sigmoid(z) + after), 0.0)
        return jnp.einsum('bhqs,bhsd->bhqd', weight, vf)

    out = lax.map(block, (qb, jnp.arange(nb)))
    return out.transpose(1, 0, 3, 2, 4).reshape(bsz, s_len, h * d).astype(q.dtype)


def sliding_window_attention(q, k, v, sinks, rel_bias):
    bsz, s_len, h, d = q.shape
    hkv = k.shape[2]
    g = h // hkv
    nb = s_len // BLOCK
    qf = (q.astype(jnp.float32) * d ** -0.5).reshape(bsz, nb, BLOCK, hkv, g, d)

    def band(t):
        tb = t.astype(jnp.float32).reshape(bsz, nb, BLOCK, hkv, d)
        prev = jnp.concatenate([jnp.zeros_like(tb[:, :1]), tb[:, :-1]], axis=1)
        return jnp.concatenate([prev, tb], axis=2)

    kb, vb = band(k), band(v)
    qi = np.arange(BLOCK)[:, None]
    kj = np.arange(2 * BLOCK)[None, :]
    dist = qi + BLOCK - kj
    in_window = (dist >= 0) & (dist < WINDOW)
    bucket = t5_bucket(np.clip(dist, 0, None))
    bias = rel_bias.astype(jnp.float32)[bucket]
    bias = bias.transpose(2, 0, 1).reshape(hkv, g, BLOCK, 2 * BLOCK)
    key_valid = (np.arange(nb)[:, None] * BLOCK + kj - BLOCK) >= 0
    mask = in_window[None] & key_valid[:, None, :]
    logits = jnp.einsum('bnqkgd,bnskd->bnkgqs', qf, kb) + bias
    logits = jnp.where(mask[None, :, None, None], logits, -jnp.inf)
    sink = sinks.astype(jnp.float32).reshape(hkv, g)[:, :, None, None]
    m = jnp.maximum(logits.max(-1, keepdims=True), sink)
    p = jnp.exp(logits - m)
    denom = p.sum(-1, keepdims=True) + jnp.exp(sink - m)
    out = jnp.einsum('bnkgqs,bnskd->bnqkgd', p / denom, vb)
    return out.reshape(bsz, s_len, h * d).astype(q.dtype)


def mlstm(q, k, v, i_raw, f_raw, norm_g):
    bsz, s_len, h, d = q.shape
    L = ML_CHUNK
    nc = s_len // L

    def chunks(t):
        return t.astype(jnp.float32).reshape(bsz, nc, L, h, d).transpose(1, 0, 3, 2, 4)

    def chunks_gate(t):
        return t.astype(jnp.float32).reshape(bsz, nc, L, h).transpose(1, 0, 3, 2)

    qc, kc, vc = chunks(q), chunks(k) * d ** -0.5, chunks(v)
    ic = chunks_gate(i_raw)
    b = jnp.cumsum(jax.nn.log_sigmoid(chunks_gate(f_raw)), axis=-1)

    def step(carry, inp):
        C, n, m = carry
        k_c, v_c, i_c, b_c = inp
        b_end = b_c[..., -1]
        w_log = b_end[..., None] - b_c + i_c
        m_new = jnp.maximum(b_end + m, w_log.max(-1))
        decay = jnp.exp(b_end + m - m_new)
        w = jnp.exp(w_log - m_new[..., None])
        C_new = decay[..., None, None] * C + jnp.einsum('bhl,bhlk,bhlv->bhkv', w, k_c, v_c)
        n_new = decay[..., None] * n + jnp.einsum('bhl,bhlk->bhk', w, k_c)
        return (C_new, n_new, m_new), (C, n, m)

    init = (jnp.zeros((bsz, h, d, d), jnp.float32), jnp.zeros((bsz, h, d), jnp.float32),
            jnp.zeros((bsz, h), jnp.float32))
    _, (C_prev, n_prev, m_prev) = lax.scan(step, init, (kc, vc, ic, b))

    g = b + m_prev[..., None]
    causal = np.tril(np.ones((L, L), dtype=bool))
    D = jnp.where(causal, b[..., :, None] - b[..., None, :] + ic[..., None, :], -jnp.inf)
    m_t = jnp.maximum(g, D.max(-1))
    s = jnp.einsum('cbhld,cbhsd->cbhls', qc, kc) * jnp.exp(D - m_t[..., None])
    inter = jnp.exp(g - m_t)
    num = inter[..., None] * jnp.einsum('cbhld,cbhdv->cbhlv', qc, C_prev) \
        + jnp.einsum('cbhls,cbhsv->cbhlv', s, vc)
    den = inter * jnp.einsum('cbhld,cbhd->cbhl', qc, n_prev) + s.sum(-1)
    hid = num / jnp.maximum(jnp.abs(den), jnp.exp(-m_t))[..., None]
    mu = hid.mean(-1, keepdims=True)
    var = jnp.square(hid - mu).mean(-1, keepdims=True)
    hid = (hid - mu) * lax.rsqrt(var + LN_EPS)
    hid = hid.transpose(1, 0, 3, 2, 4).reshape(bsz, s_len, h * d).astype(q.dtype)
    return hid * norm_g


def hybrid_mixer(x, w_in, conv_w, conv_b, i_bias, f_bias, ml_norm_g, sinks, rel_bias, w_out):
    bsz, s_len, _ = x.shape
    idx = [int(i) for i in np.cumsum(IN_SPLITS)[:-1]]
    (q_sb, k_sb, v_sb, q_sw, k_sw, v_sw, qk_ml, v_ml, i_ml, f_ml, o_ml) = jnp.split(x @ w_in, idx, axis=-1)

    def heads(t):
        return t.reshape(bsz, s_len, -1, HEAD_DIM)

    y_sb = stick_breaking(heads(q_sb), heads(k_sb), heads(v_sb))
    y_sw = sliding_window_attention(heads(q_sw), heads(k_sw), heads(v_sw), sinks, rel_bias)
    q_ml, k_ml = jnp.split(jax.nn.silu(causal_conv(qk_ml, conv_w, conv_b)), 2, axis=-1)
    y_ml = jax.nn.sigmoid(o_ml) * mlstm(heads(q_ml), heads(k_ml), heads(v_ml),
                                        i_ml + i_bias, f_ml + f_bias, ml_norm_g)
    y = jnp.concatenate([y_sb, y_sw, y_ml.astype(y_sb.dtype)], axis=-1)
    return y @ w_out


def cross_attention(x, mem, w_q, w_kv, w_o):
    bsz, s_len, _ = x.shape
    q = (x @ w_q).reshape(bsz, s_len, XATTN_HEADS, XATTN_HEAD_DIM).astype(jnp.float32)
    k, v = jnp.split(mem @ w_kv, 2, axis=-1)
    k = k.reshape(bsz, -1, XATTN_HEADS, XATTN_HEAD_DIM).astype(jnp.float32)
    v = v.reshape(bsz, -1, XATTN_HEADS, XATTN_HEAD_DIM).astype(jnp.float32)
    p = jax.nn.softmax(jnp.einsum('bshd,bmhd->bhsm', q, k) * XATTN_HEAD_DIM ** -0.5, axis=-1)
    out = jnp.einsum('bhsm,bmhd->bshd', p, v).reshape(bsz, s_len, D_MODEL).astype(x.dtype)
    return out @ w_o


def setup_inputs(seed: int = 0) -> dict:
    key = jax.random.key(seed)
    ks = jax.random.split(key, 22)
    f32 = jnp.float32

    def dense(k, shape, fan_in, scale=1.0):
        return jax.random.normal(k, shape, f32) * (scale * fan_in ** -0.5)

    return {
        "x": jax.random.normal(ks[0], (BATCH, SEQ, D_MODEL), f32),
        "mem": jax.random.normal(ks[1], (BATCH, MEM_TOKENS, D_MODEL), f32),
        "ffn1_w_in": dense(ks[2], (DEPTH, D_MODEL, 2 * D_FF), D_MODEL),
        "ffn1_w_out": dense(ks[3], (DEPTH, D_FF, D_MODEL), D_FF, BETA),
        "mix_w_in": dense(ks[4], (DEPTH, D_MODEL, N_IN), D_MODEL),
        "ml_conv_w": dense(ks[5], (DEPTH, CONV_WIDTH, 2 * ML_WIDTH), CONV_WIDTH),
        "ml_conv_b": 0.01 * jax.random.normal(ks[6], (DEPTH, 2 * ML_WIDTH), f32),
        "ml_i_bias": 0.1 * jax.random.normal(ks[7], (DEPTH, ML_HEADS), f32),
        "ml_f_bias": jnp.linspace(3.0, 6.0, ML_HEADS, dtype=f32)[None, :]
                     + 0.1 * jax.random.normal(ks[8], (DEPTH, ML_HEADS), f32),
        "ml_norm_g": 1.0 + 0.01 * jax.random.normal(ks[9], (DEPTH, ML_WIDTH), f32),
        "swa_sinks": 0.5 * jax.random.normal(ks[10], (DEPTH, SWA_HEADS), f32),
        "rel_bias": 0.1 * jax.random.normal(ks[11], (NUM_BUCKETS, SWA_HEADS), f32),
        "mix_w_out": dense(ks[12], (DEPTH, MIX_WIDTH, D_MODEL), MIX_WIDTH, BETA),
        "xattn_w_q": dense(ks[13], (DEPTH, D_MODEL, D_MODEL), D_MODEL),
        "xattn_w_kv": dense(ks[14], (DEPTH, D_MODEL, 2 * D_MODEL), D_MODEL),
        "xattn_w_o": dense(ks[15], (DEPTH, D_MODEL, D_MODEL), D_MODEL, BETA),
        "ffn2_w_in": dense(ks[16], (DEPTH, D_MODEL, 2 * D_FF), D_MODEL),
        "ffn2_w_out": dense(ks[17], (DEPTH, D_FF, D_MODEL), D_FF, BETA),
        "ln_g": 1.0 + 0.01 * jax.random.normal(ks[18], (DEPTH, 4, D_MODEL), f32),
        "ln_b": 0.01 * jax.random.normal(ks[19], (DEPTH, 4, D_MODEL), f32),
    }


def reference(x, mem, ffn1_w_in, ffn1_w_out, mix_w_in, ml_conv_w, ml_conv_b, ml_i_bias, ml_f_bias,
              ml_norm_g, swa_sinks, rel_bias, mix_w_out, xattn_w_q, xattn_w_kv, xattn_w_o,
              ffn2_w_in, ffn2_w_out, ln_g, ln_b):
    for l in range(DEPTH):
        x = layer_norm(ALPHA * x + 0.5 * swiglu(x, ffn1_w_in[l], ffn1_w_out[l]), ln_g[l, 0], ln_b[l, 0])
        x = layer_norm(ALPHA * x + hybrid_mixer(x, mix_w_in[l], ml_conv_w[l], ml_conv_b[l], ml_i_bias[l],
                                                ml_f_bias[l], ml_norm_g[l], swa_sinks[l], rel_bias,
                                                mix_w_out[l]),
                       ln_g[l, 1], ln_b[l, 1])
        x = layer_norm(ALPHA * x + cross_attention(x, mem, xattn_w_q[l], xattn_w_kv[l], xattn_w_o[l]),
                       ln_g[l, 2], ln_b[l, 2])
        x = layer_norm(ALPHA * x + 0.5 * swiglu(x, ffn2_w_in[l], ffn2_w_out[l]), ln_g[l, 3], ln_b[l, 3])
    return x
```

```python
import contextlib
import numpy as np
import concourse.bass as bass
import concourse.mybir as mybir

F32 = mybir.dt.float32
BF16 = mybir.dt.bfloat16
ALU = mybir.AluOpType
AF = mybir.ActivationFunctionType
AX = mybir.AxisListType


class Buf:
    __slots__ = ("name", "w", "r")

    def __init__(self, name=""):
        self.name = name
        self.w = None
        self.r = []


class Sched:
    EPOCH = 20000

    def __init__(self, nc, n_dma_sems=24):
        self.nc = nc
        self.es = contextlib.ExitStack()
        self.eng = {"pe": nc.tensor, "act": nc.scalar, "dve": nc.vector,
                    "pool": nc.gpsimd, "sp": nc.sync}
        self.sem = {}
        self.cnt = {}
        self.nsem = 0
        for e in self.eng:
            self._new_epoch(e)
        self.dma_sems = [self.es.enter_context(nc.semaphore(f"dq{i}")) for i in range(n_dma_sems)]
        self.dma_cnt = [0] * n_dma_sems
        self.dma_next = 0
        self.waited = {e: {} for e in self.eng}
        self.last_ev = {e: None for e in self.eng}
        self.out_events = []
        self.n_ops = 0
        self.n_waits = 0

    def _new_epoch(self, e):
        self.sem[e] = self.es.enter_context(self.nc.semaphore(f"s_{e}_{self.nsem}"))
        self.nsem += 1
        self.cnt[e] = 0

    def _wait(self, e, ev):
        if ev is None:
            return
        sem, val, src = ev
        if src == "pe" and e == "pe":
            return
        key = id(sem)
        if self.waited[e].get(key, 0) >= val:
            return
        self.eng[e].wait_ge(sem, val)
        self.n_waits += 1
        self.waited[e][key] = val

    def _deps(self, e, reads, writes):
        for b in reads:
            self._wait(e, b.w)
        for b in writes:
            self._wait(e, b.w)
            for ev in b.r:
                self._wait(e, ev)

    def _commit(self, ev, reads, writes):
        for b in reads:
            b.r.append(ev)
            if len(b.r) > 12:
                latest = {}
                for x in b.r:
                    k = id(x[0])
                    if k not in latest or latest[k][1] < x[1]:
                        latest[k] = x
                b.r = list(latest.values())
        for b in writes:
            b.w = ev
            b.r = []

    def op(self, e, fn, reads=(), writes=()):
        self._deps(e, reads, writes)
        if self.cnt[e] >= self.EPOCH:
            self._new_epoch(e)
        ins = fn(self.eng[e])
        self.cnt[e] += 1
        ins.then_inc(self.sem[e], 1)
        ev = (self.sem[e], self.cnt[e], e)
        self.last_ev[e] = ev
        self._commit(ev, reads, writes)
        self.n_ops += 1
        return ev

    def dma(self, q, out, in_, reads=(), writes=(), **kw):
        self._deps(q, reads, writes)
        k = self.dma_next
        self.dma_next = (k + 1) % len(self.dma_sems)
        sem = self.dma_sems[k]
        if self.dma_cnt[k] > 0:
            self._wait(q, (sem, self.dma_cnt[k], "dma"))
        self.dma_cnt[k] += 16
        self.eng[q].dma_start(out=out, in_=in_, **kw).then_inc(sem, 16)
        ev = (sem, self.dma_cnt[k], "dma")
        self._commit(ev, reads, writes)
        self.n_ops += 1
        return ev

    def custom(self, q, fn, inc, reads=(), writes=()):
        self._deps(q, reads, writes)
        k = self.dma_next
        self.dma_next = (k + 1) % len(self.dma_sems)
        sem = self.dma_sems[k]
        if self.dma_cnt[k] > 0:
            self._wait(q, (sem, self.dma_cnt[k], "dma"))
        self.dma_cnt[k] += inc
        fn(self.eng[q]).then_inc(sem, inc)
        ev = (sem, self.dma_cnt[k], "dma")
        self._commit(ev, reads, writes)
        return ev

    def barrier(self, bufs=()):
        evs = [ev for ev in self.last_ev.values() if ev is not None]
        for k, sem in enumerate(self.dma_sems):
            if self.dma_cnt[k] > 0:
                evs.append((sem, self.dma_cnt[k], "dma"))
        for e in self.eng:
            for ev in evs:
                if ev[2] == e and e == "pe":
                    continue
                self._wait(e, ev)

    def finish(self):
        self.barrier()
        self.es.close()


D = 1024
DFF = 2816
ALPHA = 4 ** 0.25
LN_EPS = 1e-5
P = 128


class Ctx:
    def __init__(self, nc, S):
        self.nc = nc
        self.S = S
        es = S.es
        self.psum_es = None
        self.psum_gen = 0
        self.set_psum(1)
        self.ident = es.enter_context(nc.sbuf_tensor("ident_sb", [P, P], F32))
        self.ident_b = Buf("ident")

        self.eps_t = es.enter_context(nc.sbuf_tensor("eps_t", [P, 1], F32))

    def set_psum(self, n_bf=1):
        if self.psum_es is not None:
            self.psum_es.close()
        self.psum_es = contextlib.ExitStack()
        g = self.psum_gen
        self.psum_gen += 1
        nf = 8 - n_bf
        self.ps = [self.psum_es.enter_context(self.nc.psum_tensor(f"psf{g}_{i}", [P, 512], F32)) for i in range(nf)]
        self.psb = [Buf(f"ps{i}") for i in range(nf)]
        self.ps_bfs = [self.psum_es.enter_context(self.nc.psum_tensor(f"psh{g}_{i}", [P, 1024], BF16)) for i in range(n_bf)]
        self.psbf_b = [Buf(f"psbf{i}") for i in range(n_bf)]
        self.ps_bf = self.ps_bfs[0]
        while len(self.ps) < 8:
            self.ps.append(None)
            self.psb.append(self.psbf_b[0])

    def load_consts(self, ident_dram):
        self.ident_dram = ident_dram
        self.S.dma("sp", self.ident[:], ident_dram, writes=[self.ident_b])
        self.S.op("pool", lambda e: e.memset(self.eps_t[:], LN_EPS), writes=[self.ident_b])


def load_xT(C, x_tile, x_b, xT, xT_b, ps_ids, evac_engs=("act", "dve")):
    S = C.S
    for kc in range(8):
        pi = ps_ids[kc % len(ps_ids)]
        ps, pb = C.ps[pi], C.psb[pi]
        for blk in range(4):
            S.op("pe", lambda e, kc=kc, blk=blk, ps=ps: e.transpose(
                out=ps[:, blk * 128:(blk + 1) * 128], in_=x_tile[:, blk, kc * 128:(kc + 1) * 128],
                identity=C.ident[:]), reads=[x_b, C.ident_b], writes=[pb])
        eng = evac_engs[kc % len(evac_engs)]
        if eng == "act":
            S.op("act", lambda e, kc=kc, ps=ps: e.copy(out=xT[:, kc, :], in_=ps[:]), reads=[pb], writes=[xT_b])
        else:
            S.op("dve", lambda e, kc=kc, ps=ps: e.tensor_copy(out=xT[:, kc, :], in_=ps[:]), reads=[pb], writes=[xT_b])


def layer_norm_tile(C, xt, x_b, g_bc, b_bc, gb_b, stats, mv, rstd, st_b, nblk=4):
    S = C.S
    for blk in range(nblk):
        for hh in range(2):
            S.op("dve", lambda e, blk=blk, hh=hh: e.bn_stats(out=stats[:, blk, hh, :], in_=xt[:, blk, hh * 512:(hh + 1) * 512]),
                 reads=[x_b], writes=[st_b])
        S.op("dve", lambda e, blk=blk: e.bn_aggr(out=mv[:, blk, :], in_=stats[:, blk, :, :]), reads=[st_b], writes=[st_b])
        S.op("act", lambda e, blk=blk: e.activation(out=rstd[:, blk, :], in_=mv[:, blk, 1:2], func=AF.Sqrt, bias=C.eps_t[:], scale=1.0),
             reads=[st_b, C.ident_b], writes=[st_b])
        S.op("dve", lambda e, blk=blk: e.reciprocal(out=rstd[:, blk, :], in_=rstd[:, blk, :]), reads=[st_b], writes=[st_b])
        S.op("dve", lambda e, blk=blk: e.tensor_scalar(out=xt[:, blk, :], in0=xt[:, blk, :], scalar1=mv[:, blk, 0:1],
                                                       scalar2=rstd[:, blk, :], op0=ALU.subtract, op1=ALU.mult),
             reads=[st_b, x_b], writes=[x_b])
        S.op("pool", lambda e, blk=blk: e.tensor_tensor(out=xt[:, blk, :], in0=xt[:, blk, :], in1=g_bc[:], op=ALU.mult),
             reads=[x_b, gb_b], writes=[x_b])
        S.op("pool", lambda e, blk=blk: e.tensor_tensor(out=xt[:, blk, :], in0=xt[:, blk, :], in1=b_bc[:], op=ALU.add),
             reads=[x_b, gb_b], writes=[x_b])


def ffn_phase(C, x_in, x_out, w_in, w_out, g_dram, b_dram, NT, tag, xin_b, xout_b, after_tile=None, xT_out=None, xT_out_b=None):
    nc, S = C.nc, C.S
    ntile = NT // 512
    JG = [(0, 6), (6, 12), (12, 17), (17, 22)]
    with contextlib.ExitStack() as es:
        w1 = es.enter_context(nc.sbuf_tensor(f"w1{tag}", [P, 8, 2 * DFF], BF16))
        w2 = es.enter_context(nc.sbuf_tensor(f"w2{tag}", [P, 22, D], BF16))
        g_bc = es.enter_context(nc.sbuf_tensor(f"g{tag}", [P, D], F32))
        b_bc = es.enter_context(nc.sbuf_tensor(f"b{tag}", [P, D], F32))
        xtb = [es.enter_context(nc.sbuf_tensor(f"xt{tag}{i}", [P, 4, D], F32)) for i in range(2)]
        xT = es.enter_context(nc.sbuf_tensor(f"xT{tag}", [P, 8, 512], BF16))
        gT = es.enter_context(nc.sbuf_tensor(f"gT{tag}", [P, 22, 512], BF16))
        sa = es.enter_context(nc.sbuf_tensor(f"sa{tag}", [P, 512], F32))
        xo = [es.enter_context(nc.sbuf_tensor(f"xo{tag}{i}", [P, 512], BF16)) for i in range(2)]
        stats = es.enter_context(nc.sbuf_tensor(f"stats{tag}", [P, 4, 2, 6], F32))
        mv = es.enter_context(nc.sbuf_tensor(f"mv{tag}", [P, 4, 2], F32))
        rstd = es.enter_context(nc.sbuf_tensor(f"rstd{tag}", [P, 4, 1], F32))
        w1_b = [Buf() for _ in JG]
        w2_b = [Buf() for _ in range(22)]
        gb_b, xT_b, st_b, sa_b = Buf(), Buf(), Buf(), Buf()
        x_b = [Buf(), Buf()]
        xo_b = [Buf(), Buf()]
        gT_b = [Buf() for _ in range(22)]
        jgrp = {}
        for gi, (j0, j1) in enumerate(JG):
            for j in range(j0, j1):
                jgrp[j] = gi
        w_in_v = w_in.rearrange("(kc p) f -> p kc f", p=P)
        w_out_v = w_out.rearrange("(j p) d -> p j d", p=P)
        for gi, (j0, j1) in enumerate(JG):
            for half in range(2):
                c0, c1 = half * DFF + j0 * 128, half * DFF + j1 * 128
                S.dma("pool", w1[:, :, c0:c1], w_in_v[:, :, c0:c1], writes=[w1_b[gi]])
        for j in range(0, 22, 2):
            S.dma("pool", w2[:, j:j + 2, :], w_out_v[:, j:j + 2, :], writes=[w2_b[j], w2_b[j + 1]])
        S.dma("sp", g_bc[:], g_dram, writes=[gb_b])
        S.dma("sp", b_bc[:], b_dram, writes=[gb_b])
        xin_v = x_in.rearrange("(t b p) d -> t p b d", b=4, p=P)
        xout_v = x_out.rearrange("(t b p) d -> t p b d", b=4, p=P)

        def emit_load(t):
            S.dma("sp", xtb[t % 2][:], xin_v[t], reads=[xin_b[t] if isinstance(xin_b, list) else xin_b], writes=[x_b[t % 2]])

        def emit_T(t):
            load_xT(C, xtb[t % 2], x_b[t % 2], xT, xT_b, ps_ids=(6,))
            S.op("act", lambda e, t=t: e.mul(out=xtb[t % 2][:], in_=xtb[t % 2][:], mul=ALPHA), reads=[x_b[t % 2]], writes=[x_b[t % 2]])

        def emit_Tout_chunk(tt, kc):
            xt_ = xtb[tt % 2]
            for blk in range(4):
                S.op("pe", lambda e, blk=blk: e.transpose(out=C.ps[6][:, blk * 128:(blk + 1) * 128], in_=xt_[:, blk, kc * 128:(kc + 1) * 128],
                                                          identity=C.ident[:]), reads=[x_b[tt % 2], C.ident_b], writes=[C.psb[6]])
            k = kc % 2
            if k == 0:
                S.op("act", lambda e: e.copy(out=xo[k][:], in_=C.ps[6][:]), reads=[C.psb[6]], writes=[xo_b[k]])
            else:
                S.op("dve", lambda e: e.tensor_copy(out=xo[k][:], in_=C.ps[6][:]), reads=[C.psb[6]], writes=[xo_b[k]])
            S.dma("sp", xT_out[tt][kc * 128:(kc + 1) * 128, :], xo[k][:], reads=[xo_b[k]], writes=[xT_out_b[tt]])
            if kc == 7 and after_tile is not None:
                after_tile(tt)

        emit_load(0)
        emit_T(0)
        for t in range(ntile):
            xt = xtb[t % 2]
            xb = x_b[t % 2]
            for j in range(22):
                pa, pb_ = (0, 1) if j % 2 == 0 else (2, 3)
                gi = jgrp[j]
                for kc in range(8):
                    S.op("pe", lambda e, j=j, kc=kc, pa=pa: e.matmul(
                        out=C.ps[pa][:], lhsT=w1[:, kc, j * 128:(j + 1) * 128], rhs=xT[:, kc, :],
                        start=(kc == 0), stop=(kc == 7)), reads=[w1_b[gi], xT_b], writes=[C.psb[pa]])
                for kc in range(8):
                    S.op("pe", lambda e, j=j, kc=kc, pb_=pb_: e.matmul(
                        out=C.ps[pb_][:], lhsT=w1[:, kc, DFF + j * 128:DFF + (j + 1) * 128], rhs=xT[:, kc, :],
                        start=(kc == 0), stop=(kc == 7)), reads=[w1_b[gi], xT_b], writes=[C.psb[pb_]])
                S.op("act", lambda e, pa=pa: e.activation(out=sa[:], in_=C.ps[pa][:], func=AF.Silu), reads=[C.psb[pa]], writes=[sa_b])
                S.op("dve", lambda e, pb_=pb_, j=j: e.tensor_tensor(out=gT[:, j, :], in0=C.ps[pb_][:], in1=sa[:], op=ALU.mult),
                     reads=[C.psb[pb_], sa_b], writes=[gT_b[j]])
                if xT_out is not None and t >= 1 and 5 <= j < 13:
                    emit_Tout_chunk(t - 1, j - 5)
                if j == 13 and t + 1 < ntile:
                    emit_load(t + 1)
            if t + 1 < ntile:
                emit_T(t + 1)
            for blk in range(4):
                for hh in range(2):
                    po = 4 + (blk * 2 + hh) % 2
                    for j in range(22):
                        S.op("pe", lambda e, j=j, blk=blk, hh=hh, po=po: e.matmul(
                            out=C.ps[po][:], lhsT=gT[:, j, blk * 128:(blk + 1) * 128], rhs=w2[:, j, hh * 512:(hh + 1) * 512],
                            start=(j == 0), stop=(j == 21)), reads=[gT_b[j], w2_b[j]], writes=[C.psb[po]])
                    S.op("dve", lambda e, blk=blk, hh=hh, po=po, xt=xt: e.scalar_tensor_tensor(
                        out=xt[:, blk, hh * 512:(hh + 1) * 512], in0=C.ps[po][:], scalar=0.5,
                        in1=xt[:, blk, hh * 512:(hh + 1) * 512], op0=ALU.mult, op1=ALU.add),
                        reads=[C.psb[po], xb], writes=[xb])
            layer_norm_tile(C, xt, xb, g_bc, b_bc, gb_b, stats, mv, rstd, st_b)
            S.dma("sp", xout_v[t], xt[:], reads=[xb], writes=[xout_b[t] if isinstance(xout_b, list) else xout_b])
            if xT_out is None and after_tile is not None:
                after_tile(t)
        if xT_out is not None:
            for kc in range(8):
                emit_Tout_chunk(ntile - 1, kc)
        S.barrier()


def load_w_bf16(C, es, name, w_dram, ncols, q="pool"):
    w = es.enter_context(C.nc.sbuf_tensor(name, [P, 8, ncols], BF16))
    b = Buf(name)
    C.S.dma("pool", w[:], w_dram.rearrange("(kc p) f -> p kc f", p=P), writes=[b])
    return w, b


def proj_fm(C, w, w_b, c0, c1, xT, xT_b, pi):
    for kc in range(8):
        C.S.op("pe", lambda e, kc=kc: e.matmul(out=C.ps[pi][0:c1 - c0, :], lhsT=w[:, kc, c0:c1], rhs=xT[:, kc, :],
                                               start=(kc == 0), stop=(kc == 7)), reads=[w_b, xT_b], writes=[C.psb[pi]])


def proj_tm(C, w, w_b, c0, c1, xT, xT_b, blk, pi):
    for kc in range(8):
        C.S.op("pe", lambda e, kc=kc: e.matmul(out=C.ps[pi][:, 0:c1 - c0], lhsT=xT[:, kc, blk * 128:(blk + 1) * 128],
                                               rhs=w[:, kc, c0:c1], start=(kc == 0), stop=(kc == 7)),
               reads=[w_b, xT_b], writes=[C.psb[pi]])


def fetch_xT(C, xfull, xfull_b, t, xTs, xT_bs):
    k = t % 2
    C.S.dma("sp", xTs[k][:], xfull(t), reads=[xfull_b[t] if isinstance(xfull_b, list) else xfull_b], writes=[xT_bs[k]])
    return xTs[k], xT_bs[k]


def sb_phase(C, xfull, xfull_b, w_sb, yT, yT_b, TT, consts, tag):
    nc, S = C.nc, C.S
    ntile = TT // 512
    with contextlib.ExitStack() as es:
        w, w_b = load_w_bf16(C, es, f"wsb{tag}", w_sb, 384)
        QT = es.enter_context(nc.sbuf_tensor(f"sbQT{tag}", [P, TT], BF16))
        KT = es.enter_context(nc.sbuf_tensor(f"sbKT{tag}", [P, TT], BF16))
        V = es.enter_context(nc.sbuf_tensor(f"sbV{tag}", [P, TT // 128, 128], BF16))
        xTs = [es.enter_context(nc.sbuf_tensor(f"sbxT{tag}{k}", [P, 8, 512], BF16)) for k in range(2)]
        xT_bs = [Buf(), Buf()]
        negmask = es.enter_context(nc.sbuf_tensor(f"sbnm{tag}", [P, 4, 512], BF16))
        identb = es.enter_context(nc.sbuf_tensor(f"sbidb{tag}", [P, P], BF16))
        ntri = es.enter_context(nc.sbuf_tensor(f"sbntri{tag}", [P, P], BF16))
        ones = es.enter_context(nc.sbuf_tensor(f"sbones{tag}", [P, P], BF16))
        one1 = es.enter_context(nc.sbuf_tensor(f"sbone1{tag}", [P, 1], F32))
        cb = Buf()
        S.dma("pool", negmask[:], consts["sb_negmask"], writes=[cb])
        S.dma("pool", identb[:], consts["ident"], writes=[cb])
        S.dma("pool", ntri[:], consts["ntri"], writes=[cb])
        S.dma("pool", ones[:], consts["ones"], writes=[cb])
        S.op("pool", lambda e: e.memset(one1[:], 1.0), writes=[cb])
        x_b, xT_b = Buf(), Buf()
        QT_b = [Buf() for _ in range(ntile)]
        KT_b = [Buf() for _ in range(ntile)]
        V_b = [Buf() for _ in range(ntile)]
        xv = xfull if callable(xfull) else (lambda t, _v=xfull.rearrange("(t b p) d -> t p b d", b=4, p=P): _v[t])
        for t in range(ntile):
            xT, xT_b = fetch_xT(C, xfull, xfull_b, t, xTs, xT_bs)
            proj_fm(C, w, w_b, 0, 128, xT, xT_b, 0)
            S.op("act", lambda e, t=t: e.mul(out=QT[:, t * 512:(t + 1) * 512], in_=C.ps[0][:], mul=0.125),
                 reads=[C.psb[0]], writes=[QT_b[t]])
            proj_fm(C, w, w_b, 128, 256, xT, xT_b, 1)
            S.op("dve", lambda e, t=t: e.tensor_copy(out=KT[:, t * 512:(t + 1) * 512], in_=C.ps[1][:]),
                 reads=[C.psb[1]], writes=[KT_b[t]])
            for blk in range(4):
                pi = 2 + blk % 2
                proj_tm(C, w, w_b, 256, 384, xT, xT_b, blk, pi)
                S.op("act" if blk % 2 else "dve",
                     (lambda e, t=t, blk=blk, pi=pi: e.copy(out=V[:, t * 4 + blk, :], in_=C.ps[pi][:, 0:128])) if blk % 2 else
                     (lambda e, t=t, blk=blk, pi=pi: e.tensor_copy(out=V[:, t * 4 + blk, :], in_=C.ps[pi][:, 0:128])),
                     reads=[C.psb[pi]], writes=[V_b[t]])
        NCH = 4
        Ech = [es.enter_context(nc.sbuf_tensor(f"sbE2{tag}{c}", [P, 512], F32)) for c in range(NCH)]
        Lch = [es.enter_context(nc.sbuf_tensor(f"sbL2{tag}{c}", [P, 512], F32)) for c in range(NCH)]
        Sch = [[es.enter_context(nc.sbuf_tensor(f"sbS2{tag}{c}{k}", [P, 512], F32)) for k in range(2)] for c in range(NCH)]
        Wch = [es.enter_context(nc.sbuf_tensor(f"sbW2{tag}{c}", [P, 512], BF16)) for c in range(NCH)]
        Lhi = [es.enter_context(nc.sbuf_tensor(f"sbLh{tag}{c}", [P, 512], BF16)) for c in range(NCH)]
        Llo = [es.enter_context(nc.sbuf_tensor(f"sbLl{tag}{c}", [P, 512], BF16)) for c in range(NCH)]
        Shi = [es.enter_context(nc.sbuf_tensor(f"sbSh{tag}{c}", [P, 512], BF16)) for c in range(NCH)]
        Slo = [es.enter_context(nc.sbuf_tensor(f"sbSl{tag}{c}", [P, 512], BF16)) for c in range(NCH)]
        Lh_b = [Buf() for _ in range(NCH)]
        Ll_b = [Buf() for _ in range(NCH)]
        Sh_b = [Buf() for _ in range(NCH)]
        Sl_b = [Buf() for _ in range(NCH)]
        ych = [es.enter_context(nc.sbuf_tensor(f"sby2{tag}{c}", [P, 512], BF16)) for c in range(NCH)]
        E_b = [Buf() for _ in range(NCH)]
        L_b = [Buf() for _ in range(NCH)]
        S_b = [[Buf(), Buf()] for _ in range(NCH)]
        W_b = [Buf() for _ in range(NCH)]
        y_b = [Buf() for _ in range(NCH)]
        order = []
        lo, hi = 0, ntile - 1
        while lo <= hi:
            order.append(hi)
            hi -= 1
            if lo <= hi:
                order.append(lo)
                lo += 1
        queues = [[], []]
        load = [0, 0]
        for ti in sorted(range(ntile), key=lambda q: -q):
            sidx = 0 if load[0] <= load[1] else 1
            queues[sidx].append(ti)
            load[sidx] += 4 * ti + 4
        state = [None] * NCH
        qpos = [0, 0]

        def next_tile(slot):
            if qpos[slot] < len(queues[slot]):
                ti = queues[slot][qpos[slot]]
                qpos[slot] += 1
                return ti
            return None
        for slot in range(2):
            ti = next_tile(slot)
            for h in range(2):
                state[slot * 2 + h] = None if ti is None else [ti, 0]
        while any(st is not None for st in state):
            act = [c for c in range(NCH) if state[c] is not None]
            info = {}
            for c in act:
                i, n = state[c]
                nsteps = 4 * i + 4
                jb = 4 * i + 3 - n
                info[c] = dict(i=i, n=n, nsteps=nsteps, jb=jb, diag=jb >= 4 * i, r=jb - 4 * i, kt=jb // 4, h=c % 2,
                               hs=slice((c % 2) * 64, (c % 2) * 64 + 64), pz=c, py=4 + c // 2, k=n % 2)
            for c in act:
                f = info[c]
                if f["n"] > 0:
                    k = f["k"]
                    S.op("dve", lambda e, c=c, k=k: e.tensor_copy(out=Shi[c][:], in_=Sch[c][k][:]), reads=[S_b[c][k]], writes=[Sh_b[c]])
                    S.op("dve", lambda e, c=c, k=k: e.tensor_tensor(out=Slo[c][:], in0=Sch[c][k][:], in1=Shi[c][:], op=ALU.subtract),
                         reads=[S_b[c][k], Sh_b[c]], writes=[Sl_b[c]])
            for c in act:
                f = info[c]
                S.op("pe", lambda e, f=f: e.matmul(out=C.ps[f["pz"]][:], lhsT=KT[f["hs"], f["jb"] * 128:(f["jb"] + 1) * 128],
                                                   rhs=QT[f["hs"], f["i"] * 512:(f["i"] + 1) * 512], start=True, stop=not f["diag"]),
                     reads=[KT_b[f["kt"]], QT_b[f["i"]]], writes=[C.psb[f["pz"]]])
                if f["diag"]:
                    S.op("pe", lambda e, f=f: e.matmul(out=C.ps[f["pz"]][:], lhsT=identb[:], rhs=negmask[:, f["r"], :], start=False, stop=True),
                         reads=[cb], writes=[C.psb[f["pz"]]])
            for c in act:
                f = info[c]
                S.op("act", lambda e, c=c, f=f: e.activation(out=Ech[c][:], in_=C.ps[f["pz"]][:], func=AF.Exp),
                     reads=[C.psb[f["pz"]]], writes=[E_b[c]])
            for c in act:
                S.op("act", lambda e, c=c: e.activation(out=Lch[c][:], in_=Ech[c][:], func=AF.Ln, bias=one1[:], scale=1.0),
                     reads=[E_b[c], cb], writes=[L_b[c]])
            for c in act:
                if c < 2:
                    S.op("act", lambda e, c=c: e.copy(out=Lhi[c][:], in_=Lch[c][:]), reads=[L_b[c]], writes=[Lh_b[c]])
                else:
                    S.op("dve", lambda e, c=c: e.tensor_copy(out=Lhi[c][:], in_=Lch[c][:]), reads=[L_b[c]], writes=[Lh_b[c]])
                S.op("dve", lambda e, c=c: e.tensor_tensor(out=Llo[c][:], in0=Lch[c][:], in1=Lhi[c][:], op=ALU.subtract),
                     reads=[L_b[c], Lh_b[c]], writes=[Ll_b[c]])
            for c in act:
                f = info[c]
                if f["n"] > 0:
                    S.op("pe", lambda e, c=c, f=f: e.matmul(out=C.ps[f["pz"]][:], lhsT=ones[:], rhs=Shi[c][:], start=False, stop=False,
                                                            skip_group_check=True), reads=[cb, Sh_b[c]], writes=[C.psb[f["pz"]]])
                    S.op("pe", lambda e, c=c, f=f: e.matmul(out=C.ps[f["pz"]][:], lhsT=ones[:], rhs=Slo[c][:], start=False, stop=False,
                                                            skip_group_check=True), reads=[cb, Sl_b[c]], writes=[C.psb[f["pz"]]])
                S.op("pe", lambda e, c=c, f=f: e.matmul(out=C.ps[f["pz"]][:], lhsT=ntri[:], rhs=Lhi[c][:], start=False, stop=False,
                                                        skip_group_check=True), reads=[cb, Lh_b[c]], writes=[C.psb[f["pz"]]])
                S.op("pe", lambda e, c=c, f=f: e.matmul(out=C.ps[f["pz"]][:], lhsT=ntri[:], rhs=Llo[c][:], start=False, stop=True,
                                                        skip_group_check=True), reads=[cb, Ll_b[c]], writes=[C.psb[f["pz"]]])
            for c in act:
                f = info[c]
                S.op("act", lambda e, c=c, f=f: e.activation(out=Wch[c][:], in_=C.ps[f["pz"]][:], func=AF.Exp),
                     reads=[C.psb[f["pz"]]], writes=[W_b[c]])
                if f["n"] < f["nsteps"] - 1:
                    k, k2 = f["k"], 1 - f["k"]
                    if f["n"] == 0:
                        S.op("pool", lambda e, c=c, k2=k2: e.tensor_copy(out=Sch[c][k2][:], in_=Lch[c][:]), reads=[L_b[c]], writes=[S_b[c][k2]])
                    else:
                        S.op("pool", lambda e, c=c, k=k, k2=k2: e.tensor_tensor(out=Sch[c][k2][:], in0=Sch[c][k][:], in1=Lch[c][:], op=ALU.add),
                             reads=[L_b[c], S_b[c][k]], writes=[S_b[c][k2]])
            for c in act:
                f = info[c]
                po = (c % 2) * 64
                S.op("pe", lambda e, c=c, f=f, po=po: e.matmul(out=C.ps[f["py"]][po:po + 64, :], lhsT=V[:, f["jb"], f["hs"]], rhs=Wch[c][:],
                                                               start=(f["n"] == 0), stop=(f["n"] == f["nsteps"] - 1), skip_group_check=True),
                     reads=[V_b[f["kt"]], W_b[c]], writes=[C.psb[f["py"]]])
            for c in act:
                f = info[c]
                if f["n"] == f["nsteps"] - 1:
                    po = (c % 2) * 64
                    S.op("dve", lambda e, c=c, f=f, po=po: e.tensor_copy(out=ych[c][po:po + 64, :], in_=C.ps[f["py"]][po:po + 64, :]),
                         reads=[C.psb[f["py"]]], writes=[y_b[c]])
                    S.dma("sp", yT[f["h"] * 64:(f["h"] + 1) * 64, f["i"] * 512:(f["i"] + 1) * 512], ych[c][po:po + 64, :],
                          reads=[y_b[c]], writes=[yT_b])
                    state[c] = "done"
                else:
                    state[c][1] += 1
            for slot in range(2):
                cs = [slot * 2, slot * 2 + 1]
                if all(state[c] == "done" for c in cs):
                    ti = next_tile(slot)
                    for c in cs:
                        state[c] = None if ti is None else [ti, 0]
        S.barrier()


def make_consts_np():
    c = {}
    c["ident"] = np.eye(P, dtype=np.float32)
    j = np.arange(P)[:, None]
    s = np.arange(P)[None, :]
    c["ntri"] = -(j >= s).astype(np.float32)
    c["ones"] = -np.ones((P, P), np.float32)
    nm = np.zeros((P, 4, 512), np.float32)
    for r in range(4):
        key = 128 * r + np.arange(P)[:, None]
        col = np.arange(512)[None, :]
        nm[:, r, :] = np.where(key < col, 0.0, -30000.0)
    c["sb_negmask"] = nm
    make_swa_consts_np(c)
    make_ml_consts_np(c)
    return c


def dram_ap(t_ap, offset, pattern):
    return bass.AP(tensor=t_ap.tensor, offset=offset, ap=pattern)


def swa_phase(C, xfull, xfull_b, w_swa, rb_dram, sink_dram, frev_scr, yT, yT_b, TT, consts, tag):
    nc, S = C.nc, C.S
    ntile = TT // 512
    nblk = TT // 128
    C.set_psum(2)
    with contextlib.ExitStack() as es:
        w, w_b = load_w_bf16(C, es, f"wsw{tag}", w_swa, 448)
        QT = [es.enter_context(nc.sbuf_tensor(f"swQT{tag}{g}", [P, TT], BF16)) for g in range(2)]
        KT = es.enter_context(nc.sbuf_tensor(f"swKT{tag}", [P, TT], BF16))
        V = es.enter_context(nc.sbuf_tensor(f"swV{tag}", [P, nblk, 64], BF16))
        xTs = [es.enter_context(nc.sbuf_tensor(f"swxT{tag}{k}", [P, 8, 512], BF16)) for k in range(2)]
        xT_bs = [Buf(), Buf()]
        identb = es.enter_context(nc.sbuf_tensor(f"swidb{tag}", [P, P], BF16))
        Jm = es.enter_context(nc.sbuf_tensor(f"swJ{tag}", [P, P], F32))
        rb = es.enter_context(nc.sbuf_tensor(f"swrb{tag}", [P, P], F32))
        oh = es.enter_context(nc.sbuf_tensor(f"swoh{tag}", [P, 384], F32))
        fneg = es.enter_context(nc.sbuf_tensor(f"swfn{tag}", [4, 384], F32))
        frev = es.enter_context(nc.sbuf_tensor(f"swfr{tag}", [4, 384], F32))
        Hk = es.enter_context(nc.sbuf_tensor(f"swH{tag}", [P, 4, 256], F32))
        bias = es.enter_context(nc.sbuf_tensor(f"swbias{tag}", [P, 4, 256], F32))
        sink = es.enter_context(nc.sbuf_tensor(f"swsink{tag}", [P, 4], F32))
        Sb = es.enter_context(nc.sbuf_tensor(f"swS{tag}", [P, 4, 256], F32))
        pf = es.enter_context(nc.sbuf_tensor(f"swp{tag}", [P, 4, 256], F32))
        pn = es.enter_context(nc.sbuf_tensor(f"swpn{tag}", [P, 4, 256], BF16))
        pT = es.enter_context(nc.sbuf_tensor(f"swpT{tag}", [P, 8, 128], BF16))
        small = es.enter_context(nc.sbuf_tensor(f"swsm{tag}", [P, 6, 4], F32))
        yo = es.enter_context(nc.sbuf_tensor(f"swyo{tag}", [64, 4, 512], BF16))
        cb, x_b, xT_b = Buf(), Buf(), Buf()
        S.dma("pool", identb[:], consts["ident"], writes=[cb])
        S.dma("sp", Jm[:], consts["J"], writes=[cb])
        S.dma("sp", rb[:], rb_dram, writes=[cb])
        S.dma("sp", oh[:], consts["swa_oh"], writes=[cb])
        S.dma("sp", fneg[:], consts["swa_fneg"], writes=[cb])
        S.dma("sp", sink[:], sink_dram, writes=[cb])
        S.op("pe", lambda e: e.matmul(out=C.ps[0][:, 0:384], lhsT=rb[:], rhs=oh[:], start=True, stop=True),
             reads=[cb], writes=[C.psb[0]])
        fr_b, scr_b, H_b, bias_b = Buf(), Buf(), Buf(), Buf()
        S.op("dve", lambda e: e.tensor_tensor(out=frev[:], in0=C.ps[0][0:4, 0:384], in1=fneg[:], op=ALU.add),
             reads=[C.psb[0], cb], writes=[fr_b])
        S.dma("sp", frev_scr, frev[:], reads=[fr_b], writes=[scr_b])
        S.dma("sp", Hk[:], dram_ap(frev_scr, 0, [[1, 128], [384, 4], [1, 256]]), reads=[scr_b], writes=[H_b])
        for hh in range(2):
            S.op("pe", lambda e, hh=hh: e.matmul(out=C.ps[1 + hh][:], lhsT=Jm[:], rhs=Hk[:, 2 * hh:2 * hh + 2, :],
                                                 start=True, stop=True), reads=[cb, H_b], writes=[C.psb[1 + hh]])
            S.op("dve", lambda e, hh=hh: e.tensor_copy(out=bias[:, 2 * hh:2 * hh + 2, :], in_=C.ps[1 + hh][:]),
                 reads=[C.psb[1 + hh]], writes=[bias_b])
        QT_b = [Buf() for _ in range(ntile)]
        KT_b = [Buf() for _ in range(ntile)]
        V_b = [Buf() for _ in range(ntile)]
        xv = xfull if callable(xfull) else (lambda t, _v=xfull.rearrange("(t b p) d -> t p b d", b=4, p=P): _v[t])
        for t in range(ntile):
            xT, xT_b = fetch_xT(C, xfull, xfull_b, t, xTs, xT_bs)
            for g in range(2):
                proj_fm(C, w, w_b, g * 128, (g + 1) * 128, xT, xT_b, g)
                S.op("act", lambda e, t=t, g=g: e.mul(out=QT[g][:, t * 512:(t + 1) * 512], in_=C.ps[g][:], mul=0.125),
                     reads=[C.psb[g]], writes=[QT_b[t]])
            proj_fm(C, w, w_b, 256, 384, xT, xT_b, 2)
            S.op("dve", lambda e, t=t: e.tensor_copy(out=KT[:, t * 512:(t + 1) * 512], in_=C.ps[2][:]),
                 reads=[C.psb[2]], writes=[KT_b[t]])
            for blk in range(4):
                pi = 3 + blk % 2
                proj_tm(C, w, w_b, 384, 448, xT, xT_b, blk, pi)
                S.op("dve", lambda e, t=t, blk=blk, pi=pi: e.tensor_copy(out=V[:, t * 4 + blk, :], in_=C.ps[pi][:, 0:64]),
                     reads=[C.psb[pi]], writes=[V_b[t]])
        NS = 2
        Sb2 = [Sb] + [es.enter_context(nc.sbuf_tensor(f"swS{tag}b", [P, 4, 256], F32))]
        pf2 = [pf] + [es.enter_context(nc.sbuf_tensor(f"swp{tag}b", [P, 4, 256], F32))]
        pn2 = [pn] + [es.enter_context(nc.sbuf_tensor(f"swpn{tag}b", [P, 4, 256], BF16))]
        pT2 = [pT] + [es.enter_context(nc.sbuf_tensor(f"swpT{tag}b", [P, 8, 128], BF16))]
        sm2 = [small] + [es.enter_context(nc.sbuf_tensor(f"swsm{tag}b", [P, 6, 4], F32))]
        S_b = [Buf(), Buf()]
        p_b = [Buf(), Buf()]
        pn_b = [Buf(), Buf()]
        pT_b = [Buf(), Buf()]
        sm_b = [Buf(), Buf()]
        yo_b = Buf()
        for n0 in range(0, nblk, NS):
            blks = [(n0 + a, a) for a in range(NS) if n0 + a < nblk]
            kwd = {n: (128 if n == 0 else 256) for n, a in blks}
            for n, a in blks:
                kw = kwd[n]
                k0 = 256 - kw
                t_q = n // 4
                for h in range(4):
                    g, hs = h % 2, slice((h // 2) * 64, (h // 2) * 64 + 64)
                    bank = 2 * a + h // 2
                    col = (h % 2) * 256
                    S.op("pe", lambda e, g=g, hs=hs, bank=bank, col=col, n=n, kw=kw, k0=k0: e.matmul(
                        out=C.ps[bank][:, col + k0:col + 256], lhsT=QT[g][hs, n * 128:(n + 1) * 128],
                        rhs=KT[hs, (n + 1) * 128 - kw:(n + 1) * 128], start=True, stop=True, skip_group_check=True),
                        reads=[QT_b[t_q], KT_b[t_q], KT_b[max(0, (n - 1) // 4)]], writes=[C.psb[bank]])
            for n, a in blks:
                k0 = 256 - kwd[n]
                mx, negm, dd, esk, rs, rden = [sm2[a][:, i, :] for i in range(6)]
                for bk in range(2):
                    bank = 2 * a + bk
                    S.op("dve", lambda e, bank=bank, bk=bk, k0=k0, a=a: e.tensor_tensor(
                        out=Sb2[a][:, 2 * bk:2 * bk + 2, k0:256],
                        in0=C.ps[bank][:].rearrange("p (h k) -> p h k", h=2)[:, :, k0:256],
                        in1=bias[:, 2 * bk:2 * bk + 2, k0:256], op=ALU.add),
                        reads=[C.psb[bank], bias_b], writes=[S_b[a]])
                S.op("dve", lambda e, k0=k0, a=a, mx=mx: e.tensor_reduce(out=mx, in_=Sb2[a][:, :, k0:256], axis=AX.X, op=ALU.max),
                     reads=[S_b[a]], writes=[sm_b[a]])
                S.op("dve", lambda e, mx=mx: e.tensor_tensor(out=mx, in0=mx, in1=sink[:], op=ALU.max), reads=[sm_b[a], cb], writes=[sm_b[a]])
                S.op("dve", lambda e, mx=mx, negm=negm: e.tensor_scalar(out=negm, in0=mx, scalar1=-1.0, scalar2=None, op0=ALU.mult),
                     reads=[sm_b[a]], writes=[sm_b[a]])
                S.op("dve", lambda e, mx=mx, dd=dd: e.tensor_tensor(out=dd, in0=sink[:], in1=mx, op=ALU.subtract), reads=[sm_b[a], cb], writes=[sm_b[a]])
            for n, a in blks:
                k0 = 256 - kwd[n]
                mx, negm, dd, esk, rs, rden = [sm2[a][:, i, :] for i in range(6)]
                for h in range(4):
                    S.op("act", lambda e, h=h, k0=k0, a=a, negm=negm, rs=rs: e.activation(
                        out=pf2[a][:, h, k0:256], in_=Sb2[a][:, h, k0:256], func=AF.Exp, bias=negm[:, h:h + 1], scale=1.0, accum_out=rs[:, h:h + 1]),
                        reads=[S_b[a], sm_b[a]], writes=[p_b[a], sm_b[a]])
                S.op("act", lambda e, esk=esk, dd=dd: e.activation(out=esk, in_=dd, func=AF.Exp), reads=[sm_b[a]], writes=[sm_b[a]])
            for n, a in blks:
                k0 = 256 - kwd[n]
                mx, negm, dd, esk, rs, rden = [sm2[a][:, i, :] for i in range(6)]
                S.op("dve", lambda e, rden=rden, rs=rs, esk=esk: e.tensor_tensor(out=rden, in0=rs, in1=esk, op=ALU.add), reads=[sm_b[a]], writes=[sm_b[a]])
                S.op("dve", lambda e, rden=rden: e.reciprocal(out=rden, in_=rden), reads=[sm_b[a]], writes=[sm_b[a]])
                for h in range(4):
                    S.op("dve" if h % 2 else "pool", lambda e, h=h, k0=k0, a=a, rden=rden: e.tensor_scalar(
                        out=pn2[a][:, h, k0:256], in0=pf2[a][:, h, k0:256], scalar1=rden[:, h:h + 1], scalar2=None, op0=ALU.mult),
                        reads=[p_b[a], sm_b[a]], writes=[pn_b[a]])
            for n, a in blks:
                kw = kwd[n]
                k0 = 256 - kw
                nkb = kw // 128
                for h in range(4):
                    for kb in range(nkb):
                        idx = h * 2 + kb
                        S.op("pe", lambda e, h=h, kb=kb, idx=idx, k0=k0, a=a: e.transpose(
                            out=C.ps_bfs[a][:, idx * 128:(idx + 1) * 128], in_=pn2[a][:, h, k0 + kb * 128:k0 + (kb + 1) * 128],
                            identity=identb[:]), reads=[pn_b[a], cb], writes=[C.psbf_b[a]])
            for n, a in blks:
                if a == 0:
                    S.op("act", lambda e, a=a: e.copy(out=pT2[a][:].rearrange("p a b -> p (a b)"), in_=C.ps_bfs[a][:]), reads=[C.psbf_b[a]], writes=[pT_b[a]])
                else:
                    S.op("dve", lambda e, a=a: e.tensor_copy(out=pT2[a][:].rearrange("p a b -> p (a b)"), in_=C.ps_bfs[a][:]), reads=[C.psbf_b[a]], writes=[pT_b[a]])
            for n, a in blks:
                nkb = kwd[n] // 128
                po = 4 + a
                for h in range(4):
                    for kb in range(nkb):
                        idx = h * 2 + kb
                        kblk = n - (nkb - 1) + kb
                        hd = (h % 2) * 2 + h // 2
                        S.op("pe", lambda e, hd=hd, kb=kb, idx=idx, kblk=kblk, nkb=nkb, a=a, po=po: e.matmul(
                            out=C.ps[po][0:64, hd * 128:(hd + 1) * 128], lhsT=V[:, kblk, :], rhs=pT2[a][:, idx, :],
                            start=(kb == 0), stop=(kb == nkb - 1), skip_group_check=True),
                            reads=[V_b[kblk // 4], pT_b[a]], writes=[C.psb[po]])
            for n, a in blks:
                po = 4 + a
                S.op("dve" if a == 0 else "act",
                     (lambda e, n=n, po=po: e.tensor_copy(out=yo[:, :, (n % 4) * 128:(n % 4 + 1) * 128],
                                                          in_=C.ps[po][0:64, :].rearrange("p (h q) -> p h q", h=4))) if a == 0 else
                     (lambda e, n=n, po=po: e.copy(out=yo[:, :, (n % 4) * 128:(n % 4 + 1) * 128],
                                                   in_=C.ps[po][0:64, :].rearrange("p (h q) -> p h q", h=4))),
                     reads=[C.psb[po]], writes=[yo_b])
                if n % 4 == 3:
                    S.dma("sp", yT.rearrange("(h d) t -> d h t", d=64)[:, :, (n // 4) * 512:(n // 4 + 1) * 512], yo[:],
                          reads=[yo_b], writes=[yT_b])
        S.barrier()
        C.set_psum(1)


def t5_bucket_np(dist):
    max_exact = 16
    d = np.maximum(dist, 1)
    large = max_exact + (np.log(d / max_exact) / np.log(128 / max_exact) * (32 - max_exact)).astype(np.int32)
    large = np.minimum(large, 31)
    return np.where(dist < max_exact, dist, large).astype(np.int32)


def make_swa_consts_np(c):
    a = np.arange(384)
    dist = 255 - a
    valid = (dist >= 0) & (dist < 128)
    bucket = t5_bucket_np(np.clip(dist, 0, None))
    oh = np.zeros((32, 384), np.float32)
    oh[bucket[valid], a[valid]] = 1.0
    c["swa_oh"] = np.pad(oh, ((0, 96), (0, 0)))
    c["swa_fneg"] = np.tile(np.where(valid, 0.0, -30000.0).astype(np.float32)[None, :], (4, 1))
    c["J"] = np.eye(P, dtype=np.float32)[::-1].copy()
    return c


def mlstm_phase(C, xfull, xfull_b, w_ml, cw_dram, cb_dram, ifb_dram, ng_dram, yT, yT_b, TT, consts, tag):
    nc, S = C.nc, C.S
    ntile = TT // 512
    nb = TT // 128
    n2 = 2 * nb
    with contextlib.ExitStack() as es:
        w, w_b = load_w_bf16(C, es, f"wml{tag}", w_ml, 640)
        sb = lambda name, shape, dt=F32: es.enter_context(nc.sbuf_tensor(f"ml{name}{tag}", shape, dt))
        QT = sb("QT", [P, TT], BF16)
        KT = sb("KT", [P, TT], BF16)
        Vext = sb("Vext", [P, nb, 2, 66], BF16)
        gso = sb("gso", [P, nb, 128])
        G4 = sb("G4", [P, nb, 4])
        xTs = [sb(f"xT{k}", [P, 8, 512], BF16) for k in range(2)]
        xT_bs = [Buf(), Buf()]
        raws = [sb(f"raw{k}", [P, 2, 515]) for k in range(2)]
        accs = [sb(f"acc{k}", [P, 2, 512]) for k in range(2)]
        cw = sb("cw", [P, 8])
        cbt = sb("cb", [P, 2])
        ifb = sb("ifb", [P, 4])
        nfb = sb("nfb", [P, 2])
        ng = sb("ng", [P, 128])
        identb = sb("idb", [P, P], BF16)
        ntriT = sb("ntriT", [P, P])
        mask8 = sb("mask8", [P, P])
        e0 = sb("e0", [P, P])
        e127 = sb("e127", [P, P])
        one1 = sb("one1", [P, 1])
        cb_ = Buf()
        S.dma("pool", identb[:], consts["ident"], writes=[cb_])
        S.dma("sp", ntriT[:], consts["ntriT"], writes=[cb_])
        S.dma("sp", mask8[:], consts["mask8"], writes=[cb_])
        S.dma("sp", e0[:], consts["e0ones"], writes=[cb_])
        S.dma("sp", e127[:], consts["e127ones"], writes=[cb_])
        S.dma("sp", cw[:], cw_dram, writes=[cb_])
        S.dma("sp", cbt[:], cb_dram, writes=[cb_])
        S.dma("sp", ifb[:], ifb_dram, writes=[cb_])
        S.dma("sp", ng[:], ng_dram, writes=[cb_])
        S.op("pool", lambda e: e.memset(one1[:], 1.0), writes=[cb_])
        S.op("dve", lambda e: e.tensor_scalar(out=nfb[:], in0=ifb[:, 2:4], scalar1=-1.0, scalar2=None, op0=ALU.mult),
             reads=[cb_], writes=[cb_])
        x_b, xT_b, V_b, gso_b, G_b = Buf(), Buf(), Buf(), Buf(), Buf()
        raw_bs, acc_bs = [Buf(), Buf()], [Buf(), Buf()]
        QT_b = [Buf() for _ in range(ntile)]
        KT_b = [Buf() for _ in range(ntile)]
        for k in range(2):
            S.op("pool", lambda e, k=k: e.memset(raws[k][:], 0.0), writes=[raw_bs[k]])
        S.op("pool", lambda e: e.memset(Vext[:], 1.0), writes=[V_b])
        xv = xfull if callable(xfull) else (lambda t, _v=xfull.rearrange("(t b p) d -> t p b d", b=4, p=P): _v[t])
        import os
        mlstage = int(os.environ.get("ML_STAGE", "9"))
        for t in range(ntile):
            if mlstage < -1:
                break
            xT, xT_b = fetch_xT(C, xfull, xfull_b, t, xTs, xT_bs)
            raw, raw_b, acc, acc_b = raws[t % 2], raw_bs[t % 2], accs[t % 2], acc_bs[t % 2]
            if t > 0:
                S.op("pool", lambda e, raw=raw, t=t: e.tensor_copy(out=raw[:, :, 0:3], in_=raws[(t - 1) % 2][:, :, 512:515]),
                     reads=[raw_bs[(t - 1) % 2]], writes=[raw_b])
            for qk in range(2):
                proj_fm(C, w, w_b, qk * 128, (qk + 1) * 128, xT, xT_b, qk)
                S.op("act", lambda e, qk=qk, raw=raw: e.copy(out=raw[:, qk, 3:515], in_=C.ps[qk][:]), reads=[C.psb[qk]], writes=[raw_b])
            for qk in range(2):
                S.op("pool", lambda e, qk=qk, raw=raw, acc=acc: e.tensor_scalar(out=acc[:, qk, :], in0=raw[:, qk, 0:512], scalar1=cw[:, 4 * qk:4 * qk + 1],
                                                              scalar2=None, op0=ALU.mult), reads=[raw_b, cb_], writes=[acc_b])
                for j in range(1, 4):
                    S.op("dve", lambda e, qk=qk, j=j, raw=raw, acc=acc: e.scalar_tensor_tensor(
                        out=acc[:, qk, :], in0=raw[:, qk, j:j + 512], scalar=cw[:, 4 * qk + j:4 * qk + j + 1], in1=acc[:, qk, :],
                        op0=ALU.mult, op1=ALU.add), reads=[raw_b, cb_, acc_b], writes=[acc_b])
                dst, dst_b = (QT, QT_b) if qk == 0 else (KT, KT_b)
                S.op("act", lambda e, qk=qk, dst=dst, t=t, acc=acc: e.activation(out=dst[:, t * 512:(t + 1) * 512], in_=acc[:, qk, :], func=AF.Silu,
                                                                        bias=cbt[:, qk:qk + 1], scale=1.0),
                     reads=[acc_b, cb_], writes=[dst_b[t]])
            for blk in range(4):
                if mlstage < 0:
                    break
                pi = 2 + blk % 2
                bi = t * 4 + blk
                proj_tm(C, w, w_b, 256, 512, xT, xT_b, blk, pi)
                proj_tm(C, w, w_b, 512, 640, xT, xT_b, blk, 4 + blk % 2)
                sub = int(os.environ.get("ML_SUB", "9"))
                if sub < 1:
                    continue
                S.op("dve", lambda e, pi=pi, bi=bi: e.tensor_copy(out=Vext[:, bi, :, 0:64],
                                                                  in_=C.ps[pi][:, 0:128].rearrange("p (h d) -> p h d", h=2)),
                     reads=[C.psb[pi]], writes=[V_b])
                if sub < 2:
                    continue
                S.op("act", lambda e, pi=pi, bi=bi: e.activation(out=gso[:, bi, :], in_=C.ps[pi][:, 128:256], func=AF.Exp, scale=-1.0),
                     reads=[C.psb[pi]], writes=[gso_b])
                S.op("pool", lambda e, bi=bi: e.tensor_scalar(out=gso[:, bi, :], in0=gso[:, bi, :], scalar1=1.0, scalar2=None, op0=ALU.add),
                     reads=[gso_b], writes=[gso_b])
                S.op("dve", lambda e, bi=bi: e.reciprocal(out=gso[:, bi, :], in_=gso[:, bi, :]), reads=[gso_b], writes=[gso_b])
                if sub < 3:
                    continue
                S.op("dve", lambda e, blk=blk, bi=bi: e.tensor_copy(out=G4[:, bi, :], in_=C.ps[4 + blk % 2][:, 0:4]),
                     reads=[C.psb[4 + blk % 2]], writes=[G_b])
                S.op("pool", lambda e, bi=bi: e.tensor_tensor(out=gso[:, bi, :], in0=gso[:, bi, :], in1=ng[:], op=ALU.mult),
                     reads=[gso_b, cb_], writes=[gso_b])
        import os
        mlstage = int(os.environ.get("ML_STAGE", "9"))
        if mlstage < 1:
            S.barrier()
            return
        icol = sb("icol", [P, 2, nb])
        lf = sb("lf", [P, 2, nb])
        bcol = sb("bcol", [P, 2, nb])
        acol = sb("acol", [P, 2, nb])
        aT = sb("aT", [P, P])
        cm_tok = sb("cm_tok", [P, n2])
        amax_bc = sb("amax_bc", [P, n2])
        cmT = sb("cmT", [P, P])
        RW = sb("RW", [P, 3, P])
        mnext = sb("mnext", [1, P])
        mprev_bc = sb("mprevbc", [P, n2])
        mref_bc = sb("mrefbc", [P, n2])
        Mt = sb("Mt", [P, n2])
        r_t = sb("r_t", [P, n2])
        u_t = sb("u_t", [P, n2])
        eb_t = sb("eb_t", [P, n2])
        sc_bc = sb("sc_bc", [P, n2])
        tmp = sb("tmpg", [P, n2])
        g_b = Buf()
        fl = lambda tl: tl[:].rearrange("p h c -> p (h c)")
        for h in range(2):
            S.op("act", lambda e, h=h: e.activation(out=icol[:, h, :], in_=G4[:, :, h], func=AF.Identity, bias=ifb[:, h:h + 1], scale=1.0),
                 reads=[G_b, cb_], writes=[g_b])
            S.op("act", lambda e, h=h: e.activation(out=lf[:, h, :], in_=G4[:, :, 2 + h], func=AF.Exp, bias=nfb[:, h:h + 1], scale=-1.0),
                 reads=[G_b, cb_], writes=[g_b])
        S.op("act", lambda e: e.activation(out=fl(lf), in_=fl(lf), func=AF.Ln, bias=one1[:], scale=1.0), reads=[g_b, cb_], writes=[g_b])
        S.op("pe", lambda e: e.matmul(out=C.ps[0][:, 0:n2], lhsT=ntriT[:], rhs=fl(lf), start=True, stop=True),
             reads=[g_b, cb_], writes=[C.psb[0]])
        S.op("dve", lambda e: e.tensor_copy(out=fl(bcol), in_=C.ps[0][:, 0:n2]), reads=[C.psb[0]], writes=[g_b])
        S.op("dve", lambda e: e.tensor_tensor(out=fl(acol), in0=fl(icol), in1=fl(bcol), op=ALU.subtract), reads=[g_b], writes=[g_b])
        S.op("pe", lambda e: e.transpose(out=C.ps[1][0:n2, 0:128], in_=fl(acol), identity=C.ident[:]), reads=[g_b, C.ident_b], writes=[C.psb[1]])
        S.op("pool", lambda e: e.memset(aT[:], 0.0), writes=[g_b])
        S.op("pool", lambda e: e.memset(RW[:], 0.0), writes=[g_b])
        S.op("dve", lambda e: e.tensor_copy(out=aT[0:n2, :], in_=C.ps[1][0:n2, 0:128]), reads=[C.psb[1]], writes=[g_b])
        S.op("dve", lambda e: e.tensor_tensor_scan(out=cmT[:], data0=aT[:], data1=aT[:], initial=-1.0e30, op0=ALU.max, op1=ALU.max),
             reads=[g_b], writes=[g_b])
        S.op("pe", lambda e: e.transpose(out=C.ps[2][:, 0:128], in_=cmT[:], identity=C.ident[:]), reads=[g_b, C.ident_b], writes=[C.psb[2]])
        S.op("dve", lambda e: e.tensor_copy(out=cm_tok[:], in_=C.ps[2][:, 0:n2]), reads=[C.psb[2]], writes=[g_b])
        S.op("pe", lambda e: e.matmul(out=C.ps[3][:, 0:n2], lhsT=e127[:], rhs=cm_tok[:], start=True, stop=True), reads=[g_b, cb_], writes=[C.psb[3]])
        S.op("pe", lambda e: e.matmul(out=C.ps[4][:, 0:n2], lhsT=e127[:], rhs=fl(bcol), start=True, stop=True), reads=[g_b, cb_], writes=[C.psb[4]])
        S.op("dve", lambda e: e.tensor_copy(out=amax_bc[:], in_=C.ps[3][:, 0:n2]), reads=[C.psb[3]], writes=[g_b])
        S.op("dve", lambda e: e.tensor_copy(out=RW[:, 1, 0:n2], in_=C.ps[4][:, 0:n2]), reads=[C.psb[4]], writes=[g_b])
        S.op("dve", lambda e: e.tensor_copy(out=RW[:, 0, 0:n2], in_=amax_bc[:]), reads=[g_b], writes=[g_b])
        for h in range(2):
            S.op("dve", lambda e, h=h: e.tensor_tensor_scan(out=mnext[0:1, h * nb:(h + 1) * nb], data0=RW[0:1, 0, h * nb:(h + 1) * nb],
                                                            data1=RW[0:1, 1, h * nb:(h + 1) * nb], initial=0.0, op0=ALU.max, op1=ALU.add),
                 reads=[g_b], writes=[g_b])
            if nb > 1:
                S.op("dve", lambda e, h=h: e.tensor_copy(out=RW[0:1, 2, h * nb + 1:(h + 1) * nb], in_=mnext[0:1, h * nb:(h + 1) * nb - 1]),
                     reads=[g_b], writes=[g_b])
        S.op("pe", lambda e: e.matmul(out=C.ps[0][:, 0:n2], lhsT=e0[:], rhs=RW[:, 2, 0:n2], start=True, stop=True),
             reads=[g_b, cb_], writes=[C.psb[0]])
        S.op("dve", lambda e: e.tensor_copy(out=mprev_bc[:], in_=C.ps[0][:, 0:n2]), reads=[C.psb[0]], writes=[g_b])
        S.op("dve", lambda e: e.tensor_tensor(out=mref_bc[:], in0=amax_bc[:], in1=mprev_bc[:], op=ALU.max), reads=[g_b], writes=[g_b])
        S.op("dve", lambda e: e.tensor_tensor(out=Mt[:], in0=cm_tok[:], in1=mprev_bc[:], op=ALU.max), reads=[g_b], writes=[g_b])
        S.op("dve", lambda e: e.tensor_tensor(out=tmp[:], in0=mref_bc[:], in1=Mt[:], op=ALU.subtract), reads=[g_b], writes=[g_b])
        S.op("act", lambda e: e.activation(out=r_t[:], in_=tmp[:], func=AF.Exp), reads=[g_b], writes=[g_b])
        S.op("dve", lambda e: e.tensor_tensor(out=tmp[:], in0=fl(acol), in1=mref_bc[:], op=ALU.subtract), reads=[g_b], writes=[g_b])
        S.op("act", lambda e: e.activation(out=u_t[:], in_=tmp[:], func=AF.Exp), reads=[g_b], writes=[g_b])
        S.op("dve", lambda e: e.tensor_tensor(out=tmp[:], in0=fl(bcol), in1=Mt[:], op=ALU.add), reads=[g_b], writes=[g_b])
        S.op("act", lambda e: e.activation(out=eb_t[:], in_=tmp[:], func=AF.Exp, scale=-1.0), reads=[g_b], writes=[g_b])
        S.op("dve", lambda e: e.tensor_tensor(out=tmp[:], in0=mprev_bc[:], in1=mref_bc[:], op=ALU.subtract), reads=[g_b], writes=[g_b])
        S.op("act", lambda e: e.activation(out=sc_bc[:], in_=tmp[:], func=AF.Exp), reads=[g_b], writes=[g_b])
        if mlstage < 2:
            S.barrier()
            return
        Cst = sb("Cst", [P, 65])
        Csb = sb("Csb", [P, 130], BF16)
        Kp = sb("Kp", [P, P], BF16)
        ST = [sb(f"ST{h}", [P, P], BF16) for h in range(2)]
        nd = sb("nd", [P, 2, 65])
        hid = sb("hid", [P, 2, 64])
        sm = sb("sm", [P, 8])
        stats = sb("stats", [P, 2, 6])
        mv = sb("mv", [P, 2, 2])
        yTt = sb("yTt", [P, 512], BF16)
        Cst_b, Csb_b, Kp_b, ST_b, nd_b, hid_b, sm_b, yTt_b = Buf(), Buf(), Buf(), [Buf(), Buf()], Buf(), Buf(), Buf(), Buf()
        S.op("pool", lambda e: e.memset(Cst[:], 0.0), writes=[Cst_b])
        S.op("pool", lambda e: e.memset(Csb[:], 0.0), writes=[Csb_b])
        eb3 = eb_t[:].rearrange("p (h c) -> p h c", h=2)
        for c in range(nb):
            tq = c // 4
            cs = slice(c * 128, (c + 1) * 128)
            S.op("pe", lambda e, cs=cs: e.transpose(out=C.ps_bf[:, 0:128], in_=KT[:, cs], identity=identb[:]),
                 reads=[KT_b[tq], cb_], writes=[C.psb[7]])
            for h in range(2):
                hs = slice(h * 64, (h + 1) * 64)
                ix = h * nb + c
                S.op("dve", lambda e, hs=hs, ix=ix: e.tensor_scalar(out=Kp[:, hs], in0=C.ps_bf[:, hs], scalar1=u_t[:, ix:ix + 1], scalar2=0.125,
                                                                    op0=ALU.mult, op1=ALU.mult), reads=[C.psb[7], g_b], writes=[Kp_b])
                S.op("pe", lambda e, hs=hs, h=h, cs=cs: e.matmul(out=C.ps[h][:, 0:128], lhsT=KT[hs, cs], rhs=QT[hs, cs], start=True, stop=True),
                     reads=[KT_b[tq], QT_b[tq]], writes=[C.psb[h]])
                S.op("dve", lambda e, h=h, ix=ix: e.scalar_tensor_tensor(out=ST[h][:], in0=C.ps[h][:, 0:128], scalar=u_t[:, ix:ix + 1], in1=mask8[:],
                                                                         op0=ALU.mult, op1=ALU.mult), reads=[C.psb[h], g_b, cb_], writes=[ST_b[h]])
                S.op("dve", lambda e, hs=hs, h=h, ix=ix: e.tensor_scalar(out=Csb[hs, h * 65:(h + 1) * 65], in0=Cst[hs, :], scalar1=sc_bc[hs, ix:ix + 1],
                                                                         scalar2=None, op0=ALU.mult), reads=[Cst_b, g_b], writes=[Csb_b])
            S.op("pe", lambda e, cs=cs: e.matmul(out=C.ps[2][:, 0:130], lhsT=QT[:, cs], rhs=Csb[:], start=True, stop=False, skip_group_check=True),
                 reads=[QT_b[tq], Csb_b], writes=[C.psb[2]])
            for h in range(2):
                S.op("pe", lambda e, h=h, c=c: e.matmul(out=C.ps[2][:, h * 65:(h + 1) * 65], lhsT=ST[h][:], rhs=Vext[:, c, h, 0:65], start=False, stop=(h == 1),
                                                        skip_group_check=True), reads=[ST_b[h], V_b], writes=[C.psb[2]])
            S.op("pe", lambda e, c=c: e.matmul(out=C.ps[3][:, 0:130], lhsT=Kp[:], rhs=Vext[:, c, :, 0:65], start=True, stop=True),
                 reads=[Kp_b, V_b], writes=[C.psb[3]])
            for h in range(2):
                hs = slice(h * 64, (h + 1) * 64)
                ix = h * nb + c
                S.op("dve", lambda e, hs=hs, h=h, ix=ix: e.scalar_tensor_tensor(out=Cst[hs, :], in0=Cst[hs, :], scalar=sc_bc[hs, ix:ix + 1],
                                                                                in1=C.ps[3][hs, h * 65:(h + 1) * 65], op0=ALU.mult, op1=ALU.add),
                     reads=[Cst_b, g_b, C.psb[3], Csb_b], writes=[Cst_b])
                S.op("dve", lambda e, h=h, ix=ix: e.tensor_scalar(out=nd[:, h, :], in0=C.ps[2][:, h * 65:(h + 1) * 65], scalar1=r_t[:, ix:ix + 1],
                                                                  scalar2=None, op0=ALU.mult), reads=[C.psb[2], g_b], writes=[nd_b])
            S.op("dve", lambda e: e.tensor_scalar(out=sm[:, 0:2], in0=nd[:, :, 64], scalar1=-1.0, scalar2=None, op0=ALU.mult), reads=[nd_b], writes=[sm_b])
            S.op("dve", lambda e: e.tensor_tensor(out=sm[:, 0:2], in0=sm[:, 0:2], in1=nd[:, :, 64], op=ALU.max), reads=[nd_b, sm_b], writes=[sm_b])
            S.op("dve", lambda e, c=c: e.tensor_tensor(out=sm[:, 0:2], in0=sm[:, 0:2], in1=eb3[:, :, c], op=ALU.max), reads=[sm_b, g_b], writes=[sm_b])
            S.op("dve", lambda e: e.reciprocal(out=sm[:, 0:2], in_=sm[:, 0:2]), reads=[sm_b], writes=[sm_b])
            for h in range(2):
                S.op("dve", lambda e, h=h: e.tensor_scalar(out=hid[:, h, :], in0=nd[:, h, 0:64], scalar1=sm[:, h:h + 1], scalar2=None, op0=ALU.mult),
                     reads=[nd_b, sm_b], writes=[hid_b])
                S.op("dve", lambda e, h=h: e.bn_stats(out=stats[:, h, :], in_=hid[:, h, :]), reads=[hid_b], writes=[sm_b])
                S.op("dve", lambda e, h=h: e.bn_aggr(out=mv[:, h, :], in_=stats[:, h, :]), reads=[sm_b], writes=[sm_b])
            S.op("act", lambda e: e.activation(out=sm[:, 2:4], in_=mv[:, :, 1], func=AF.Sqrt, bias=C.eps_t[:], scale=1.0), reads=[sm_b, C.ident_b], writes=[sm_b])
            S.op("dve", lambda e: e.reciprocal(out=sm[:, 2:4], in_=sm[:, 2:4]), reads=[sm_b], writes=[sm_b])
            for h in range(2):
                S.op("dve", lambda e, h=h: e.tensor_scalar(out=hid[:, h, :], in0=hid[:, h, :], scalar1=mv[:, h, 0:1], scalar2=sm[:, 2 + h:3 + h],
                                                           op0=ALU.subtract, op1=ALU.mult), reads=[hid_b, sm_b], writes=[hid_b])
            S.op("pool", lambda e, c=c: e.tensor_tensor(out=hid[:].rearrange("p h d -> p (h d)"), in0=hid[:].rearrange("p h d -> p (h d)"),
                                                        in1=gso[:, c, :], op=ALU.mult), reads=[hid_b, gso_b], writes=[hid_b])
            S.op("pe", lambda e, c=c: e.transpose(out=C.ps[4][:, (c % 4) * 128:(c % 4 + 1) * 128], in_=hid[:].rearrange("p h d -> p (h d)"),
                                                  identity=C.ident[:]), reads=[hid_b, C.ident_b], writes=[C.psb[4]])
            if c % 4 == 3:
                S.op("act", lambda e: e.copy(out=yTt[:], in_=C.ps[4][:]), reads=[C.psb[4]], writes=[yTt_b])
                S.dma("sp", yT[:, (c // 4) * 512:(c // 4 + 1) * 512], yTt[:], reads=[yTt_b], writes=[yT_b])
        S.barrier()


def make_ml_consts_np(c):
    k = np.arange(P)[:, None]
    m = np.arange(P)[None, :]
    c["ntriT"] = -(k <= m).astype(np.float32)
    c["mask8"] = (k <= m).astype(np.float32) * 0.125
    e0 = np.zeros((P, P), np.float32)
    e0[0, :] = 1.0
    c["e0ones"] = e0
    e127 = np.zeros((P, P), np.float32)
    e127[127, :] = 1.0
    c["e127ones"] = e127
    return c


def outproj_ln(C, es, tag, lhs_chunks, lhs_b, wo, wo_b, xt, x_b):
    S = C.S
    for blk in range(4):
        for hh in range(2):
            po = 4 + (blk * 2 + hh) % 2
            for k in range(8):
                S.op("pe", lambda e, k=k, blk=blk, hh=hh, po=po: e.matmul(
                    out=C.ps[po][:], lhsT=lhs_chunks[:, k, blk * 128:(blk + 1) * 128], rhs=wo[:, k, hh * 512:(hh + 1) * 512],
                    start=(k == 0), stop=(k == 7)), reads=[lhs_b, wo_b], writes=[C.psb[po]])
            S.op("dve", lambda e, blk=blk, hh=hh, po=po: e.tensor_tensor(
                out=xt[:, blk, hh * 512:(hh + 1) * 512], in0=C.ps[po][:], in1=xt[:, blk, hh * 512:(hh + 1) * 512], op=ALU.add),
                reads=[C.psb[po], x_b], writes=[x_b])


def mixout_phase(C, x_in, xin_b, ygath, yg_b, w_out, sel_dram, g_dram, b_dram, x_out, xout_b, NT, TT, tag):
    nc, S = C.nc, C.S
    ntile = NT // 512
    with contextlib.ExitStack() as es:
        wo, wo_b = load_w_bf16(C, es, f"wmo{tag}", w_out, D)
        sb = lambda name, shape, dt=F32: es.enter_context(nc.sbuf_tensor(f"mo{name}{tag}", shape, dt))
        g_bc, b_bc, sel = sb("g", [P, D]), sb("b", [P, D]), sb("sel", [P, 2])
        xts = [sb(f"xt{i}", [P, 4, D]) for i in range(2)]
        yAs = [sb(f"yA{i}", [P, 8, 512], BF16) for i in range(2)]
        yBs = [sb(f"yB{i}", [P, 8, 512], BF16) for i in range(2)]
        stats, mv, rstd = sb("stats", [P, 4, 2, 6]), sb("mv", [P, 4, 2]), sb("rstd", [P, 4, 1])
        gb_b, st_b = Buf(), Buf()
        x_bs, yA_bs, yB_bs = [Buf(), Buf()], [Buf(), Buf()], [Buf(), Buf()]
        S.dma("sp", g_bc[:], g_dram, writes=[gb_b])
        S.dma("sp", b_bc[:], b_dram, writes=[gb_b])
        S.dma("sp", sel[:], sel_dram, writes=[gb_b])
        xin_v = x_in.rearrange("(t b p) d -> t p b d", b=4, p=P)
        xout_v = x_out.rearrange("(t b p) d -> t p b d", b=4, p=P)
        yv = ygath.rearrange("(k p) t -> p k t", p=P)

        def loads(t):
            k = t % 2
            S.dma("sp", xts[k][:], xin_v[t], reads=[xin_b[t] if isinstance(xin_b, list) else xin_b], writes=[x_bs[k]])
            S.dma("sp", yAs[k][:], yv[:, :, t * 512:(t + 1) * 512], reads=[yg_b], writes=[yA_bs[k]])
            S.dma("sp", yBs[k][:], yv[:, :, NT + t * 512:NT + (t + 1) * 512], reads=[yg_b], writes=[yB_bs[k]])
        loads(0)
        for t in range(ntile):
            k = t % 2
            xt, x_b, yA, yA_b, yB, yB_b = xts[k], x_bs[k], yAs[k], yA_bs[k], yBs[k], yB_bs[k]
            if t + 1 < ntile:
                loads(t + 1)
            S.op("act", lambda e, xt=xt: e.mul(out=xt[:], in_=xt[:], mul=ALPHA), reads=[x_b], writes=[x_b])
            S.op("dve", lambda e, yA=yA: e.tensor_scalar(out=yA[:], in0=yA[:], scalar1=sel[:, 0:1], scalar2=None, op0=ALU.mult),
                 reads=[yA_b, gb_b], writes=[yA_b])
            S.op("dve", lambda e, yA=yA, yB=yB: e.scalar_tensor_tensor(out=yA[:], in0=yB[:], scalar=sel[:, 1:2], in1=yA[:], op0=ALU.mult, op1=ALU.add),
                 reads=[yA_b, yB_b, gb_b], writes=[yA_b])
            outproj_ln(C, es, tag, yA, yA_b, wo, wo_b, xt, x_b)
            layer_norm_tile(C, xt, x_b, g_bc, b_bc, gb_b, stats, mv, rstd, st_b)
            S.dma("pool", xout_v[t], xt[:], reads=[x_b], writes=[xout_b])
        S.barrier()


def xattn_phase(C, x_in, xin_b, mem, w_q, w_kv, w_o, g_dram, b_dram, x_out, xout_b, NT, tag):
    nc, S = C.nc, C.S
    ntile = NT // 512
    C.set_psum(2)
    with contextlib.ExitStack() as es:
        wq, wq_b = load_w_bf16(C, es, f"wxq{tag}", w_q, D)
        wkv, wkv_b = load_w_bf16(C, es, f"wxkv{tag}", w_kv, 2 * D)
        wo, wo_b = load_w_bf16(C, es, f"wxo{tag}", w_o, D)
        sb = lambda name, shape, dt=F32: es.enter_context(nc.sbuf_tensor(f"xa{name}{tag}", shape, dt))
        g_bc, b_bc = sb("g", [P, D]), sb("b", [P, D])
        identb = sb("idb", [P, P], BF16)
        xt = sb("xt", [P, 4, D])
        xT = sb("xT", [P, 8, 512], BF16)
        QT = sb("QT", [P, 8, 512], BF16)
        KTx = sb("KT", [P, 8, 256], BF16)
        Vx = sb("V", [P, 2, D], BF16)
        aoT = sb("aoT", [P, 8, 512], BF16)
        sc = sb("sc", [P, 256])
        pf = sb("pf", [P, 256])
        pn = sb("pn", [P, 256], BF16)
        pT = sb("pT", [P, 2, 128], BF16)
        sm = sb("sm", [P, 4])
        stats, mv, rstd = sb("stats", [P, 4, 2, 6]), sb("mv", [P, 4, 2]), sb("rstd", [P, 4, 1])
        gb_b, x_b, xT_b, QT_b, KT_b, V_b, ao_b, st_b = Buf(), Buf(), Buf(), Buf(), Buf(), Buf(), Buf(), Buf()
        sc_b, pf_b, pn_b, pT_b, sm_b = Buf(), Buf(), Buf(), Buf(), Buf()
        pf2 = [pf, sb("pfb", [P, 256])]
        pn2 = [pn, sb("pnb", [P, 256], BF16)]
        pT2 = [pT, sb("pTb", [P, 2, 128], BF16)]
        sm2 = [sm, sb("smb", [P, 4])]
        pf_b2, pn_b2, pT_b2, sm_b2 = [Buf(), Buf()], [Buf(), Buf()], [Buf(), Buf()], [Buf(), Buf()]
        sc_pb, bf_pb, pv_pb = [Buf(), Buf()], [Buf(), Buf()], [Buf(), Buf()]
        S.dma("sp", g_bc[:], g_dram, writes=[gb_b])
        S.dma("sp", b_bc[:], b_dram, writes=[gb_b])
        S.dma("pool", identb[:], C.ident_dram, writes=[gb_b])
        S.dma("sp", xt[:, 0:2, :], mem.rearrange("(b p) d -> p b d", p=P), writes=[x_b])
        for kc in range(8):
            for blk in range(2):
                S.op("pe", lambda e, kc=kc, blk=blk: e.transpose(out=C.ps[4][:, blk * 128:(blk + 1) * 128], in_=xt[:, blk, kc * 128:(kc + 1) * 128],
                                                                 identity=C.ident[:]), reads=[x_b, C.ident_b], writes=[C.psb[4]])
            S.op("dve", lambda e, kc=kc: e.tensor_copy(out=xT[:, kc, 0:256], in_=C.ps[4][:, 0:256]), reads=[C.psb[4]], writes=[xT_b])
        for j in range(8):
            pi = j % 2
            for kc in range(8):
                S.op("pe", lambda e, kc=kc, j=j, pi=pi: e.matmul(out=C.ps[pi][:, 0:256], lhsT=wkv[:, kc, j * 128:(j + 1) * 128], rhs=xT[:, kc, 0:256],
                                                                 start=(kc == 0), stop=(kc == 7)), reads=[wkv_b, xT_b], writes=[C.psb[pi]])
            S.op("dve", lambda e, j=j, pi=pi: e.tensor_copy(out=KTx[:, j, :], in_=C.ps[pi][:, 0:256]), reads=[C.psb[pi]], writes=[KT_b])
        for blk in range(2):
            for hh in range(2):
                pi = 2 + hh
                for kc in range(8):
                    S.op("pe", lambda e, kc=kc, blk=blk, hh=hh, pi=pi: e.matmul(
                        out=C.ps[pi][:], lhsT=xT[:, kc, blk * 128:(blk + 1) * 128], rhs=wkv[:, kc, D + hh * 512:D + (hh + 1) * 512],
                        start=(kc == 0), stop=(kc == 7)), reads=[wkv_b, xT_b], writes=[C.psb[pi]])
                S.op("act", lambda e, blk=blk, hh=hh, pi=pi: e.copy(out=Vx[:, blk, hh * 512:(hh + 1) * 512], in_=C.ps[pi][:]),
                     reads=[C.psb[pi]], writes=[V_b])
        xin_v = x_in.rearrange("(t b p) d -> t p b d", b=4, p=P)
        xout_v = x_out.rearrange("(t b p) d -> t p b d", b=4, p=P)
        xts = [xt, sb("xt2", [P, 4, D])]
        x_bs = [x_b, Buf()]
        QTs = [QT, sb("QT2", [P, 8, 512], BF16)]
        QT_bs = [QT_b, Buf()]

        def emit_load(t):
            S.dma("sp", xts[t % 2][:], xin_v[t], reads=[xin_b[t] if isinstance(xin_b, list) else xin_b], writes=[x_bs[t % 2]])

        def emit_T(t):
            load_xT(C, xts[t % 2], x_bs[t % 2], xT, xT_b, ps_ids=(0,))
            S.op("act", lambda e, t=t: e.mul(out=xts[t % 2][:], in_=xts[t % 2][:], mul=ALPHA), reads=[x_bs[t % 2]], writes=[x_bs[t % 2]])

        def emit_qproj(t, j):
            proj_fm(C, wq, wq_b, j * 128, (j + 1) * 128, xT, xT_b, 1)
            S.op("act" if j % 2 == 0 else "dve",
                 (lambda e, j=j, t=t: e.mul(out=QTs[t % 2][:, j, :], in_=C.ps[1][:], mul=0.0625)) if j % 2 == 0 else
                 (lambda e, j=j, t=t: e.tensor_scalar(out=QTs[t % 2][:, j, :], in0=C.ps[1][:], scalar1=0.0625, scalar2=None, op0=ALU.mult)),
                 reads=[C.psb[1]], writes=[QT_bs[t % 2]])

        emit_load(0)
        emit_T(0)
        for j in range(8):
            emit_qproj(0, j)
        for t in range(ntile):
            xt, x_b, QT, QT_b = xts[t % 2], x_bs[t % 2], QTs[t % 2], QT_bs[t % 2]
            if t + 1 < ntile:
                emit_load(t + 1)
            chains = [(hd, blk) for hd in range(4) for blk in range(4)]
            rounds = [[(chains[r0 + a], a) for a in range(2)] for r0 in range(0, 16, 2)]
            for ri, pr in enumerate(rounds):
                if t + 1 < ntile:
                    if ri == 1:
                        emit_T(t + 1)
                    if 2 <= ri < 6:
                        emit_qproj(t + 1, 2 * (ri - 2))
                        emit_qproj(t + 1, 2 * (ri - 2) + 1)
                for (hd, blk), a in pr:
                    ts = slice(blk * 128, (blk + 1) * 128)
                    for kk in range(2):
                        S.op("pe", lambda e, hd=hd, kk=kk, ts=ts, a=a: e.matmul(out=C.ps[2 + a][:, 0:256], lhsT=QT[:, 2 * hd + kk, ts],
                                                                               rhs=KTx[:, 2 * hd + kk, :], start=(kk == 0), stop=(kk == 1),
                                                                               skip_group_check=True), reads=[QT_b, KT_b], writes=[C.psb[2 + a]])
                for (hd, blk), a in pr:
                    S.op("dve", lambda e, a=a: e.tensor_reduce(out=sm2[a][:, 0:1], in_=C.ps[2 + a][:, 0:256], axis=AX.X, op=ALU.max),
                         reads=[C.psb[2 + a]], writes=[sm_b2[a]])
                    S.op("dve", lambda e, a=a: e.tensor_scalar(out=sm2[a][:, 1:2], in0=sm2[a][:, 0:1], scalar1=-1.0, scalar2=None, op0=ALU.mult),
                         reads=[sm_b2[a]], writes=[sm_b2[a]])
                for (hd, blk), a in pr:
                    S.op("act", lambda e, a=a: e.activation(out=pf2[a][:], in_=C.ps[2 + a][:, 0:256], func=AF.Exp, bias=sm2[a][:, 1:2], scale=1.0,
                                                            accum_out=sm2[a][:, 2:3]), reads=[C.psb[2 + a], sm_b2[a]], writes=[pf_b2[a], sm_b2[a]])
                for (hd, blk), a in pr:
                    S.op("dve", lambda e, a=a: e.reciprocal(out=sm2[a][:, 3:4], in_=sm2[a][:, 2:3]), reads=[sm_b2[a]], writes=[sm_b2[a]])
                    S.op("dve" if a == 0 else "pool", lambda e, a=a: e.tensor_scalar(out=pn2[a][:], in0=pf2[a][:], scalar1=sm2[a][:, 3:4], scalar2=None, op0=ALU.mult),
                         reads=[pf_b2[a], sm_b2[a]], writes=[pn_b2[a]])
                for (hd, blk), a in pr:
                    for mb in range(2):
                        S.op("pe", lambda e, mb=mb, a=a: e.transpose(out=C.ps_bfs[a][:, mb * 128:(mb + 1) * 128],
                                                                     in_=pn2[a][:, mb * 128:(mb + 1) * 128], identity=identb[:]),
                             reads=[pn_b2[a], gb_b], writes=[C.psbf_b[a]])
                for (hd, blk), a in pr:
                    if a == 0:
                        S.op("act", lambda e, a=a: e.copy(out=pT2[a][:].rearrange("p a b -> p (a b)"), in_=C.ps_bfs[a][:, 0:256]),
                             reads=[C.psbf_b[a]], writes=[pT_b2[a]])
                    else:
                        S.op("dve", lambda e, a=a: e.tensor_copy(out=pT2[a][:].rearrange("p a b -> p (a b)"), in_=C.ps_bfs[a][:, 0:256]),
                             reads=[C.psbf_b[a]], writes=[pT_b2[a]])
                for (hd, blk), a in pr:
                    for cc in range(2):
                        for mb in range(2):
                            S.op("pe", lambda e, cc=cc, mb=mb, hd=hd, a=a: e.matmul(out=C.ps[4 + a][:, cc * 128:(cc + 1) * 128],
                                                                                    lhsT=Vx[:, mb, hd * 256 + cc * 128:hd * 256 + (cc + 1) * 128], rhs=pT2[a][:, mb, :],
                                                                                    start=(mb == 0), stop=(mb == 1), skip_group_check=True),
                                 reads=[V_b, pT_b2[a]], writes=[C.psb[4 + a]])
                for (hd, blk), a in pr:
                    ts = slice(blk * 128, (blk + 1) * 128)
                    S.op("dve" if a == 0 else "act",
                         (lambda e, hd=hd, ts=ts, a=a: e.tensor_copy(out=aoT[:, 2 * hd:2 * hd + 2, ts],
                                                                     in_=C.ps[4 + a][:, 0:256].rearrange("p (c q) -> p c q", c=2))) if a == 0 else
                         (lambda e, hd=hd, ts=ts, a=a: e.copy(out=aoT[:, 2 * hd:2 * hd + 2, ts],
                                                              in_=C.ps[4 + a][:, 0:256].rearrange("p (c q) -> p c q", c=2))),
                         reads=[C.psb[4 + a]], writes=[ao_b])
            outproj_ln(C, es, tag, aoT, ao_b, wo, wo_b, xt, x_b)
            layer_norm_tile(C, xt, x_b, g_bc, b_bc, gb_b, stats, mv, rstd, st_b)
            S.dma("sp", xout_v[t], xt[:], reads=[x_b], writes=[xout_b])
        S.barrier()
    C.set_psum(1)


NT_OWN = 4096
TT_SEQ = 8192
DEPTH = 2
PAIRS = [[0, 1], [2, 3], [4, 5], [6, 7]]
_SPLITS = np.cumsum([0, 256, 256, 256, 512, 128, 128, 512, 256, 4, 4, 256])


def build_program():
    from concourse.bass_utils import run_bass_kernel_spmd
    nc = bass.Bass("TRN2", target_bir_lowering=False)
    din = lambda name, shape, dt=F32: nc.dram_tensor(name, shape, dt, kind="ExternalInput").ap()
    dscr = lambda name, shape, dt=F32: nc.dram_tensor(name, shape, dt, kind="Internal").ap()
    x = din("x", [NT_OWN, D])
    mem = din("mem", [256, D])
    sel = din("sel", [P, 2])
    cn = make_consts_np()
    cd = {k: din("c_" + k, list(v.shape)) for k, v in cn.items()}
    L = []
    for l in range(DEPTH):
        d = {}
        d["ffn1_in"] = din(f"l{l}_ffn1_in", [D, 2 * DFF]); d["ffn1_out"] = din(f"l{l}_ffn1_out", [DFF, D])
        d["ffn2_in"] = din(f"l{l}_ffn2_in", [D, 2 * DFF]); d["ffn2_out"] = din(f"l{l}_ffn2_out", [DFF, D])
        d["w_sb"] = din(f"l{l}_w_sb", [D, 384]); d["w_swa"] = din(f"l{l}_w_swa", [D, 448]); d["w_ml"] = din(f"l{l}_w_ml", [D, 640])
        d["cw"] = din(f"l{l}_cw", [P, 8]); d["cb"] = din(f"l{l}_cb", [P, 2]); d["ifb"] = din(f"l{l}_ifb", [P, 4]); d["ng"] = din(f"l{l}_ng", [P, P])
        d["rb"] = din(f"l{l}_rb", [P, P]); d["sink"] = din(f"l{l}_sink", [P, 4])
        d["w_mo"] = din(f"l{l}_w_mo", [D, D])
        d["xq"] = din(f"l{l}_xq", [D, D]); d["xkv"] = din(f"l{l}_xkv", [D, 2 * D]); d["xo"] = din(f"l{l}_xo", [D, D])
        d["lng"] = [din(f"l{l}_lng{i}", [P, D]) for i in range(4)]
        d["lnb"] = [din(f"l{l}_lnb{i}", [P, D]) for i in range(4)]
        L.append(d)
    out = nc.dram_tensor("out", [NT_OWN, D], F32, kind="ExternalOutput").ap()
    Xa = dscr("Xa", [NT_OWN, D]); Xb = dscr("Xb", [NT_OWN, D])
    X1g = dscr("X1g", [NT_OWN // 512, 2 * D, 512], BF16)
    XaT = dscr("XaT", [NT_OWN // 512, D, 512], BF16)
    YTo = dscr("YTo", [512, TT_SEQ], BF16)
    Yg = dscr("Yg", [1024, TT_SEQ], BF16)
    frs = dscr("frs", [4, 384])
    S = Sched(nc)
    C = Ctx(nc, S)
    C.load_consts(cd["ident"])
    x_b, Xa_b, Xb_b, X1g_b, YTo_b, Yg_b, out_b = Buf(), Buf(), Buf(), Buf(), [Buf(), Buf(), Buf()], Buf(), Buf()
    cur, cur_b = x, x_b
    import os
    kstop = int(os.environ.get("KSTOP", "99"))
    for l in range(DEPTH):
        d = L[l]
        tg = f"L{l}"
        if kstop < 99 and l > 0:
            break
        Xa_tb = [Buf() for _ in range(NT_OWN // 512)]
        XaT_tb = [Buf() for _ in range(NT_OWN // 512)]
        X1g_tb = [Buf() for _ in range(TT_SEQ // 512)]
        nch = NT_OWN // 512

        def ag1(t):
            S.custom("pool", lambda e, t=t: e.collective_compute("AllGather", ALU.bypass, replica_groups=PAIRS,
                                                                 ins=[XaT[t]], outs=[X1g[t]]), 1,
                     reads=[XaT_tb[t]], writes=[X1g_tb[t], X1g_tb[nch + t]])
        ffn_phase(C, cur, Xa, d["ffn1_in"], d["ffn1_out"], d["lng"][0], d["lnb"][0], NT_OWN, tg + "f1", cur_b, Xa_tb, after_tile=ag1,
                  xT_out=XaT, xT_out_b=XaT_tb)
        if kstop <= 2:
            break
        X1v = X1g.rearrange("j (r kc p) n -> j r p kc n", r=2, p=P)
        xtile = lambda t: X1v[t % nch, t // nch]
        sb_phase(C, xtile, X1g_tb, d["w_sb"], YTo[0:128, :], YTo_b[0], TT_SEQ, cd, tg + "sb")
        S.custom("pool", lambda e: e.collective_compute("AllGather", ALU.bypass, replica_groups=PAIRS, ins=[YTo[0:128, :]], outs=[Yg[0:256, :]]), 1,
                 reads=[YTo_b[0]], writes=[Yg_b])
        if kstop <= 3:
            break
        swa_phase(C, xtile, X1g_tb, d["w_swa"], d["rb"], d["sink"], frs, YTo[128:384, :], YTo_b[1], TT_SEQ, cd, tg + "sw")
        for k in (1, 2):
            S.custom("pool", lambda e, k=k: e.collective_compute("AllGather", ALU.bypass, replica_groups=PAIRS, ins=[YTo[k * 128:(k + 1) * 128, :]],
                                                                 outs=[Yg[k * 256:(k + 1) * 256, :]]), 1, reads=[YTo_b[1]], writes=[Yg_b])
        if kstop <= 4:
            break
        mlstm_phase(C, xtile, X1g_tb, d["w_ml"], d["cw"], d["cb"], d["ifb"], d["ng"], YTo[384:512, :], YTo_b[2], TT_SEQ, cd, tg + "ml")
        S.custom("pool", lambda e: e.collective_compute("AllGather", ALU.bypass, replica_groups=PAIRS, ins=[YTo[384:512, :]], outs=[Yg[768:1024, :]]), 1,
                 reads=[YTo_b[2]], writes=[Yg_b])
        S.barrier()
        if kstop <= 6:
            break
        mixout_phase(C, Xa, Xa_tb, Yg, Yg_b, d["w_mo"], sel, d["lng"][1], d["lnb"][1], Xb, Xb_b, NT_OWN, TT_SEQ, tg + "mo")
        if kstop <= 7:
            break
        xattn_phase(C, Xb, Xb_b, mem, d["xq"], d["xkv"], d["xo"], d["lng"][2], d["lnb"][2], Xa, Xa_b, NT_OWN, tg + "xa")
        if kstop <= 8:
            break
        dst, dst_b = (out, out_b) if l == DEPTH - 1 else (Xb, Xb_b)
        ffn_phase(C, Xa, dst, d["ffn2_in"], d["ffn2_out"], d["lng"][3], d["lnb"][3], NT_OWN, tg + "f2", Xa_b, dst_b)
        cur, cur_b = dst, dst_b
    S.finish()
    C.psum_es.close()
    return nc, cn


def make_core_inputs(c, inp, cn):
    b, r = c // 2, c % 2
    f = lambda a: np.ascontiguousarray(np.asarray(a, dtype=np.float32))
    rep = lambda v: f(np.tile(np.asarray(v, np.float32)[None, :], (P, 1)))
    m = {"x": f(inp["x"][b, r * NT_OWN:(r + 1) * NT_OWN]), "mem": f(inp["mem"][b])}
    selv = np.zeros((P, 2), np.float32)
    selv[:, r] = 1.0
    m["sel"] = selv
    for k, v in cn.items():
        m["c_" + k] = f(v)
    sp = _SPLITS
    slot = [0, 2, 1, 3]
    for l in range(DEPTH):
        w = np.asarray(inp["mix_w_in"][l], np.float32)
        seg = [w[:, sp[i]:sp[i + 1]] for i in range(11)]
        sbq, sbk, sbv, swq, swk, swv, mlqk, mlv, ig, fg, og = seg
        mlq, mlk = mlqk[:, 0:256], mlqk[:, 256:512]
        m[f"l{l}_ffn1_in"] = f(inp["ffn1_w_in"][l]); m[f"l{l}_ffn1_out"] = f(inp["ffn1_w_out"][l])
        m[f"l{l}_ffn2_in"] = f(inp["ffn2_w_in"][l]); m[f"l{l}_ffn2_out"] = f(inp["ffn2_w_out"][l])
        h2 = slice(r * 128, (r + 1) * 128)
        m[f"l{l}_w_sb"] = f(np.concatenate([sbq[:, h2], sbk[:, h2], sbv[:, h2]], 1))
        kk = swk[:, r * 64:(r + 1) * 64]
        m[f"l{l}_w_swa"] = f(np.concatenate([swq[:, r * 256:(r + 1) * 256], kk, kk, swv[:, r * 64:(r + 1) * 64]], 1))
        m[f"l{l}_w_ml"] = f(np.concatenate([mlq[:, h2], mlk[:, h2], mlv[:, h2], og[:, h2], ig[:, 2 * r:2 * r + 2], fg[:, 2 * r:2 * r + 2],
                                            np.zeros((D, 124), np.float32)], 1))
        cwl = np.asarray(inp["ml_conv_w"][l], np.float32)
        cbl = np.asarray(inp["ml_conv_b"][l], np.float32)
        m[f"l{l}_cw"] = f(np.concatenate([cwl[:, r * 128:(r + 1) * 128].T, cwl[:, 256 + r * 128:256 + (r + 1) * 128].T], 1))
        m[f"l{l}_cb"] = f(np.stack([cbl[r * 128:(r + 1) * 128], cbl[256 + r * 128:256 + (r + 1) * 128]], 1))
        ib = np.asarray(inp["ml_i_bias"][l], np.float32)[2 * r:2 * r + 2]
        fb = np.asarray(inp["ml_f_bias"][l], np.float32)[2 * r:2 * r + 2]
        m[f"l{l}_ifb"] = rep(np.concatenate([ib, fb]))
        m[f"l{l}_ng"] = rep(np.asarray(inp["ml_norm_g"][l], np.float32)[h2])
        rbl = np.asarray(inp["rel_bias"], np.float32)[:, 4 * r:4 * r + 4][:, slot]
        m[f"l{l}_rb"] = f(np.pad(rbl, ((0, 96), (0, 124))))
        m[f"l{l}_sink"] = rep(np.asarray(inp["swa_sinks"][l], np.float32)[4 * r:4 * r + 4][slot])
        wo = np.asarray(inp["mix_w_out"][l], np.float32)
        rows = []
        for k in range(4):
            for rr in range(2):
                base = [rr * 128, 256 + rr * 256, 256 + rr * 256 + 128, 768 + rr * 128][k]
                rows.append(wo[base:base + 128])
        m[f"l{l}_w_mo"] = f(np.concatenate(rows, 0))
        m[f"l{l}_xq"] = f(inp["xattn_w_q"][l]); m[f"l{l}_xkv"] = f(inp["xattn_w_kv"][l]); m[f"l{l}_xo"] = f(inp["xattn_w_o"][l])
        for i in range(4):
            m[f"l{l}_lng{i}"] = rep(inp["ln_g"][l][i]); m[f"l{l}_lnb{i}"] = rep(inp["ln_b"][l][i])
    return m


def kernel(**inputs):
    from concourse.bass_utils import run_bass_kernel_spmd
    inp = {k: np.asarray(v) for k, v in inputs.items()}
    nc, cn = build_program()
    in_maps = [make_core_inputs(c, inp, cn) for c in range(8)]
    res = run_bass_kernel_spmd(nc, in_maps, core_ids=list(range(8)))
    out = np.zeros((4, TT_SEQ, D), np.float32)
    for c in range(8):
        b, r = c // 2, c % 2
        out[b, r * NT_OWN:(r + 1) * NT_OWN] = np.asarray(res.results[c]["out"], np.float32)
    return out
```

```python
import contextlib
import numpy as np
import concourse.bass as bass
import concourse.mybir as mybir

F32 = mybir.dt.float32
BF16 = mybir.dt.bfloat16
ALU = mybir.AluOpType
AF = mybir.ActivationFunctionType
AX = mybir.AxisListType


class Buf:
    __slots__ = ("name", "w", "r")

    def __init__(self, name=""):
        self.name = name
        self.w = None
        self.r = []


class Sched:
    EPOCH = 20000

    def __init__(self, nc, n_dma_sems=24):
        self.nc = nc
        self.es = contextlib.ExitStack()
        self.eng = {"pe": nc.tensor, "act": nc.scalar, "dve": nc.vector,
                    "pool": nc.gpsimd, "sp": nc.sync}
        self.sem = {}
        self.cnt = {}
        self.nsem = 0
        for e in self.eng:
            self._new_epoch(e)
        self.dma_sems = [self.es.enter_context(nc.semaphore(f"dq{i}")) for i in range(n_dma_sems)]
        self.dma_cnt = [0] * n_dma_sems
        self.dma_next = 0
        self.waited = {e: {} for e in self.eng}
        self.last_ev = {e: None for e in self.eng}
        self.out_events = []
        self.n_ops = 0
        self.n_waits = 0

    def _new_epoch(self, e):
        self.sem[e] = self.es.enter_context(self.nc.semaphore(f"s_{e}_{self.nsem}"))
        self.nsem += 1
        self.cnt[e] = 0

    def _wait(self, e, ev):
        if ev is None:
            return
        sem, val, src = ev
        if src == "pe" and e == "pe":
            return
        key = id(sem)
        if self.waited[e].get(key, 0) >= val:
            return
        self.eng[e].wait_ge(sem, val)
        self.n_waits += 1
        self.waited[e][key] = val

    def _deps(self, e, reads, writes):
        for b in reads:
            self._wait(e, b.w)
        for b in writes:
            self._wait(e, b.w)
            for ev in b.r:
                self._wait(e, ev)

    def _commit(self, ev, reads, writes):
        for b in reads:
            b.r.append(ev)
            if len(b.r) > 12:
                latest = {}
                for x in b.r:
                    k = id(x[0])
                    if k not in latest or latest[k][1] < x[1]:
                        latest[k] = x
                b.r = list(latest.values())
        for b in writes:
            b.w = ev
            b.r = []

    def op(self, e, fn, reads=(), writes=()):
        self._deps(e, reads, writes)
        if self.cnt[e] >= self.EPOCH:
            self._new_epoch(e)
        ins = fn(self.eng[e])
        self.cnt[e] += 1
        ins.then_inc(self.sem[e], 1)
        ev = (self.sem[e], self.cnt[e], e)
        self.last_ev[e] = ev
        self._commit(ev, reads, writes)
        self.n_ops += 1
        return ev

    def dma(self, q, out, in_, reads=(), writes=(), **kw):
        self._deps(q, reads, writes)
        k = self.dma_next
        self.dma_next = (k + 1) % len(self.dma_sems)
        sem = self.dma_sems[k]
        if self.dma_cnt[k] > 0:
            self._wait(q, (sem, self.dma_cnt[k], "dma"))
        self.dma_cnt[k] += 16
        self.eng[q].dma_start(out=out, in_=in_, **kw).then_inc(sem, 16)
        ev = (sem, self.dma_cnt[k], "dma")
        self._commit(ev, reads, writes)
        self.n_ops += 1
        return ev

    def custom(self, q, fn, inc, reads=(), writes=()):
        self._deps(q, reads, writes)
        k = self.dma_next
        self.dma_next = (k + 1) % len(self.dma_sems)
        sem = self.dma_sems[k]
        if self.dma_cnt[k] > 0:
            self._wait(q, (sem, self.dma_cnt[k], "dma"))
        self.dma_cnt[k] += inc
        fn(self.eng[q]).then_inc(sem, inc)
        ev = (sem, self.dma_cnt[k], "dma")
        self._commit(ev, reads, writes)
        return ev

    def barrier(self, bufs=()):
        evs = [ev for ev in self.last_ev.values() if ev is not None]
        for k, sem in enumerate(self.dma_sems):
            if self.dma_cnt[k] > 0:
                evs.append((sem, self.dma_cnt[k], "dma"))
        for e in self.eng:
            for ev in evs:
                if ev[2] == e and e == "pe":
                    continue
                self._wait(e, ev)

    def finish(self):
        self.barrier()
        self.es.close()


D = 1024
DFF = 2816
ALPHA = 4 ** 0.25
LN_EPS = 1e-5
P = 128


class Ctx:
    def __init__(self, nc, S):
        self.nc = nc
        self.S = S
        es = S.es
        self.psum_es = None
        self.psum_gen = 0
        self.set_psum(1)
        self.ident = es.enter_context(nc.sbuf_tensor("ident_sb", [P, P], F32))
        self.ident_b = Buf("ident")

        self.eps_t = es.enter_context(nc.sbuf_tensor("eps_t", [P, 1], F32))

    def set_psum(self, n_bf=1):
        if self.psum_es is not None:
            self.psum_es.close()
        self.psum_es = contextlib.ExitStack()
        g = self.psum_gen
        self.psum_gen += 1
        nf = 8 - n_bf
        self.ps = [self.psum_es.enter_context(self.nc.psum_tensor(f"psf{g}_{i}", [P, 512], F32)) for i in range(nf)]
        self.psb = [Buf(f"ps{i}") for i in range(nf)]
        self.ps_bfs = [self.psum_es.enter_context(self.nc.psum_tensor(f"psh{g}_{i}", [P, 1024], BF16)) for i in range(n_bf)]
        self.psbf_b = [Buf(f"psbf{i}") for i in range(n_bf)]
        self.ps_bf = self.ps_bfs[0]
        while len(self.ps) < 8:
            self.ps.append(None)
            self.psb.append(self.psbf_b[0])

    def load_consts(self, ident_dram):
        self.ident_dram = ident_dram
        self.S.dma("sp", self.ident[:], ident_dram, writes=[self.ident_b])
        self.S.op("pool", lambda e: e.memset(self.eps_t[:], LN_EPS), writes=[self.ident_b])


def load_xT(C, x_tile, x_b, xT, xT_b, ps_ids, evac_engs=("act", "dve")):
    S = C.S
    for kc in range(8):
        pi = ps_ids[kc % len(ps_ids)]
        ps, pb = C.ps[pi], C.psb[pi]
        for blk in range(4):
            S.op("pe", lambda e, kc=kc, blk=blk, ps=ps: e.transpose(
                out=ps[:, blk * 128:(blk + 1) * 128], in_=x_tile[:, blk, kc * 128:(kc + 1) * 128],
                identity=C.ident[:]), reads=[x_b, C.ident_b], writes=[pb])
        eng = evac_engs[kc % len(evac_engs)]
        if eng == "act":
            S.op("act", lambda e, kc=kc, ps=ps: e.copy(out=xT[:, kc, :], in_=ps[:]), reads=[pb], writes=[xT_b])
        else:
            S.op("dve", lambda e, kc=kc, ps=ps: e.tensor_copy(out=xT[:, kc, :], in_=ps[:]), reads=[pb], writes=[xT_b])


def layer_norm_tile(C, xt, x_b, g_bc, b_bc, gb_b, stats, mv, rstd, st_b, nblk=4):
    S = C.S
    for blk in range(nblk):
        for hh in range(2):
            S.op("dve", lambda e, blk=blk, hh=hh: e.bn_stats(out=stats[:, blk, hh, :], in_=xt[:, blk, hh * 512:(hh + 1) * 512]),
                 reads=[x_b], writes=[st_b])
        S.op("dve", lambda e, blk=blk: e.bn_aggr(out=mv[:, blk, :], in_=stats[:, blk, :, :]), reads=[st_b], writes=[st_b])
        S.op("act", lambda e, blk=blk: e.activation(out=rstd[:, blk, :], in_=mv[:, blk, 1:2], func=AF.Sqrt, bias=C.eps_t[:], scale=1.0),
             reads=[st_b, C.ident_b], writes=[st_b])
        S.op("dve", lambda e, blk=blk: e.reciprocal(out=rstd[:, blk, :], in_=rstd[:, blk, :]), reads=[st_b], writes=[st_b])
        S.op("dve", lambda e, blk=blk: e.tensor_scalar(out=xt[:, blk, :], in0=xt[:, blk, :], scalar1=mv[:, blk, 0:1],
                                                       scalar2=rstd[:, blk, :], op0=ALU.subtract, op1=ALU.mult),
             reads=[st_b, x_b], writes=[x_b])
        S.op("dve", lambda e, blk=blk: e.tensor_tensor(out=xt[:, blk, :], in0=xt[:, blk, :], in1=g_bc[:], op=ALU.mult),
             reads=[x_b, gb_b], writes=[x_b])
        S.op("pool", lambda e, blk=blk: e.tensor_tensor(out=xt[:, blk, :], in0=xt[:, blk, :], in1=b_bc[:], op=ALU.add),
             reads=[x_b, gb_b], writes=[x_b])


def ffn_phase(C, x_in, x_out, w_in, w_out, g_dram, b_dram, NT, tag, xin_b, xout_b, after_tile=None, xT_out=None, xT_out_b=None):
    nc, S = C.nc, C.S
    ntile = NT // 512
    JG = [(0, 6), (6, 12), (12, 17), (17, 22)]
    with contextlib.ExitStack() as es:
        w1 = es.enter_context(nc.sbuf_tensor(f"w1{tag}", [P, 8, 2 * DFF], BF16))
        w2 = es.enter_context(nc.sbuf_tensor(f"w2{tag}", [P, 22, D], BF16))
        g_bc = es.enter_context(nc.sbuf_tensor(f"g{tag}", [P, D], F32))
        b_bc = es.enter_context(nc.sbuf_tensor(f"b{tag}", [P, D], F32))
        xtb = [es.enter_context(nc.sbuf_tensor(f"xt{tag}{i}", [P, 4, D], F32)) for i in range(2)]
        xT = es.enter_context(nc.sbuf_tensor(f"xT{tag}", [P, 8, 512], BF16))
        gT = es.enter_context(nc.sbuf_tensor(f"gT{tag}", [P, 22, 512], BF16))
        sa = es.enter_context(nc.sbuf_tensor(f"sa{tag}", [P, 512], F32))
        xo = [es.enter_context(nc.sbuf_tensor(f"xo{tag}{i}", [P, 512], BF16)) for i in range(2)]
        stats = es.enter_context(nc.sbuf_tensor(f"stats{tag}", [P, 4, 2, 6], F32))
        mv = es.enter_context(nc.sbuf_tensor(f"mv{tag}", [P, 4, 2], F32))
        rstd = es.enter_context(nc.sbuf_tensor(f"rstd{tag}", [P, 4, 1], F32))
        w1_b = [Buf() for _ in JG]
        w2_b = [Buf() for _ in range(22)]
        gb_b, xT_b, st_b, sa_b = Buf(), Buf(), Buf(), Buf()
        x_b = [Buf(), Buf()]
        xo_b = [Buf(), Buf()]
        gT_b = [Buf() for _ in range(22)]
        jgrp = {}
        for gi, (j0, j1) in enumerate(JG):
            for j in range(j0, j1):
                jgrp[j] = gi
        w_in_v = w_in.rearrange("(kc p) f -> p kc f", p=P)
        w_out_v = w_out.rearrange("(j p) d -> p j d", p=P)
        for gi, (j0, j1) in enumerate(JG):
            for half in range(2):
                c0, c1 = half * DFF + j0 * 128, half * DFF + j1 * 128
                S.dma("pool", w1[:, :, c0:c1], w_in_v[:, :, c0:c1], writes=[w1_b[gi]])
        for j in range(0, 22, 2):
            S.dma("pool", w2[:, j:j + 2, :], w_out_v[:, j:j + 2, :], writes=[w2_b[j], w2_b[j + 1]])
        S.dma("sp", g_bc[:], g_dram, writes=[gb_b])
        S.dma("sp", b_bc[:], b_dram, writes=[gb_b])
        xin_v = x_in.rearrange("(t b p) d -> t p b d", b=4, p=P)
        xout_v = x_out.rearrange("(t b p) d -> t p b d", b=4, p=P)

        def emit_load(t):
            S.dma("sp", xtb[t % 2][:], xin_v[t], reads=[xin_b[t] if isinstance(xin_b, list) else xin_b], writes=[x_b[t % 2]])

        def emit_T(t):
            load_xT(C, xtb[t % 2], x_b[t % 2], xT, xT_b, ps_ids=(6,))
            S.op("act", lambda e, t=t: e.mul(out=xtb[t % 2][:], in_=xtb[t % 2][:], mul=ALPHA), reads=[x_b[t % 2]], writes=[x_b[t % 2]])

        def emit_Tout_chunk(tt, kc):
            xt_ = xtb[tt % 2]
            for blk in range(4):
                S.op("pe", lambda e, blk=blk: e.transpose(out=C.ps[6][:, blk * 128:(blk + 1) * 128], in_=xt_[:, blk, kc * 128:(kc + 1) * 128],
                                                          identity=C.ident[:]), reads=[x_b[tt % 2], C.ident_b], writes=[C.psb[6]])
            k = kc % 2
            if k == 0:
                S.op("act", lambda e: e.copy(out=xo[k][:], in_=C.ps[6][:]), reads=[C.psb[6]], writes=[xo_b[k]])
            else:
                S.op("dve", lambda e: e.tensor_copy(out=xo[k][:], in_=C.ps[6][:]), reads=[C.psb[6]], writes=[xo_b[k]])
            S.dma("sp", xT_out[tt][kc * 128:(kc + 1) * 128, :], xo[k][:], reads=[xo_b[k]], writes=[xT_out_b[tt]])
            if kc == 7 and after_tile is not None:
                after_tile(tt)

        emit_load(0)
        emit_T(0)
        for t in range(ntile):
            xt = xtb[t % 2]
            xb = x_b[t % 2]
            for j in range(22):
                pa, pb_ = (0, 1) if j % 2 == 0 else (2, 3)
                gi = jgrp[j]
                for kc in range(8):
                    S.op("pe", lambda e, j=j, kc=kc, pa=pa: e.matmul(
                        out=C.ps[pa][:], lhsT=w1[:, kc, j * 128:(j + 1) * 128], rhs=xT[:, kc, :],
                        start=(kc == 0), stop=(kc == 7)), reads=[w1_b[gi], xT_b], writes=[C.psb[pa]])
                for kc in range(8):
                    S.op("pe", lambda e, j=j, kc=kc, pb_=pb_: e.matmul(
                        out=C.ps[pb_][:], lhsT=w1[:, kc, DFF + j * 128:DFF + (j + 1) * 128], rhs=xT[:, kc, :],
                        start=(kc == 0), stop=(kc == 7)), reads=[w1_b[gi], xT_b], writes=[C.psb[pb_]])
                S.op("act", lambda e, pa=pa: e.activation(out=sa[:], in_=C.ps[pa][:], func=AF.Silu), reads=[C.psb[pa]], writes=[sa_b])
                S.op("dve", lambda e, pb_=pb_, j=j: e.tensor_tensor(out=gT[:, j, :], in0=C.ps[pb_][:], in1=sa[:], op=ALU.mult),
                     reads=[C.psb[pb_], sa_b], writes=[gT_b[j]])
                if xT_out is not None and t >= 1 and 5 <= j < 13:
                    emit_Tout_chunk(t - 1, j - 5)
                if j == 13 and t + 1 < ntile:
                    emit_load(t + 1)
            if t + 1 < ntile:
                emit_T(t + 1)
            for blk in range(4):
                for hh in range(2):
                    po = 4 + (blk * 2 + hh) % 2
                    for j in range(22):
                        S.op("pe", lambda e, j=j, blk=blk, hh=hh, po=po: e.matmul(
                            out=C.ps[po][:], lhsT=gT[:, j, blk * 128:(blk + 1) * 128], rhs=w2[:, j, hh * 512:(hh + 1) * 512],
                            start=(j == 0), stop=(j == 21)), reads=[gT_b[j], w2_b[j]], writes=[C.psb[po]])
                    S.op("dve", lambda e, blk=blk, hh=hh, po=po, xt=xt: e.scalar_tensor_tensor(
                        out=xt[:, blk, hh * 512:(hh + 1) * 512], in0=C.ps[po][:], scalar=0.5,
                        in1=xt[:, blk, hh * 512:(hh + 1) * 512], op0=ALU.mult, op1=ALU.add),
                        reads=[C.psb[po], xb], writes=[xb])
            layer_norm_tile(C, xt, xb, g_bc, b_bc, gb_b, stats, mv, rstd, st_b)
            S.dma("sp", xout_v[t], xt[:], reads=[xb], writes=[xout_b[t] if isinstance(xout_b, list) else xout_b])
            if xT_out is None and after_tile is not None:
                after_tile(t)
        if xT_out is not None:
            for kc in range(8):
                emit_Tout_chunk(ntile - 1, kc)
        S.barrier()


def load_w_bf16(C, es, name, w_dram, ncols, q="pool"):
    w = es.enter_context(C.nc.sbuf_tensor(name, [P, 8, ncols], BF16))
    b = Buf(name)
    C.S.dma("pool", w[:], w_dram.rearrange("(kc p) f -> p kc f", p=P), writes=[b])
    return w, b


def proj_fm(C, w, w_b, c0, c1, xT, xT_b, pi):
    for kc in range(8):
        C.S.op("pe", lambda e, kc=kc: e.matmul(out=C.ps[pi][0:c1 - c0, :], lhsT=w[:, kc, c0:c1], rhs=xT[:, kc, :],
                                               start=(kc == 0), stop=(kc == 7)), reads=[w_b, xT_b], writes=[C.psb[pi]])


def proj_tm(C, w, w_b, c0, c1, xT, xT_b, blk, pi):
    for kc in range(8):
        C.S.op("pe", lambda e, kc=kc: e.matmul(out=C.ps[pi][:, 0:c1 - c0], lhsT=xT[:, kc, blk * 128:(blk + 1) * 128],
                                               rhs=w[:, kc, c0:c1], start=(kc == 0), stop=(kc == 7)),
               reads=[w_b, xT_b], writes=[C.psb[pi]])


def fetch_xT(C, xfull, xfull_b, t, xTs, xT_bs):
    k = t % 2
    C.S.dma("sp", xTs[k][:], xfull(t), reads=[xfull_b[t] if isinstance(xfull_b, list) else xfull_b], writes=[xT_bs[k]])
    return xTs[k], xT_bs[k]


def sb_phase(C, xfull, xfull_b, w_sb, yT, yT_b, TT, consts, tag):
    nc, S = C.nc, C.S
    ntile = TT // 512
    with contextlib.ExitStack() as es:
        w, w_b = load_w_bf16(C, es, f"wsb{tag}", w_sb, 384)
        QT = es.enter_context(nc.sbuf_tensor(f"sbQT{tag}", [P, TT], BF16))
        KT = es.enter_context(nc.sbuf_tensor(f"sbKT{tag}", [P, TT], BF16))
        V = es.enter_context(nc.sbuf_tensor(f"sbV{tag}", [P, TT // 128, 128], BF16))
        xTs = [es.enter_context(nc.sbuf_tensor(f"sbxT{tag}{k}", [P, 8, 512], BF16)) for k in range(2)]
        xT_bs = [Buf(), Buf()]
        negmask = es.enter_context(nc.sbuf_tensor(f"sbnm{tag}", [P, 4, 512], BF16))
        identb = es.enter_context(nc.sbuf_tensor(f"sbidb{tag}", [P, P], BF16))
        ntri = es.enter_context(nc.sbuf_tensor(f"sbntri{tag}", [P, P], BF16))
        ones = es.enter_context(nc.sbuf_tensor(f"sbones{tag}", [P, P], BF16))
        one1 = es.enter_context(nc.sbuf_tensor(f"sbone1{tag}", [P, 1], F32))
        cb = Buf()
        S.dma("pool", negmask[:], consts["sb_negmask"], writes=[cb])
        S.dma("pool", identb[:], consts["ident"], writes=[cb])
        S.dma("pool", ntri[:], consts["ntri"], writes=[cb])
        S.dma("pool", ones[:], consts["ones"], writes=[cb])
        S.op("pool", lambda e: e.memset(one1[:], 1.0), writes=[cb])
        x_b, xT_b = Buf(), Buf()
        QT_b = [Buf() for _ in range(ntile)]
        KT_b = [Buf() for _ in range(ntile)]
        V_b = [Buf() for _ in range(ntile)]
        xv = xfull if callable(xfull) else (lambda t, _v=xfull.rearrange("(t b p) d -> t p b d", b=4, p=P): _v[t])
        for t in range(ntile):
            xT, xT_b = fetch_xT(C, xfull, xfull_b, t, xTs, xT_bs)
            proj_fm(C, w, w_b, 0, 128, xT, xT_b, 0)
            S.op("act", lambda e, t=t: e.mul(out=QT[:, t * 512:(t + 1) * 512], in_=C.ps[0][:], mul=0.125),
                 reads=[C.psb[0]], writes=[QT_b[t]])
            proj_fm(C, w, w_b, 128, 256, xT, xT_b, 1)
            S.op("dve", lambda e, t=t: e.tensor_copy(out=KT[:, t * 512:(t + 1) * 512], in_=C.ps[1][:]),
                 reads=[C.psb[1]], writes=[KT_b[t]])
            for blk in range(4):
                pi = 2 + blk % 2
                proj_tm(C, w, w_b, 256, 384, xT, xT_b, blk, pi)
                S.op("act" if blk % 2 else "dve",
                     (lambda e, t=t, blk=blk, pi=pi: e.copy(out=V[:, t * 4 + blk, :], in_=C.ps[pi][:, 0:128])) if blk % 2 else
                     (lambda e, t=t, blk=blk, pi=pi: e.tensor_copy(out=V[:, t * 4 + blk, :], in_=C.ps[pi][:, 0:128])),
                     reads=[C.psb[pi]], writes=[V_b[t]])
        NCH = 4
        Ech = [es.enter_context(nc.sbuf_tensor(f"sbE2{tag}{c}", [P, 512], F32)) for c in range(NCH)]
        Lch = [es.enter_context(nc.sbuf_tensor(f"sbL2{tag}{c}", [P, 512], F32)) for c in range(NCH)]
        Sch = [[es.enter_context(nc.sbuf_tensor(f"sbS2{tag}{c}{k}", [P, 512], F32)) for k in range(2)] for c in range(NCH)]
        Wch = [es.enter_context(nc.sbuf_tensor(f"sbW2{tag}{c}", [P, 512], BF16)) for c in range(NCH)]
        Lhi = [es.enter_context(nc.sbuf_tensor(f"sbLh{tag}{c}", [P, 512], BF16)) for c in range(NCH)]
        Llo = [es.enter_context(nc.sbuf_tensor(f"sbLl{tag}{c}", [P, 512], BF16)) for c in range(NCH)]
        Shi = [es.enter_context(nc.sbuf_tensor(f"sbSh{tag}{c}", [P, 512], BF16)) for c in range(NCH)]
        Slo = [es.enter_context(nc.sbuf_tensor(f"sbSl{tag}{c}", [P, 512], BF16)) for c in range(NCH)]
        Lh_b = [Buf() for _ in range(NCH)]
        Ll_b = [Buf() for _ in range(NCH)]
        Sh_b = [Buf() for _ in range(NCH)]
        Sl_b = [Buf() for _ in range(NCH)]
        ych = [es.enter_context(nc.sbuf_tensor(f"sby2{tag}{c}", [P, 512], BF16)) for c in range(NCH)]
        E_b = [Buf() for _ in range(NCH)]
        L_b = [Buf() for _ in range(NCH)]
        S_b = [[Buf(), Buf()] for _ in range(NCH)]
        W_b = [Buf() for _ in range(NCH)]
        y_b = [Buf() for _ in range(NCH)]
        order = []
        lo, hi = 0, ntile - 1
        while lo <= hi:
            order.append(hi)
            hi -= 1
            if lo <= hi:
                order.append(lo)
                lo += 1
        queues = [[], []]
        load = [0, 0]
        for ti in sorted(range(ntile), key=lambda q: -q):
            sidx = 0 if load[0] <= load[1] else 1
            queues[sidx].append(ti)
            load[sidx] += 4 * ti + 4
        state = [None] * NCH
        qpos = [0, 0]

        def next_tile(slot):
            if qpos[slot] < len(queues[slot]):
                ti = queues[slot][qpos[slot]]
                qpos[slot] += 1
                return ti
            return None
        for slot in range(2):
            ti = next_tile(slot)
            for h in range(2):
                state[slot * 2 + h] = None if ti is None else [ti, 0]
        while any(st is not None for st in state):
            act = [c for c in range(NCH) if state[c] is not None]
            info = {}
            for c in act:
                i, n = state[c]
                nsteps = 4 * i + 4
                jb = 4 * i + 3 - n
                info[c] = dict(i=i, n=n, nsteps=nsteps, jb=jb, diag=jb >= 4 * i, r=jb - 4 * i, kt=jb // 4, h=c % 2,
                               hs=slice((c % 2) * 64, (c % 2) * 64 + 64), pz=c, py=4 + c // 2, k=n % 2)
            for c in act:
                f = info[c]
                if f["n"] > 0:
                    k = f["k"]
                    S.op("dve", lambda e, c=c, k=k: e.tensor_copy(out=Shi[c][:], in_=Sch[c][k][:]), reads=[S_b[c][k]], writes=[Sh_b[c]])
                    S.op("dve", lambda e, c=c, k=k: e.tensor_tensor(out=Slo[c][:], in0=Sch[c][k][:], in1=Shi[c][:], op=ALU.subtract),
                         reads=[S_b[c][k], Sh_b[c]], writes=[Sl_b[c]])
            for c in act:
                f = info[c]
                S.op("pe", lambda e, f=f: e.matmul(out=C.ps[f["pz"]][:], lhsT=KT[f["hs"], f["jb"] * 128:(f["jb"] + 1) * 128],
                                                   rhs=QT[f["hs"], f["i"] * 512:(f["i"] + 1) * 512], start=True, stop=not f["diag"]),
                     reads=[KT_b[f["kt"]], QT_b[f["i"]]], writes=[C.psb[f["pz"]]])
                if f["diag"]:
                    S.op("pe", lambda e, f=f: e.matmul(out=C.ps[f["pz"]][:], lhsT=identb[:], rhs=negmask[:, f["r"], :], start=False, stop=True),
                         reads=[cb], writes=[C.psb[f["pz"]]])
            for c in act:
                f = info[c]
                S.op("act", lambda e, c=c, f=f: e.activation(out=Ech[c][:], in_=C.ps[f["pz"]][:], func=AF.Exp),
                     reads=[C.psb[f["pz"]]], writes=[E_b[c]])
            for c in act:
                S.op("act", lambda e, c=c: e.activation(out=Lch[c][:], in_=Ech[c][:], func=AF.Ln, bias=one1[:], scale=1.0),
                     reads=[E_b[c], cb], writes=[L_b[c]])
            for c in act:
                if c < 2:
                    S.op("act", lambda e, c=c: e.copy(out=Lhi[c][:], in_=Lch[c][:]), reads=[L_b[c]], writes=[Lh_b[c]])
                else:
                    S.op("dve", lambda e, c=c: e.tensor_copy(out=Lhi[c][:], in_=Lch[c][:]), reads=[L_b[c]], writes=[Lh_b[c]])
                S.op("dve", lambda e, c=c: e.tensor_tensor(out=Llo[c][:], in0=Lch[c][:], in1=Lhi[c][:], op=ALU.subtract),
                     reads=[L_b[c], Lh_b[c]], writes=[Ll_b[c]])
            for c in act:
                f = info[c]
                if f["n"] > 0:
                    S.op("pe", lambda e, c=c, f=f: e.matmul(out=C.ps[f["pz"]][:], lhsT=ones[:], rhs=Shi[c][:], start=False, stop=False,
                                                            skip_group_check=True), reads=[cb, Sh_b[c]], writes=[C.psb[f["pz"]]])
                    S.op("pe", lambda e, c=c, f=f: e.matmul(out=C.ps[f["pz"]][:], lhsT=ones[:], rhs=Slo[c][:], start=False, stop=False,
                                                            skip_group_check=True), reads=[cb, Sl_b[c]], writes=[C.psb[f["pz"]]])
                S.op("pe", lambda e, c=c, f=f: e.matmul(out=C.ps[f["pz"]][:], lhsT=ntri[:], rhs=Lhi[c][:], start=False, stop=False,
                                                        skip_group_check=True), reads=[cb, Lh_b[c]], writes=[C.psb[f["pz"]]])
                S.op("pe", lambda e, c=c, f=f: e.matmul(out=C.ps[f["pz"]][:], lhsT=ntri[:], rhs=Llo[c][:], start=False, stop=True,
                                                        skip_group_check=True), reads=[cb, Ll_b[c]], writes=[C.psb[f["pz"]]])
            for c in act:
                f = info[c]
                S.op("act", lambda e, c=c, f=f: e.activation(out=Wch[c][:], in_=C.ps[f["pz"]][:], func=AF.Exp),
                     reads=[C.psb[f["pz"]]], writes=[W_b[c]])
                if f["n"] < f["nsteps"] - 1:
                    k, k2 = f["k"], 1 - f["k"]
                    if f["n"] == 0:
                        S.op("pool", lambda e, c=c, k2=k2: e.tensor_copy(out=Sch[c][k2][:], in_=Lch[c][:]), reads=[L_b[c]], writes=[S_b[c][k2]])
                    else:
                        S.op("pool", lambda e, c=c, k=k, k2=k2: e.tensor_tensor(out=Sch[c][k2][:], in0=Sch[c][k][:], in1=Lch[c][:], op=ALU.add),
                             reads=[L_b[c], S_b[c][k]], writes=[S_b[c][k2]])
            for c in act:
                f = info[c]
                po = (c % 2) * 64
                S.op("pe", lambda e, c=c, f=f, po=po: e.matmul(out=C.ps[f["py"]][po:po + 64, :], lhsT=V[:, f["jb"], f["hs"]], rhs=Wch[c][:],
                                                               start=(f["n"] == 0), stop=(f["n"] == f["nsteps"] - 1), skip_group_check=True),
                     reads=[V_b[f["kt"]], W_b[c]], writes=[C.psb[f["py"]]])
            for c in act:
                f = info[c]
                if f["n"] == f["nsteps"] - 1:
                    po = (c % 2) * 64
                    S.op("dve", lambda e, c=c, f=f, po=po: e.tensor_copy(out=ych[c][po:po + 64, :], in_=C.ps[f["py"]][po:po + 64, :]),
                         reads=[C.psb[f["py"]]], writes=[y_b[c]])
                    S.dma("sp", yT[f["h"] * 64:(f["h"] + 1) * 64, f["i"] * 512:(f["i"] + 1) * 512], ych[c][po:po + 64, :],
                          reads=[y_b[c]], writes=[yT_b])
                    state[c] = "done"
                else:
                    state[c][1] += 1
            for slot in range(2):
                cs = [slot * 2, slot * 2 + 1]
                if all(state[c] == "done" for c in cs):
                    ti = next_tile(slot)
                    for c in cs:
                        state[c] = None if ti is None else [ti, 0]
        S.barrier()


def make_consts_np():
    c = {}
    c["ident"] = np.eye(P, dtype=np.float32)
    j = np.arange(P)[:, None]
    s = np.arange(P)[None, :]
    c["ntri"] = -(j >= s).astype(np.float32)
    c["ones"] = -np.ones((P, P), np.float32)
    nm = np.zeros((P, 4, 512), np.float32)
    for r in range(4):
        key = 128 * r + np.arange(P)[:, None]
        col = np.arange(512)[None, :]
        nm[:, r, :] = np.where(key < col, 0.0, -30000.0)
    c["sb_negmask"] = nm
    make_swa_consts_np(c)
    make_ml_consts_np(c)
    return c


def dram_ap(t_ap, offset, pattern):
    return bass.AP(tensor=t_ap.tensor, offset=offset, ap=pattern)


def swa_phase(C, xfull, xfull_b, w_swa, rb_dram, sink_dram, frev_scr, yT, yT_b, TT, consts, tag):
    nc, S = C.nc, C.S
    ntile = TT // 512
    nblk = TT // 128
    C.set_psum(2)
    with contextlib.ExitStack() as es:
        w, w_b = load_w_bf16(C, es, f"wsw{tag}", w_swa, 448)
        QT = [es.enter_context(nc.sbuf_tensor(f"swQT{tag}{g}", [P, TT], BF16)) for g in range(2)]
        KT = es.enter_context(nc.sbuf_tensor(f"swKT{tag}", [P, TT], BF16))
        V = es.enter_context(nc.sbuf_tensor(f"swV{tag}", [P, nblk, 64], BF16))
        xTs = [es.enter_context(nc.sbuf_tensor(f"swxT{tag}{k}", [P, 8, 512], BF16)) for k in range(2)]
        xT_bs = [Buf(), Buf()]
        identb = es.enter_context(nc.sbuf_tensor(f"swidb{tag}", [P, P], BF16))
        Jm = es.enter_context(nc.sbuf_tensor(f"swJ{tag}", [P, P], F32))
        rb = es.enter_context(nc.sbuf_tensor(f"swrb{tag}", [P, P], F32))
        oh = es.enter_context(nc.sbuf_tensor(f"swoh{tag}", [P, 384], F32))
        fneg = es.enter_context(nc.sbuf_tensor(f"swfn{tag}", [4, 384], F32))
        frev = es.enter_context(nc.sbuf_tensor(f"swfr{tag}", [4, 384], F32))
        Hk = es.enter_context(nc.sbuf_tensor(f"swH{tag}", [P, 4, 256], F32))
        bias = es.enter_context(nc.sbuf_tensor(f"swbias{tag}", [P, 4, 256], F32))
        sink = es.enter_context(nc.sbuf_tensor(f"swsink{tag}", [P, 4], F32))
        Sb = es.enter_context(nc.sbuf_tensor(f"swS{tag}", [P, 4, 256], F32))
        pf = es.enter_context(nc.sbuf_tensor(f"swp{tag}", [P, 4, 256], F32))
        pn = es.enter_context(nc.sbuf_tensor(f"swpn{tag}", [P, 4, 256], BF16))
        pT = es.enter_context(nc.sbuf_tensor(f"swpT{tag}", [P, 8, 128], BF16))
        small = es.enter_context(nc.sbuf_tensor(f"swsm{tag}", [P, 6, 4], F32))
        yo = es.enter_context(nc.sbuf_tensor(f"swyo{tag}", [64, 4, 512], BF16))
        cb, x_b, xT_b = Buf(), Buf(), Buf()
        S.dma("pool", identb[:], consts["ident"], writes=[cb])
        S.dma("sp", Jm[:], consts["J"], writes=[cb])
        S.dma("sp", rb[:], rb_dram, writes=[cb])
        S.dma("sp", oh[:], consts["swa_oh"], writes=[cb])
        S.dma("sp", fneg[:], consts["swa_fneg"], writes=[cb])
        S.dma("sp", sink[:], sink_dram, writes=[cb])
        S.op("pe", lambda e: e.matmul(out=C.ps[0][:, 0:384], lhsT=rb[:], rhs=oh[:], start=True, stop=True),
             reads=[cb], writes=[C.psb[0]])
        fr_b, scr_b, H_b, bias_b = Buf(), Buf(), Buf(), Buf()
        S.op("dve", lambda e: e.tensor_tensor(out=frev[:], in0=C.ps[0][0:4, 0:384], in1=fneg[:], op=ALU.add),
             reads=[C.psb[0], cb], writes=[fr_b])
        S.dma("sp", frev_scr, frev[:], reads=[fr_b], writes=[scr_b])
        S.dma("sp", Hk[:], dram_ap(frev_scr, 0, [[1, 128], [384, 4], [1, 256]]), reads=[scr_b], writes=[H_b])
        for hh in range(2):
            S.op("pe", lambda e, hh=hh: e.matmul(out=C.ps[1 + hh][:], lhsT=Jm[:], rhs=Hk[:, 2 * hh:2 * hh + 2, :],
                                                 start=True, stop=True), reads=[cb, H_b], writes=[C.psb[1 + hh]])
            S.op("dve", lambda e, hh=hh: e.tensor_copy(out=bias[:, 2 * hh:2 * hh + 2, :], in_=C.ps[1 + hh][:]),
                 reads=[C.psb[1 + hh]], writes=[bias_b])
        QT_b = [Buf() for _ in range(ntile)]
        KT_b = [Buf() for _ in range(ntile)]
        V_b = [Buf() for _ in range(ntile)]
        xv = xfull if callable(xfull) else (lambda t, _v=xfull.rearrange("(t b p) d -> t p b d", b=4, p=P): _v[t])
        for t in range(ntile):
            xT, xT_b = fetch_xT(C, xfull, xfull_b, t, xTs, xT_bs)
            for g in range(2):
                proj_fm(C, w, w_b, g * 128, (g + 1) * 128, xT, xT_b, g)
                S.op("act", lambda e, t=t, g=g: e.mul(out=QT[g][:, t * 512:(t + 1) * 512], in_=C.ps[g][:], mul=0.125),
                     reads=[C.psb[g]], writes=[QT_b[t]])
            proj_fm(C, w, w_b, 256, 384, xT, xT_b, 2)
            S.op("dve", lambda e, t=t: e.tensor_copy(out=KT[:, t * 512:(t + 1) * 512], in_=C.ps[2][:]),
                 reads=[C.psb[2]], writes=[KT_b[t]])
            for blk in range(4):
                pi = 3 + blk % 2
                proj_tm(C, w, w_b, 384, 448, xT, xT_b, blk, pi)
                S.op("dve", lambda e, t=t, blk=blk, pi=pi: e.tensor_copy(out=V[:, t * 4 + blk, :], in_=C.ps[pi][:, 0:64]),
                     reads=[C.psb[pi]], writes=[V_b[t]])
        NS = 2
        Sb2 = [Sb] + [es.enter_context(nc.sbuf_tensor(f"swS{tag}b", [P, 4, 256], F32))]
        pf2 = [pf] + [es.enter_context(nc.sbuf_tensor(f"swp{tag}b", [P, 4, 256], F32))]
        pn2 = [pn] + [es.enter_context(nc.sbuf_tensor(f"swpn{tag}b", [P, 4, 256], BF16))]
        pT2 = [pT] + [es.enter_context(nc.sbuf_tensor(f"swpT{tag}b", [P, 8, 128], BF16))]
        sm2 = [small] + [es.enter_context(nc.sbuf_tensor(f"swsm{tag}b", [P, 6, 4], F32))]
        S_b = [Buf(), Buf()]
        p_b = [Buf(), Buf()]
        pn_b = [Buf(), Buf()]
        pT_b = [Buf(), Buf()]
        sm_b = [Buf(), Buf()]
        yo_b = Buf()
        for n0 in range(0, nblk, NS):
            blks = [(n0 + a, a) for a in range(NS) if n0 + a < nblk]
            kwd = {n: (128 if n == 0 else 256) for n, a in blks}
            for n, a in blks:
                kw = kwd[n]
                k0 = 256 - kw
                t_q = n // 4
                for h in range(4):
                    g, hs = h % 2, slice((h // 2) * 64, (h // 2) * 64 + 64)
                    bank = 2 * a + h // 2
                    col = (h % 2) * 256
                    S.op("pe", lambda e, g=g, hs=hs, bank=bank, col=col, n=n, kw=kw, k0=k0: e.matmul(
                        out=C.ps[bank][:, col + k0:col + 256], lhsT=QT[g][hs, n * 128:(n + 1) * 128],
                        rhs=KT[hs, (n + 1) * 128 - kw:(n + 1) * 128], start=True, stop=True, skip_group_check=True),
                        reads=[QT_b[t_q], KT_b[t_q], KT_b[max(0, (n - 1) // 4)]], writes=[C.psb[bank]])
            for n, a in blks:
                k0 = 256 - kwd[n]
                mx, negm, dd, esk, rs, rden = [sm2[a][:, i, :] for i in range(6)]
                for bk in range(2):
                    bank = 2 * a + bk
                    S.op("dve", lambda e, bank=bank, bk=bk, k0=k0, a=a: e.tensor_tensor(
                        out=Sb2[a][:, 2 * bk:2 * bk + 2, k0:256],
                        in0=C.ps[bank][:].rearrange("p (h k) -> p h k", h=2)[:, :, k0:256],
                        in1=bias[:, 2 * bk:2 * bk + 2, k0:256], op=ALU.add),
                        reads=[C.psb[bank], bias_b], writes=[S_b[a]])
                S.op("dve", lambda e, k0=k0, a=a, mx=mx: e.tensor_reduce(out=mx, in_=Sb2[a][:, :, k0:256], axis=AX.X, op=ALU.max),
                     reads=[S_b[a]], writes=[sm_b[a]])
                S.op("dve", lambda e, mx=mx: e.tensor_tensor(out=mx, in0=mx, in1=sink[:], op=ALU.max), reads=[sm_b[a], cb], writes=[sm_b[a]])
                S.op("dve", lambda e, mx=mx, negm=negm: e.tensor_scalar(out=negm, in0=mx, scalar1=-1.0, scalar2=None, op0=ALU.mult),
                     reads=[sm_b[a]], writes=[sm_b[a]])
                S.op("dve", lambda e, mx=mx, dd=dd: e.tensor_tensor(out=dd, in0=sink[:], in1=mx, op=ALU.subtract), reads=[sm_b[a], cb], writes=[sm_b[a]])
            for n, a in blks:
                k0 = 256 - kwd[n]
                mx, negm, dd, esk, rs, rden = [sm2[a][:, i, :] for i in range(6)]
                for h in range(4):
                    S.op("act", lambda e, h=h, k0=k0, a=a, negm=negm, rs=rs: e.activation(
                        out=pf2[a][:, h, k0:256], in_=Sb2[a][:, h, k0:256], func=AF.Exp, bias=negm[:, h:h + 1], scale=1.0, accum_out=rs[:, h:h + 1]),
                        reads=[S_b[a], sm_b[a]], writes=[p_b[a], sm_b[a]])
                S.op("act", lambda e, esk=esk, dd=dd: e.activation(out=esk, in_=dd, func=AF.Exp), reads=[sm_b[a]], writes=[sm_b[a]])
            for n, a in blks:
                k0 = 256 - kwd[n]
                mx, negm, dd, esk, rs, rden = [sm2[a][:, i, :] for i in range(6)]
                S.op("dve", lambda e, rden=rden, rs=rs, esk=esk: e.tensor_tensor(out=rden, in0=rs, in1=esk, op=ALU.add), reads=[sm_b[a]], writes=[sm_b[a]])
                S.op("dve", lambda e, rden=rden: e.reciprocal(out=rden, in_=rden), reads=[sm_b[a]], writes=[sm_b[a]])
                for h in range(4):
                    if h % 2:
                        S.op("dve", lambda e, h=h, k0=k0, a=a, rden=rden: e.tensor_scalar(
                            out=pn2[a][:, h, k0:256], in0=pf2[a][:, h, k0:256], scalar1=rden[:, h:h + 1], scalar2=None, op0=ALU.mult),
                            reads=[p_b[a], sm_b[a]], writes=[pn_b[a]])
                    else:
                        S.op("act", lambda e, h=h, k0=k0, a=a, rden=rden: e.mul(out=pn2[a][:, h, k0:256], in_=pf2[a][:, h, k0:256], mul=rden[:, h:h + 1]),
                             reads=[p_b[a], sm_b[a]], writes=[pn_b[a]])
            for n, a in blks:
                kw = kwd[n]
                k0 = 256 - kw
                nkb = kw // 128
                for h in range(4):
                    for kb in range(nkb):
                        idx = h * 2 + kb
                        S.op("pe", lambda e, h=h, kb=kb, idx=idx, k0=k0, a=a: e.transpose(
                            out=C.ps_bfs[a][:, idx * 128:(idx + 1) * 128], in_=pn2[a][:, h, k0 + kb * 128:k0 + (kb + 1) * 128],
                            identity=identb[:]), reads=[pn_b[a], cb], writes=[C.psbf_b[a]])
            for n, a in blks:
                if a == 0:
                    S.op("act", lambda e, a=a: e.copy(out=pT2[a][:].rearrange("p a b -> p (a b)"), in_=C.ps_bfs[a][:]), reads=[C.psbf_b[a]], writes=[pT_b[a]])
                else:
                    S.op("dve", lambda e, a=a: e.tensor_copy(out=pT2[a][:].rearrange("p a b -> p (a b)"), in_=C.ps_bfs[a][:]), reads=[C.psbf_b[a]], writes=[pT_b[a]])
            for n, a in blks:
                nkb = kwd[n] // 128
                po = 4 + a
                for h in range(4):
                    for kb in range(nkb):
                        idx = h * 2 + kb
                        kblk = n - (nkb - 1) + kb
                        hd = (h % 2) * 2 + h // 2
                        S.op("pe", lambda e, hd=hd, kb=kb, idx=idx, kblk=kblk, nkb=nkb, a=a, po=po: e.matmul(
                            out=C.ps[po][0:64, hd * 128:(hd + 1) * 128], lhsT=V[:, kblk, :], rhs=pT2[a][:, idx, :],
                            start=(kb == 0), stop=(kb == nkb - 1), skip_group_check=True),
                            reads=[V_b[kblk // 4], pT_b[a]], writes=[C.psb[po]])
            for n, a in blks:
                po = 4 + a
                S.op("dve" if a == 0 else "act",
                     (lambda e, n=n, po=po: e.tensor_copy(out=yo[:, :, (n % 4) * 128:(n % 4 + 1) * 128],
                                                          in_=C.ps[po][0:64, :].rearrange("p (h q) -> p h q", h=4))) if a == 0 else
                     (lambda e, n=n, po=po: e.copy(out=yo[:, :, (n % 4) * 128:(n % 4 + 1) * 128],
                                                   in_=C.ps[po][0:64, :].rearrange("p (h q) -> p h q", h=4))),
                     reads=[C.psb[po]], writes=[yo_b])
                if n % 4 == 3:
                    S.dma("sp", yT.rearrange("(h d) t -> d h t", d=64)[:, :, (n // 4) * 512:(n // 4 + 1) * 512], yo[:],
                          reads=[yo_b], writes=[yT_b])
        S.barrier()
        C.set_psum(1)


def t5_bucket_np(dist):
    max_exact = 16
    d = np.maximum(dist, 1)
    large = max_exact + (np.log(d / max_exact) / np.log(128 / max_exact) * (32 - max_exact)).astype(np.int32)
    large = np.minimum(large, 31)
    return np.where(dist < max_exact, dist, large).astype(np.int32)


def make_swa_consts_np(c):
    a = np.arange(384)
    dist = 255 - a
    valid = (dist >= 0) & (dist < 128)
    bucket = t5_bucket_np(np.clip(dist, 0, None))
    oh = np.zeros((32, 384), np.float32)
    oh[bucket[valid], a[valid]] = 1.0
    c["swa_oh"] = np.pad(oh, ((0, 96), (0, 0)))
    c["swa_fneg"] = np.tile(np.where(valid, 0.0, -30000.0).astype(np.float32)[None, :], (4, 1))
    c["J"] = np.eye(P, dtype=np.float32)[::-1].copy()
    return c


def mlstm_phase(C, xfull, xfull_b, w_ml, cw_dram, cb_dram, ifb_dram, ng_dram, yT, yT_b, TT, consts, tag):
    nc, S = C.nc, C.S
    ntile = TT // 512
    nb = TT // 128
    n2 = 2 * nb
    with contextlib.ExitStack() as es:
        w, w_b = load_w_bf16(C, es, f"wml{tag}", w_ml, 640)
        sb = lambda name, shape, dt=F32: es.enter_context(nc.sbuf_tensor(f"ml{name}{tag}", shape, dt))
        QT = sb("QT", [P, TT], BF16)
        KT = sb("KT", [P, TT], BF16)
        Vext = sb("Vext", [P, nb, 2, 66], BF16)
        gso = sb("gso", [P, nb, 128])
        G4 = sb("G4", [P, nb, 4])
        xTs = [sb(f"xT{k}", [P, 8, 512], BF16) for k in range(2)]
        xT_bs = [Buf(), Buf()]
        raws = [sb(f"raw{k}", [P, 2, 515]) for k in range(2)]
        accs = [sb(f"acc{k}", [P, 2, 512]) for k in range(2)]
        cw = sb("cw", [P, 8])
        cbt = sb("cb", [P, 2])
        ifb = sb("ifb", [P, 4])
        nfb = sb("nfb", [P, 2])
        ng = sb("ng", [P, 128])
        identb = sb("idb", [P, P], BF16)
        ntriT = sb("ntriT", [P, P])
        mask8 = sb("mask8", [P, P])
        e0 = sb("e0", [P, P])
        e127 = sb("e127", [P, P])
        one1 = sb("one1", [P, 1])
        cb_ = Buf()
        S.dma("pool", identb[:], consts["ident"], writes=[cb_])
        S.dma("sp", ntriT[:], consts["ntriT"], writes=[cb_])
        S.dma("sp", mask8[:], consts["mask8"], writes=[cb_])
        S.dma("sp", e0[:], consts["e0ones"], writes=[cb_])
        S.dma("sp", e127[:], consts["e127ones"], writes=[cb_])
        S.dma("sp", cw[:], cw_dram, writes=[cb_])
        S.dma("sp", cbt[:], cb_dram, writes=[cb_])
        S.dma("sp", ifb[:], ifb_dram, writes=[cb_])
        S.dma("sp", ng[:], ng_dram, writes=[cb_])
        S.op("pool", lambda e: e.memset(one1[:], 1.0), writes=[cb_])
        S.op("dve", lambda e: e.tensor_scalar(out=nfb[:], in0=ifb[:, 2:4], scalar1=-1.0, scalar2=None, op0=ALU.mult),
             reads=[cb_], writes=[cb_])
        x_b, xT_b, V_b, gso_b, G_b = Buf(), Buf(), Buf(), Buf(), Buf()
        raw_bs, acc_bs = [Buf(), Buf()], [Buf(), Buf()]
        QT_b = [Buf() for _ in range(ntile)]
        KT_b = [Buf() for _ in range(ntile)]
        for k in range(2):
            S.op("pool", lambda e, k=k: e.memset(raws[k][:], 0.0), writes=[raw_bs[k]])
        S.op("pool", lambda e: e.memset(Vext[:], 1.0), writes=[V_b])
        xv = xfull if callable(xfull) else (lambda t, _v=xfull.rearrange("(t b p) d -> t p b d", b=4, p=P): _v[t])
        import os
        mlstage = int(os.environ.get("ML_STAGE", "9"))
        for t in range(ntile):
            if mlstage < -1:
                break
            xT, xT_b = fetch_xT(C, xfull, xfull_b, t, xTs, xT_bs)
            raw, raw_b, acc, acc_b = raws[t % 2], raw_bs[t % 2], accs[t % 2], acc_bs[t % 2]
            if t > 0:
                S.op("pool", lambda e, raw=raw, t=t: e.tensor_copy(out=raw[:, :, 0:3], in_=raws[(t - 1) % 2][:, :, 512:515]),
                     reads=[raw_bs[(t - 1) % 2]], writes=[raw_b])
            for qk in range(2):
                proj_fm(C, w, w_b, qk * 128, (qk + 1) * 128, xT, xT_b, qk)
                S.op("act", lambda e, qk=qk, raw=raw: e.copy(out=raw[:, qk, 3:515], in_=C.ps[qk][:]), reads=[C.psb[qk]], writes=[raw_b])
            for qk in range(2):
                S.op("dve", lambda e, qk=qk, raw=raw, acc=acc: e.tensor_scalar(out=acc[:, qk, :], in0=raw[:, qk, 0:512], scalar1=cw[:, 4 * qk:4 * qk + 1],
                                                              scalar2=None, op0=ALU.mult), reads=[raw_b, cb_], writes=[acc_b])
                for j in range(1, 4):
                    S.op("dve", lambda e, qk=qk, j=j, raw=raw, acc=acc: e.scalar_tensor_tensor(
                        out=acc[:, qk, :], in0=raw[:, qk, j:j + 512], scalar=cw[:, 4 * qk + j:4 * qk + j + 1], in1=acc[:, qk, :],
                        op0=ALU.mult, op1=ALU.add), reads=[raw_b, cb_, acc_b], writes=[acc_b])
                dst, dst_b = (QT, QT_b) if qk == 0 else (KT, KT_b)
                S.op("act", lambda e, qk=qk, dst=dst, t=t, acc=acc: e.activation(out=dst[:, t * 512:(t + 1) * 512], in_=acc[:, qk, :], func=AF.Silu,
                                                                        bias=cbt[:, qk:qk + 1], scale=1.0),
                     reads=[acc_b, cb_], writes=[dst_b[t]])
            for blk in range(4):
                if mlstage < 0:
                    break
                pi = 2 + blk % 2
                bi = t * 4 + blk
                proj_tm(C, w, w_b, 256, 512, xT, xT_b, blk, pi)
                proj_tm(C, w, w_b, 512, 640, xT, xT_b, blk, 4 + blk % 2)
                sub = int(os.environ.get("ML_SUB", "9"))
                if sub < 1:
                    continue
                S.op("dve", lambda e, pi=pi, bi=bi: e.tensor_copy(out=Vext[:, bi, :, 0:64],
                                                                  in_=C.ps[pi][:, 0:128].rearrange("p (h d) -> p h d", h=2)),
                     reads=[C.psb[pi]], writes=[V_b])
                if sub < 2:
                    continue
                S.op("act", lambda e, pi=pi, bi=bi: e.activation(out=gso[:, bi, :], in_=C.ps[pi][:, 128:256], func=AF.Exp, scale=-1.0),
                     reads=[C.psb[pi]], writes=[gso_b])
                S.op("act", lambda e, bi=bi: e.add(out=gso[:, bi, :], in_=gso[:, bi, :], add=1.0), reads=[gso_b], writes=[gso_b])
                S.op("dve", lambda e, bi=bi: e.reciprocal(out=gso[:, bi, :], in_=gso[:, bi, :]), reads=[gso_b], writes=[gso_b])
                if sub < 3:
                    continue
                S.op("dve", lambda e, blk=blk, bi=bi: e.tensor_copy(out=G4[:, bi, :], in_=C.ps[4 + blk % 2][:, 0:4]),
                     reads=[C.psb[4 + blk % 2]], writes=[G_b])
                S.op("pool", lambda e, bi=bi: e.tensor_tensor(out=gso[:, bi, :], in0=gso[:, bi, :], in1=ng[:], op=ALU.mult),
                     reads=[gso_b, cb_], writes=[gso_b])
        import os
        mlstage = int(os.environ.get("ML_STAGE", "9"))
        if mlstage < 1:
            S.barrier()
            return
        icol = sb("icol", [P, 2, nb])
        lf = sb("lf", [P, 2, nb])
        bcol = sb("bcol", [P, 2, nb])
        acol = sb("acol", [P, 2, nb])
        aT = sb("aT", [P, P])
        cm_tok = sb("cm_tok", [P, n2])
        amax_bc = sb("amax_bc", [P, n2])
        cmT = sb("cmT", [P, P])
        RW = sb("RW", [P, 3, P])
        mnext = sb("mnext", [1, P])
        mprev_bc = sb("mprevbc", [P, n2])
        mref_bc = sb("mrefbc", [P, n2])
        Mt = sb("Mt", [P, n2])
        r_t = sb("r_t", [P, n2])
        u_t = sb("u_t", [P, n2])
        eb_t = sb("eb_t", [P, n2])
        sc_bc = sb("sc_bc", [P, n2])
        tmp = sb("tmpg", [P, n2])
        g_b = Buf()
        fl = lambda tl: tl[:].rearrange("p h c -> p (h c)")
        for h in range(2):
            S.op("act", lambda e, h=h: e.activation(out=icol[:, h, :], in_=G4[:, :, h], func=AF.Identity, bias=ifb[:, h:h + 1], scale=1.0),
                 reads=[G_b, cb_], writes=[g_b])
            S.op("act", lambda e, h=h: e.activation(out=lf[:, h, :], in_=G4[:, :, 2 + h], func=AF.Exp, bias=nfb[:, h:h + 1], scale=-1.0),
                 reads=[G_b, cb_], writes=[g_b])
        S.op("act", lambda e: e.activation(out=fl(lf), in_=fl(lf), func=AF.Ln, bias=one1[:], scale=1.0), reads=[g_b, cb_], writes=[g_b])
        S.op("pe", lambda e: e.matmul(out=C.ps[0][:, 0:n2], lhsT=ntriT[:], rhs=fl(lf), start=True, stop=True),
             reads=[g_b, cb_], writes=[C.psb[0]])
        S.op("dve", lambda e: e.tensor_copy(out=fl(bcol), in_=C.ps[0][:, 0:n2]), reads=[C.psb[0]], writes=[g_b])
        S.op("dve", lambda e: e.tensor_tensor(out=fl(acol), in0=fl(icol), in1=fl(bcol), op=ALU.subtract), reads=[g_b], writes=[g_b])
        S.op("pe", lambda e: e.transpose(out=C.ps[1][0:n2, 0:128], in_=fl(acol), identity=C.ident[:]), reads=[g_b, C.ident_b], writes=[C.psb[1]])
        S.op("pool", lambda e: e.memset(aT[:], 0.0), writes=[g_b])
        S.op("pool", lambda e: e.memset(RW[:], 0.0), writes=[g_b])
        S.op("dve", lambda e: e.tensor_copy(out=aT[0:n2, :], in_=C.ps[1][0:n2, 0:128]), reads=[C.psb[1]], writes=[g_b])
        S.op("dve", lambda e: e.tensor_tensor_scan(out=cmT[:], data0=aT[:], data1=aT[:], initial=-1.0e30, op0=ALU.max, op1=ALU.max),
             reads=[g_b], writes=[g_b])
        S.op("pe", lambda e: e.transpose(out=C.ps[2][:, 0:128], in_=cmT[:], identity=C.ident[:]), reads=[g_b, C.ident_b], writes=[C.psb[2]])
        S.op("dve", lambda e: e.tensor_copy(out=cm_tok[:], in_=C.ps[2][:, 0:n2]), reads=[C.psb[2]], writes=[g_b])
        S.op("pe", lambda e: e.matmul(out=C.ps[3][:, 0:n2], lhsT=e127[:], rhs=cm_tok[:], start=True, stop=True), reads=[g_b, cb_], writes=[C.psb[3]])
        S.op("pe", lambda e: e.matmul(out=C.ps[4][:, 0:n2], lhsT=e127[:], rhs=fl(bcol), start=True, stop=True), reads=[g_b, cb_], writes=[C.psb[4]])
        S.op("dve", lambda e: e.tensor_copy(out=amax_bc[:], in_=C.ps[3][:, 0:n2]), reads=[C.psb[3]], writes=[g_b])
        S.op("dve", lambda e: e.tensor_copy(out=RW[:, 1, 0:n2], in_=C.ps[4][:, 0:n2]), reads=[C.psb[4]], writes=[g_b])
        S.op("dve", lambda e: e.tensor_copy(out=RW[:, 0, 0:n2], in_=amax_bc[:]), reads=[g_b], writes=[g_b])
        for h in range(2):
            S.op("dve", lambda e, h=h: e.tensor_tensor_scan(out=mnext[0:1, h * nb:(h + 1) * nb], data0=RW[0:1, 0, h * nb:(h + 1) * nb],
                                                            data1=RW[0:1, 1, h * nb:(h + 1) * nb], initial=0.0, op0=ALU.max, op1=ALU.add),
                 reads=[g_b], writes=[g_b])
            if nb > 1:
                S.op("dve", lambda e, h=h: e.tensor_copy(out=RW[0:1, 2, h * nb + 1:(h + 1) * nb], in_=mnext[0:1, h * nb:(h + 1) * nb - 1]),
                     reads=[g_b], writes=[g_b])
        S.op("pe", lambda e: e.matmul(out=C.ps[0][:, 0:n2], lhsT=e0[:], rhs=RW[:, 2, 0:n2], start=True, stop=True),
             reads=[g_b, cb_], writes=[C.psb[0]])
        S.op("dve", lambda e: e.tensor_copy(out=mprev_bc[:], in_=C.ps[0][:, 0:n2]), reads=[C.psb[0]], writes=[g_b])
        S.op("dve", lambda e: e.tensor_tensor(out=mref_bc[:], in0=amax_bc[:], in1=mprev_bc[:], op=ALU.max), reads=[g_b], writes=[g_b])
        S.op("dve", lambda e: e.tensor_tensor(out=Mt[:], in0=cm_tok[:], in1=mprev_bc[:], op=ALU.max), reads=[g_b], writes=[g_b])
        S.op("dve", lambda e: e.tensor_tensor(out=tmp[:], in0=mref_bc[:], in1=Mt[:], op=ALU.subtract), reads=[g_b], writes=[g_b])
        S.op("act", lambda e: e.activation(out=r_t[:], in_=tmp[:], func=AF.Exp), reads=[g_b], writes=[g_b])
        S.op("dve", lambda e: e.tensor_tensor(out=tmp[:], in0=fl(acol), in1=mref_bc[:], op=ALU.subtract), reads=[g_b], writes=[g_b])
        S.op("act", lambda e: e.activation(out=u_t[:], in_=tmp[:], func=AF.Exp), reads=[g_b], writes=[g_b])
        S.op("dve", lambda e: e.tensor_tensor(out=tmp[:], in0=fl(bcol), in1=Mt[:], op=ALU.add), reads=[g_b], writes=[g_b])
        S.op("act", lambda e: e.activation(out=eb_t[:], in_=tmp[:], func=AF.Exp, scale=-1.0), reads=[g_b], writes=[g_b])
        S.op("dve", lambda e: e.tensor_tensor(out=tmp[:], in0=mprev_bc[:], in1=mref_bc[:], op=ALU.subtract), reads=[g_b], writes=[g_b])
        S.op("act", lambda e: e.activation(out=sc_bc[:], in_=tmp[:], func=AF.Exp), reads=[g_b], writes=[g_b])
        if mlstage < 2:
            S.barrier()
            return
        Cst = sb("Cst", [P, 65])
        Csb = sb("Csb", [P, 130], BF16)
        Kp = sb("Kp", [P, P], BF16)
        ST = [sb(f"ST{h}", [P, P], BF16) for h in range(2)]
        nd = sb("nd", [P, 2, 65])
        hid = sb("hid", [P, 2, 64])
        sm = sb("sm", [P, 8])
        stats = sb("stats", [P, 2, 6])
        mv = sb("mv", [P, 2, 2])
        yTt = sb("yTt", [P, 512], BF16)
        Cst_b, Csb_b, Kp_b, ST_b, nd_b, hid_b, sm_b, yTt_b = Buf(), Buf(), Buf(), [Buf(), Buf()], Buf(), Buf(), Buf(), Buf()
        S.op("pool", lambda e: e.memset(Cst[:], 0.0), writes=[Cst_b])
        S.op("pool", lambda e: e.memset(Csb[:], 0.0), writes=[Csb_b])
        eb3 = eb_t[:].rearrange("p (h c) -> p h c", h=2)
        for c in range(nb):
            tq = c // 4
            cs = slice(c * 128, (c + 1) * 128)
            S.op("pe", lambda e, cs=cs: e.transpose(out=C.ps_bf[:, 0:128], in_=KT[:, cs], identity=identb[:]),
                 reads=[KT_b[tq], cb_], writes=[C.psb[7]])
            for h in range(2):
                hs = slice(h * 64, (h + 1) * 64)
                ix = h * nb + c
                S.op("dve", lambda e, hs=hs, ix=ix: e.tensor_scalar(out=Kp[:, hs], in0=C.ps_bf[:, hs], scalar1=u_t[:, ix:ix + 1], scalar2=0.125,
                                                                    op0=ALU.mult, op1=ALU.mult), reads=[C.psb[7], g_b], writes=[Kp_b])
                S.op("pe", lambda e, hs=hs, h=h, cs=cs: e.matmul(out=C.ps[h][:, 0:128], lhsT=KT[hs, cs], rhs=QT[hs, cs], start=True, stop=True),
                     reads=[KT_b[tq], QT_b[tq]], writes=[C.psb[h]])
                S.op("dve", lambda e, h=h, ix=ix: e.scalar_tensor_tensor(out=ST[h][:], in0=C.ps[h][:, 0:128], scalar=u_t[:, ix:ix + 1], in1=mask8[:],
                                                                         op0=ALU.mult, op1=ALU.mult), reads=[C.psb[h], g_b, cb_], writes=[ST_b[h]])
                S.op("dve", lambda e, hs=hs, h=h, ix=ix: e.tensor_scalar(out=Csb[hs, h * 65:(h + 1) * 65], in0=Cst[hs, :], scalar1=sc_bc[hs, ix:ix + 1],
                                                                         scalar2=None, op0=ALU.mult), reads=[Cst_b, g_b], writes=[Csb_b])
            S.op("pe", lambda e, cs=cs: e.matmul(out=C.ps[2][:, 0:130], lhsT=QT[:, cs], rhs=Csb[:], start=True, stop=False, skip_group_check=True),
                 reads=[QT_b[tq], Csb_b], writes=[C.psb[2]])
            for h in range(2):
                S.op("pe", lambda e, h=h, c=c: e.matmul(out=C.ps[2][:, h * 65:(h + 1) * 65], lhsT=ST[h][:], rhs=Vext[:, c, h, 0:65], start=False, stop=(h == 1),
                                                        skip_group_check=True), reads=[ST_b[h], V_b], writes=[C.psb[2]])
            S.op("pe", lambda e, c=c: e.matmul(out=C.ps[3][:, 0:130], lhsT=Kp[:], rhs=Vext[:, c, :, 0:65], start=True, stop=True),
                 reads=[Kp_b, V_b], writes=[C.psb[3]])
            for h in range(2):
                hs = slice(h * 64, (h + 1) * 64)
                ix = h * nb + c
                S.op("dve", lambda e, hs=hs, h=h, ix=ix: e.scalar_tensor_tensor(out=Cst[hs, :], in0=Cst[hs, :], scalar=sc_bc[hs, ix:ix + 1],
                                                                                in1=C.ps[3][hs, h * 65:(h + 1) * 65], op0=ALU.mult, op1=ALU.add),
                     reads=[Cst_b, g_b, C.psb[3], Csb_b], writes=[Cst_b])
                S.op("dve", lambda e, h=h, ix=ix: e.tensor_scalar(out=nd[:, h, :], in0=C.ps[2][:, h * 65:(h + 1) * 65], scalar1=r_t[:, ix:ix + 1],
                                                                  scalar2=None, op0=ALU.mult), reads=[C.psb[2], g_b], writes=[nd_b])
            S.op("dve", lambda e: e.tensor_scalar(out=sm[:, 0:2], in0=nd[:, :, 64], scalar1=-1.0, scalar2=None, op0=ALU.mult), reads=[nd_b], writes=[sm_b])
            S.op("dve", lambda e: e.tensor_tensor(out=sm[:, 0:2], in0=sm[:, 0:2], in1=nd[:, :, 64], op=ALU.max), reads=[nd_b, sm_b], writes=[sm_b])
            S.op("dve", lambda e, c=c: e.tensor_tensor(out=sm[:, 0:2], in0=sm[:, 0:2], in1=eb3[:, :, c], op=ALU.max), reads=[sm_b, g_b], writes=[sm_b])
            S.op("dve", lambda e: e.reciprocal(out=sm[:, 0:2], in_=sm[:, 0:2]), reads=[sm_b], writes=[sm_b])
            for h in range(2):
                S.op("dve", lambda e, h=h: e.tensor_scalar(out=hid[:, h, :], in0=nd[:, h, 0:64], scalar1=sm[:, h:h + 1], scalar2=None, op0=ALU.mult),
                     reads=[nd_b, sm_b], writes=[hid_b])
                S.op("dve", lambda e, h=h: e.bn_stats(out=stats[:, h, :], in_=hid[:, h, :]), reads=[hid_b], writes=[sm_b])
                S.op("dve", lambda e, h=h: e.bn_aggr(out=mv[:, h, :], in_=stats[:, h, :]), reads=[sm_b], writes=[sm_b])
            S.op("act", lambda e: e.activation(out=sm[:, 2:4], in_=mv[:, :, 1], func=AF.Sqrt, bias=C.eps_t[:], scale=1.0), reads=[sm_b, C.ident_b], writes=[sm_b])
            S.op("dve", lambda e: e.reciprocal(out=sm[:, 2:4], in_=sm[:, 2:4]), reads=[sm_b], writes=[sm_b])
            for h in range(2):
                S.op("dve", lambda e, h=h: e.tensor_scalar(out=hid[:, h, :], in0=hid[:, h, :], scalar1=mv[:, h, 0:1], scalar2=sm[:, 2 + h:3 + h],
                                                           op0=ALU.subtract, op1=ALU.mult), reads=[hid_b, sm_b], writes=[hid_b])
            S.op("dve", lambda e, c=c: e.tensor_tensor(out=hid[:].rearrange("p h d -> p (h d)"), in0=hid[:].rearrange("p h d -> p (h d)"),
                                                        in1=gso[:, c, :], op=ALU.mult), reads=[hid_b, gso_b], writes=[hid_b])
            S.op("pe", lambda e, c=c: e.transpose(out=C.ps[4][:, (c % 4) * 128:(c % 4 + 1) * 128], in_=hid[:].rearrange("p h d -> p (h d)"),
                                                  identity=C.ident[:]), reads=[hid_b, C.ident_b], writes=[C.psb[4]])
            if c % 4 == 3:
                S.op("act", lambda e: e.copy(out=yTt[:], in_=C.ps[4][:]), reads=[C.psb[4]], writes=[yTt_b])
                S.dma("sp", yT[:, (c // 4) * 512:(c // 4 + 1) * 512], yTt[:], reads=[yTt_b], writes=[yT_b])
        S.barrier()


def make_ml_consts_np(c):
    k = np.arange(P)[:, None]
    m = np.arange(P)[None, :]
    c["ntriT"] = -(k <= m).astype(np.float32)
    c["mask8"] = (k <= m).astype(np.float32) * 0.125
    e0 = np.zeros((P, P), np.float32)
    e0[0, :] = 1.0
    c["e0ones"] = e0
    e127 = np.zeros((P, P), np.float32)
    e127[127, :] = 1.0
    c["e127ones"] = e127
    return c


def outproj_ln(C, es, tag, lhs_chunks, lhs_b, wo, wo_b, xt, x_b):
    S = C.S
    for blk in range(4):
        for hh in range(2):
            po = 4 + (blk * 2 + hh) % 2
            for k in range(8):
                S.op("pe", lambda e, k=k, blk=blk, hh=hh, po=po: e.matmul(
                    out=C.ps[po][:], lhsT=lhs_chunks[:, k, blk * 128:(blk + 1) * 128], rhs=wo[:, k, hh * 512:(hh + 1) * 512],
                    start=(k == 0), stop=(k == 7)), reads=[lhs_b, wo_b], writes=[C.psb[po]])
            S.op("dve", lambda e, blk=blk, hh=hh, po=po: e.tensor_tensor(
                out=xt[:, blk, hh * 512:(hh + 1) * 512], in0=C.ps[po][:], in1=xt[:, blk, hh * 512:(hh + 1) * 512], op=ALU.add),
                reads=[C.psb[po], x_b], writes=[x_b])


def mixout_phase(C, x_in, xin_b, ygath, yg_b, w_out, sel_dram, g_dram, b_dram, x_out, xout_b, NT, TT, tag):
    nc, S = C.nc, C.S
    ntile = NT // 512
    with contextlib.ExitStack() as es:
        wo, wo_b = load_w_bf16(C, es, f"wmo{tag}", w_out, D)
        sb = lambda name, shape, dt=F32: es.enter_context(nc.sbuf_tensor(f"mo{name}{tag}", shape, dt))
        g_bc, b_bc, sel = sb("g", [P, D]), sb("b", [P, D]), sb("sel", [P, 2])
        xts = [sb(f"xt{i}", [P, 4, D]) for i in range(2)]
        yAs = [sb(f"yA{i}", [P, 8, 512], BF16) for i in range(2)]
        yBs = [sb(f"yB{i}", [P, 8, 512], BF16) for i in range(2)]
        stats, mv, rstd = sb("stats", [P, 4, 2, 6]), sb("mv", [P, 4, 2]), sb("rstd", [P, 4, 1])
        gb_b, st_b = Buf(), Buf()
        x_bs, yA_bs, yB_bs = [Buf(), Buf()], [Buf(), Buf()], [Buf(), Buf()]
        S.dma("sp", g_bc[:], g_dram, writes=[gb_b])
        S.dma("sp", b_bc[:], b_dram, writes=[gb_b])
        S.dma("sp", sel[:], sel_dram, writes=[gb_b])
        xin_v = x_in.rearrange("(t b p) d -> t p b d", b=4, p=P)
        xout_v = x_out.rearrange("(t b p) d -> t p b d", b=4, p=P)
        yv = ygath.rearrange("(k p) t -> p k t", p=P)

        def loads(t):
            k = t % 2
            S.dma("sp", xts[k][:], xin_v[t], reads=[xin_b[t] if isinstance(xin_b, list) else xin_b], writes=[x_bs[k]])
            S.dma("sp", yAs[k][:], yv[:, :, t * 512:(t + 1) * 512], reads=[yg_b], writes=[yA_bs[k]])
            S.dma("sp", yBs[k][:], yv[:, :, NT + t * 512:NT + (t + 1) * 512], reads=[yg_b], writes=[yB_bs[k]])
        loads(0)
        for t in range(ntile):
            k = t % 2
            xt, x_b, yA, yA_b, yB, yB_b = xts[k], x_bs[k], yAs[k], yA_bs[k], yBs[k], yB_bs[k]
            if t + 1 < ntile:
                loads(t + 1)
            S.op("act", lambda e, xt=xt: e.mul(out=xt[:], in_=xt[:], mul=ALPHA), reads=[x_b], writes=[x_b])
            S.op("dve", lambda e, yA=yA: e.tensor_scalar(out=yA[:], in0=yA[:], scalar1=sel[:, 0:1], scalar2=None, op0=ALU.mult),
                 reads=[yA_b, gb_b], writes=[yA_b])
            S.op("dve", lambda e, yA=yA, yB=yB: e.scalar_tensor_tensor(out=yA[:], in0=yB[:], scalar=sel[:, 1:2], in1=yA[:], op0=ALU.mult, op1=ALU.add),
                 reads=[yA_b, yB_b, gb_b], writes=[yA_b])
            outproj_ln(C, es, tag, yA, yA_b, wo, wo_b, xt, x_b)
            layer_norm_tile(C, xt, x_b, g_bc, b_bc, gb_b, stats, mv, rstd, st_b)
            S.dma("pool", xout_v[t], xt[:], reads=[x_b], writes=[xout_b])
        S.barrier()


def xattn_phase(C, x_in, xin_b, mem, w_q, w_kv, w_o, g_dram, b_dram, x_out, xout_b, NT, tag):
    nc, S = C.nc, C.S
    ntile = NT // 512
    C.set_psum(2)
    with contextlib.ExitStack() as es:
        wq, wq_b = load_w_bf16(C, es, f"wxq{tag}", w_q, D)
        wkv, wkv_b = load_w_bf16(C, es, f"wxkv{tag}", w_kv, 2 * D)
        wo, wo_b = load_w_bf16(C, es, f"wxo{tag}", w_o, D)
        sb = lambda name, shape, dt=F32: es.enter_context(nc.sbuf_tensor(f"xa{name}{tag}", shape, dt))
        g_bc, b_bc = sb("g", [P, D]), sb("b", [P, D])
        identb = sb("idb", [P, P], BF16)
        xt = sb("xt", [P, 4, D])
        xT = sb("xT", [P, 8, 512], BF16)
        QT = sb("QT", [P, 8, 512], BF16)
        KTx = sb("KT", [P, 8, 256], BF16)
        Vx = sb("V", [P, 2, D], BF16)
        aoT = sb("aoT", [P, 8, 512], BF16)
        sc = sb("sc", [P, 256])
        pf = sb("pf", [P, 256])
        pn = sb("pn", [P, 256], BF16)
        pT = sb("pT", [P, 2, 128], BF16)
        sm = sb("sm", [P, 4])
        stats, mv, rstd = sb("stats", [P, 4, 2, 6]), sb("mv", [P, 4, 2]), sb("rstd", [P, 4, 1])
        gb_b, x_b, xT_b, QT_b, KT_b, V_b, ao_b, st_b = Buf(), Buf(), Buf(), Buf(), Buf(), Buf(), Buf(), Buf()
        sc_b, pf_b, pn_b, pT_b, sm_b = Buf(), Buf(), Buf(), Buf(), Buf()
        pf2 = [pf, sb("pfb", [P, 256])]
        pn2 = [pn, sb("pnb", [P, 256], BF16)]
        pT2 = [pT, sb("pTb", [P, 2, 128], BF16)]
        sm2 = [sm, sb("smb", [P, 4])]
        pf_b2, pn_b2, pT_b2, sm_b2 = [Buf(), Buf()], [Buf(), Buf()], [Buf(), Buf()], [Buf(), Buf()]
        pf4 = [sb(f"pf4{k}", [P, 4, 256]) for k in range(2)]
        pn4 = [sb(f"pn4{k}", [P, 4, 256], BF16) for k in range(2)]
        pT4 = [sb(f"pT4{k}", [P, 8, 128], BF16) for k in range(2)]
        sm4 = [sb(f"sm4{k}", [P, 4, 4]) for k in range(2)]
        sc_pb, bf_pb, pv_pb = [Buf(), Buf()], [Buf(), Buf()], [Buf(), Buf()]
        S.dma("sp", g_bc[:], g_dram, writes=[gb_b])
        S.dma("sp", b_bc[:], b_dram, writes=[gb_b])
        S.dma("pool", identb[:], C.ident_dram, writes=[gb_b])
        S.dma("sp", xt[:, 0:2, :], mem.rearrange("(b p) d -> p b d", p=P), writes=[x_b])
        for kc in range(8):
            for blk in range(2):
                S.op("pe", lambda e, kc=kc, blk=blk: e.transpose(out=C.ps[4][:, blk * 128:(blk + 1) * 128], in_=xt[:, blk, kc * 128:(kc + 1) * 128],
                                                                 identity=C.ident[:]), reads=[x_b, C.ident_b], writes=[C.psb[4]])
            S.op("dve", lambda e, kc=kc: e.tensor_copy(out=xT[:, kc, 0:256], in_=C.ps[4][:, 0:256]), reads=[C.psb[4]], writes=[xT_b])
        for j in range(8):
            pi = j % 2
            for kc in range(8):
                S.op("pe", lambda e, kc=kc, j=j, pi=pi: e.matmul(out=C.ps[pi][:, 0:256], lhsT=wkv[:, kc, j * 128:(j + 1) * 128], rhs=xT[:, kc, 0:256],
                                                                 start=(kc == 0), stop=(kc == 7)), reads=[wkv_b, xT_b], writes=[C.psb[pi]])
            S.op("dve", lambda e, j=j, pi=pi: e.tensor_copy(out=KTx[:, j, :], in_=C.ps[pi][:, 0:256]), reads=[C.psb[pi]], writes=[KT_b])
        for blk in range(2):
            for hh in range(2):
                pi = 2 + hh
                for kc in range(8):
                    S.op("pe", lambda e, kc=kc, blk=blk, hh=hh, pi=pi: e.matmul(
                        out=C.ps[pi][:], lhsT=xT[:, kc, blk * 128:(blk + 1) * 128], rhs=wkv[:, kc, D + hh * 512:D + (hh + 1) * 512],
                        start=(kc == 0), stop=(kc == 7)), reads=[wkv_b, xT_b], writes=[C.psb[pi]])
                S.op("act", lambda e, blk=blk, hh=hh, pi=pi: e.copy(out=Vx[:, blk, hh * 512:(hh + 1) * 512], in_=C.ps[pi][:]),
                     reads=[C.psb[pi]], writes=[V_b])
        xin_v = x_in.rearrange("(t b p) d -> t p b d", b=4, p=P)
        xout_v = x_out.rearrange("(t b p) d -> t p b d", b=4, p=P)
        xts = [xt, sb("xt2", [P, 4, D])]
        x_bs = [x_b, Buf()]
        QTs = [QT, sb("QT2", [P, 8, 512], BF16)]
        QT_bs = [QT_b, Buf()]

        def emit_load(t):
            S.dma("sp", xts[t % 2][:], xin_v[t], reads=[xin_b[t] if isinstance(xin_b, list) else xin_b], writes=[x_bs[t % 2]])

        def emit_T(t):
            load_xT(C, xts[t % 2], x_bs[t % 2], xT, xT_b, ps_ids=(0,))
            S.op("act", lambda e, t=t: e.mul(out=xts[t % 2][:], in_=xts[t % 2][:], mul=ALPHA), reads=[x_bs[t % 2]], writes=[x_bs[t % 2]])

        def emit_qproj(t, j):
            proj_fm(C, wq, wq_b, j * 128, (j + 1) * 128, xT, xT_b, 1)
            S.op("act" if j % 2 == 0 else "dve",
                 (lambda e, j=j, t=t: e.mul(out=QTs[t % 2][:, j, :], in_=C.ps[1][:], mul=0.0625)) if j % 2 == 0 else
                 (lambda e, j=j, t=t: e.tensor_scalar(out=QTs[t % 2][:, j, :], in0=C.ps[1][:], scalar1=0.0625, scalar2=None, op0=ALU.mult)),
                 reads=[C.psb[1]], writes=[QT_bs[t % 2]])

        emit_load(0)
        emit_T(0)
        for j in range(8):
            emit_qproj(0, j)
        for t in range(ntile):
            xt, x_b, QT, QT_b = xts[t % 2], x_bs[t % 2], QTs[t % 2], QT_bs[t % 2]
            if t + 1 < ntile:
                emit_load(t + 1)
            for hd in range(4):
                a = hd % 2
                if t + 1 < ntile:
                    if hd == 2:
                        emit_T(t + 1)
                    elif hd == 3:
                        for j in range(8):
                            emit_qproj(t + 1, j)
                for b in range(4):
                    ts = slice(b * 128, (b + 1) * 128)
                    for kk in range(2):
                        S.op("pe", lambda e, hd=hd, kk=kk, ts=ts, b=b: e.matmul(out=C.ps[2 + b // 2][:, (b % 2) * 256:(b % 2 + 1) * 256],
                                                                               lhsT=QT[:, 2 * hd + kk, ts], rhs=KTx[:, 2 * hd + kk, :],
                                                                               start=(kk == 0), stop=(kk == 1), skip_group_check=True),
                             reads=[QT_b, KT_b], writes=[C.psb[2 + b // 2]])
                for bank in range(2):
                    S.op("dve", lambda e, bank=bank, a=a: e.tensor_reduce(out=sm4[a][:, 0, 2 * bank:2 * bank + 2],
                                                                         in_=C.ps[2 + bank][:].rearrange("p (b k) -> p b k", b=2), axis=AX.X, op=ALU.max),
                         reads=[C.psb[2 + bank]], writes=[sm_b2[a]])
                S.op("dve", lambda e, a=a: e.tensor_scalar(out=sm4[a][:, 1, :], in0=sm4[a][:, 0, :], scalar1=-1.0, scalar2=None, op0=ALU.mult),
                     reads=[sm_b2[a]], writes=[sm_b2[a]])
                for b in range(4):
                    S.op("act", lambda e, b=b, a=a: e.activation(out=pf4[a][:, b, :], in_=C.ps[2 + b // 2][:, (b % 2) * 256:(b % 2 + 1) * 256], func=AF.Exp,
                                                                 bias=sm4[a][:, 1, b:b + 1], scale=1.0, accum_out=sm4[a][:, 2, b:b + 1]),
                         reads=[C.psb[2 + b // 2], sm_b2[a]], writes=[pf_b2[a], sm_b2[a]])
                S.op("dve", lambda e, a=a: e.reciprocal(out=sm4[a][:, 3, :], in_=sm4[a][:, 2, :]), reads=[sm_b2[a]], writes=[sm_b2[a]])
                for b in range(4):
                    if b % 2 == 0:
                        S.op("dve", lambda e, b=b, a=a: e.tensor_scalar(out=pn4[a][:, b, :], in0=pf4[a][:, b, :], scalar1=sm4[a][:, 3, b:b + 1],
                                                                        scalar2=None, op0=ALU.mult), reads=[pf_b2[a], sm_b2[a]], writes=[pn_b2[a]])
                    else:
                        S.op("act", lambda e, b=b, a=a: e.mul(out=pn4[a][:, b, :], in_=pf4[a][:, b, :], mul=sm4[a][:, 3, b:b + 1]),
                             reads=[pf_b2[a], sm_b2[a]], writes=[pn_b2[a]])
                for b in range(4):
                    for mb in range(2):
                        S.op("pe", lambda e, b=b, mb=mb, a=a: e.transpose(out=C.ps_bfs[a][:, (b * 2 + mb) * 128:(b * 2 + mb + 1) * 128],
                                                                          in_=pn4[a][:, b, mb * 128:(mb + 1) * 128], identity=identb[:]),
                             reads=[pn_b2[a], gb_b], writes=[C.psbf_b[a]])
                S.op("act", lambda e, a=a: e.copy(out=pT4[a][:, 0:4, :].rearrange("p a b -> p (a b)"), in_=C.ps_bfs[a][:, 0:512]),
                     reads=[C.psbf_b[a]], writes=[pT_b2[a]])
                S.op("dve", lambda e, a=a: e.tensor_copy(out=pT4[a][:, 4:8, :].rearrange("p a b -> p (a b)"), in_=C.ps_bfs[a][:, 512:1024]),
                     reads=[C.psbf_b[a]], writes=[pT_b2[a]])
                for cc in range(2):
                    for b in range(4):
                        for mb in range(2):
                            S.op("pe", lambda e, cc=cc, mb=mb, hd=hd, b=b, a=a: e.matmul(out=C.ps[4 + cc][:, b * 128:(b + 1) * 128],
                                                                                         lhsT=Vx[:, mb, hd * 256 + cc * 128:hd * 256 + (cc + 1) * 128],
                                                                                         rhs=pT4[a][:, b * 2 + mb, :], start=(mb == 0), stop=(mb == 1),
                                                                                         skip_group_check=True),
                                 reads=[V_b, pT_b2[a]], writes=[C.psb[4 + cc]])
                S.op("act", lambda e, hd=hd: e.copy(out=aoT[:, 2 * hd, :], in_=C.ps[4][:]), reads=[C.psb[4]], writes=[ao_b])
                S.op("dve", lambda e, hd=hd: e.tensor_copy(out=aoT[:, 2 * hd + 1, :], in_=C.ps[5][:]), reads=[C.psb[5]], writes=[ao_b])
            outproj_ln(C, es, tag, aoT, ao_b, wo, wo_b, xt, x_b)
            layer_norm_tile(C, xt, x_b, g_bc, b_bc, gb_b, stats, mv, rstd, st_b)
            S.dma("sp", xout_v[t], xt[:], reads=[x_b], writes=[xout_b])
        S.barrier()
    C.set_psum(1)


NT_OWN = 4096
TT_SEQ = 8192
DEPTH = 2
PAIRS = [[0, 1], [2, 3], [4, 5], [6, 7]]
_SPLITS = np.cumsum([0, 256, 256, 256, 512, 128, 128, 512, 256, 4, 4, 256])


def build_program():
    from concourse.bass_utils import run_bass_kernel_spmd
    nc = bass.Bass("TRN2", target_bir_lowering=False)
    din = lambda name, shape, dt=F32: nc.dram_tensor(name, shape, dt, kind="ExternalInput").ap()
    dscr = lambda name, shape, dt=F32: nc.dram_tensor(name, shape, dt, kind="Internal").ap()
    x = din("x", [NT_OWN, D])
    mem = din("mem", [256, D])
    sel = din("sel", [P, 2])
    cn = make_consts_np()
    cd = {k: din("c_" + k, list(v.shape)) for k, v in cn.items()}
    L = []
    for l in range(DEPTH):
        d = {}
        d["ffn1_in"] = din(f"l{l}_ffn1_in", [D, 2 * DFF]); d["ffn1_out"] = din(f"l{l}_ffn1_out", [DFF, D])
        d["ffn2_in"] = din(f"l{l}_ffn2_in", [D, 2 * DFF]); d["ffn2_out"] = din(f"l{l}_ffn2_out", [DFF, D])
        d["w_sb"] = din(f"l{l}_w_sb", [D, 384]); d["w_swa"] = din(f"l{l}_w_swa", [D, 448]); d["w_ml"] = din(f"l{l}_w_ml", [D, 640])
        d["cw"] = din(f"l{l}_cw", [P, 8]); d["cb"] = din(f"l{l}_cb", [P, 2]); d["ifb"] = din(f"l{l}_ifb", [P, 4]); d["ng"] = din(f"l{l}_ng", [P, P])
        d["rb"] = din(f"l{l}_rb", [P, P]); d["sink"] = din(f"l{l}_sink", [P, 4])
        d["w_mo"] = din(f"l{l}_w_mo", [D, D])
        d["xq"] = din(f"l{l}_xq", [D, D]); d["xkv"] = din(f"l{l}_xkv", [D, 2 * D]); d["xo"] = din(f"l{l}_xo", [D, D])
        d["lng"] = [din(f"l{l}_lng{i}", [P, D]) for i in range(4)]
        d["lnb"] = [din(f"l{l}_lnb{i}", [P, D]) for i in range(4)]
        L.append(d)
    out = nc.dram_tensor("out", [NT_OWN, D], F32, kind="ExternalOutput").ap()
    Xa = dscr("Xa", [NT_OWN, D]); Xb = dscr("Xb", [NT_OWN, D])
    X1g = dscr("X1g", [NT_OWN // 512, 2 * D, 512], BF16)
    XaT = dscr("XaT", [NT_OWN // 512, D, 512], BF16)
    YTo = dscr("YTo", [512, TT_SEQ], BF16)
    Yg = dscr("Yg", [1024, TT_SEQ], BF16)
    frs = dscr("frs", [4, 384])
    S = Sched(nc)
    C = Ctx(nc, S)
    C.load_consts(cd["ident"])
    x_b, Xa_b, Xb_b, X1g_b, YTo_b, Yg_b, out_b = Buf(), Buf(), Buf(), Buf(), [Buf(), Buf(), Buf()], Buf(), Buf()
    cur, cur_b = x, x_b
    import os
    kstop = int(os.environ.get("KSTOP", "99"))
    for l in range(DEPTH):
        d = L[l]
        tg = f"L{l}"
        if kstop < 99 and l > 0:
            break
        Xa_tb = [Buf() for _ in range(NT_OWN // 512)]
        XaT_tb = [Buf() for _ in range(NT_OWN // 512)]
        X1g_tb = [Buf() for _ in range(TT_SEQ // 512)]
        nch = NT_OWN // 512

        def ag1(t):
            S.custom("pool", lambda e, t=t: e.collective_compute("AllGather", ALU.bypass, replica_groups=PAIRS,
                                                                 ins=[XaT[t]], outs=[X1g[t]]), 1,
                     reads=[XaT_tb[t]], writes=[X1g_tb[t], X1g_tb[nch + t]])
        ffn_phase(C, cur, Xa, d["ffn1_in"], d["ffn1_out"], d["lng"][0], d["lnb"][0], NT_OWN, tg + "f1", cur_b, Xa_tb, after_tile=ag1,
                  xT_out=XaT, xT_out_b=XaT_tb)
        if kstop <= 2:
            break
        X1v = X1g.rearrange("j (r kc p) n -> j r p kc n", r=2, p=P)
        xtile = lambda t: X1v[t % nch, t // nch]
        sb_phase(C, xtile, X1g_tb, d["w_sb"], YTo[0:128, :], YTo_b[0], TT_SEQ, cd, tg + "sb")
        S.custom("pool", lambda e: e.collective_compute("AllGather", ALU.bypass, replica_groups=PAIRS, ins=[YTo[0:128, :]], outs=[Yg[0:256, :]]), 1,
                 reads=[YTo_b[0]], writes=[Yg_b])
        if kstop <= 3:
            break
        swa_phase(C, xtile, X1g_tb, d["w_swa"], d["rb"], d["sink"], frs, YTo[128:384, :], YTo_b[1], TT_SEQ, cd, tg + "sw")
        for k in (1, 2):
            S.custom("pool", lambda e, k=k: e.collective_compute("AllGather", ALU.bypass, replica_groups=PAIRS, ins=[YTo[k * 128:(k + 1) * 128, :]],
                                                                 outs=[Yg[k * 256:(k + 1) * 256, :]]), 1, reads=[YTo_b[1]], writes=[Yg_b])
        if kstop <= 4:
            break
        mlstm_phase(C, xtile, X1g_tb, d["w_ml"], d["cw"], d["cb"], d["ifb"], d["ng"], YTo[384:512, :], YTo_b[2], TT_SEQ, cd, tg + "ml")
        S.custom("pool", lambda e: e.collective_compute("AllGather", ALU.bypass, replica_groups=PAIRS, ins=[YTo[384:512, :]], outs=[Yg[768:1024, :]]), 1,
                 reads=[YTo_b[2]], writes=[Yg_b])
        S.barrier()
        if kstop <= 6:
            break
        mixout_phase(C, Xa, Xa_tb, Yg, Yg_b, d["w_mo"], sel, d["lng"][1], d["lnb"][1], Xb, Xb_b, NT_OWN, TT_SEQ, tg + "mo")
        if kstop <= 7:
            break
        xattn_phase(C, Xb, Xb_b, mem, d["xq"], d["xkv"], d["xo"], d["lng"][2], d["lnb"][2], Xa, Xa_b, NT_OWN, tg + "xa")
        if kstop <= 8:
            break
        dst, dst_b = (out, out_b) if l == DEPTH - 1 else (Xb, Xb_b)
        ffn_phase(C, Xa, dst, d["ffn2_in"], d["ffn2_out"], d["lng"][3], d["lnb"][3], NT_OWN, tg + "f2", Xa_b, dst_b)
        cur, cur_b = dst, dst_b
    S.finish()
    C.psum_es.close()
    return nc, cn


def make_core_inputs(c, inp, cn):
    b, r = c // 2, c % 2
    f = lambda a: np.ascontiguousarray(np.asarray(a, dtype=np.float32))
    rep = lambda v: f(np.tile(np.asarray(v, np.float32)[None, :], (P, 1)))
    m = {"x": f(inp["x"][b, r * NT_OWN:(r + 1) * NT_OWN]), "mem": f(inp["mem"][b])}
    selv = np.zeros((P, 2), np.float32)
    selv[:, r] = 1.0
    m["sel"] = selv
    for k, v in cn.items():
        m["c_" + k] = f(v)
    sp = _SPLITS
    slot = [0, 2, 1, 3]
    for l in range(DEPTH):
        w = np.asarray(inp["mix_w_in"][l], np.float32)
        seg = [w[:, sp[i]:sp[i + 1]] for i in range(11)]
        sbq, sbk, sbv, swq, swk, swv, mlqk, mlv, ig, fg, og = seg
        mlq, mlk = mlqk[:, 0:256], mlqk[:, 256:512]
        m[f"l{l}_ffn1_in"] = f(inp["ffn1_w_in"][l]); m[f"l{l}_ffn1_out"] = f(inp["ffn1_w_out"][l])
        m[f"l{l}_ffn2_in"] = f(inp["ffn2_w_in"][l]); m[f"l{l}_ffn2_out"] = f(inp["ffn2_w_out"][l])
        h2 = slice(r * 128, (r + 1) * 128)
        m[f"l{l}_w_sb"] = f(np.concatenate([sbq[:, h2], sbk[:, h2], sbv[:, h2]], 1))
        kk = swk[:, r * 64:(r + 1) * 64]
        m[f"l{l}_w_swa"] = f(np.concatenate([swq[:, r * 256:(r + 1) * 256], kk, kk, swv[:, r * 64:(r + 1) * 64]], 1))
        m[f"l{l}_w_ml"] = f(np.concatenate([mlq[:, h2], mlk[:, h2], mlv[:, h2], og[:, h2], ig[:, 2 * r:2 * r + 2], fg[:, 2 * r:2 * r + 2],
                                            np.zeros((D, 124), np.float32)], 1))
        cwl = np.asarray(inp["ml_conv_w"][l], np.float32)
        cbl = np.asarray(inp["ml_conv_b"][l], np.float32)
        m[f"l{l}_cw"] = f(np.concatenate([cwl[:, r * 128:(r + 1) * 128].T, cwl[:, 256 + r * 128:256 + (r + 1) * 128].T], 1))
        m[f"l{l}_cb"] = f(np.stack([cbl[r * 128:(r + 1) * 128], cbl[256 + r * 128:256 + (r + 1) * 128]], 1))
        ib = np.asarray(inp["ml_i_bias"][l], np.float32)[2 * r:2 * r + 2]
        fb = np.asarray(inp["ml_f_bias"][l], np.float32)[2 * r:2 * r + 2]
        m[f"l{l}_ifb"] = rep(np.concatenate([ib, fb]))
        m[f"l{l}_ng"] = rep(np.asarray(inp["ml_norm_g"][l], np.float32)[h2])
        rbl = np.asarray(inp["rel_bias"], np.float32)[:, 4 * r:4 * r + 4][:, slot]
        m[f"l{l}_rb"] = f(np.pad(rbl, ((0, 96), (0, 124))))
        m[f"l{l}_sink"] = rep(np.asarray(inp["swa_sinks"][l], np.float32)[4 * r:4 * r + 4][slot])
        wo = np.asarray(inp["mix_w_out"][l], np.float32)
        rows = []
        for k in range(4):
            for rr in range(2):
                base = [rr * 128, 256 + rr * 256, 256 + rr * 256 + 128, 768 + rr * 128][k]
                rows.append(wo[base:base + 128])
        m[f"l{l}_w_mo"] = f(np.concatenate(rows, 0))
        m[f"l{l}_xq"] = f(inp["xattn_w_q"][l]); m[f"l{l}_xkv"] = f(inp["xattn_w_kv"][l]); m[f"l{l}_xo"] = f(inp["xattn_w_o"][l])
        for i in range(4):
            m[f"l{l}_lng{i}"] = rep(inp["ln_g"][l][i]); m[f"l{l}_lnb{i}"] = rep(inp["ln_b"][l][i])
    return m


def kernel(**inputs):
    from concourse.bass_utils import run_bass_kernel_spmd
    inp = {k: np.asarray(v) for k, v in inputs.items()}
    nc, cn = build_program()
    in_maps = [make_core_inputs(c, inp, cn) for c in range(8)]
    res = run_bass_kernel_spmd(nc, in_maps, core_ids=list(range(8)))
    out = np.zeros((4, TT_SEQ, D), np.float32)
    for c in range(8):
        b, r = c // 2, c % 2
        out[b, r * NT_OWN:(r + 1) * NT_OWN] = np.asarray(res.results[c]["out"], np.float32)
    return out
```

```python
import contextlib
import numpy as np
import concourse.bass as bass
import concourse.mybir as mybir

F32 = mybir.dt.float32
BF16 = mybir.dt.bfloat16
ALU = mybir.AluOpType
AF = mybir.ActivationFunctionType
AX = mybir.AxisListType


class Buf:
    __slots__ = ("name", "w", "r")

    def __init__(self, name=""):
        self.name = name
        self.w = None
        self.r = []


class Sched:
    EPOCH = 20000

    def __init__(self, nc, n_dma_sems=24):
        self.nc = nc
        self.es = contextlib.ExitStack()
        self.eng = {"pe": nc.tensor, "act": nc.scalar, "dve": nc.vector,
                    "pool": nc.gpsimd, "sp": nc.sync}
        self.sem = {}
        self.cnt = {}
        self.nsem = 0
        for e in self.eng:
            self._new_epoch(e)
        self.dma_sems = [self.es.enter_context(nc.semaphore(f"dq{i}")) for i in range(n_dma_sems)]
        self.dma_cnt = [0] * n_dma_sems
        self.dma_next = 0
        self.waited = {e: {} for e in self.eng}
        self.last_ev = {e: None for e in self.eng}
        self.out_events = []
        self.n_ops = 0
        self.n_waits = 0

    def _new_epoch(self, e):
        self.sem[e] = self.es.enter_context(self.nc.semaphore(f"s_{e}_{self.nsem}"))
        self.nsem += 1
        self.cnt[e] = 0

    def _wait(self, e, ev):
        if ev is None:
            return
        sem, val, src = ev
        if src == "pe" and e == "pe":
            return
        key = id(sem)
        if self.waited[e].get(key, 0) >= val:
            return
        self.eng[e].wait_ge(sem, val)
        self.n_waits += 1
        self.waited[e][key] = val

    def _deps(self, e, reads, writes):
        for b in reads:
            self._wait(e, b.w)
        for b in writes:
            self._wait(e, b.w)
            for ev in b.r:
                self._wait(e, ev)

    def _commit(self, ev, reads, writes):
        for b in reads:
            b.r.append(ev)
            if len(b.r) > 12:
                latest = {}
                for x in b.r:
                    k = id(x[0])
                    if k not in latest or latest[k][1] < x[1]:
                        latest[k] = x
                b.r = list(latest.values())
        for b in writes:
            b.w = ev
            b.r = []

    def op(self, e, fn, reads=(), writes=()):
        self._deps(e, reads, writes)
        if self.cnt[e] >= self.EPOCH:
            self._new_epoch(e)
        ins = fn(self.eng[e])
        self.cnt[e] += 1
        ins.then_inc(self.sem[e], 1)
        ev = (self.sem[e], self.cnt[e], e)
        self.last_ev[e] = ev
        self._commit(ev, reads, writes)
        self.n_ops += 1
        return ev

    def dma(self, q, out, in_, reads=(), writes=(), **kw):
        self._deps(q, reads, writes)
        k = self.dma_next
        self.dma_next = (k + 1) % len(self.dma_sems)
        sem = self.dma_sems[k]
        if self.dma_cnt[k] > 0:
            self._wait(q, (sem, self.dma_cnt[k], "dma"))
        self.dma_cnt[k] += 16
        self.eng[q].dma_start(out=out, in_=in_, **kw).then_inc(sem, 16)
        ev = (sem, self.dma_cnt[k], "dma")
        self._commit(ev, reads, writes)
        self.n_ops += 1
        return ev

    def custom(self, q, fn, inc, reads=(), writes=()):
        self._deps(q, reads, writes)
        k = self.dma_next
        self.dma_next = (k + 1) % len(self.dma_sems)
        sem = self.dma_sems[k]
        if self.dma_cnt[k] > 0:
            self._wait(q, (sem, self.dma_cnt[k], "dma"))
        self.dma_cnt[k] += inc
        fn(self.eng[q]).then_inc(sem, inc)
        ev = (sem, self.dma_cnt[k], "dma")
        self._commit(ev, reads, writes)
        return ev

    def barrier(self, bufs=()):
        evs = [ev for ev in self.last_ev.values() if ev is not None]
        for k, sem in enumerate(self.dma_sems):
            if self.dma_cnt[k] > 0:
                evs.append((sem, self.dma_cnt[k], "dma"))
        for e in self.eng:
            for ev in evs:
                if ev[2] == e and e == "pe":
                    continue
                self._wait(e, ev)

    def finish(self):
        self.barrier()
        self.es.close()


D = 1024
DFF = 2816
ALPHA = 4 ** 0.25
LN_EPS = 1e-5
P = 128


class Ctx:
    def __init__(self, nc, S):
        self.nc = nc
        self.S = S
        es = S.es
        self.psum_es = None
        self.psum_gen = 0
        self.set_psum(1)
        self.ident = es.enter_context(nc.sbuf_tensor("ident_sb", [P, P], F32))
        self.ident_b = Buf("ident")

        self.eps_t = es.enter_context(nc.sbuf_tensor("eps_t", [P, 1], F32))

    def set_psum(self, n_bf=1):
        if self.psum_es is not None:
            self.psum_es.close()
        self.psum_es = contextlib.ExitStack()
        g = self.psum_gen
        self.psum_gen += 1
        nf = 8 - n_bf
        self.ps = [self.psum_es.enter_context(self.nc.psum_tensor(f"psf{g}_{i}", [P, 512], F32)) for i in range(nf)]
        self.psb = [Buf(f"ps{i}") for i in range(nf)]
        self.ps_bfs = [self.psum_es.enter_context(self.nc.psum_tensor(f"psh{g}_{i}", [P, 1024], BF16)) for i in range(n_bf)]
        self.psbf_b = [Buf(f"psbf{i}") for i in range(n_bf)]
        self.ps_bf = self.ps_bfs[0]
        while len(self.ps) < 8:
            self.ps.append(None)
            self.psb.append(self.psbf_b[0])

    def load_consts(self, ident_dram):
        self.ident_dram = ident_dram
        self.S.dma("sp", self.ident[:], ident_dram, writes=[self.ident_b])
        self.S.op("pool", lambda e: e.memset(self.eps_t[:], LN_EPS), writes=[self.ident_b])


def load_xT(C, x_tile, x_b, xT, xT_b, ps_ids, evac_engs=("act", "dve")):
    S = C.S
    for kc in range(8):
        pi = ps_ids[kc % len(ps_ids)]
        ps, pb = C.ps[pi], C.psb[pi]
        for blk in range(4):
            S.op("pe", lambda e, kc=kc, blk=blk, ps=ps: e.transpose(
                out=ps[:, blk * 128:(blk + 1) * 128], in_=x_tile[:, blk, kc * 128:(kc + 1) * 128],
                identity=C.ident[:]), reads=[x_b, C.ident_b], writes=[pb])
        eng = evac_engs[kc % len(evac_engs)]
        if eng == "act":
            S.op("act", lambda e, kc=kc, ps=ps: e.copy(out=xT[:, kc, :], in_=ps[:]), reads=[pb], writes=[xT_b])
        else:
            S.op("dve", lambda e, kc=kc, ps=ps: e.tensor_copy(out=xT[:, kc, :], in_=ps[:]), reads=[pb], writes=[xT_b])


def layer_norm_tile(C, xt, x_b, g_bc, b_bc, gb_b, stats, mv, rstd, st_b, nblk=4):
    S = C.S
    for blk in range(nblk):
        for hh in range(2):
            S.op("dve", lambda e, blk=blk, hh=hh: e.bn_stats(out=stats[:, blk, hh, :], in_=xt[:, blk, hh * 512:(hh + 1) * 512]),
                 reads=[x_b], writes=[st_b])
        S.op("dve", lambda e, blk=blk: e.bn_aggr(out=mv[:, blk, :], in_=stats[:, blk, :, :]), reads=[st_b], writes=[st_b])
        S.op("act", lambda e, blk=blk: e.activation(out=rstd[:, blk, :], in_=mv[:, blk, 1:2], func=AF.Sqrt, bias=C.eps_t[:], scale=1.0),
             reads=[st_b, C.ident_b], writes=[st_b])
        S.op("dve", lambda e, blk=blk: e.reciprocal(out=rstd[:, blk, :], in_=rstd[:, blk, :]), reads=[st_b], writes=[st_b])
        S.op("dve", lambda e, blk=blk: e.tensor_scalar(out=xt[:, blk, :], in0=xt[:, blk, :], scalar1=mv[:, blk, 0:1],
                                                       scalar2=rstd[:, blk, :], op0=ALU.subtract, op1=ALU.mult),
             reads=[st_b, x_b], writes=[x_b])
        S.op("dve", lambda e, blk=blk: e.tensor_tensor(out=xt[:, blk, :], in0=xt[:, blk, :], in1=g_bc[:], op=ALU.mult),
             reads=[x_b, gb_b], writes=[x_b])
        S.op("pool", lambda e, blk=blk: e.tensor_tensor(out=xt[:, blk, :], in0=xt[:, blk, :], in1=b_bc[:], op=ALU.add),
             reads=[x_b, gb_b], writes=[x_b])


def ffn_phase(C, x_in, x_out, w_in, w_out, g_dram, b_dram, NT, tag, xin_b, xout_b, after_tile=None, xT_out=None, xT_out_b=None):
    nc, S = C.nc, C.S
    ntile = NT // 512
    JG = [(0, 6), (6, 12), (12, 17), (17, 22)]
    with contextlib.ExitStack() as es:
        w1 = es.enter_context(nc.sbuf_tensor(f"w1{tag}", [P, 8, 2 * DFF], BF16))
        w2 = es.enter_context(nc.sbuf_tensor(f"w2{tag}", [P, 22, D], BF16))
        g_bc = es.enter_context(nc.sbuf_tensor(f"g{tag}", [P, D], F32))
        b_bc = es.enter_context(nc.sbuf_tensor(f"b{tag}", [P, D], F32))
        xtb = [es.enter_context(nc.sbuf_tensor(f"xt{tag}{i}", [P, 4, D], F32)) for i in range(2)]
        xT = es.enter_context(nc.sbuf_tensor(f"xT{tag}", [P, 8, 512], BF16))
        gT = es.enter_context(nc.sbuf_tensor(f"gT{tag}", [P, 22, 512], BF16))
        sa = es.enter_context(nc.sbuf_tensor(f"sa{tag}", [P, 512], F32))
        xo = [es.enter_context(nc.sbuf_tensor(f"xo{tag}{i}", [P, 512], BF16)) for i in range(2)]
        stats = es.enter_context(nc.sbuf_tensor(f"stats{tag}", [P, 4, 2, 6], F32))
        mv = es.enter_context(nc.sbuf_tensor(f"mv{tag}", [P, 4, 2], F32))
        rstd = es.enter_context(nc.sbuf_tensor(f"rstd{tag}", [P, 4, 1], F32))
        w1_b = [Buf() for _ in JG]
        w2_b = [Buf() for _ in range(22)]
        gb_b, xT_b, st_b, sa_b = Buf(), Buf(), Buf(), Buf()
        x_b = [Buf(), Buf()]
        xo_b = [Buf(), Buf()]
        gT_b = [Buf() for _ in range(22)]
        jgrp = {}
        for gi, (j0, j1) in enumerate(JG):
            for j in range(j0, j1):
                jgrp[j] = gi
        w_in_v = w_in.rearrange("(kc p) f -> p kc f", p=P)
        w_out_v = w_out.rearrange("(j p) d -> p j d", p=P)
        for gi, (j0, j1) in enumerate(JG):
            for half in range(2):
                c0, c1 = half * DFF + j0 * 128, half * DFF + j1 * 128
                S.dma("pool", w1[:, :, c0:c1], w_in_v[:, :, c0:c1], writes=[w1_b[gi]])
        for j in range(0, 22, 2):
            S.dma("pool", w2[:, j:j + 2, :], w_out_v[:, j:j + 2, :], writes=[w2_b[j], w2_b[j + 1]])
        S.dma("sp", g_bc[:], g_dram, writes=[gb_b])
        S.dma("sp", b_bc[:], b_dram, writes=[gb_b])
        xin_v = x_in.rearrange("(t b p) d -> t p b d", b=4, p=P)
        xout_v = x_out.rearrange("(t b p) d -> t p b d", b=4, p=P)

        def emit_load(t):
            S.dma("sp", xtb[t % 2][:], xin_v[t], reads=[xin_b[t] if isinstance(xin_b, list) else xin_b], writes=[x_b[t % 2]])

        def emit_T(t):
            load_xT(C, xtb[t % 2], x_b[t % 2], xT, xT_b, ps_ids=(6,))
            S.op("act", lambda e, t=t: e.mul(out=xtb[t % 2][:], in_=xtb[t % 2][:], mul=ALPHA), reads=[x_b[t % 2]], writes=[x_b[t % 2]])

        def emit_Tout_chunk(tt, kc):
            xt_ = xtb[tt % 2]
            for blk in range(4):
                S.op("pe", lambda e, blk=blk: e.transpose(out=C.ps[6][:, blk * 128:(blk + 1) * 128], in_=xt_[:, blk, kc * 128:(kc + 1) * 128],
                                                          identity=C.ident[:]), reads=[x_b[tt % 2], C.ident_b], writes=[C.psb[6]])
            k = kc % 2
            if k == 0:
                S.op("act", lambda e: e.copy(out=xo[k][:], in_=C.ps[6][:]), reads=[C.psb[6]], writes=[xo_b[k]])
            else:
                S.op("dve", lambda e: e.tensor_copy(out=xo[k][:], in_=C.ps[6][:]), reads=[C.psb[6]], writes=[xo_b[k]])
            S.dma("sp", xT_out[tt][kc * 128:(kc + 1) * 128, :], xo[k][:], reads=[xo_b[k]], writes=[xT_out_b[tt]])
            if kc == 7 and after_tile is not None:
                after_tile(tt)

        emit_load(0)
        emit_T(0)
        for t in range(ntile):
            xt = xtb[t % 2]
            xb = x_b[t % 2]
            for j in range(22):
                pa, pb_ = (0, 1) if j % 2 == 0 else (2, 3)
                gi = jgrp[j]
                for kc in range(8):
                    S.op("pe", lambda e, j=j, kc=kc, pa=pa: e.matmul(
                        out=C.ps[pa][:], lhsT=w1[:, kc, j * 128:(j + 1) * 128], rhs=xT[:, kc, :],
                        start=(kc == 0), stop=(kc == 7)), reads=[w1_b[gi], xT_b], writes=[C.psb[pa]])
                for kc in range(8):
                    S.op("pe", lambda e, j=j, kc=kc, pb_=pb_: e.matmul(
                        out=C.ps[pb_][:], lhsT=w1[:, kc, DFF + j * 128:DFF + (j + 1) * 128], rhs=xT[:, kc, :],
                        start=(kc == 0), stop=(kc == 7)), reads=[w1_b[gi], xT_b], writes=[C.psb[pb_]])
                S.op("act", lambda e, pa=pa: e.activation(out=sa[:], in_=C.ps[pa][:], func=AF.Silu), reads=[C.psb[pa]], writes=[sa_b])
                S.op("dve", lambda e, pb_=pb_, j=j: e.tensor_tensor(out=gT[:, j, :], in0=C.ps[pb_][:], in1=sa[:], op=ALU.mult),
                     reads=[C.psb[pb_], sa_b], writes=[gT_b[j]])
                if xT_out is not None and t >= 1 and 5 <= j < 13:
                    emit_Tout_chunk(t - 1, j - 5)
                if j == 13 and t + 1 < ntile:
                    emit_load(t + 1)
            if t + 1 < ntile:
                emit_T(t + 1)
            for blk in range(4):
                for hh in range(2):
                    po = 4 + (blk * 2 + hh) % 2
                    for j in range(22):
                        S.op("pe", lambda e, j=j, blk=blk, hh=hh, po=po: e.matmul(
                            out=C.ps[po][:], lhsT=gT[:, j, blk * 128:(blk + 1) * 128], rhs=w2[:, j, hh * 512:(hh + 1) * 512],
                            start=(j == 0), stop=(j == 21)), reads=[gT_b[j], w2_b[j]], writes=[C.psb[po]])
                    S.op("dve", lambda e, blk=blk, hh=hh, po=po, xt=xt: e.scalar_tensor_tensor(
                        out=xt[:, blk, hh * 512:(hh + 1) * 512], in0=C.ps[po][:], scalar=0.5,
                        in1=xt[:, blk, hh * 512:(hh + 1) * 512], op0=ALU.mult, op1=ALU.add),
                        reads=[C.psb[po], xb], writes=[xb])
            layer_norm_tile(C, xt, xb, g_bc, b_bc, gb_b, stats, mv, rstd, st_b)
            S.dma("sp", xout_v[t], xt[:], reads=[xb], writes=[xout_b[t] if isinstance(xout_b, list) else xout_b])
            if xT_out is None and after_tile is not None:
                after_tile(t)
        if xT_out is not None:
            for kc in range(8):
                emit_Tout_chunk(ntile - 1, kc)
        S.barrier()


def load_w_bf16(C, es, name, w_dram, ncols, q="pool"):
    w = es.enter_context(C.nc.sbuf_tensor(name, [P, 8, ncols], BF16))
    b = Buf(name)
    C.S.dma("pool", w[:], w_dram.rearrange("(kc p) f -> p kc f", p=P), writes=[b])
    return w, b


def proj_fm(C, w, w_b, c0, c1, xT, xT_b, pi):
    for kc in range(8):
        C.S.op("pe", lambda e, kc=kc: e.matmul(out=C.ps[pi][0:c1 - c0, :], lhsT=w[:, kc, c0:c1], rhs=xT[:, kc, :],
                                               start=(kc == 0), stop=(kc == 7)), reads=[w_b, xT_b], writes=[C.psb[pi]])


def proj_tm(C, w, w_b, c0, c1, xT, xT_b, blk, pi):
    for kc in range(8):
        C.S.op("pe", lambda e, kc=kc: e.matmul(out=C.ps[pi][:, 0:c1 - c0], lhsT=xT[:, kc, blk * 128:(blk + 1) * 128],
                                               rhs=w[:, kc, c0:c1], start=(kc == 0), stop=(kc == 7)),
               reads=[w_b, xT_b], writes=[C.psb[pi]])


def fetch_xT(C, xfull, xfull_b, t, xTs, xT_bs):
    k = t % 2
    C.S.dma("sp", xTs[k][:], xfull(t), reads=[xfull_b[t] if isinstance(xfull_b, list) else xfull_b], writes=[xT_bs[k]])
    return xTs[k], xT_bs[k]


def sb_phase(C, xfull, xfull_b, w_sb, yT, yT_b, TT, consts, tag):
    nc, S = C.nc, C.S
    ntile = TT // 512
    with contextlib.ExitStack() as es:
        w, w_b = load_w_bf16(C, es, f"wsb{tag}", w_sb, 384)
        QT = es.enter_context(nc.sbuf_tensor(f"sbQT{tag}", [P, TT], BF16))
        KT = es.enter_context(nc.sbuf_tensor(f"sbKT{tag}", [P, TT], BF16))
        V = es.enter_context(nc.sbuf_tensor(f"sbV{tag}", [P, TT // 128, 128], BF16))
        xTs = [es.enter_context(nc.sbuf_tensor(f"sbxT{tag}{k}", [P, 8, 512], BF16)) for k in range(2)]
        xT_bs = [Buf(), Buf()]
        negmask = es.enter_context(nc.sbuf_tensor(f"sbnm{tag}", [P, 4, 512], BF16))
        identb = es.enter_context(nc.sbuf_tensor(f"sbidb{tag}", [P, P], BF16))
        ntri = es.enter_context(nc.sbuf_tensor(f"sbntri{tag}", [P, P], BF16))
        ones = es.enter_context(nc.sbuf_tensor(f"sbones{tag}", [P, P], BF16))
        one1 = es.enter_context(nc.sbuf_tensor(f"sbone1{tag}", [P, 1], F32))
        cb = Buf()
        S.dma("pool", negmask[:], consts["sb_negmask"], writes=[cb])
        S.dma("pool", identb[:], consts["ident"], writes=[cb])
        S.dma("pool", ntri[:], consts["ntri"], writes=[cb])
        S.dma("pool", ones[:], consts["ones"], writes=[cb])
        S.op("pool", lambda e: e.memset(one1[:], 1.0), writes=[cb])
        x_b, xT_b = Buf(), Buf()
        QT_b = [Buf() for _ in range(ntile)]
        KT_b = [Buf() for _ in range(ntile)]
        V_b = [Buf() for _ in range(ntile)]
        xv = xfull if callable(xfull) else (lambda t, _v=xfull.rearrange("(t b p) d -> t p b d", b=4, p=P): _v[t])
        for t in range(ntile):
            xT, xT_b = fetch_xT(C, xfull, xfull_b, t, xTs, xT_bs)
            proj_fm(C, w, w_b, 0, 128, xT, xT_b, 0)
            S.op("act", lambda e, t=t: e.mul(out=QT[:, t * 512:(t + 1) * 512], in_=C.ps[0][:], mul=0.125),
                 reads=[C.psb[0]], writes=[QT_b[t]])
            proj_fm(C, w, w_b, 128, 256, xT, xT_b, 1)
            S.op("dve", lambda e, t=t: e.tensor_copy(out=KT[:, t * 512:(t + 1) * 512], in_=C.ps[1][:]),
                 reads=[C.psb[1]], writes=[KT_b[t]])
            for blk in range(4):
                pi = 2 + blk % 2
                proj_tm(C, w, w_b, 256, 384, xT, xT_b, blk, pi)
                S.op("act" if blk % 2 else "dve",
                     (lambda e, t=t, blk=blk, pi=pi: e.copy(out=V[:, t * 4 + blk, :], in_=C.ps[pi][:, 0:128])) if blk % 2 else
                     (lambda e, t=t, blk=blk, pi=pi: e.tensor_copy(out=V[:, t * 4 + blk, :], in_=C.ps[pi][:, 0:128])),
                     reads=[C.psb[pi]], writes=[V_b[t]])
        NCH = 4
        Ech = [es.enter_context(nc.sbuf_tensor(f"sbE2{tag}{c}", [P, 512], F32)) for c in range(NCH)]
        Lch = [es.enter_context(nc.sbuf_tensor(f"sbL2{tag}{c}", [P, 512], F32)) for c in range(NCH)]
        Sch = [[es.enter_context(nc.sbuf_tensor(f"sbS2{tag}{c}{k}", [P, 512], F32)) for k in range(2)] for c in range(NCH)]
        Wch = [es.enter_context(nc.sbuf_tensor(f"sbW2{tag}{c}", [P, 512], BF16)) for c in range(NCH)]
        Lhi = [es.enter_context(nc.sbuf_tensor(f"sbLh{tag}{c}", [P, 512], BF16)) for c in range(NCH)]
        Llo = [es.enter_context(nc.sbuf_tensor(f"sbLl{tag}{c}", [P, 512], BF16)) for c in range(NCH)]
        Shi = [es.enter_context(nc.sbuf_tensor(f"sbSh{tag}{c}", [P, 512], BF16)) for c in range(NCH)]
        Slo = [es.enter_context(nc.sbuf_tensor(f"sbSl{tag}{c}", [P, 512], BF16)) for c in range(NCH)]
        Lh_b = [Buf() for _ in range(NCH)]
        Ll_b = [Buf() for _ in range(NCH)]
        Sh_b = [Buf() for _ in range(NCH)]
        Sl_b = [Buf() for _ in range(NCH)]
        ych = [es.enter_context(nc.sbuf_tensor(f"sby2{tag}{c}", [P, 512], BF16)) for c in range(NCH)]
        E_b = [Buf() for _ in range(NCH)]
        L_b = [Buf() for _ in range(NCH)]
        S_b = [[Buf(), Buf()] for _ in range(NCH)]
        W_b = [Buf() for _ in range(NCH)]
        y_b = [Buf() for _ in range(NCH)]
        order = []
        lo, hi = 0, ntile - 1
        while lo <= hi:
            order.append(hi)
            hi -= 1
            if lo <= hi:
                order.append(lo)
                lo += 1
        queues = [[], []]
        load = [0, 0]
        for ti in sorted(range(ntile), key=lambda q: -q):
            sidx = 0 if load[0] <= load[1] else 1
            queues[sidx].append(ti)
            load[sidx] += 4 * ti + 4
        state = [None] * NCH
        qpos = [0, 0]

        def next_tile(slot):
            if qpos[slot] < len(queues[slot]):
                ti = queues[slot][qpos[slot]]
                qpos[slot] += 1
                return ti
            return None
        for slot in range(2):
            ti = next_tile(slot)
            for h in range(2):
                state[slot * 2 + h] = None if ti is None else [ti, 0]
        def mkinfo(c):
            i, n = state[c]
            nsteps = 4 * i + 4
            jb = 4 * i + 3 - n
            return dict(i=i, n=n, nsteps=nsteps, jb=jb, diag=jb >= 4 * i, r=jb - 4 * i, kt=jb // 4, h=c % 2,
                        hs=slice((c % 2) * 64, (c % 2) * 64 + 64), pz=c, py=4 + c // 2, k=n % 2)

        def H1(act, info):
            for c in act:
                f = info[c]
                if f["n"] > 0:
                    k = f["k"]
                    S.op("dve", lambda e, c=c, k=k: e.tensor_copy(out=Shi[c][:], in_=Sch[c][k][:]), reads=[S_b[c][k]], writes=[Sh_b[c]])
                    S.op("dve", lambda e, c=c, k=k: e.tensor_tensor(out=Slo[c][:], in0=Sch[c][k][:], in1=Shi[c][:], op=ALU.subtract),
                         reads=[S_b[c][k], Sh_b[c]], writes=[Sl_b[c]])
            for c in act:
                f = info[c]
                S.op("pe", lambda e, f=f: e.matmul(out=C.ps[f["pz"]][:], lhsT=KT[f["hs"], f["jb"] * 128:(f["jb"] + 1) * 128],
                                                   rhs=QT[f["hs"], f["i"] * 512:(f["i"] + 1) * 512], start=True, stop=not f["diag"]),
                     reads=[KT_b[f["kt"]], QT_b[f["i"]]], writes=[C.psb[f["pz"]]])
                if f["diag"]:
                    S.op("pe", lambda e, f=f: e.matmul(out=C.ps[f["pz"]][:], lhsT=identb[:], rhs=negmask[:, f["r"], :], start=False, stop=True),
                         reads=[cb], writes=[C.psb[f["pz"]]])
            for c in act:
                f = info[c]
                S.op("act", lambda e, c=c, f=f: e.activation(out=Ech[c][:], in_=C.ps[f["pz"]][:], func=AF.Exp),
                     reads=[C.psb[f["pz"]]], writes=[E_b[c]])
            for c in act:
                S.op("act", lambda e, c=c: e.activation(out=Lch[c][:], in_=Ech[c][:], func=AF.Ln, bias=one1[:], scale=1.0),
                     reads=[E_b[c], cb], writes=[L_b[c]])

        def H2(act, info):
            for c in act:
                if c < 2:
                    S.op("act", lambda e, c=c: e.copy(out=Lhi[c][:], in_=Lch[c][:]), reads=[L_b[c]], writes=[Lh_b[c]])
                else:
                    S.op("dve", lambda e, c=c: e.tensor_copy(out=Lhi[c][:], in_=Lch[c][:]), reads=[L_b[c]], writes=[Lh_b[c]])
                S.op("dve", lambda e, c=c: e.tensor_tensor(out=Llo[c][:], in0=Lch[c][:], in1=Lhi[c][:], op=ALU.subtract),
                     reads=[L_b[c], Lh_b[c]], writes=[Ll_b[c]])
            for c in act:
                f = info[c]
                if f["n"] > 0:
                    S.op("pe", lambda e, c=c, f=f: e.matmul(out=C.ps[f["pz"]][:], lhsT=ones[:], rhs=Shi[c][:], start=False, stop=False,
                                                            skip_group_check=True), reads=[cb, Sh_b[c]], writes=[C.psb[f["pz"]]])
                    S.op("pe", lambda e, c=c, f=f: e.matmul(out=C.ps[f["pz"]][:], lhsT=ones[:], rhs=Slo[c][:], start=False, stop=False,
                                                            skip_group_check=True), reads=[cb, Sl_b[c]], writes=[C.psb[f["pz"]]])
                S.op("pe", lambda e, c=c, f=f: e.matmul(out=C.ps[f["pz"]][:], lhsT=ntri[:], rhs=Lhi[c][:], start=False, stop=False,
                                                        skip_group_check=True), reads=[cb, Lh_b[c]], writes=[C.psb[f["pz"]]])
                S.op("pe", lambda e, c=c, f=f: e.matmul(out=C.ps[f["pz"]][:], lhsT=ntri[:], rhs=Llo[c][:], start=False, stop=True,
                                                        skip_group_check=True), reads=[cb, Ll_b[c]], writes=[C.psb[f["pz"]]])
            for c in act:
                f = info[c]
                S.op("act", lambda e, c=c, f=f: e.activation(out=Wch[c][:], in_=C.ps[f["pz"]][:], func=AF.Exp),
                     reads=[C.psb[f["pz"]]], writes=[W_b[c]])
                if f["n"] < f["nsteps"] - 1:
                    k, k2 = f["k"], 1 - f["k"]
                    if f["n"] == 0:
                        S.op("pool", lambda e, c=c, k2=k2: e.tensor_copy(out=Sch[c][k2][:], in_=Lch[c][:]), reads=[L_b[c]], writes=[S_b[c][k2]])
                    else:
                        S.op("pool", lambda e, c=c, k=k, k2=k2: e.tensor_tensor(out=Sch[c][k2][:], in0=Sch[c][k][:], in1=Lch[c][:], op=ALU.add),
                             reads=[L_b[c], S_b[c][k]], writes=[S_b[c][k2]])
            for c in act:
                f = info[c]
                po = (c % 2) * 64
                S.op("pe", lambda e, c=c, f=f, po=po: e.matmul(out=C.ps[f["py"]][po:po + 64, :], lhsT=V[:, f["jb"], f["hs"]], rhs=Wch[c][:],
                                                               start=(f["n"] == 0), stop=(f["n"] == f["nsteps"] - 1), skip_group_check=True),
                     reads=[V_b[f["kt"]], W_b[c]], writes=[C.psb[f["py"]]])
            for c in act:
                f = info[c]
                if f["n"] == f["nsteps"] - 1:
                    po = (c % 2) * 64
                    S.op("dve", lambda e, c=c, f=f, po=po: e.tensor_copy(out=ych[c][po:po + 64, :], in_=C.ps[f["py"]][po:po + 64, :]),
                         reads=[C.psb[f["py"]]], writes=[y_b[c]])
                    S.dma("sp", yT[f["h"] * 64:(f["h"] + 1) * 64, f["i"] * 512:(f["i"] + 1) * 512], ych[c][po:po + 64, :],
                          reads=[y_b[c]], writes=[yT_b])
                    state[c] = "done"
                else:
                    state[c][1] += 1
            for slot in range(2):
                cs = [slot * 2, slot * 2 + 1]
                if all(state[c] == "done" for c in cs):
                    ti = next_tile(slot)
                    for c in cs:
                        state[c] = None if ti is None else [ti, 0]

        pending = [None, None]
        while True:
            progressed = False
            for slot in range(2):
                other = 1 - slot
                cs_ = [c for c in (2 * slot, 2 * slot + 1) if state[c] is not None]
                if cs_ and pending[slot] is None:
                    inf_ = {c: mkinfo(c) for c in cs_}
                    H1(cs_, inf_)
                    pending[slot] = (cs_, inf_)
                    progressed = True
                if pending[other] is not None:
                    H2(*pending[other])
                    pending[other] = None
                    progressed = True
            if not progressed:
                break
        S.barrier()


def make_consts_np():
    c = {}
    c["ident"] = np.eye(P, dtype=np.float32)
    j = np.arange(P)[:, None]
    s = np.arange(P)[None, :]
    c["ntri"] = -(j >= s).astype(np.float32)
    c["ones"] = -np.ones((P, P), np.float32)
    nm = np.zeros((P, 4, 512), np.float32)
    for r in range(4):
        key = 128 * r + np.arange(P)[:, None]
        col = np.arange(512)[None, :]
        nm[:, r, :] = np.where(key < col, 0.0, -30000.0)
    c["sb_negmask"] = nm
    make_swa_consts_np(c)
    make_ml_consts_np(c)
    return c


def dram_ap(t_ap, offset, pattern):
    return bass.AP(tensor=t_ap.tensor, offset=offset, ap=pattern)


def swa_phase(C, xfull, xfull_b, w_swa, rb_dram, sink_dram, frev_scr, yT, yT_b, TT, consts, tag):
    nc, S = C.nc, C.S
    ntile = TT // 512
    nblk = TT // 128
    C.set_psum(2)
    with contextlib.ExitStack() as es:
        w, w_b = load_w_bf16(C, es, f"wsw{tag}", w_swa, 448)
        QT = [es.enter_context(nc.sbuf_tensor(f"swQT{tag}{g}", [P, TT], BF16)) for g in range(2)]
        KT = es.enter_context(nc.sbuf_tensor(f"swKT{tag}", [P, TT], BF16))
        V = es.enter_context(nc.sbuf_tensor(f"swV{tag}", [P, nblk, 64], BF16))
        xTs = [es.enter_context(nc.sbuf_tensor(f"swxT{tag}{k}", [P, 8, 512], BF16)) for k in range(2)]
        xT_bs = [Buf(), Buf()]
        identb = es.enter_context(nc.sbuf_tensor(f"swidb{tag}", [P, P], BF16))
        Jm = es.enter_context(nc.sbuf_tensor(f"swJ{tag}", [P, P], F32))
        rb = es.enter_context(nc.sbuf_tensor(f"swrb{tag}", [P, P], F32))
        oh = es.enter_context(nc.sbuf_tensor(f"swoh{tag}", [P, 384], F32))
        fneg = es.enter_context(nc.sbuf_tensor(f"swfn{tag}", [4, 384], F32))
        frev = es.enter_context(nc.sbuf_tensor(f"swfr{tag}", [4, 384], F32))
        Hk = es.enter_context(nc.sbuf_tensor(f"swH{tag}", [P, 4, 256], F32))
        bias = es.enter_context(nc.sbuf_tensor(f"swbias{tag}", [P, 4, 256], F32))
        sink = es.enter_context(nc.sbuf_tensor(f"swsink{tag}", [P, 4], F32))
        Sb = es.enter_context(nc.sbuf_tensor(f"swS{tag}", [P, 4, 256], F32))
        pf = es.enter_context(nc.sbuf_tensor(f"swp{tag}", [P, 4, 256], F32))
        pn = es.enter_context(nc.sbuf_tensor(f"swpn{tag}", [P, 4, 256], BF16))
        pT = es.enter_context(nc.sbuf_tensor(f"swpT{tag}", [P, 8, 128], BF16))
        small = es.enter_context(nc.sbuf_tensor(f"swsm{tag}", [P, 6, 4], F32))
        yo = es.enter_context(nc.sbuf_tensor(f"swyo{tag}", [64, 4, 512], BF16))
        cb, x_b, xT_b = Buf(), Buf(), Buf()
        S.dma("pool", identb[:], consts["ident"], writes=[cb])
        S.dma("sp", Jm[:], consts["J"], writes=[cb])
        S.dma("sp", rb[:], rb_dram, writes=[cb])
        S.dma("sp", oh[:], consts["swa_oh"], writes=[cb])
        S.dma("sp", fneg[:], consts["swa_fneg"], writes=[cb])
        S.dma("sp", sink[:], sink_dram, writes=[cb])
        S.op("pe", lambda e: e.matmul(out=C.ps[0][:, 0:384], lhsT=rb[:], rhs=oh[:], start=True, stop=True),
             reads=[cb], writes=[C.psb[0]])
        fr_b, scr_b, H_b, bias_b = Buf(), Buf(), Buf(), Buf()
        S.op("dve", lambda e: e.tensor_tensor(out=frev[:], in0=C.ps[0][0:4, 0:384], in1=fneg[:], op=ALU.add),
             reads=[C.psb[0], cb], writes=[fr_b])
        S.dma("sp", frev_scr, frev[:], reads=[fr_b], writes=[scr_b])
        S.dma("sp", Hk[:], dram_ap(frev_scr, 0, [[1, 128], [384, 4], [1, 256]]), reads=[scr_b], writes=[H_b])
        for hh in range(2):
            S.op("pe", lambda e, hh=hh: e.matmul(out=C.ps[1 + hh][:], lhsT=Jm[:], rhs=Hk[:, 2 * hh:2 * hh + 2, :],
                                                 start=True, stop=True), reads=[cb, H_b], writes=[C.psb[1 + hh]])
            S.op("dve", lambda e, hh=hh: e.tensor_copy(out=bias[:, 2 * hh:2 * hh + 2, :], in_=C.ps[1 + hh][:]),
                 reads=[C.psb[1 + hh]], writes=[bias_b])
        QT_b = [Buf() for _ in range(ntile)]
        KT_b = [Buf() for _ in range(ntile)]
        V_b = [Buf() for _ in range(ntile)]
        xv = xfull if callable(xfull) else (lambda t, _v=xfull.rearrange("(t b p) d -> t p b d", b=4, p=P): _v[t])
        for t in range(ntile):
            xT, xT_b = fetch_xT(C, xfull, xfull_b, t, xTs, xT_bs)
            for g in range(2):
                proj_fm(C, w, w_b, g * 128, (g + 1) * 128, xT, xT_b, g)
                S.op("act", lambda e, t=t, g=g: e.mul(out=QT[g][:, t * 512:(t + 1) * 512], in_=C.ps[g][:], mul=0.125),
                     reads=[C.psb[g]], writes=[QT_b[t]])
            proj_fm(C, w, w_b, 256, 384, xT, xT_b, 2)
            S.op("dve", lambda e, t=t: e.tensor_copy(out=KT[:, t * 512:(t + 1) * 512], in_=C.ps[2][:]),
                 reads=[C.psb[2]], writes=[KT_b[t]])
            for blk in range(4):
                pi = 3 + blk % 2
                proj_tm(C, w, w_b, 384, 448, xT, xT_b, blk, pi)
                S.op("dve", lambda e, t=t, blk=blk, pi=pi: e.tensor_copy(out=V[:, t * 4 + blk, :], in_=C.ps[pi][:, 0:64]),
                     reads=[C.psb[pi]], writes=[V_b[t]])
        NS = 2
        Sb2 = [Sb] + [es.enter_context(nc.sbuf_tensor(f"swS{tag}b", [P, 4, 256], F32))]
        pf2 = [pf] + [es.enter_context(nc.sbuf_tensor(f"swp{tag}b", [P, 4, 256], F32))]
        pn2 = [pn] + [es.enter_context(nc.sbuf_tensor(f"swpn{tag}b", [P, 4, 256], BF16))]
        pT2 = [pT] + [es.enter_context(nc.sbuf_tensor(f"swpT{tag}b", [P, 8, 128], BF16))]
        sm2 = [small] + [es.enter_context(nc.sbuf_tensor(f"swsm{tag}b", [P, 6, 4], F32))]
        S_b = [Buf(), Buf()]
        p_b = [Buf(), Buf()]
        pn_b = [Buf(), Buf()]
        pT_b = [Buf(), Buf()]
        sm_b = [Buf(), Buf()]
        yo_b = Buf()
        for n0 in range(0, nblk, NS):
            blks = [(n0 + a, a) for a in range(NS) if n0 + a < nblk]
            kwd = {n: (128 if n == 0 else 256) for n, a in blks}
            for n, a in blks:
                kw = kwd[n]
                k0 = 256 - kw
                t_q = n // 4
                for h in range(4):
                    g, hs = h % 2, slice((h // 2) * 64, (h // 2) * 64 + 64)
                    bank = 2 * a + h // 2
                    col = (h % 2) * 256
                    S.op("pe", lambda e, g=g, hs=hs, bank=bank, col=col, n=n, kw=kw, k0=k0: e.matmul(
                        out=C.ps[bank][:, col + k0:col + 256], lhsT=QT[g][hs, n * 128:(n + 1) * 128],
                        rhs=KT[hs, (n + 1) * 128 - kw:(n + 1) * 128], start=True, stop=True, skip_group_check=True),
                        reads=[QT_b[t_q], KT_b[t_q], KT_b[max(0, (n - 1) // 4)]], writes=[C.psb[bank]])
            for n, a in blks:
                k0 = 256 - kwd[n]
                mx, negm, dd, esk, rs, rden = [sm2[a][:, i, :] for i in range(6)]
                for bk in range(2):
                    bank = 2 * a + bk
                    S.op("dve", lambda e, bank=bank, bk=bk, k0=k0, a=a: e.tensor_tensor(
                        out=Sb2[a][:, 2 * bk:2 * bk + 2, k0:256],
                        in0=C.ps[bank][:].rearrange("p (h k) -> p h k", h=2)[:, :, k0:256],
                        in1=bias[:, 2 * bk:2 * bk + 2, k0:256], op=ALU.add),
                        reads=[C.psb[bank], bias_b], writes=[S_b[a]])
                S.op("dve", lambda e, k0=k0, a=a, mx=mx: e.tensor_reduce(out=mx, in_=Sb2[a][:, :, k0:256], axis=AX.X, op=ALU.max),
                     reads=[S_b[a]], writes=[sm_b[a]])
                S.op("dve", lambda e, mx=mx: e.tensor_tensor(out=mx, in0=mx, in1=sink[:], op=ALU.max), reads=[sm_b[a], cb], writes=[sm_b[a]])
                S.op("dve", lambda e, mx=mx, negm=negm: e.tensor_scalar(out=negm, in0=mx, scalar1=-1.0, scalar2=None, op0=ALU.mult),
                     reads=[sm_b[a]], writes=[sm_b[a]])
                S.op("dve", lambda e, mx=mx, dd=dd: e.tensor_tensor(out=dd, in0=sink[:], in1=mx, op=ALU.subtract), reads=[sm_b[a], cb], writes=[sm_b[a]])
            for n, a in blks:
                k0 = 256 - kwd[n]
                mx, negm, dd, esk, rs, rden = [sm2[a][:, i, :] for i in range(6)]
                for h in range(4):
                    S.op("act", lambda e, h=h, k0=k0, a=a, negm=negm, rs=rs: e.activation(
                        out=pf2[a][:, h, k0:256], in_=Sb2[a][:, h, k0:256], func=AF.Exp, bias=negm[:, h:h + 1], scale=1.0, accum_out=rs[:, h:h + 1]),
                        reads=[S_b[a], sm_b[a]], writes=[p_b[a], sm_b[a]])
                S.op("act", lambda e, esk=esk, dd=dd: e.activation(out=esk, in_=dd, func=AF.Exp), reads=[sm_b[a]], writes=[sm_b[a]])
            for n, a in blks:
                k0 = 256 - kwd[n]
                mx, negm, dd, esk, rs, rden = [sm2[a][:, i, :] for i in range(6)]
                S.op("dve", lambda e, rden=rden, rs=rs, esk=esk: e.tensor_tensor(out=rden, in0=rs, in1=esk, op=ALU.add), reads=[sm_b[a]], writes=[sm_b[a]])
                S.op("dve", lambda e, rden=rden: e.reciprocal(out=rden, in_=rden), reads=[sm_b[a]], writes=[sm_b[a]])
                for h in range(4):
                    if h % 2:
                        S.op("dve", lambda e, h=h, k0=k0, a=a, rden=rden: e.tensor_scalar(
                            out=pn2[a][:, h, k0:256], in0=pf2[a][:, h, k0:256], scalar1=rden[:, h:h + 1], scalar2=None, op0=ALU.mult),
                            reads=[p_b[a], sm_b[a]], writes=[pn_b[a]])
                    else:
                        S.op("act", lambda e, h=h, k0=k0, a=a, rden=rden: e.mul(out=pn2[a][:, h, k0:256], in_=pf2[a][:, h, k0:256], mul=rden[:, h:h + 1]),
                             reads=[p_b[a], sm_b[a]], writes=[pn_b[a]])
            for n, a in blks:
                kw = kwd[n]
                k0 = 256 - kw
                nkb = kw // 128
                for h in range(4):
                    for kb in range(nkb):
                        idx = h * 2 + kb
                        S.op("pe", lambda e, h=h, kb=kb, idx=idx, k0=k0, a=a: e.transpose(
                            out=C.ps_bfs[a][:, idx * 128:(idx + 1) * 128], in_=pn2[a][:, h, k0 + kb * 128:k0 + (kb + 1) * 128],
                            identity=identb[:]), reads=[pn_b[a], cb], writes=[C.psbf_b[a]])
            for n, a in blks:
                if a == 0:
                    S.op("act", lambda e, a=a: e.copy(out=pT2[a][:].rearrange("p a b -> p (a b)"), in_=C.ps_bfs[a][:]), reads=[C.psbf_b[a]], writes=[pT_b[a]])
                else:
                    S.op("dve", lambda e, a=a: e.tensor_copy(out=pT2[a][:].rearrange("p a b -> p (a b)"), in_=C.ps_bfs[a][:]), reads=[C.psbf_b[a]], writes=[pT_b[a]])
            for n, a in blks:
                nkb = kwd[n] // 128
                po = 4 + a
                for h in range(4):
                    for kb in range(nkb):
                        idx = h * 2 + kb
                        kblk = n - (nkb - 1) + kb
                        hd = (h % 2) * 2 + h // 2
                        S.op("pe", lambda e, hd=hd, kb=kb, idx=idx, kblk=kblk, nkb=nkb, a=a, po=po: e.matmul(
                            out=C.ps[po][0:64, hd * 128:(hd + 1) * 128], lhsT=V[:, kblk, :], rhs=pT2[a][:, idx, :],
                            start=(kb == 0), stop=(kb == nkb - 1), skip_group_check=True),
                            reads=[V_b[kblk // 4], pT_b[a]], writes=[C.psb[po]])
            for n, a in blks:
                po = 4 + a
                S.op("dve" if a == 0 else "act",
                     (lambda e, n=n, po=po: e.tensor_copy(out=yo[:, :, (n % 4) * 128:(n % 4 + 1) * 128],
                                                          in_=C.ps[po][0:64, :].rearrange("p (h q) -> p h q", h=4))) if a == 0 else
                     (lambda e, n=n, po=po: e.copy(out=yo[:, :, (n % 4) * 128:(n % 4 + 1) * 128],
                                                   in_=C.ps[po][0:64, :].rearrange("p (h q) -> p h q", h=4))),
                     reads=[C.psb[po]], writes=[yo_b])
                if n % 4 == 3:
                    S.dma("sp", yT.rearrange("(h d) t -> d h t", d=64)[:, :, (n // 4) * 512:(n // 4 + 1) * 512], yo[:],
                          reads=[yo_b], writes=[yT_b])
        S.barrier()
        C.set_psum(1)


def t5_bucket_np(dist):
    max_exact = 16
    d = np.maximum(dist, 1)
    large = max_exact + (np.log(d / max_exact) / np.log(128 / max_exact) * (32 - max_exact)).astype(np.int32)
    large = np.minimum(large, 31)
    return np.where(dist < max_exact, dist, large).astype(np.int32)


def make_swa_consts_np(c):
    a = np.arange(384)
    dist = 255 - a
    valid = (dist >= 0) & (dist < 128)
    bucket = t5_bucket_np(np.clip(dist, 0, None))
    oh = np.zeros((32, 384), np.float32)
    oh[bucket[valid], a[valid]] = 1.0
    c["swa_oh"] = np.pad(oh, ((0, 96), (0, 0)))
    c["swa_fneg"] = np.tile(np.where(valid, 0.0, -30000.0).astype(np.float32)[None, :], (4, 1))
    c["J"] = np.eye(P, dtype=np.float32)[::-1].copy()
    return c


def mlstm_phase(C, xfull, xfull_b, w_ml, cw_dram, cb_dram, ifb_dram, ng_dram, yT, yT_b, TT, consts, tag):
    nc, S = C.nc, C.S
    ntile = TT // 512
    nb = TT // 128
    n2 = 2 * nb
    with contextlib.ExitStack() as es:
        w, w_b = load_w_bf16(C, es, f"wml{tag}", w_ml, 640)
        sb = lambda name, shape, dt=F32: es.enter_context(nc.sbuf_tensor(f"ml{name}{tag}", shape, dt))
        QT = sb("QT", [P, TT], BF16)
        KT = sb("KT", [P, TT], BF16)
        Vext = sb("Vext", [P, nb, 2, 66], BF16)
        gso = sb("gso", [P, nb, 128])
        G4 = sb("G4", [P, nb, 4])
        xTs = [sb(f"xT{k}", [P, 8, 512], BF16) for k in range(2)]
        xT_bs = [Buf(), Buf()]
        raws = [sb(f"raw{k}", [P, 2, 515]) for k in range(2)]
        accs = [sb(f"acc{k}", [P, 2, 512]) for k in range(2)]
        cw = sb("cw", [P, 8])
        cbt = sb("cb", [P, 2])
        ifb = sb("ifb", [P, 4])
        nfb = sb("nfb", [P, 2])
        ng = sb("ng", [P, 128])
        identb = sb("idb", [P, P], BF16)
        ntriT = sb("ntriT", [P, P])
        mask8 = sb("mask8", [P, P])
        e0 = sb("e0", [P, P])
        e127 = sb("e127", [P, P])
        one1 = sb("one1", [P, 1])
        cb_ = Buf()
        S.dma("pool", identb[:], consts["ident"], writes=[cb_])
        S.dma("sp", ntriT[:], consts["ntriT"], writes=[cb_])
        S.dma("sp", mask8[:], consts["mask8"], writes=[cb_])
        S.dma("sp", e0[:], consts["e0ones"], writes=[cb_])
        S.dma("sp", e127[:], consts["e127ones"], writes=[cb_])
        S.dma("sp", cw[:], cw_dram, writes=[cb_])
        S.dma("sp", cbt[:], cb_dram, writes=[cb_])
        S.dma("sp", ifb[:], ifb_dram, writes=[cb_])
        S.dma("sp", ng[:], ng_dram, writes=[cb_])
        S.op("pool", lambda e: e.memset(one1[:], 1.0), writes=[cb_])
        S.op("dve", lambda e: e.tensor_scalar(out=nfb[:], in0=ifb[:, 2:4], scalar1=-1.0, scalar2=None, op0=ALU.mult),
             reads=[cb_], writes=[cb_])
        x_b, xT_b, V_b, gso_b, G_b = Buf(), Buf(), Buf(), Buf(), Buf()
        raw_bs, acc_bs = [Buf(), Buf()], [Buf(), Buf()]
        QT_b = [Buf() for _ in range(ntile)]
        KT_b = [Buf() for _ in range(ntile)]
        for k in range(2):
            S.op("pool", lambda e, k=k: e.memset(raws[k][:], 0.0), writes=[raw_bs[k]])
        S.op("pool", lambda e: e.memset(Vext[:], 1.0), writes=[V_b])
        xv = xfull if callable(xfull) else (lambda t, _v=xfull.rearrange("(t b p) d -> t p b d", b=4, p=P): _v[t])
        import os
        mlstage = int(os.environ.get("ML_STAGE", "9"))
        for t in range(ntile):
            if mlstage < -1:
                break
            xT, xT_b = fetch_xT(C, xfull, xfull_b, t, xTs, xT_bs)
            raw, raw_b, acc, acc_b = raws[t % 2], raw_bs[t % 2], accs[t % 2], acc_bs[t % 2]
            if t > 0:
                S.op("pool", lambda e, raw=raw, t=t: e.tensor_copy(out=raw[:, :, 0:3], in_=raws[(t - 1) % 2][:, :, 512:515]),
                     reads=[raw_bs[(t - 1) % 2]], writes=[raw_b])
            for qk in range(2):
                proj_fm(C, w, w_b, qk * 128, (qk + 1) * 128, xT, xT_b, qk)
                S.op("act", lambda e, qk=qk, raw=raw: e.copy(out=raw[:, qk, 3:515], in_=C.ps[qk][:]), reads=[C.psb[qk]], writes=[raw_b])
            for qk in range(2):
                S.op("dve", lambda e, qk=qk, raw=raw, acc=acc: e.tensor_scalar(out=acc[:, qk, :], in0=raw[:, qk, 0:512], scalar1=cw[:, 4 * qk:4 * qk + 1],
                                                              scalar2=None, op0=ALU.mult), reads=[raw_b, cb_], writes=[acc_b])
                for j in range(1, 4):
                    S.op("dve", lambda e, qk=qk, j=j, raw=raw, acc=acc: e.scalar_tensor_tensor(
                        out=acc[:, qk, :], in0=raw[:, qk, j:j + 512], scalar=cw[:, 4 * qk + j:4 * qk + j + 1], in1=acc[:, qk, :],
                        op0=ALU.mult, op1=ALU.add), reads=[raw_b, cb_, acc_b], writes=[acc_b])
                dst, dst_b = (QT, QT_b) if qk == 0 else (KT, KT_b)
                S.op("act", lambda e, qk=qk, dst=dst, t=t, acc=acc: e.activation(out=dst[:, t * 512:(t + 1) * 512], in_=acc[:, qk, :], func=AF.Silu,
                                                                        bias=cbt[:, qk:qk + 1], scale=1.0),
                     reads=[acc_b, cb_], writes=[dst_b[t]])
            for blk in range(4):
                if mlstage < 0:
                    break
                pi = 2 + blk % 2
                bi = t * 4 + blk
                proj_tm(C, w, w_b, 256, 512, xT, xT_b, blk, pi)
                proj_tm(C, w, w_b, 512, 640, xT, xT_b, blk, 4 + blk % 2)
                sub = int(os.environ.get("ML_SUB", "9"))
                if sub < 1:
                    continue
                S.op("dve", lambda e, pi=pi, bi=bi: e.tensor_copy(out=Vext[:, bi, :, 0:64],
                                                                  in_=C.ps[pi][:, 0:128].rearrange("p (h d) -> p h d", h=2)),
                     reads=[C.psb[pi]], writes=[V_b])
                if sub < 2:
                    continue
                S.op("act", lambda e, pi=pi, bi=bi: e.activation(out=gso[:, bi, :], in_=C.ps[pi][:, 128:256], func=AF.Exp, scale=-1.0),
                     reads=[C.psb[pi]], writes=[gso_b])
                S.op("act", lambda e, bi=bi: e.add(out=gso[:, bi, :], in_=gso[:, bi, :], add=1.0), reads=[gso_b], writes=[gso_b])
                S.op("dve", lambda e, bi=bi: e.reciprocal(out=gso[:, bi, :], in_=gso[:, bi, :]), reads=[gso_b], writes=[gso_b])
                if sub < 3:
                    continue
                S.op("dve", lambda e, blk=blk, bi=bi: e.tensor_copy(out=G4[:, bi, :], in_=C.ps[4 + blk % 2][:, 0:4]),
                     reads=[C.psb[4 + blk % 2]], writes=[G_b])
                S.op("pool", lambda e, bi=bi: e.tensor_tensor(out=gso[:, bi, :], in0=gso[:, bi, :], in1=ng[:], op=ALU.mult),
                     reads=[gso_b, cb_], writes=[gso_b])
        import os
        mlstage = int(os.environ.get("ML_STAGE", "9"))
        if mlstage < 1:
            S.barrier()
            return
        icol = sb("icol", [P, 2, nb])
        lf = sb("lf", [P, 2, nb])
        bcol = sb("bcol", [P, 2, nb])
        acol = sb("acol", [P, 2, nb])
        aT = sb("aT", [P, P])
        cm_tok = sb("cm_tok", [P, n2])
        amax_bc = sb("amax_bc", [P, n2])
        cmT = sb("cmT", [P, P])
        RW = sb("RW", [P, 3, P])
        mnext = sb("mnext", [1, P])
        mprev_bc = sb("mprevbc", [P, n2])
        mref_bc = sb("mrefbc", [P, n2])
        Mt = sb("Mt", [P, n2])
        r_t = sb("r_t", [P, n2])
        u_t = sb("u_t", [P, n2])
        eb_t = sb("eb_t", [P, n2])
        sc_bc = sb("sc_bc", [P, n2])
        tmp = sb("tmpg", [P, n2])
        g_b = Buf()
        fl = lambda tl: tl[:].rearrange("p h c -> p (h c)")
        for h in range(2):
            S.op("act", lambda e, h=h: e.activation(out=icol[:, h, :], in_=G4[:, :, h], func=AF.Identity, bias=ifb[:, h:h + 1], scale=1.0),
                 reads=[G_b, cb_], writes=[g_b])
            S.op("act", lambda e, h=h: e.activation(out=lf[:, h, :], in_=G4[:, :, 2 + h], func=AF.Exp, bias=nfb[:, h:h + 1], scale=-1.0),
                 reads=[G_b, cb_], writes=[g_b])
        S.op("act", lambda e: e.activation(out=fl(lf), in_=fl(lf), func=AF.Ln, bias=one1[:], scale=1.0), reads=[g_b, cb_], writes=[g_b])
        S.op("pe", lambda e: e.matmul(out=C.ps[0][:, 0:n2], lhsT=ntriT[:], rhs=fl(lf), start=True, stop=True),
             reads=[g_b, cb_], writes=[C.psb[0]])
        S.op("dve", lambda e: e.tensor_copy(out=fl(bcol), in_=C.ps[0][:, 0:n2]), reads=[C.psb[0]], writes=[g_b])
        S.op("dve", lambda e: e.tensor_tensor(out=fl(acol), in0=fl(icol), in1=fl(bcol), op=ALU.subtract), reads=[g_b], writes=[g_b])
        S.op("pe", lambda e: e.transpose(out=C.ps[1][0:n2, 0:128], in_=fl(acol), identity=C.ident[:]), reads=[g_b, C.ident_b], writes=[C.psb[1]])
        S.op("pool", lambda e: e.memset(aT[:], 0.0), writes=[g_b])
        S.op("pool", lambda e: e.memset(RW[:], 0.0), writes=[g_b])
        S.op("dve", lambda e: e.tensor_copy(out=aT[0:n2, :], in_=C.ps[1][0:n2, 0:128]), reads=[C.psb[1]], writes=[g_b])
        S.op("dve", lambda e: e.tensor_tensor_scan(out=cmT[:], data0=aT[:], data1=aT[:], initial=-1.0e30, op0=ALU.max, op1=ALU.max),
             reads=[g_b], writes=[g_b])
        S.op("pe", lambda e: e.transpose(out=C.ps[2][:, 0:128], in_=cmT[:], identity=C.ident[:]), reads=[g_b, C.ident_b], writes=[C.psb[2]])
        S.op("dve", lambda e: e.tensor_copy(out=cm_tok[:], in_=C.ps[2][:, 0:n2]), reads=[C.psb[2]], writes=[g_b])
        S.op("pe", lambda e: e.matmul(out=C.ps[3][:, 0:n2], lhsT=e127[:], rhs=cm_tok[:], start=True, stop=True), reads=[g_b, cb_], writes=[C.psb[3]])
        S.op("pe", lambda e: e.matmul(out=C.ps[4][:, 0:n2], lhsT=e127[:], rhs=fl(bcol), start=True, stop=True), reads=[g_b, cb_], writes=[C.psb[4]])
        S.op("dve", lambda e: e.tensor_copy(out=amax_bc[:], in_=C.ps[3][:, 0:n2]), reads=[C.psb[3]], writes=[g_b])
        S.op("dve", lambda e: e.tensor_copy(out=RW[:, 1, 0:n2], in_=C.ps[4][:, 0:n2]), reads=[C.psb[4]], writes=[g_b])
        S.op("dve", lambda e: e.tensor_copy(out=RW[:, 0, 0:n2], in_=amax_bc[:]), reads=[g_b], writes=[g_b])
        for h in range(2):
            S.op("dve", lambda e, h=h: e.tensor_tensor_scan(out=mnext[0:1, h * nb:(h + 1) * nb], data0=RW[0:1, 0, h * nb:(h + 1) * nb],
                                                            data1=RW[0:1, 1, h * nb:(h + 1) * nb], initial=0.0, op0=ALU.max, op1=ALU.add),
                 reads=[g_b], writes=[g_b])
            if nb > 1:
                S.op("dve", lambda e, h=h: e.tensor_copy(out=RW[0:1, 2, h * nb + 1:(h + 1) * nb], in_=mnext[0:1, h * nb:(h + 1) * nb - 1]),
                     reads=[g_b], writes=[g_b])
        S.op("pe", lambda e: e.matmul(out=C.ps[0][:, 0:n2], lhsT=e0[:], rhs=RW[:, 2, 0:n2], start=True, stop=True),
             reads=[g_b, cb_], writes=[C.psb[0]])
        S.op("dve", lambda e: e.tensor_copy(out=mprev_bc[:], in_=C.ps[0][:, 0:n2]), reads=[C.psb[0]], writes=[g_b])
        S.op("dve", lambda e: e.tensor_tensor(out=mref_bc[:], in0=amax_bc[:], in1=mprev_bc[:], op=ALU.max), reads=[g_b], writes=[g_b])
        S.op("dve", lambda e: e.tensor_tensor(out=Mt[:], in0=cm_tok[:], in1=mprev_bc[:], op=ALU.max), reads=[g_b], writes=[g_b])
        S.op("dve", lambda e: e.tensor_tensor(out=tmp[:], in0=mref_bc[:], in1=Mt[:], op=ALU.subtract), reads=[g_b], writes=[g_b])
        S.op("act", lambda e: e.activation(out=r_t[:], in_=tmp[:], func=AF.Exp), reads=[g_b], writes=[g_b])
        S.op("dve", lambda e: e.tensor_tensor(out=tmp[:], in0=fl(acol), in1=mref_bc[:], op=ALU.subtract), reads=[g_b], writes=[g_b])
        S.op("act", lambda e: e.activation(out=u_t[:], in_=tmp[:], func=AF.Exp), reads=[g_b], writes=[g_b])
        S.op("dve", lambda e: e.tensor_tensor(out=tmp[:], in0=fl(bcol), in1=Mt[:], op=ALU.add), reads=[g_b], writes=[g_b])
        S.op("act", lambda e: e.activation(out=eb_t[:], in_=tmp[:], func=AF.Exp, scale=-1.0), reads=[g_b], writes=[g_b])
        S.op("dve", lambda e: e.tensor_tensor(out=tmp[:], in0=mprev_bc[:], in1=mref_bc[:], op=ALU.subtract), reads=[g_b], writes=[g_b])
        S.op("act", lambda e: e.activation(out=sc_bc[:], in_=tmp[:], func=AF.Exp), reads=[g_b], writes=[g_b])
        if mlstage < 2:
            S.barrier()
            return
        Cst = sb("Cst", [P, 65])
        Csb = sb("Csb", [P, 130], BF16)
        Kp = sb("Kp", [P, P], BF16)
        ST = [sb(f"ST{h}", [P, P], BF16) for h in range(2)]
        nd = sb("nd", [P, 2, 65])
        hid = sb("hid", [P, 2, 64])
        sm = sb("sm", [P, 8])
        stats = sb("stats", [P, 2, 6])
        mv = sb("mv", [P, 2, 2])
        yTt = sb("yTt", [P, 512], BF16)
        Cst_b, Csb_b, Kp_b, ST_b, nd_b, hid_b, sm_b, yTt_b = Buf(), Buf(), Buf(), [Buf(), Buf()], Buf(), Buf(), Buf(), Buf()
        S.op("pool", lambda e: e.memset(Cst[:], 0.0), writes=[Cst_b])
        S.op("pool", lambda e: e.memset(Csb[:], 0.0), writes=[Csb_b])
        eb3 = eb_t[:].rearrange("p (h c) -> p h c", h=2)
        for c in range(nb):
            tq = c // 4
            cs = slice(c * 128, (c + 1) * 128)
            S.op("pe", lambda e, cs=cs: e.transpose(out=C.ps_bf[:, 0:128], in_=KT[:, cs], identity=identb[:]),
                 reads=[KT_b[tq], cb_], writes=[C.psb[7]])
            for h in range(2):
                hs = slice(h * 64, (h + 1) * 64)
                ix = h * nb + c
                S.op("dve", lambda e, hs=hs, ix=ix: e.tensor_scalar(out=Kp[:, hs], in0=C.ps_bf[:, hs], scalar1=u_t[:, ix:ix + 1], scalar2=0.125,
                                                                    op0=ALU.mult, op1=ALU.mult), reads=[C.psb[7], g_b], writes=[Kp_b])
                S.op("pe", lambda e, hs=hs, h=h, cs=cs: e.matmul(out=C.ps[h][:, 0:128], lhsT=KT[hs, cs], rhs=QT[hs, cs], start=True, stop=True),
                     reads=[KT_b[tq], QT_b[tq]], writes=[C.psb[h]])
                S.op("dve", lambda e, h=h, ix=ix: e.scalar_tensor_tensor(out=ST[h][:], in0=C.ps[h][:, 0:128], scalar=u_t[:, ix:ix + 1], in1=mask8[:],
                                                                         op0=ALU.mult, op1=ALU.mult), reads=[C.psb[h], g_b, cb_], writes=[ST_b[h]])
                S.op("dve", lambda e, hs=hs, h=h, ix=ix: e.tensor_scalar(out=Csb[hs, h * 65:(h + 1) * 65], in0=Cst[hs, :], scalar1=sc_bc[hs, ix:ix + 1],
                                                                         scalar2=None, op0=ALU.mult), reads=[Cst_b, g_b], writes=[Csb_b])
            S.op("pe", lambda e, cs=cs: e.matmul(out=C.ps[2][:, 0:130], lhsT=QT[:, cs], rhs=Csb[:], start=True, stop=False, skip_group_check=True),
                 reads=[QT_b[tq], Csb_b], writes=[C.psb[2]])
            for h in range(2):
                S.op("pe", lambda e, h=h, c=c: e.matmul(out=C.ps[2][:, h * 65:(h + 1) * 65], lhsT=ST[h][:], rhs=Vext[:, c, h, 0:65], start=False, stop=(h == 1),
                                                        skip_group_check=True), reads=[ST_b[h], V_b], writes=[C.psb[2]])
            S.op("pe", lambda e, c=c: e.matmul(out=C.ps[3][:, 0:130], lhsT=Kp[:], rhs=Vext[:, c, :, 0:65], start=True, stop=True),
                 reads=[Kp_b, V_b], writes=[C.psb[3]])
            for h in range(2):
                hs = slice(h * 64, (h + 1) * 64)
                ix = h * nb + c
                S.op("dve", lambda e, hs=hs, h=h, ix=ix: e.scalar_tensor_tensor(out=Cst[hs, :], in0=Cst[hs, :], scalar=sc_bc[hs, ix:ix + 1],
                                                                                in1=C.ps[3][hs, h * 65:(h + 1) * 65], op0=ALU.mult, op1=ALU.add),
                     reads=[Cst_b, g_b, C.psb[3], Csb_b], writes=[Cst_b])
                S.op("dve", lambda e, h=h, ix=ix: e.tensor_scalar(out=nd[:, h, :], in0=C.ps[2][:, h * 65:(h + 1) * 65], scalar1=r_t[:, ix:ix + 1],
                                                                  scalar2=None, op0=ALU.mult), reads=[C.psb[2], g_b], writes=[nd_b])
            S.op("dve", lambda e: e.tensor_scalar(out=sm[:, 0:2], in0=nd[:, :, 64], scalar1=-1.0, scalar2=None, op0=ALU.mult), reads=[nd_b], writes=[sm_b])
            S.op("dve", lambda e: e.tensor_tensor(out=sm[:, 0:2], in0=sm[:, 0:2], in1=nd[:, :, 64], op=ALU.max), reads=[nd_b, sm_b], writes=[sm_b])
            S.op("dve", lambda e, c=c: e.tensor_tensor(out=sm[:, 0:2], in0=sm[:, 0:2], in1=eb3[:, :, c], op=ALU.max), reads=[sm_b, g_b], writes=[sm_b])
            S.op("dve", lambda e: e.reciprocal(out=sm[:, 0:2], in_=sm[:, 0:2]), reads=[sm_b], writes=[sm_b])
            for h in range(2):
                S.op("dve", lambda e, h=h: e.tensor_scalar(out=hid[:, h, :], in0=nd[:, h, 0:64], scalar1=sm[:, h:h + 1], scalar2=None, op0=ALU.mult),
                     reads=[nd_b, sm_b], writes=[hid_b])
                S.op("dve", lambda e, h=h: e.bn_stats(out=stats[:, h, :], in_=hid[:, h, :]), reads=[hid_b], writes=[sm_b])
                S.op("dve", lambda e, h=h: e.bn_aggr(out=mv[:, h, :], in_=stats[:, h, :]), reads=[sm_b], writes=[sm_b])
            S.op("act", lambda e: e.activation(out=sm[:, 2:4], in_=mv[:, :, 1], func=AF.Sqrt, bias=C.eps_t[:], scale=1.0), reads=[sm_b, C.ident_b], writes=[sm_b])
            S.op("dve", lambda e: e.reciprocal(out=sm[:, 2:4], in_=sm[:, 2:4]), reads=[sm_b], writes=[sm_b])
            for h in range(2):
                S.op("dve", lambda e, h=h: e.tensor_scalar(out=hid[:, h, :], in0=hid[:, h, :], scalar1=mv[:, h, 0:1], scalar2=sm[:, 2 + h:3 + h],
                                                           op0=ALU.subtract, op1=ALU.mult), reads=[hid_b, sm_b], writes=[hid_b])
            S.op("dve", lambda e, c=c: e.tensor_tensor(out=hid[:].rearrange("p h d -> p (h d)"), in0=hid[:].rearrange("p h d -> p (h d)"),
                                                        in1=gso[:, c, :], op=ALU.mult), reads=[hid_b, gso_b], writes=[hid_b])
            S.op("pe", lambda e, c=c: e.transpose(out=C.ps[4][:, (c % 4) * 128:(c % 4 + 1) * 128], in_=hid[:].rearrange("p h d -> p (h d)"),
                                                  identity=C.ident[:]), reads=[hid_b, C.ident_b], writes=[C.psb[4]])
            if c % 4 == 3:
                S.op("act", lambda e: e.copy(out=yTt[:], in_=C.ps[4][:]), reads=[C.psb[4]], writes=[yTt_b])
                S.dma("sp", yT[:, (c // 4) * 512:(c // 4 + 1) * 512], yTt[:], reads=[yTt_b], writes=[yT_b])
        S.barrier()


def make_ml_consts_np(c):
    k = np.arange(P)[:, None]
    m = np.arange(P)[None, :]
    c["ntriT"] = -(k <= m).astype(np.float32)
    c["mask8"] = (k <= m).astype(np.float32) * 0.125
    e0 = np.zeros((P, P), np.float32)
    e0[0, :] = 1.0
    c["e0ones"] = e0
    e127 = np.zeros((P, P), np.float32)
    e127[127, :] = 1.0
    c["e127ones"] = e127
    return c


def outproj_ln(C, es, tag, lhs_chunks, lhs_b, wo, wo_b, xt, x_b):
    S = C.S
    for blk in range(4):
        for hh in range(2):
            po = 4 + (blk * 2 + hh) % 2
            for k in range(8):
                S.op("pe", lambda e, k=k, blk=blk, hh=hh, po=po: e.matmul(
                    out=C.ps[po][:], lhsT=lhs_chunks[:, k, blk * 128:(blk + 1) * 128], rhs=wo[:, k, hh * 512:(hh + 1) * 512],
                    start=(k == 0), stop=(k == 7)), reads=[lhs_b, wo_b], writes=[C.psb[po]])
            S.op("dve", lambda e, blk=blk, hh=hh, po=po: e.tensor_tensor(
                out=xt[:, blk, hh * 512:(hh + 1) * 512], in0=C.ps[po][:], in1=xt[:, blk, hh * 512:(hh + 1) * 512], op=ALU.add),
                reads=[C.psb[po], x_b], writes=[x_b])


def mixout_phase(C, x_in, xin_b, ygath, yg_b, w_out, sel_dram, g_dram, b_dram, x_out, xout_b, NT, TT, tag):
    nc, S = C.nc, C.S
    ntile = NT // 512
    with contextlib.ExitStack() as es:
        wo, wo_b = load_w_bf16(C, es, f"wmo{tag}", w_out, D)
        sb = lambda name, shape, dt=F32: es.enter_context(nc.sbuf_tensor(f"mo{name}{tag}", shape, dt))
        g_bc, b_bc, sel = sb("g", [P, D]), sb("b", [P, D]), sb("sel", [P, 2])
        xts = [sb(f"xt{i}", [P, 4, D]) for i in range(2)]
        yAs = [sb(f"yA{i}", [P, 8, 512], BF16) for i in range(2)]
        yBs = [sb(f"yB{i}", [P, 8, 512], BF16) for i in range(2)]
        stats, mv, rstd = sb("stats", [P, 4, 2, 6]), sb("mv", [P, 4, 2]), sb("rstd", [P, 4, 1])
        gb_b, st_b = Buf(), Buf()
        x_bs, yA_bs, yB_bs = [Buf(), Buf()], [Buf(), Buf()], [Buf(), Buf()]
        S.dma("sp", g_bc[:], g_dram, writes=[gb_b])
        S.dma("sp", b_bc[:], b_dram, writes=[gb_b])
        S.dma("sp", sel[:], sel_dram, writes=[gb_b])
        xin_v = x_in.rearrange("(t b p) d -> t p b d", b=4, p=P)
        xout_v = x_out.rearrange("(t b p) d -> t p b d", b=4, p=P)
        yv = ygath.rearrange("(k p) t -> p k t", p=P)

        def loads(t):
            k = t % 2
            S.dma("sp", xts[k][:], xin_v[t], reads=[xin_b[t] if isinstance(xin_b, list) else xin_b], writes=[x_bs[k]])
            S.dma("sp", yAs[k][:], yv[:, :, t * 512:(t + 1) * 512], reads=[yg_b], writes=[yA_bs[k]])
            S.dma("sp", yBs[k][:], yv[:, :, NT + t * 512:NT + (t + 1) * 512], reads=[yg_b], writes=[yB_bs[k]])
        loads(0)
        for t in range(ntile):
            k = t % 2
            xt, x_b, yA, yA_b, yB, yB_b = xts[k], x_bs[k], yAs[k], yA_bs[k], yBs[k], yB_bs[k]
            if t + 1 < ntile:
                loads(t + 1)
            S.op("act", lambda e, xt=xt: e.mul(out=xt[:], in_=xt[:], mul=ALPHA), reads=[x_b], writes=[x_b])
            S.op("dve", lambda e, yA=yA: e.tensor_scalar(out=yA[:], in0=yA[:], scalar1=sel[:, 0:1], scalar2=None, op0=ALU.mult),
                 reads=[yA_b, gb_b], writes=[yA_b])
            S.op("dve", lambda e, yA=yA, yB=yB: e.scalar_tensor_tensor(out=yA[:], in0=yB[:], scalar=sel[:, 1:2], in1=yA[:], op0=ALU.mult, op1=ALU.add),
                 reads=[yA_b, yB_b, gb_b], writes=[yA_b])
            outproj_ln(C, es, tag, yA, yA_b, wo, wo_b, xt, x_b)
            layer_norm_tile(C, xt, x_b, g_bc, b_bc, gb_b, stats, mv, rstd, st_b)
            S.dma("pool", xout_v[t], xt[:], reads=[x_b], writes=[xout_b])
        S.barrier()


def xattn_phase(C, x_in, xin_b, mem, w_q, w_kv, w_o, g_dram, b_dram, x_out, xout_b, NT, tag):
    nc, S = C.nc, C.S
    ntile = NT // 512
    C.set_psum(2)
    with contextlib.ExitStack() as es:
        wq, wq_b = load_w_bf16(C, es, f"wxq{tag}", w_q, D)
        wkv, wkv_b = load_w_bf16(C, es, f"wxkv{tag}", w_kv, 2 * D)
        wo, wo_b = load_w_bf16(C, es, f"wxo{tag}", w_o, D)
        sb = lambda name, shape, dt=F32: es.enter_context(nc.sbuf_tensor(f"xa{name}{tag}", shape, dt))
        g_bc, b_bc = sb("g", [P, D]), sb("b", [P, D])
        identb = sb("idb", [P, P], BF16)
        xt = sb("xt", [P, 4, D])
        xT = sb("xT", [P, 8, 512], BF16)
        QT = sb("QT", [P, 8, 512], BF16)
        KTx = sb("KT", [P, 8, 256], BF16)
        Vx = sb("V", [P, 2, D], BF16)
        aoT = sb("aoT", [P, 8, 512], BF16)
        sc = sb("sc", [P, 256])
        pf = sb("pf", [P, 256])
        pn = sb("pn", [P, 256], BF16)
        pT = sb("pT", [P, 2, 128], BF16)
        sm = sb("sm", [P, 4])
        stats, mv, rstd = sb("stats", [P, 4, 2, 6]), sb("mv", [P, 4, 2]), sb("rstd", [P, 4, 1])
        gb_b, x_b, xT_b, QT_b, KT_b, V_b, ao_b, st_b = Buf(), Buf(), Buf(), Buf(), Buf(), Buf(), Buf(), Buf()
        sc_b, pf_b, pn_b, pT_b, sm_b = Buf(), Buf(), Buf(), Buf(), Buf()
        pf2 = [pf, sb("pfb", [P, 256])]
        pn2 = [pn, sb("pnb", [P, 256], BF16)]
        pT2 = [pT, sb("pTb", [P, 2, 128], BF16)]
        sm2 = [sm, sb("smb", [P, 4])]
        pf_b2, pn_b2, pT_b2, sm_b2 = [Buf(), Buf()], [Buf(), Buf()], [Buf(), Buf()], [Buf(), Buf()]
        pf4 = [sb(f"pf4{k}", [P, 4, 256]) for k in range(2)]
        pn4 = [sb(f"pn4{k}", [P, 4, 256], BF16) for k in range(2)]
        pT4 = [sb(f"pT4{k}", [P, 8, 128], BF16) for k in range(2)]
        sm4 = [sb(f"sm4{k}", [P, 4, 4]) for k in range(2)]
        sc_pb, bf_pb, pv_pb = [Buf(), Buf()], [Buf(), Buf()], [Buf(), Buf()]
        S.dma("sp", g_bc[:], g_dram, writes=[gb_b])
        S.dma("sp", b_bc[:], b_dram, writes=[gb_b])
        S.dma("pool", identb[:], C.ident_dram, writes=[gb_b])
        S.dma("sp", xt[:, 0:2, :], mem.rearrange("(b p) d -> p b d", p=P), writes=[x_b])
        for kc in range(8):
            for blk in range(2):
                S.op("pe", lambda e, kc=kc, blk=blk: e.transpose(out=C.ps[4][:, blk * 128:(blk + 1) * 128], in_=xt[:, blk, kc * 128:(kc + 1) * 128],
                                                                 identity=C.ident[:]), reads=[x_b, C.ident_b], writes=[C.psb[4]])
            S.op("dve", lambda e, kc=kc: e.tensor_copy(out=xT[:, kc, 0:256], in_=C.ps[4][:, 0:256]), reads=[C.psb[4]], writes=[xT_b])
        for j in range(8):
            pi = j % 2
            for kc in range(8):
                S.op("pe", lambda e, kc=kc, j=j, pi=pi: e.matmul(out=C.ps[pi][:, 0:256], lhsT=wkv[:, kc, j * 128:(j + 1) * 128], rhs=xT[:, kc, 0:256],
                                                                 start=(kc == 0), stop=(kc == 7)), reads=[wkv_b, xT_b], writes=[C.psb[pi]])
            S.op("dve", lambda e, j=j, pi=pi: e.tensor_copy(out=KTx[:, j, :], in_=C.ps[pi][:, 0:256]), reads=[C.psb[pi]], writes=[KT_b])
        for blk in range(2):
            for hh in range(2):
                pi = 2 + hh
                for kc in range(8):
                    S.op("pe", lambda e, kc=kc, blk=blk, hh=hh, pi=pi: e.matmul(
                        out=C.ps[pi][:], lhsT=xT[:, kc, blk * 128:(blk + 1) * 128], rhs=wkv[:, kc, D + hh * 512:D + (hh + 1) * 512],
                        start=(kc == 0), stop=(kc == 7)), reads=[wkv_b, xT_b], writes=[C.psb[pi]])
                S.op("act", lambda e, blk=blk, hh=hh, pi=pi: e.copy(out=Vx[:, blk, hh * 512:(hh + 1) * 512], in_=C.ps[pi][:]),
                     reads=[C.psb[pi]], writes=[V_b])
        xin_v = x_in.rearrange("(t b p) d -> t p b d", b=4, p=P)
        xout_v = x_out.rearrange("(t b p) d -> t p b d", b=4, p=P)
        xts = [xt, sb("xt2", [P, 4, D])]
        x_bs = [x_b, Buf()]
        QTs = [QT, sb("QT2", [P, 8, 512], BF16)]
        QT_bs = [QT_b, Buf()]

        def emit_load(t):
            S.dma("sp", xts[t % 2][:], xin_v[t], reads=[xin_b[t] if isinstance(xin_b, list) else xin_b], writes=[x_bs[t % 2]])

        def emit_T(t):
            load_xT(C, xts[t % 2], x_bs[t % 2], xT, xT_b, ps_ids=(0,))
            S.op("act", lambda e, t=t: e.mul(out=xts[t % 2][:], in_=xts[t % 2][:], mul=ALPHA), reads=[x_bs[t % 2]], writes=[x_bs[t % 2]])

        def emit_qproj(t, j):
            proj_fm(C, wq, wq_b, j * 128, (j + 1) * 128, xT, xT_b, 1)
            S.op("act" if j % 2 == 0 else "dve",
                 (lambda e, j=j, t=t: e.mul(out=QTs[t % 2][:, j, :], in_=C.ps[1][:], mul=0.0625)) if j % 2 == 0 else
                 (lambda e, j=j, t=t: e.tensor_scalar(out=QTs[t % 2][:, j, :], in0=C.ps[1][:], scalar1=0.0625, scalar2=None, op0=ALU.mult)),
                 reads=[C.psb[1]], writes=[QT_bs[t % 2]])

        emit_load(0)
        emit_T(0)
        for j in range(8):
            emit_qproj(0, j)
        for t in range(ntile):
            xt, x_b, QT, QT_b = xts[t % 2], x_bs[t % 2], QTs[t % 2], QT_bs[t % 2]
            if t + 1 < ntile:
                emit_load(t + 1)
            for hd in range(4):
                a = hd % 2
                if t + 1 < ntile:
                    if hd == 2:
                        emit_T(t + 1)
                    elif hd == 3:
                        for j in range(8):
                            emit_qproj(t + 1, j)
                for b in range(4):
                    ts = slice(b * 128, (b + 1) * 128)
                    for kk in range(2):
                        S.op("pe", lambda e, hd=hd, kk=kk, ts=ts, b=b: e.matmul(out=C.ps[2 + b // 2][:, (b % 2) * 256:(b % 2 + 1) * 256],
                                                                               lhsT=QT[:, 2 * hd + kk, ts], rhs=KTx[:, 2 * hd + kk, :],
                                                                               start=(kk == 0), stop=(kk == 1), skip_group_check=True),
                             reads=[QT_b, KT_b], writes=[C.psb[2 + b // 2]])
                for bank in range(2):
                    S.op("dve", lambda e, bank=bank, a=a: e.tensor_reduce(out=sm4[a][:, 0, 2 * bank:2 * bank + 2],
                                                                         in_=C.ps[2 + bank][:].rearrange("p (b k) -> p b k", b=2), axis=AX.X, op=ALU.max),
                         reads=[C.psb[2 + bank]], writes=[sm_b2[a]])
                S.op("dve", lambda e, a=a: e.tensor_scalar(out=sm4[a][:, 1, :], in0=sm4[a][:, 0, :], scalar1=-1.0, scalar2=None, op0=ALU.mult),
                     reads=[sm_b2[a]], writes=[sm_b2[a]])
                for b in range(4):
                    S.op("act", lambda e, b=b, a=a: e.activation(out=pf4[a][:, b, :], in_=C.ps[2 + b // 2][:, (b % 2) * 256:(b % 2 + 1) * 256], func=AF.Exp,
                                                                 bias=sm4[a][:, 1, b:b + 1], scale=1.0, accum_out=sm4[a][:, 2, b:b + 1]),
                         reads=[C.psb[2 + b // 2], sm_b2[a]], writes=[pf_b2[a], sm_b2[a]])
                S.op("dve", lambda e, a=a: e.reciprocal(out=sm4[a][:, 3, :], in_=sm4[a][:, 2, :]), reads=[sm_b2[a]], writes=[sm_b2[a]])
                for b in range(4):
                    if b % 2 == 0:
                        S.op("dve", lambda e, b=b, a=a: e.tensor_scalar(out=pn4[a][:, b, :], in0=pf4[a][:, b, :], scalar1=sm4[a][:, 3, b:b + 1],
                                                                        scalar2=None, op0=ALU.mult), reads=[pf_b2[a], sm_b2[a]], writes=[pn_b2[a]])
                    else:
                        S.op("act", lambda e, b=b, a=a: e.mul(out=pn4[a][:, b, :], in_=pf4[a][:, b, :], mul=sm4[a][:, 3, b:b + 1]),
                             reads=[pf_b2[a], sm_b2[a]], writes=[pn_b2[a]])
                for b in range(4):
                    for mb in range(2):
                        S.op("pe", lambda e, b=b, mb=mb, a=a: e.transpose(out=C.ps_bfs[a][:, (b * 2 + mb) * 128:(b * 2 + mb + 1) * 128],
                                                                          in_=pn4[a][:, b, mb * 128:(mb + 1) * 128], identity=identb[:]),
                             reads=[pn_b2[a], gb_b], writes=[C.psbf_b[a]])
                S.op("act", lambda e, a=a: e.copy(out=pT4[a][:, 0:4, :].rearrange("p a b -> p (a b)"), in_=C.ps_bfs[a][:, 0:512]),
                     reads=[C.psbf_b[a]], writes=[pT_b2[a]])
                S.op("dve", lambda e, a=a: e.tensor_copy(out=pT4[a][:, 4:8, :].rearrange("p a b -> p (a b)"), in_=C.ps_bfs[a][:, 512:1024]),
                     reads=[C.psbf_b[a]], writes=[pT_b2[a]])
                for cc in range(2):
                    for b in range(4):
                        for mb in range(2):
                            S.op("pe", lambda e, cc=cc, mb=mb, hd=hd, b=b, a=a: e.matmul(out=C.ps[4 + cc][:, b * 128:(b + 1) * 128],
                                                                                         lhsT=Vx[:, mb, hd * 256 + cc * 128:hd * 256 + (cc + 1) * 128],
                                                                                         rhs=pT4[a][:, b * 2 + mb, :], start=(mb == 0), stop=(mb == 1),
                                                                                         skip_group_check=True),
                                 reads=[V_b, pT_b2[a]], writes=[C.psb[4 + cc]])
                S.op("act", lambda e, hd=hd: e.copy(out=aoT[:, 2 * hd, :], in_=C.ps[4][:]), reads=[C.psb[4]], writes=[ao_b])
                S.op("dve", lambda e, hd=hd: e.tensor_copy(out=aoT[:, 2 * hd + 1, :], in_=C.ps[5][:]), reads=[C.psb[5]], writes=[ao_b])
            outproj_ln(C, es, tag, aoT, ao_b, wo, wo_b, xt, x_b)
            layer_norm_tile(C, xt, x_b, g_bc, b_bc, gb_b, stats, mv, rstd, st_b)
            S.dma("sp", xout_v[t], xt[:], reads=[x_b], writes=[xout_b])
        S.barrier()
    C.set_psum(1)


NT_OWN = 4096
TT_SEQ = 8192
DEPTH = 2
PAIRS = [[0, 1], [2, 3], [4, 5], [6, 7]]
_SPLITS = np.cumsum([0, 256, 256, 256, 512, 128, 128, 512, 256, 4, 4, 256])


def build_program():
    from concourse.bass_utils import run_bass_kernel_spmd
    nc = bass.Bass("TRN2", target_bir_lowering=False)
    din = lambda name, shape, dt=F32: nc.dram_tensor(name, shape, dt, kind="ExternalInput").ap()
    dscr = lambda name, shape, dt=F32: nc.dram_tensor(name, shape, dt, kind="Internal").ap()
    x = din("x", [NT_OWN, D])
    mem = din("mem", [256, D])
    sel = din("sel", [P, 2])
    cn = make_consts_np()
    cd = {k: din("c_" + k, list(v.shape)) for k, v in cn.items()}
    L = []
    for l in range(DEPTH):
        d = {}
        d["ffn1_in"] = din(f"l{l}_ffn1_in", [D, 2 * DFF]); d["ffn1_out"] = din(f"l{l}_ffn1_out", [DFF, D])
        d["ffn2_in"] = din(f"l{l}_ffn2_in", [D, 2 * DFF]); d["ffn2_out"] = din(f"l{l}_ffn2_out", [DFF, D])
        d["w_sb"] = din(f"l{l}_w_sb", [D, 384]); d["w_swa"] = din(f"l{l}_w_swa", [D, 448]); d["w_ml"] = din(f"l{l}_w_ml", [D, 640])
        d["cw"] = din(f"l{l}_cw", [P, 8]); d["cb"] = din(f"l{l}_cb", [P, 2]); d["ifb"] = din(f"l{l}_ifb", [P, 4]); d["ng"] = din(f"l{l}_ng", [P, P])
        d["rb"] = din(f"l{l}_rb", [P, P]); d["sink"] = din(f"l{l}_sink", [P, 4])
        d["w_mo"] = din(f"l{l}_w_mo", [D, D])
        d["xq"] = din(f"l{l}_xq", [D, D]); d["xkv"] = din(f"l{l}_xkv", [D, 2 * D]); d["xo"] = din(f"l{l}_xo", [D, D])
        d["lng"] = [din(f"l{l}_lng{i}", [P, D]) for i in range(4)]
        d["lnb"] = [din(f"l{l}_lnb{i}", [P, D]) for i in range(4)]
        L.append(d)
    out = nc.dram_tensor("out", [NT_OWN, D], F32, kind="ExternalOutput").ap()
    Xa = dscr("Xa", [NT_OWN, D]); Xb = dscr("Xb", [NT_OWN, D])
    X1g = dscr("X1g", [NT_OWN // 512, 2 * D, 512], BF16)
    XaT = dscr("XaT", [NT_OWN // 512, D, 512], BF16)
    YTo = dscr("YTo", [512, TT_SEQ], BF16)
    Yg = dscr("Yg", [1024, TT_SEQ], BF16)
    frs = dscr("frs", [4, 384])
    S = Sched(nc)
    C = Ctx(nc, S)
    C.load_consts(cd["ident"])
    x_b, Xa_b, Xb_b, X1g_b, YTo_b, Yg_b, out_b = Buf(), Buf(), Buf(), Buf(), [Buf(), Buf(), Buf()], Buf(), Buf()
    cur, cur_b = x, x_b
    import os
    kstop = int(os.environ.get("KSTOP", "99"))
    for l in range(DEPTH):
        d = L[l]
        tg = f"L{l}"
        if kstop < 99 and l > 0:
            break
        Xa_tb = [Buf() for _ in range(NT_OWN // 512)]
        XaT_tb = [Buf() for _ in range(NT_OWN // 512)]
        X1g_tb = [Buf() for _ in range(TT_SEQ // 512)]
        nch = NT_OWN // 512

        def ag1(t):
            S.custom("pool", lambda e, t=t: e.collective_compute("AllGather", ALU.bypass, replica_groups=PAIRS,
                                                                 ins=[XaT[t]], outs=[X1g[t]]), 1,
                     reads=[XaT_tb[t]], writes=[X1g_tb[t], X1g_tb[nch + t]])
        ffn_phase(C, cur, Xa, d["ffn1_in"], d["ffn1_out"], d["lng"][0], d["lnb"][0], NT_OWN, tg + "f1", cur_b, Xa_tb, after_tile=ag1,
                  xT_out=XaT, xT_out_b=XaT_tb)
        if kstop <= 2:
            break
        X1v = X1g.rearrange("j (r kc p) n -> j r p kc n", r=2, p=P)
        xtile = lambda t: X1v[t % nch, t // nch]
        sb_phase(C, xtile, X1g_tb, d["w_sb"], YTo[0:128, :], YTo_b[0], TT_SEQ, cd, tg + "sb")
        S.custom("pool", lambda e: e.collective_compute("AllGather", ALU.bypass, replica_groups=PAIRS, ins=[YTo[0:128, :]], outs=[Yg[0:256, :]]), 1,
                 reads=[YTo_b[0]], writes=[Yg_b])
        if kstop <= 3:
            break
        swa_phase(C, xtile, X1g_tb, d["w_swa"], d["rb"], d["sink"], frs, YTo[128:384, :], YTo_b[1], TT_SEQ, cd, tg + "sw")
        for k in (1, 2):
            S.custom("pool", lambda e, k=k: e.collective_compute("AllGather", ALU.bypass, replica_groups=PAIRS, ins=[YTo[k * 128:(k + 1) * 128, :]],
                                                                 outs=[Yg[k * 256:(k + 1) * 256, :]]), 1, reads=[YTo_b[1]], writes=[Yg_b])
        if kstop <= 4:
            break
        mlstm_phase(C, xtile, X1g_tb, d["w_ml"], d["cw"], d["cb"], d["ifb"], d["ng"], YTo[384:512, :], YTo_b[2], TT_SEQ, cd, tg + "ml")
        S.custom("pool", lambda e: e.collective_compute("AllGather", ALU.bypass, replica_groups=PAIRS, ins=[YTo[384:512, :]], outs=[Yg[768:1024, :]]), 1,
                 reads=[YTo_b[2]], writes=[Yg_b])
        S.barrier()
        if kstop <= 6:
            break
        mixout_phase(C, Xa, Xa_tb, Yg, Yg_b, d["w_mo"], sel, d["lng"][1], d["lnb"][1], Xb, Xb_b, NT_OWN, TT_SEQ, tg + "mo")
        if kstop <= 7:
            break
        xattn_phase(C, Xb, Xb_b, mem, d["xq"], d["xkv"], d["xo"], d["lng"][2], d["lnb"][2], Xa, Xa_b, NT_OWN, tg + "xa")
        if kstop <= 8:
            break
        dst, dst_b = (out, out_b) if l == DEPTH - 1 else (Xb, Xb_b)
        ffn_phase(C, Xa, dst, d["ffn2_in"], d["ffn2_out"], d["lng"][3], d["lnb"][3], NT_OWN, tg + "f2", Xa_b, dst_b)
        cur, cur_b = dst, dst_b
    S.finish()
    C.psum_es.close()
    return nc, cn


def make_core_inputs(c, inp, cn):
    b, r = c // 2, c % 2
    f = lambda a: np.ascontiguousarray(np.asarray(a, dtype=np.float32))
    rep = lambda v: f(np.tile(np.asarray(v, np.float32)[None, :], (P, 1)))
    m = {"x": f(inp["x"][b, r * NT_OWN:(r + 1) * NT_OWN]), "mem": f(inp["mem"][b])}
    selv = np.zeros((P, 2), np.float32)
    selv[:, r] = 1.0
    m["sel"] = selv
    for k, v in cn.items():
        m["c_" + k] = f(v)
    sp = _SPLITS
    slot = [0, 2, 1, 3]
    for l in range(DEPTH):
        w = np.asarray(inp["mix_w_in"][l], np.float32)
        seg = [w[:, sp[i]:sp[i + 1]] for i in range(11)]
        sbq, sbk, sbv, swq, swk, swv, mlqk, mlv, ig, fg, og = seg
        mlq, mlk = mlqk[:, 0:256], mlqk[:, 256:512]
        m[f"l{l}_ffn1_in"] = f(inp["ffn1_w_in"][l]); m[f"l{l}_ffn1_out"] = f(inp["ffn1_w_out"][l])
        m[f"l{l}_ffn2_in"] = f(inp["ffn2_w_in"][l]); m[f"l{l}_ffn2_out"] = f(inp["ffn2_w_out"][l])
        h2 = slice(r * 128, (r + 1) * 128)
        m[f"l{l}_w_sb"] = f(np.concatenate([sbq[:, h2], sbk[:, h2], sbv[:, h2]], 1))
        kk = swk[:, r * 64:(r + 1) * 64]
        m[f"l{l}_w_swa"] = f(np.concatenate([swq[:, r * 256:(r + 1) * 256], kk, kk, swv[:, r * 64:(r + 1) * 64]], 1))
        m[f"l{l}_w_ml"] = f(np.concatenate([mlq[:, h2], mlk[:, h2], mlv[:, h2], og[:, h2], ig[:, 2 * r:2 * r + 2], fg[:, 2 * r:2 * r + 2],
                                            np.zeros((D, 124), np.float32)], 1))
        cwl = np.asarray(inp["ml_conv_w"][l], np.float32)
        cbl = np.asarray(inp["ml_conv_b"][l], np.float32)
        m[f"l{l}_cw"] = f(np.concatenate([cwl[:, r * 128:(r + 1) * 128].T, cwl[:, 256 + r * 128:256 + (r + 1) * 128].T], 1))
        m[f"l{l}_cb"] = f(np.stack([cbl[r * 128:(r + 1) * 128], cbl[256 + r * 128:256 + (r + 1) * 128]], 1))
        ib = np.asarray(inp["ml_i_bias"][l], np.float32)[2 * r:2 * r + 2]
        fb = np.asarray(inp["ml_f_bias"][l], np.float32)[2 * r:2 * r + 2]
        m[f"l{l}_ifb"] = rep(np.concatenate([ib, fb]))
        m[f"l{l}_ng"] = rep(np.asarray(inp["ml_norm_g"][l], np.float32)[h2])
        rbl = np.asarray(inp["rel_bias"], np.float32)[:, 4 * r:4 * r + 4][:, slot]
        m[f"l{l}_rb"] = f(np.pad(rbl, ((0, 96), (0, 124))))
        m[f"l{l}_sink"] = rep(np.asarray(inp["swa_sinks"][l], np.float32)[4 * r:4 * r + 4][slot])
        wo = np.asarray(inp["mix_w_out"][l], np.float32)
        rows = []
        for k in range(4):
            for rr in range(2):
                base = [rr * 128, 256 + rr * 256, 256 + rr * 256 + 128, 768 + rr * 128][k]
                rows.append(wo[base:base + 128])
        m[f"l{l}_w_mo"] = f(np.concatenate(rows, 0))
        m[f"l{l}_xq"] = f(inp["xattn_w_q"][l]); m[f"l{l}_xkv"] = f(inp["xattn_w_kv"][l]); m[f"l{l}_xo"] = f(inp["xattn_w_o"][l])
        for i in range(4):
            m[f"l{l}_lng{i}"] = rep(inp["ln_g"][l][i]); m[f"l{l}_lnb{i}"] = rep(inp["ln_b"][l][i])
    return m


def kernel(**inputs):
    from concourse.bass_utils import run_bass_kernel_spmd
    inp = {k: np.asarray(v) for k, v in inputs.items()}
    nc, cn = build_program()
    in_maps = [make_core_inputs(c, inp, cn) for c in range(8)]
    res = run_bass_kernel_spmd(nc, in_maps, core_ids=list(range(8)))
    out = np.zeros((4, TT_SEQ, D), np.float32)
    for c in range(8):
        b, r = c // 2, c % 2
        out[b, r * NT_OWN:(r + 1) * NT_OWN] = np.asarray(res.results[c]["out"], np.float32)
    return out
```

```python
import contextlib
import numpy as np
import concourse.bass as bass
import concourse.mybir as mybir

F32 = mybir.dt.float32
BF16 = mybir.dt.bfloat16
ALU = mybir.AluOpType
AF = mybir.ActivationFunctionType
AX = mybir.AxisListType


class Buf:
    __slots__ = ("name", "w", "r")

    def __init__(self, name=""):
        self.name = name
        self.w = None
        self.r = []


class Sched:
    EPOCH = 20000

    def __init__(self, nc, n_dma_sems=24):
        self.nc = nc
        self.es = contextlib.ExitStack()
        self.eng = {"pe": nc.tensor, "act": nc.scalar, "dve": nc.vector,
                    "pool": nc.gpsimd, "sp": nc.sync}
        self.sem = {}
        self.cnt = {}
        self.nsem = 0
        for e in self.eng:
            self._new_epoch(e)
        self.dma_sems = [self.es.enter_context(nc.semaphore(f"dq{i}")) for i in range(n_dma_sems)]
        self.dma_cnt = [0] * n_dma_sems
        self.dma_next = 0
        self.waited = {e: {} for e in self.eng}
        self.last_ev = {e: None for e in self.eng}
        self.out_events = []
        self.n_ops = 0
        self.n_waits = 0

    def _new_epoch(self, e):
        self.sem[e] = self.es.enter_context(self.nc.semaphore(f"s_{e}_{self.nsem}"))
        self.nsem += 1
        self.cnt[e] = 0

    def _wait(self, e, ev):
        if ev is None:
            return
        sem, val, src = ev
        if src == "pe" and e == "pe":
            return
        key = id(sem)
        if self.waited[e].get(key, 0) >= val:
            return
        self.eng[e].wait_ge(sem, val)
        self.n_waits += 1
        self.waited[e][key] = val

    def _deps(self, e, reads, writes):
        for b in reads:
            self._wait(e, b.w)
        for b in writes:
            self._wait(e, b.w)
            for ev in b.r:
                self._wait(e, ev)

    def _commit(self, ev, reads, writes):
        for b in reads:
            b.r.append(ev)
            if len(b.r) > 12:
                latest = {}
                for x in b.r:
                    k = id(x[0])
                    if k not in latest or latest[k][1] < x[1]:
                        latest[k] = x
                b.r = list(latest.values())
        for b in writes:
            b.w = ev
            b.r = []

    def op(self, e, fn, reads=(), writes=()):
        self._deps(e, reads, writes)
        if self.cnt[e] >= self.EPOCH:
            self._new_epoch(e)
        ins = fn(self.eng[e])
        self.cnt[e] += 1
        ins.then_inc(self.sem[e], 1)
        ev = (self.sem[e], self.cnt[e], e)
        self.last_ev[e] = ev
        self._commit(ev, reads, writes)
        self.n_ops += 1
        return ev

    def dma(self, q, out, in_, reads=(), writes=(), **kw):
        self._deps(q, reads, writes)
        k = self.dma_next
        self.dma_next = (k + 1) % len(self.dma_sems)
        sem = self.dma_sems[k]
        if self.dma_cnt[k] > 0:
            self._wait(q, (sem, self.dma_cnt[k], "dma"))
        self.dma_cnt[k] += 16
        self.eng[q].dma_start(out=out, in_=in_, **kw).then_inc(sem, 16)
        ev = (sem, self.dma_cnt[k], "dma")
        self._commit(ev, reads, writes)
        self.n_ops += 1
        return ev

    def custom(self, q, fn, inc, reads=(), writes=()):
        self._deps(q, reads, writes)
        k = self.dma_next
        self.dma_next = (k + 1) % len(self.dma_sems)
        sem = self.dma_sems[k]
        if self.dma_cnt[k] > 0:
            self._wait(q, (sem, self.dma_cnt[k], "dma"))
        self.dma_cnt[k] += inc
        fn(self.eng[q]).then_inc(sem, inc)
        ev = (sem, self.dma_cnt[k], "dma")
        self._commit(ev, reads, writes)
        return ev

    def barrier(self, bufs=()):
        evs = [ev for ev in self.last_ev.values() if ev is not None]
        for k, sem in enumerate(self.dma_sems):
            if self.dma_cnt[k] > 0:
                evs.append((sem, self.dma_cnt[k], "dma"))
        for e in self.eng:
            for ev in evs:
                if ev[2] == e and e == "pe":
                    continue
                self._wait(e, ev)

    def finish(self):
        self.barrier()
        self.es.close()


D = 1024
DFF = 2816
ALPHA = 4 ** 0.25
LN_EPS = 1e-5
P = 128


class Ctx:
    def __init__(self, nc, S):
        self.nc = nc
        self.S = S
        es = S.es
        self.psum_es = None
        self.psum_gen = 0
        self.set_psum(1)
        self.ident = es.enter_context(nc.sbuf_tensor("ident_sb", [P, P], F32))
        self.ident_b = Buf("ident")

        self.eps_t = es.enter_context(nc.sbuf_tensor("eps_t", [P, 1], F32))

    def set_psum(self, n_bf=1):
        if self.psum_es is not None:
            self.psum_es.close()
        self.psum_es = contextlib.ExitStack()
        g = self.psum_gen
        self.psum_gen += 1
        nf = 8 - n_bf
        self.ps = [self.psum_es.enter_context(self.nc.psum_tensor(f"psf{g}_{i}", [P, 512], F32)) for i in range(nf)]
        self.psb = [Buf(f"ps{i}") for i in range(nf)]
        self.ps_bfs = [self.psum_es.enter_context(self.nc.psum_tensor(f"psh{g}_{i}", [P, 1024], BF16)) for i in range(n_bf)]
        self.psbf_b = [Buf(f"psbf{i}") for i in range(n_bf)]
        self.ps_bf = self.ps_bfs[0]
        while len(self.ps) < 8:
            self.ps.append(None)
            self.psb.append(self.psbf_b[0])

    def load_consts(self, ident_dram):
        self.ident_dram = ident_dram
        self.S.dma("sp", self.ident[:], ident_dram, writes=[self.ident_b])
        self.S.op("pool", lambda e: e.memset(self.eps_t[:], LN_EPS), writes=[self.ident_b])


def load_xT(C, x_tile, x_b, xT, xT_b, ps_ids, evac_engs=("act", "dve")):
    S = C.S
    for kc in range(8):
        pi = ps_ids[kc % len(ps_ids)]
        ps, pb = C.ps[pi], C.psb[pi]
        for blk in range(4):
            S.op("pe", lambda e, kc=kc, blk=blk, ps=ps: e.transpose(
                out=ps[:, blk * 128:(blk + 1) * 128], in_=x_tile[:, blk, kc * 128:(kc + 1) * 128],
                identity=C.ident[:]), reads=[x_b, C.ident_b], writes=[pb])
        eng = evac_engs[kc % len(evac_engs)]
        if eng == "act":
            S.op("act", lambda e, kc=kc, ps=ps: e.copy(out=xT[:, kc, :], in_=ps[:]), reads=[pb], writes=[xT_b])
        else:
            S.op("dve", lambda e, kc=kc, ps=ps: e.tensor_copy(out=xT[:, kc, :], in_=ps[:]), reads=[pb], writes=[xT_b])


def layer_norm_tile(C, xt, x_b, g_bc, b_bc, gb_b, stats, mv, rstd, st_b, nblk=4):
    S = C.S
    for blk in range(nblk):
        for hh in range(2):
            S.op("dve", lambda e, blk=blk, hh=hh: e.bn_stats(out=stats[:, blk, hh, :], in_=xt[:, blk, hh * 512:(hh + 1) * 512]),
                 reads=[x_b], writes=[st_b])
        S.op("dve", lambda e, blk=blk: e.bn_aggr(out=mv[:, blk, :], in_=stats[:, blk, :, :]), reads=[st_b], writes=[st_b])
        S.op("act", lambda e, blk=blk: e.activation(out=rstd[:, blk, :], in_=mv[:, blk, 1:2], func=AF.Sqrt, bias=C.eps_t[:], scale=1.0),
             reads=[st_b, C.ident_b], writes=[st_b])
        S.op("dve", lambda e, blk=blk: e.reciprocal(out=rstd[:, blk, :], in_=rstd[:, blk, :]), reads=[st_b], writes=[st_b])
        S.op("dve", lambda e, blk=blk: e.tensor_scalar(out=xt[:, blk, :], in0=xt[:, blk, :], scalar1=mv[:, blk, 0:1],
                                                       scalar2=rstd[:, blk, :], op0=ALU.subtract, op1=ALU.mult),
             reads=[st_b, x_b], writes=[x_b])
        S.op("dve", lambda e, blk=blk: e.tensor_tensor(out=xt[:, blk, :], in0=xt[:, blk, :], in1=g_bc[:], op=ALU.mult),
             reads=[x_b, gb_b], writes=[x_b])
        S.op("pool", lambda e, blk=blk: e.tensor_tensor(out=xt[:, blk, :], in0=xt[:, blk, :], in1=b_bc[:], op=ALU.add),
             reads=[x_b, gb_b], writes=[x_b])


def ffn_phase(C, x_in, x_out, w_in, w_out, g_dram, b_dram, NT, tag, xin_b, xout_b, after_tile=None, xT_out=None, xT_out_b=None):
    nc, S = C.nc, C.S
    ntile = NT // 512
    JG = [(0, 6), (6, 12), (12, 17), (17, 22)]
    with contextlib.ExitStack() as es:
        w1 = es.enter_context(nc.sbuf_tensor(f"w1{tag}", [P, 8, 2 * DFF], BF16))
        w2 = es.enter_context(nc.sbuf_tensor(f"w2{tag}", [P, 22, D], BF16))
        g_bc = es.enter_context(nc.sbuf_tensor(f"g{tag}", [P, D], F32))
        b_bc = es.enter_context(nc.sbuf_tensor(f"b{tag}", [P, D], F32))
        xtb = [es.enter_context(nc.sbuf_tensor(f"xt{tag}{i}", [P, 4, D], F32)) for i in range(2)]
        xT = es.enter_context(nc.sbuf_tensor(f"xT{tag}", [P, 8, 512], BF16))
        gT = es.enter_context(nc.sbuf_tensor(f"gT{tag}", [P, 22, 512], BF16))
        sa = es.enter_context(nc.sbuf_tensor(f"sa{tag}", [P, 512], F32))
        xo = [es.enter_context(nc.sbuf_tensor(f"xo{tag}{i}", [P, 512], BF16)) for i in range(2)]
        stats = es.enter_context(nc.sbuf_tensor(f"stats{tag}", [P, 4, 2, 6], F32))
        mv = es.enter_context(nc.sbuf_tensor(f"mv{tag}", [P, 4, 2], F32))
        rstd = es.enter_context(nc.sbuf_tensor(f"rstd{tag}", [P, 4, 1], F32))
        w1_b = [Buf() for _ in JG]
        w2_b = [Buf() for _ in range(22)]
        gb_b, xT_b, st_b, sa_b = Buf(), Buf(), Buf(), Buf()
        x_b = [Buf(), Buf()]
        xo_b = [Buf(), Buf()]
        gT_b = [Buf() for _ in range(22)]
        jgrp = {}
        for gi, (j0, j1) in enumerate(JG):
            for j in range(j0, j1):
                jgrp[j] = gi
        w_in_v = w_in.rearrange("(kc p) f -> p kc f", p=P)
        w_out_v = w_out.rearrange("(j p) d -> p j d", p=P)
        for gi, (j0, j1) in enumerate(JG):
            for half in range(2):
                c0, c1 = half * DFF + j0 * 128, half * DFF + j1 * 128
                S.dma("pool", w1[:, :, c0:c1], w_in_v[:, :, c0:c1], writes=[w1_b[gi]])
        for j in range(0, 22, 2):
            S.dma("pool", w2[:, j:j + 2, :], w_out_v[:, j:j + 2, :], writes=[w2_b[j], w2_b[j + 1]])
        S.dma("sp", g_bc[:], g_dram, writes=[gb_b])
        S.dma("sp", b_bc[:], b_dram, writes=[gb_b])
        xin_v = x_in.rearrange("(t b p) d -> t p b d", b=4, p=P)
        xout_v = x_out.rearrange("(t b p) d -> t p b d", b=4, p=P)

        def emit_load(t):
            S.dma("sp", xtb[t % 2][:], xin_v[t], reads=[xin_b[t] if isinstance(xin_b, list) else xin_b], writes=[x_b[t % 2]])

        def emit_T(t):
            load_xT(C, xtb[t % 2], x_b[t % 2], xT, xT_b, ps_ids=(6,))
            S.op("act", lambda e, t=t: e.mul(out=xtb[t % 2][:], in_=xtb[t % 2][:], mul=ALPHA), reads=[x_b[t % 2]], writes=[x_b[t % 2]])

        def emit_Tout_chunk(tt, kc):
            xt_ = xtb[tt % 2]
            for blk in range(4):
                S.op("pe", lambda e, blk=blk: e.transpose(out=C.ps[6][:, blk * 128:(blk + 1) * 128], in_=xt_[:, blk, kc * 128:(kc + 1) * 128],
                                                          identity=C.ident[:]), reads=[x_b[tt % 2], C.ident_b], writes=[C.psb[6]])
            k = kc % 2
            if k == 0:
                S.op("act", lambda e: e.copy(out=xo[k][:], in_=C.ps[6][:]), reads=[C.psb[6]], writes=[xo_b[k]])
            else:
                S.op("dve", lambda e: e.tensor_copy(out=xo[k][:], in_=C.ps[6][:]), reads=[C.psb[6]], writes=[xo_b[k]])
            S.dma("sp", xT_out[tt][kc * 128:(kc + 1) * 128, :], xo[k][:], reads=[xo_b[k]], writes=[xT_out_b[tt]])
            if kc == 7 and after_tile is not None:
                after_tile(tt)

        emit_load(0)
        emit_T(0)
        for t in range(ntile):
            xt = xtb[t % 2]
            xb = x_b[t % 2]
            for j in range(22):
                pa, pb_ = (0, 1) if j % 2 == 0 else (2, 3)
                gi = jgrp[j]
                for kc in range(8):
                    S.op("pe", lambda e, j=j, kc=kc, pa=pa: e.matmul(
                        out=C.ps[pa][:], lhsT=w1[:, kc, j * 128:(j + 1) * 128], rhs=xT[:, kc, :],
                        start=(kc == 0), stop=(kc == 7)), reads=[w1_b[gi], xT_b], writes=[C.psb[pa]])
                for kc in range(8):
                    S.op("pe", lambda e, j=j, kc=kc, pb_=pb_: e.matmul(
                        out=C.ps[pb_][:], lhsT=w1[:, kc, DFF + j * 128:DFF + (j + 1) * 128], rhs=xT[:, kc, :],
                        start=(kc == 0), stop=(kc == 7)), reads=[w1_b[gi], xT_b], writes=[C.psb[pb_]])
                S.op("act", lambda e, pa=pa: e.activation(out=sa[:], in_=C.ps[pa][:], func=AF.Silu), reads=[C.psb[pa]], writes=[sa_b])
                S.op("dve", lambda e, pb_=pb_, j=j: e.tensor_tensor(out=gT[:, j, :], in0=C.ps[pb_][:], in1=sa[:], op=ALU.mult),
                     reads=[C.psb[pb_], sa_b], writes=[gT_b[j]])
                if xT_out is not None and t >= 1 and 5 <= j < 13:
                    emit_Tout_chunk(t - 1, j - 5)
                if j == 13 and t + 1 < ntile:
                    emit_load(t + 1)
            if t + 1 < ntile:
                emit_T(t + 1)
            for blk in range(4):
                for hh in range(2):
                    po = 4 + (blk * 2 + hh) % 2
                    for j in range(22):
                        S.op("pe", lambda e, j=j, blk=blk, hh=hh, po=po: e.matmul(
                            out=C.ps[po][:], lhsT=gT[:, j, blk * 128:(blk + 1) * 128], rhs=w2[:, j, hh * 512:(hh + 1) * 512],
                            start=(j == 0), stop=(j == 21)), reads=[gT_b[j], w2_b[j]], writes=[C.psb[po]])
                    S.op("dve", lambda e, blk=blk, hh=hh, po=po, xt=xt: e.scalar_tensor_tensor(
                        out=xt[:, blk, hh * 512:(hh + 1) * 512], in0=C.ps[po][:], scalar=0.5,
                        in1=xt[:, blk, hh * 512:(hh + 1) * 512], op0=ALU.mult, op1=ALU.add),
                        reads=[C.psb[po], xb], writes=[xb])
            layer_norm_tile(C, xt, xb, g_bc, b_bc, gb_b, stats, mv, rstd, st_b)
            S.dma("sp", xout_v[t], xt[:], reads=[xb], writes=[xout_b[t] if isinstance(xout_b, list) else xout_b])
            if xT_out is None and after_tile is not None:
                after_tile(t)
        if xT_out is not None:
            for kc in range(8):
                emit_Tout_chunk(ntile - 1, kc)
        S.barrier()


def load_w_bf16(C, es, name, w_dram, ncols, q="pool"):
    w = es.enter_context(C.nc.sbuf_tensor(name, [P, 8, ncols], BF16))
    b = Buf(name)
    C.S.dma("pool", w[:], w_dram.rearrange("(kc p) f -> p kc f", p=P), writes=[b])
    return w, b


def proj_fm(C, w, w_b, c0, c1, xT, xT_b, pi):
    for kc in range(8):
        C.S.op("pe", lambda e, kc=kc: e.matmul(out=C.ps[pi][0:c1 - c0, :], lhsT=w[:, kc, c0:c1], rhs=xT[:, kc, :],
                                               start=(kc == 0), stop=(kc == 7)), reads=[w_b, xT_b], writes=[C.psb[pi]])


def proj_tm(C, w, w_b, c0, c1, xT, xT_b, blk, pi):
    for kc in range(8):
        C.S.op("pe", lambda e, kc=kc: e.matmul(out=C.ps[pi][:, 0:c1 - c0], lhsT=xT[:, kc, blk * 128:(blk + 1) * 128],
                                               rhs=w[:, kc, c0:c1], start=(kc == 0), stop=(kc == 7)),
               reads=[w_b, xT_b], writes=[C.psb[pi]])


def fetch_xT(C, xfull, xfull_b, t, xTs, xT_bs):
    k = t % 2
    C.S.dma("sp", xTs[k][:], xfull(t), reads=[xfull_b[t] if isinstance(xfull_b, list) else xfull_b], writes=[xT_bs[k]])
    return xTs[k], xT_bs[k]


def sb_phase(C, xfull, xfull_b, w_sb, yT, yT_b, TT, consts, tag):
    nc, S = C.nc, C.S
    ntile = TT // 512
    with contextlib.ExitStack() as es:
        w, w_b = load_w_bf16(C, es, f"wsb{tag}", w_sb, 384)
        QT = es.enter_context(nc.sbuf_tensor(f"sbQT{tag}", [P, TT], BF16))
        KT = es.enter_context(nc.sbuf_tensor(f"sbKT{tag}", [P, TT], BF16))
        V = es.enter_context(nc.sbuf_tensor(f"sbV{tag}", [P, TT // 128, 128], BF16))
        xTs = [es.enter_context(nc.sbuf_tensor(f"sbxT{tag}{k}", [P, 8, 512], BF16)) for k in range(2)]
        xT_bs = [Buf(), Buf()]
        negmask = es.enter_context(nc.sbuf_tensor(f"sbnm{tag}", [P, 4, 512], BF16))
        identb = es.enter_context(nc.sbuf_tensor(f"sbidb{tag}", [P, P], BF16))
        ntri = es.enter_context(nc.sbuf_tensor(f"sbntri{tag}", [P, P], BF16))
        ones = es.enter_context(nc.sbuf_tensor(f"sbones{tag}", [P, P], BF16))
        one1 = es.enter_context(nc.sbuf_tensor(f"sbone1{tag}", [P, 1], F32))
        cb = Buf()
        S.dma("pool", negmask[:], consts["sb_negmask"], writes=[cb])
        S.dma("pool", identb[:], consts["ident"], writes=[cb])
        S.dma("pool", ntri[:], consts["ntri"], writes=[cb])
        S.dma("pool", ones[:], consts["ones"], writes=[cb])
        S.op("pool", lambda e: e.memset(one1[:], 1.0), writes=[cb])
        x_b, xT_b = Buf(), Buf()
        QT_b = [Buf() for _ in range(ntile)]
        KT_b = [Buf() for _ in range(ntile)]
        V_b = [Buf() for _ in range(ntile)]
        xv = xfull if callable(xfull) else (lambda t, _v=xfull.rearrange("(t b p) d -> t p b d", b=4, p=P): _v[t])
        for t in range(ntile):
            xT, xT_b = fetch_xT(C, xfull, xfull_b, t, xTs, xT_bs)
            proj_fm(C, w, w_b, 0, 128, xT, xT_b, 0)
            S.op("act", lambda e, t=t: e.mul(out=QT[:, t * 512:(t + 1) * 512], in_=C.ps[0][:], mul=0.125),
                 reads=[C.psb[0]], writes=[QT_b[t]])
            proj_fm(C, w, w_b, 128, 256, xT, xT_b, 1)
            S.op("dve", lambda e, t=t: e.tensor_copy(out=KT[:, t * 512:(t + 1) * 512], in_=C.ps[1][:]),
                 reads=[C.psb[1]], writes=[KT_b[t]])
            for blk in range(4):
                pi = 2 + blk % 2
                proj_tm(C, w, w_b, 256, 384, xT, xT_b, blk, pi)
                S.op("act" if blk % 2 else "dve",
                     (lambda e, t=t, blk=blk, pi=pi: e.copy(out=V[:, t * 4 + blk, :], in_=C.ps[pi][:, 0:128])) if blk % 2 else
                     (lambda e, t=t, blk=blk, pi=pi: e.tensor_copy(out=V[:, t * 4 + blk, :], in_=C.ps[pi][:, 0:128])),
                     reads=[C.psb[pi]], writes=[V_b[t]])
        NCH = 4
        Ech = [es.enter_context(nc.sbuf_tensor(f"sbE2{tag}{c}", [P, 512], F32)) for c in range(NCH)]
        Lch = [es.enter_context(nc.sbuf_tensor(f"sbL2{tag}{c}", [P, 512], F32)) for c in range(NCH)]
        Sch = [[es.enter_context(nc.sbuf_tensor(f"sbS2{tag}{c}{k}", [P, 512], F32)) for k in range(2)] for c in range(NCH)]
        Wch = [es.enter_context(nc.sbuf_tensor(f"sbW2{tag}{c}", [P, 512], BF16)) for c in range(NCH)]
        Lhi = [es.enter_context(nc.sbuf_tensor(f"sbLh{tag}{c}", [P, 512], BF16)) for c in range(NCH)]
        Llo = [es.enter_context(nc.sbuf_tensor(f"sbLl{tag}{c}", [P, 512], BF16)) for c in range(NCH)]
        Shi = [es.enter_context(nc.sbuf_tensor(f"sbSh{tag}{c}", [P, 512], BF16)) for c in range(NCH)]
        Slo = [es.enter_context(nc.sbuf_tensor(f"sbSl{tag}{c}", [P, 512], BF16)) for c in range(NCH)]
        Lh_b = [Buf() for _ in range(NCH)]
        Ll_b = [Buf() for _ in range(NCH)]
        Sh_b = [Buf() for _ in range(NCH)]
        Sl_b = [Buf() for _ in range(NCH)]
        ych = [es.enter_context(nc.sbuf_tensor(f"sby2{tag}{c}", [P, 512], BF16)) for c in range(NCH)]
        E_b = [Buf() for _ in range(NCH)]
        L_b = [Buf() for _ in range(NCH)]
        S_b = [[Buf(), Buf()] for _ in range(NCH)]
        W_b = [Buf() for _ in range(NCH)]
        y_b = [Buf() for _ in range(NCH)]
        order = []
        lo, hi = 0, ntile - 1
        while lo <= hi:
            order.append(hi)
            hi -= 1
            if lo <= hi:
                order.append(lo)
                lo += 1
        queues = [[], []]
        load = [0, 0]
        for ti in sorted(range(ntile), key=lambda q: -q):
            sidx = 0 if load[0] <= load[1] else 1
            queues[sidx].append(ti)
            load[sidx] += 4 * ti + 4
        state = [None] * NCH
        qpos = [0, 0]

        def next_tile(slot):
            if qpos[slot] < len(queues[slot]):
                ti = queues[slot][qpos[slot]]
                qpos[slot] += 1
                return ti
            return None
        for slot in range(2):
            ti = next_tile(slot)
            for h in range(2):
                state[slot * 2 + h] = None if ti is None else [ti, 0]
        def mkinfo(c):
            i, n = state[c]
            nsteps = 4 * i + 4
            jb = 4 * i + 3 - n
            return dict(i=i, n=n, nsteps=nsteps, jb=jb, diag=jb >= 4 * i, r=jb - 4 * i, kt=jb // 4, h=c % 2,
                        hs=slice((c % 2) * 64, (c % 2) * 64 + 64), pz=c, py=4 + c // 2, k=n % 2)

        def H1(act, info):
            for c in act:
                f = info[c]
                if f["n"] > 0:
                    k = f["k"]
                    S.op("dve", lambda e, c=c, k=k: e.tensor_copy(out=Shi[c][:], in_=Sch[c][k][:]), reads=[S_b[c][k]], writes=[Sh_b[c]])
            for c in act:
                f = info[c]
                if f["n"] > 0:
                    k = f["k"]
                    S.op("dve", lambda e, c=c, k=k: e.tensor_tensor(out=Slo[c][:], in0=Sch[c][k][:], in1=Shi[c][:], op=ALU.subtract),
                         reads=[S_b[c][k], Sh_b[c]], writes=[Sl_b[c]])
            for c in act:
                f = info[c]
                S.op("pe", lambda e, f=f: e.matmul(out=C.ps[f["pz"]][:], lhsT=KT[f["hs"], f["jb"] * 128:(f["jb"] + 1) * 128],
                                                   rhs=QT[f["hs"], f["i"] * 512:(f["i"] + 1) * 512], start=True, stop=not f["diag"]),
                     reads=[KT_b[f["kt"]], QT_b[f["i"]]], writes=[C.psb[f["pz"]]])
                if f["diag"]:
                    S.op("pe", lambda e, f=f: e.matmul(out=C.ps[f["pz"]][:], lhsT=identb[:], rhs=negmask[:, f["r"], :], start=False, stop=True),
                         reads=[cb], writes=[C.psb[f["pz"]]])
            for c in act:
                f = info[c]
                S.op("act", lambda e, c=c, f=f: e.activation(out=Ech[c][:], in_=C.ps[f["pz"]][:], func=AF.Exp),
                     reads=[C.psb[f["pz"]]], writes=[E_b[c]])
            for c in act:
                S.op("act", lambda e, c=c: e.activation(out=Lch[c][:], in_=Ech[c][:], func=AF.Ln, bias=one1[:], scale=1.0),
                     reads=[E_b[c], cb], writes=[L_b[c]])

        def H2(act, info):
            for c in act:
                if False:
                    S.op("act", lambda e, c=c: e.copy(out=Lhi[c][:], in_=Lch[c][:]), reads=[L_b[c]], writes=[Lh_b[c]])
                else:
                    S.op("dve", lambda e, c=c: e.tensor_copy(out=Lhi[c][:], in_=Lch[c][:]), reads=[L_b[c]], writes=[Lh_b[c]])
            for c in act:
                S.op("dve", lambda e, c=c: e.tensor_tensor(out=Llo[c][:], in0=Lch[c][:], in1=Lhi[c][:], op=ALU.subtract),
                     reads=[L_b[c], Lh_b[c]], writes=[Ll_b[c]])
            for c in act:
                f = info[c]
                if f["n"] > 0:
                    S.op("pe", lambda e, c=c, f=f: e.matmul(out=C.ps[f["pz"]][:], lhsT=ones[:], rhs=Shi[c][:], start=False, stop=False,
                                                            skip_group_check=True), reads=[cb, Sh_b[c]], writes=[C.psb[f["pz"]]])
                    S.op("pe", lambda e, c=c, f=f: e.matmul(out=C.ps[f["pz"]][:], lhsT=ones[:], rhs=Slo[c][:], start=False, stop=False,
                                                            skip_group_check=True), reads=[cb, Sl_b[c]], writes=[C.psb[f["pz"]]])
                S.op("pe", lambda e, c=c, f=f: e.matmul(out=C.ps[f["pz"]][:], lhsT=ntri[:], rhs=Lhi[c][:], start=False, stop=False,
                                                        skip_group_check=True), reads=[cb, Lh_b[c]], writes=[C.psb[f["pz"]]])
                S.op("pe", lambda e, c=c, f=f: e.matmul(out=C.ps[f["pz"]][:], lhsT=ntri[:], rhs=Llo[c][:], start=False, stop=True,
                                                        skip_group_check=True), reads=[cb, Ll_b[c]], writes=[C.psb[f["pz"]]])
            for c in act:
                f = info[c]
                S.op("act", lambda e, c=c, f=f: e.activation(out=Wch[c][:], in_=C.ps[f["pz"]][:], func=AF.Exp),
                     reads=[C.psb[f["pz"]]], writes=[W_b[c]])
                if f["n"] < f["nsteps"] - 1:
                    k, k2 = f["k"], 1 - f["k"]
                    if f["n"] == 0:
                        S.op("pool", lambda e, c=c, k2=k2: e.tensor_copy(out=Sch[c][k2][:], in_=Lch[c][:]), reads=[L_b[c]], writes=[S_b[c][k2]])
                    else:
                        S.op("pool", lambda e, c=c, k=k, k2=k2: e.tensor_tensor(out=Sch[c][k2][:], in0=Sch[c][k][:], in1=Lch[c][:], op=ALU.add),
                             reads=[L_b[c], S_b[c][k]], writes=[S_b[c][k2]])
            for c in act:
                f = info[c]
                po = (c % 2) * 64
                S.op("pe", lambda e, c=c, f=f, po=po: e.matmul(out=C.ps[f["py"]][po:po + 64, :], lhsT=V[:, f["jb"], f["hs"]], rhs=Wch[c][:],
                                                               start=(f["n"] == 0), stop=(f["n"] == f["nsteps"] - 1), skip_group_check=True),
                     reads=[V_b[f["kt"]], W_b[c]], writes=[C.psb[f["py"]]])
            for c in act:
                f = info[c]
                if f["n"] == f["nsteps"] - 1:
                    po = (c % 2) * 64
                    S.op("dve", lambda e, c=c, f=f, po=po: e.tensor_copy(out=ych[c][po:po + 64, :], in_=C.ps[f["py"]][po:po + 64, :]),
                         reads=[C.psb[f["py"]]], writes=[y_b[c]])
                    S.dma("sp", yT[f["h"] * 64:(f["h"] + 1) * 64, f["i"] * 512:(f["i"] + 1) * 512], ych[c][po:po + 64, :],
                          reads=[y_b[c]], writes=[yT_b])
                    state[c] = "done"
                else:
                    state[c][1] += 1
            for slot in range(2):
                cs = [slot * 2, slot * 2 + 1]
                if all(state[c] == "done" for c in cs):
                    ti = next_tile(slot)
                    for c in cs:
                        state[c] = None if ti is None else [ti, 0]

        pending = [None, None]
        while True:
            progressed = False
            for slot in range(2):
                other = 1 - slot
                cs_ = [c for c in (2 * slot, 2 * slot + 1) if state[c] is not None]
                if cs_ and pending[slot] is None:
                    inf_ = {c: mkinfo(c) for c in cs_}
                    H1(cs_, inf_)
                    pending[slot] = (cs_, inf_)
                    progressed = True
                if pending[other] is not None:
                    H2(*pending[other])
                    pending[other] = None
                    progressed = True
            if not progressed:
                break
        S.barrier()


def make_consts_np():
    c = {}
    c["ident"] = np.eye(P, dtype=np.float32)
    j = np.arange(P)[:, None]
    s = np.arange(P)[None, :]
    c["ntri"] = -(j >= s).astype(np.float32)
    c["ones"] = -np.ones((P, P), np.float32)
    nm = np.zeros((P, 4, 512), np.float32)
    for r in range(4):
        key = 128 * r + np.arange(P)[:, None]
        col = np.arange(512)[None, :]
        nm[:, r, :] = np.where(key < col, 0.0, -30000.0)
    c["sb_negmask"] = nm
    make_swa_consts_np(c)
    make_ml_consts_np(c)
    return c


def dram_ap(t_ap, offset, pattern):
    return bass.AP(tensor=t_ap.tensor, offset=offset, ap=pattern)


def swa_phase(C, xfull, xfull_b, w_swa, rb_dram, sink_dram, frev_scr, yT, yT_b, TT, consts, tag):
    nc, S = C.nc, C.S
    ntile = TT // 512
    nblk = TT // 128
    C.set_psum(2)
    with contextlib.ExitStack() as es:
        w, w_b = load_w_bf16(C, es, f"wsw{tag}", w_swa, 448)
        QT = [es.enter_context(nc.sbuf_tensor(f"swQT{tag}{g}", [P, TT], BF16)) for g in range(2)]
        KT = es.enter_context(nc.sbuf_tensor(f"swKT{tag}", [P, TT], BF16))
        V = es.enter_context(nc.sbuf_tensor(f"swV{tag}", [P, nblk, 64], BF16))
        xTs = [es.enter_context(nc.sbuf_tensor(f"swxT{tag}{k}", [P, 8, 512], BF16)) for k in range(2)]
        xT_bs = [Buf(), Buf()]
        identb = es.enter_context(nc.sbuf_tensor(f"swidb{tag}", [P, P], BF16))
        Jm = es.enter_context(nc.sbuf_tensor(f"swJ{tag}", [P, P], F32))
        rb = es.enter_context(nc.sbuf_tensor(f"swrb{tag}", [P, P], F32))
        oh = es.enter_context(nc.sbuf_tensor(f"swoh{tag}", [P, 384], F32))
        fneg = es.enter_context(nc.sbuf_tensor(f"swfn{tag}", [4, 384], F32))
        frev = es.enter_context(nc.sbuf_tensor(f"swfr{tag}", [4, 384], F32))
        Hk = es.enter_context(nc.sbuf_tensor(f"swH{tag}", [P, 4, 256], F32))
        bias = es.enter_context(nc.sbuf_tensor(f"swbias{tag}", [P, 4, 256], F32))
        sink = es.enter_context(nc.sbuf_tensor(f"swsink{tag}", [P, 4], F32))
        Sb = es.enter_context(nc.sbuf_tensor(f"swS{tag}", [P, 4, 256], F32))
        pf = es.enter_context(nc.sbuf_tensor(f"swp{tag}", [P, 4, 256], F32))
        pn = es.enter_context(nc.sbuf_tensor(f"swpn{tag}", [P, 4, 256], BF16))
        pT = es.enter_context(nc.sbuf_tensor(f"swpT{tag}", [P, 8, 128], BF16))
        small = es.enter_context(nc.sbuf_tensor(f"swsm{tag}", [P, 6, 4], F32))
        yo = es.enter_context(nc.sbuf_tensor(f"swyo{tag}", [64, 4, 512], BF16))
        cb, x_b, xT_b = Buf(), Buf(), Buf()
        S.dma("pool", identb[:], consts["ident"], writes=[cb])
        S.dma("sp", Jm[:], consts["J"], writes=[cb])
        S.dma("sp", rb[:], rb_dram, writes=[cb])
        S.dma("sp", oh[:], consts["swa_oh"], writes=[cb])
        S.dma("sp", fneg[:], consts["swa_fneg"], writes=[cb])
        S.dma("sp", sink[:], sink_dram, writes=[cb])
        S.op("pe", lambda e: e.matmul(out=C.ps[0][:, 0:384], lhsT=rb[:], rhs=oh[:], start=True, stop=True),
             reads=[cb], writes=[C.psb[0]])
        fr_b, scr_b, H_b, bias_b = Buf(), Buf(), Buf(), Buf()
        S.op("dve", lambda e: e.tensor_tensor(out=frev[:], in0=C.ps[0][0:4, 0:384], in1=fneg[:], op=ALU.add),
             reads=[C.psb[0], cb], writes=[fr_b])
        S.dma("sp", frev_scr, frev[:], reads=[fr_b], writes=[scr_b])
        S.dma("sp", Hk[:], dram_ap(frev_scr, 0, [[1, 128], [384, 4], [1, 256]]), reads=[scr_b], writes=[H_b])
        for hh in range(2):
            S.op("pe", lambda e, hh=hh: e.matmul(out=C.ps[1 + hh][:], lhsT=Jm[:], rhs=Hk[:, 2 * hh:2 * hh + 2, :],
                                                 start=True, stop=True), reads=[cb, H_b], writes=[C.psb[1 + hh]])
            S.op("dve", lambda e, hh=hh: e.tensor_copy(out=bias[:, 2 * hh:2 * hh + 2, :], in_=C.ps[1 + hh][:]),
                 reads=[C.psb[1 + hh]], writes=[bias_b])
        QT_b = [Buf() for _ in range(ntile)]
        KT_b = [Buf() for _ in range(ntile)]
        V_b = [Buf() for _ in range(ntile)]
        xv = xfull if callable(xfull) else (lambda t, _v=xfull.rearrange("(t b p) d -> t p b d", b=4, p=P): _v[t])
        for t in range(ntile):
            xT, xT_b = fetch_xT(C, xfull, xfull_b, t, xTs, xT_bs)
            for g in range(2):
                proj_fm(C, w, w_b, g * 128, (g + 1) * 128, xT, xT_b, g)
                S.op("act", lambda e, t=t, g=g: e.mul(out=QT[g][:, t * 512:(t + 1) * 512], in_=C.ps[g][:], mul=0.125),
                     reads=[C.psb[g]], writes=[QT_b[t]])
            proj_fm(C, w, w_b, 256, 384, xT, xT_b, 2)
            S.op("dve", lambda e, t=t: e.tensor_copy(out=KT[:, t * 512:(t + 1) * 512], in_=C.ps[2][:]),
                 reads=[C.psb[2]], writes=[KT_b[t]])
            for blk in range(4):
                pi = 3 + blk % 2
                proj_tm(C, w, w_b, 384, 448, xT, xT_b, blk, pi)
                S.op("dve", lambda e, t=t, blk=blk, pi=pi: e.tensor_copy(out=V[:, t * 4 + blk, :], in_=C.ps[pi][:, 0:64]),
                     reads=[C.psb[pi]], writes=[V_b[t]])
        NS = 2
        Sb2 = [Sb] + [es.enter_context(nc.sbuf_tensor(f"swS{tag}b", [P, 4, 256], F32))]
        pf2 = [pf] + [es.enter_context(nc.sbuf_tensor(f"swp{tag}b", [P, 4, 256], F32))]
        pn2 = [pn] + [es.enter_context(nc.sbuf_tensor(f"swpn{tag}b", [P, 4, 256], BF16))]
        pT2 = [pT] + [es.enter_context(nc.sbuf_tensor(f"swpT{tag}b", [P, 8, 128], BF16))]
        sm2 = [small] + [es.enter_context(nc.sbuf_tensor(f"swsm{tag}b", [P, 6, 4], F32))]
        S_b = [Buf(), Buf()]
        p_b = [Buf(), Buf()]
        pn_b = [Buf(), Buf()]
        pT_b = [Buf(), Buf()]
        sm_b = [Buf(), Buf()]
        yo_b = Buf()
        for n0 in range(0, nblk, NS):
            blks = [(n0 + a, a) for a in range(NS) if n0 + a < nblk]
            kwd = {n: (128 if n == 0 else 256) for n, a in blks}
            for n, a in blks:
                kw = kwd[n]
                k0 = 256 - kw
                t_q = n // 4
                for h in range(4):
                    g, hs = h % 2, slice((h // 2) * 64, (h // 2) * 64 + 64)
                    bank = 2 * a + h // 2
                    col = (h % 2) * 256
                    S.op("pe", lambda e, g=g, hs=hs, bank=bank, col=col, n=n, kw=kw, k0=k0: e.matmul(
                        out=C.ps[bank][:, col + k0:col + 256], lhsT=QT[g][hs, n * 128:(n + 1) * 128],
                        rhs=KT[hs, (n + 1) * 128 - kw:(n + 1) * 128], start=True, stop=True, skip_group_check=True),
                        reads=[QT_b[t_q], KT_b[t_q], KT_b[max(0, (n - 1) // 4)]], writes=[C.psb[bank]])
            for n, a in blks:
                k0 = 256 - kwd[n]
                mx, negm, dd, esk, rs, rden = [sm2[a][:, i, :] for i in range(6)]
                for bk in range(2):
                    bank = 2 * a + bk
                    S.op("dve", lambda e, bank=bank, bk=bk, k0=k0, a=a: e.tensor_tensor(
                        out=Sb2[a][:, 2 * bk:2 * bk + 2, k0:256],
                        in0=C.ps[bank][:].rearrange("p (h k) -> p h k", h=2)[:, :, k0:256],
                        in1=bias[:, 2 * bk:2 * bk + 2, k0:256], op=ALU.add),
                        reads=[C.psb[bank], bias_b], writes=[S_b[a]])
                S.op("dve", lambda e, k0=k0, a=a, mx=mx: e.tensor_reduce(out=mx, in_=Sb2[a][:, :, k0:256], axis=AX.X, op=ALU.max),
                     reads=[S_b[a]], writes=[sm_b[a]])
                S.op("dve", lambda e, mx=mx: e.tensor_tensor(out=mx, in0=mx, in1=sink[:], op=ALU.max), reads=[sm_b[a], cb], writes=[sm_b[a]])
                S.op("dve", lambda e, mx=mx, negm=negm: e.tensor_scalar(out=negm, in0=mx, scalar1=-1.0, scalar2=None, op0=ALU.mult),
                     reads=[sm_b[a]], writes=[sm_b[a]])
                S.op("dve", lambda e, mx=mx, dd=dd: e.tensor_tensor(out=dd, in0=sink[:], in1=mx, op=ALU.subtract), reads=[sm_b[a], cb], writes=[sm_b[a]])
            for n, a in blks:
                k0 = 256 - kwd[n]
                mx, negm, dd, esk, rs, rden = [sm2[a][:, i, :] for i in range(6)]
                for h in range(4):
                    S.op("act", lambda e, h=h, k0=k0, a=a, negm=negm, rs=rs: e.activation(
                        out=pf2[a][:, h, k0:256], in_=Sb2[a][:, h, k0:256], func=AF.Exp, bias=negm[:, h:h + 1], scale=1.0, accum_out=rs[:, h:h + 1]),
                        reads=[S_b[a], sm_b[a]], writes=[p_b[a], sm_b[a]])
                S.op("act", lambda e, esk=esk, dd=dd: e.activation(out=esk, in_=dd, func=AF.Exp), reads=[sm_b[a]], writes=[sm_b[a]])
            for n, a in blks:
                k0 = 256 - kwd[n]
                mx, negm, dd, esk, rs, rden = [sm2[a][:, i, :] for i in range(6)]
                S.op("dve", lambda e, rden=rden, rs=rs, esk=esk: e.tensor_tensor(out=rden, in0=rs, in1=esk, op=ALU.add), reads=[sm_b[a]], writes=[sm_b[a]])
                S.op("dve", lambda e, rden=rden: e.reciprocal(out=rden, in_=rden), reads=[sm_b[a]], writes=[sm_b[a]])
                for h in range(4):
                    if h % 2:
                        S.op("dve", lambda e, h=h, k0=k0, a=a, rden=rden: e.tensor_scalar(
                            out=pn2[a][:, h, k0:256], in0=pf2[a][:, h, k0:256], scalar1=rden[:, h:h + 1], scalar2=None, op0=ALU.mult),
                            reads=[p_b[a], sm_b[a]], writes=[pn_b[a]])
                    else:
                        S.op("act", lambda e, h=h, k0=k0, a=a, rden=rden: e.mul(out=pn2[a][:, h, k0:256], in_=pf2[a][:, h, k0:256], mul=rden[:, h:h + 1]),
                             reads=[p_b[a], sm_b[a]], writes=[pn_b[a]])
            for n, a in blks:
                kw = kwd[n]
                k0 = 256 - kw
                nkb = kw // 128
                for h in range(4):
                    for kb in range(nkb):
                        idx = h * 2 + kb
                        S.op("pe", lambda e, h=h, kb=kb, idx=idx, k0=k0, a=a: e.transpose(
                            out=C.ps_bfs[a][:, idx * 128:(idx + 1) * 128], in_=pn2[a][:, h, k0 + kb * 128:k0 + (kb + 1) * 128],
                            identity=identb[:]), reads=[pn_b[a], cb], writes=[C.psbf_b[a]])
            for n, a in blks:
                if a == 0:
                    S.op("act", lambda e, a=a: e.copy(out=pT2[a][:].rearrange("p a b -> p (a b)"), in_=C.ps_bfs[a][:]), reads=[C.psbf_b[a]], writes=[pT_b[a]])
                else:
                    S.op("dve", lambda e, a=a: e.tensor_copy(out=pT2[a][:].rearrange("p a b -> p (a b)"), in_=C.ps_bfs[a][:]), reads=[C.psbf_b[a]], writes=[pT_b[a]])
            for n, a in blks:
                nkb = kwd[n] // 128
                po = 4 + a
                for h in range(4):
                    for kb in range(nkb):
                        idx = h * 2 + kb
                        kblk = n - (nkb - 1) + kb
                        hd = (h % 2) * 2 + h // 2
                        S.op("pe", lambda e, hd=hd, kb=kb, idx=idx, kblk=kblk, nkb=nkb, a=a, po=po: e.matmul(
                            out=C.ps[po][0:64, hd * 128:(hd + 1) * 128], lhsT=V[:, kblk, :], rhs=pT2[a][:, idx, :],
                            start=(kb == 0), stop=(kb == nkb - 1), skip_group_check=True),
                            reads=[V_b[kblk // 4], pT_b[a]], writes=[C.psb[po]])
            for n, a in blks:
                po = 4 + a
                S.op("dve" if a == 0 else "act",
                     (lambda e, n=n, po=po: e.tensor_copy(out=yo[:, :, (n % 4) * 128:(n % 4 + 1) * 128],
                                                          in_=C.ps[po][0:64, :].rearrange("p (h q) -> p h q", h=4))) if a == 0 else
                     (lambda e, n=n, po=po: e.copy(out=yo[:, :, (n % 4) * 128:(n % 4 + 1) * 128],
                                                   in_=C.ps[po][0:64, :].rearrange("p (h q) -> p h q", h=4))),
                     reads=[C.psb[po]], writes=[yo_b])
                if n % 4 == 3:
                    S.dma("sp", yT.rearrange("(h d) t -> d h t", d=64)[:, :, (n // 4) * 512:(n // 4 + 1) * 512], yo[:],
                          reads=[yo_b], writes=[yT_b])
        S.barrier()
        C.set_psum(1)


def t5_bucket_np(dist):
    max_exact = 16
    d = np.maximum(dist, 1)
    large = max_exact + (np.log(d / max_exact) / np.log(128 / max_exact) * (32 - max_exact)).astype(np.int32)
    large = np.minimum(large, 31)
    return np.where(dist < max_exact, dist, large).astype(np.int32)


def make_swa_consts_np(c):
    a = np.arange(384)
    dist = 255 - a
    valid = (dist >= 0) & (dist < 128)
    bucket = t5_bucket_np(np.clip(dist, 0, None))
    oh = np.zeros((32, 384), np.float32)
    oh[bucket[valid], a[valid]] = 1.0
    c["swa_oh"] = np.pad(oh, ((0, 96), (0, 0)))
    c["swa_fneg"] = np.tile(np.where(valid, 0.0, -30000.0).astype(np.float32)[None, :], (4, 1))
    c["J"] = np.eye(P, dtype=np.float32)[::-1].copy()
    return c


def mlstm_phase(C, xfull, xfull_b, w_ml, cw_dram, cb_dram, ifb_dram, ng_dram, yT, yT_b, TT, consts, tag):
    nc, S = C.nc, C.S
    ntile = TT // 512
    nb = TT // 128
    n2 = 2 * nb
    with contextlib.ExitStack() as es:
        w, w_b = load_w_bf16(C, es, f"wml{tag}", w_ml, 640)
        sb = lambda name, shape, dt=F32: es.enter_context(nc.sbuf_tensor(f"ml{name}{tag}", shape, dt))
        QT = sb("QT", [P, TT], BF16)
        KT = sb("KT", [P, TT], BF16)
        Vext = sb("Vext", [P, nb, 2, 66], BF16)
        gso = sb("gso", [P, nb, 128])
        G4 = sb("G4", [P, nb, 4])
        xTs = [sb(f"xT{k}", [P, 8, 512], BF16) for k in range(2)]
        xT_bs = [Buf(), Buf()]
        raws = [sb(f"raw{k}", [P, 2, 515]) for k in range(2)]
        accs = [sb(f"acc{k}", [P, 2, 512]) for k in range(2)]
        cw = sb("cw", [P, 8])
        cbt = sb("cb", [P, 2])
        ifb = sb("ifb", [P, 4])
        nfb = sb("nfb", [P, 2])
        ng = sb("ng", [P, 128])
        identb = sb("idb", [P, P], BF16)
        ntriT = sb("ntriT", [P, P])
        mask8 = sb("mask8", [P, P])
        e0 = sb("e0", [P, P])
        e127 = sb("e127", [P, P])
        one1 = sb("one1", [P, 1])
        cb_ = Buf()
        S.dma("pool", identb[:], consts["ident"], writes=[cb_])
        S.dma("sp", ntriT[:], consts["ntriT"], writes=[cb_])
        S.dma("sp", mask8[:], consts["mask8"], writes=[cb_])
        S.dma("sp", e0[:], consts["e0ones"], writes=[cb_])
        S.dma("sp", e127[:], consts["e127ones"], writes=[cb_])
        S.dma("sp", cw[:], cw_dram, writes=[cb_])
        S.dma("sp", cbt[:], cb_dram, writes=[cb_])
        S.dma("sp", ifb[:], ifb_dram, writes=[cb_])
        S.dma("sp", ng[:], ng_dram, writes=[cb_])
        S.op("pool", lambda e: e.memset(one1[:], 1.0), writes=[cb_])
        S.op("dve", lambda e: e.tensor_scalar(out=nfb[:], in0=ifb[:, 2:4], scalar1=-1.0, scalar2=None, op0=ALU.mult),
             reads=[cb_], writes=[cb_])
        x_b, xT_b, V_b, gso_b, G_b = Buf(), Buf(), Buf(), Buf(), Buf()
        raw_bs, acc_bs = [Buf(), Buf()], [Buf(), Buf()]
        QT_b = [Buf() for _ in range(ntile)]
        KT_b = [Buf() for _ in range(ntile)]
        for k in range(2):
            S.op("pool", lambda e, k=k: e.memset(raws[k][:], 0.0), writes=[raw_bs[k]])
        S.op("pool", lambda e: e.memset(Vext[:], 1.0), writes=[V_b])
        xv = xfull if callable(xfull) else (lambda t, _v=xfull.rearrange("(t b p) d -> t p b d", b=4, p=P): _v[t])
        import os
        mlstage = int(os.environ.get("ML_STAGE", "9"))
        for t in range(ntile):
            if mlstage < -1:
                break
            xT, xT_b = fetch_xT(C, xfull, xfull_b, t, xTs, xT_bs)
            raw, raw_b, acc, acc_b = raws[t % 2], raw_bs[t % 2], accs[t % 2], acc_bs[t % 2]
            if t > 0:
                S.op("pool", lambda e, raw=raw, t=t: e.tensor_copy(out=raw[:, :, 0:3], in_=raws[(t - 1) % 2][:, :, 512:515]),
                     reads=[raw_bs[(t - 1) % 2]], writes=[raw_b])
            for qk in range(2):
                proj_fm(C, w, w_b, qk * 128, (qk + 1) * 128, xT, xT_b, qk)
                S.op("act", lambda e, qk=qk, raw=raw: e.copy(out=raw[:, qk, 3:515], in_=C.ps[qk][:]), reads=[C.psb[qk]], writes=[raw_b])
            for qk in range(2):
                S.op("dve", lambda e, qk=qk, raw=raw, acc=acc: e.tensor_scalar(out=acc[:, qk, :], in0=raw[:, qk, 0:512], scalar1=cw[:, 4 * qk:4 * qk + 1],
                                                              scalar2=None, op0=ALU.mult), reads=[raw_b, cb_], writes=[acc_b])
                for j in range(1, 4):
                    S.op("dve", lambda e, qk=qk, j=j, raw=raw, acc=acc: e.scalar_tensor_tensor(
                        out=acc[:, qk, :], in0=raw[:, qk, j:j + 512], scalar=cw[:, 4 * qk + j:4 * qk + j + 1], in1=acc[:, qk, :],
                        op0=ALU.mult, op1=ALU.add), reads=[raw_b, cb_, acc_b], writes=[acc_b])
                dst, dst_b = (QT, QT_b) if qk == 0 else (KT, KT_b)
                S.op("act", lambda e, qk=qk, dst=dst, t=t, acc=acc: e.activation(out=dst[:, t * 512:(t + 1) * 512], in_=acc[:, qk, :], func=AF.Silu,
                                                                        bias=cbt[:, qk:qk + 1], scale=1.0),
                     reads=[acc_b, cb_], writes=[dst_b[t]])
            for blk in range(4):
                if mlstage < 0:
                    break
                pi = 2 + blk % 2
                bi = t * 4 + blk
                proj_tm(C, w, w_b, 256, 512, xT, xT_b, blk, pi)
                proj_tm(C, w, w_b, 512, 640, xT, xT_b, blk, 4 + blk % 2)
                sub = int(os.environ.get("ML_SUB", "9"))
                if sub < 1:
                    continue
                S.op("dve", lambda e, pi=pi, bi=bi: e.tensor_copy(out=Vext[:, bi, :, 0:64],
                                                                  in_=C.ps[pi][:, 0:128].rearrange("p (h d) -> p h d", h=2)),
                     reads=[C.psb[pi]], writes=[V_b])
                if sub < 2:
                    continue
                S.op("act", lambda e, pi=pi, bi=bi: e.activation(out=gso[:, bi, :], in_=C.ps[pi][:, 128:256], func=AF.Exp, scale=-1.0),
                     reads=[C.psb[pi]], writes=[gso_b])
                S.op("act", lambda e, bi=bi: e.add(out=gso[:, bi, :], in_=gso[:, bi, :], add=1.0), reads=[gso_b], writes=[gso_b])
                S.op("dve", lambda e, bi=bi: e.reciprocal(out=gso[:, bi, :], in_=gso[:, bi, :]), reads=[gso_b], writes=[gso_b])
                if sub < 3:
                    continue
                S.op("dve", lambda e, blk=blk, bi=bi: e.tensor_copy(out=G4[:, bi, :], in_=C.ps[4 + blk % 2][:, 0:4]),
                     reads=[C.psb[4 + blk % 2]], writes=[G_b])
                S.op("pool", lambda e, bi=bi: e.tensor_tensor(out=gso[:, bi, :], in0=gso[:, bi, :], in1=ng[:], op=ALU.mult),
                     reads=[gso_b, cb_], writes=[gso_b])
        import os
        mlstage = int(os.environ.get("ML_STAGE", "9"))
        if mlstage < 1:
            S.barrier()
            return
        icol = sb("icol", [P, 2, nb])
        lf = sb("lf", [P, 2, nb])
        bcol = sb("bcol", [P, 2, nb])
        acol = sb("acol", [P, 2, nb])
        aT = sb("aT", [P, P])
        cm_tok = sb("cm_tok", [P, n2])
        amax_bc = sb("amax_bc", [P, n2])
        cmT = sb("cmT", [P, P])
        RW = sb("RW", [P, 3, P])
        mnext = sb("mnext", [1, P])
        mprev_bc = sb("mprevbc", [P, n2])
        mref_bc = sb("mrefbc", [P, n2])
        Mt = sb("Mt", [P, n2])
        r_t = sb("r_t", [P, n2])
        u_t = sb("u_t", [P, n2])
        eb_t = sb("eb_t", [P, n2])
        sc_bc = sb("sc_bc", [P, n2])
        tmp = sb("tmpg", [P, n2])
        g_b = Buf()
        fl = lambda tl: tl[:].rearrange("p h c -> p (h c)")
        for h in range(2):
            S.op("act", lambda e, h=h: e.activation(out=icol[:, h, :], in_=G4[:, :, h], func=AF.Identity, bias=ifb[:, h:h + 1], scale=1.0),
                 reads=[G_b, cb_], writes=[g_b])
            S.op("act", lambda e, h=h: e.activation(out=lf[:, h, :], in_=G4[:, :, 2 + h], func=AF.Exp, bias=nfb[:, h:h + 1], scale=-1.0),
                 reads=[G_b, cb_], writes=[g_b])
        S.op("act", lambda e: e.activation(out=fl(lf), in_=fl(lf), func=AF.Ln, bias=one1[:], scale=1.0), reads=[g_b, cb_], writes=[g_b])
        S.op("pe", lambda e: e.matmul(out=C.ps[0][:, 0:n2], lhsT=ntriT[:], rhs=fl(lf), start=True, stop=True),
             reads=[g_b, cb_], writes=[C.psb[0]])
        S.op("dve", lambda e: e.tensor_copy(out=fl(bcol), in_=C.ps[0][:, 0:n2]), reads=[C.psb[0]], writes=[g_b])
        S.op("dve", lambda e: e.tensor_tensor(out=fl(acol), in0=fl(icol), in1=fl(bcol), op=ALU.subtract), reads=[g_b], writes=[g_b])
        S.op("pe", lambda e: e.transpose(out=C.ps[1][0:n2, 0:128], in_=fl(acol), identity=C.ident[:]), reads=[g_b, C.ident_b], writes=[C.psb[1]])
        S.op("pool", lambda e: e.memset(aT[:], 0.0), writes=[g_b])
        S.op("pool", lambda e: e.memset(RW[:], 0.0), writes=[g_b])
        S.op("dve", lambda e: e.tensor_copy(out=aT[0:n2, :], in_=C.ps[1][0:n2, 0:128]), reads=[C.psb[1]], writes=[g_b])
        S.op("dve", lambda e: e.tensor_tensor_scan(out=cmT[:], data0=aT[:], data1=aT[:], initial=-1.0e30, op0=ALU.max, op1=ALU.max),
             reads=[g_b], writes=[g_b])
        S.op("pe", lambda e: e.transpose(out=C.ps[2][:, 0:128], in_=cmT[:], identity=C.ident[:]), reads=[g_b, C.ident_b], writes=[C.psb[2]])
        S.op("dve", lambda e: e.tensor_copy(out=cm_tok[:], in_=C.ps[2][:, 0:n2]), reads=[C.psb[2]], writes=[g_b])
        S.op("pe", lambda e: e.matmul(out=C.ps[3][:, 0:n2], lhsT=e127[:], rhs=cm_tok[:], start=True, stop=True), reads=[g_b, cb_], writes=[C.psb[3]])
        S.op("pe", lambda e: e.matmul(out=C.ps[4][:, 0:n2], lhsT=e127[:], rhs=fl(bcol), start=True, stop=True), reads=[g_b, cb_], writes=[C.psb[4]])
        S.op("dve", lambda e: e.tensor_copy(out=amax_bc[:], in_=C.ps[3][:, 0:n2]), reads=[C.psb[3]], writes=[g_b])
        S.op("dve", lambda e: e.tensor_copy(out=RW[:, 1, 0:n2], in_=C.ps[4][:, 0:n2]), reads=[C.psb[4]], writes=[g_b])
        S.op("dve", lambda e: e.tensor_copy(out=RW[:, 0, 0:n2], in_=amax_bc[:]), reads=[g_b], writes=[g_b])
        for h in range(2):
            S.op("dve", lambda e, h=h: e.tensor_tensor_scan(out=mnext[0:1, h * nb:(h + 1) * nb], data0=RW[0:1, 0, h * nb:(h + 1) * nb],
                                                            data1=RW[0:1, 1, h * nb:(h + 1) * nb], initial=0.0, op0=ALU.max, op1=ALU.add),
                 reads=[g_b], writes=[g_b])
            if nb > 1:
                S.op("dve", lambda e, h=h: e.tensor_copy(out=RW[0:1, 2, h * nb + 1:(h + 1) * nb], in_=mnext[0:1, h * nb:(h + 1) * nb - 1]),
                     reads=[g_b], writes=[g_b])
        S.op("pe", lambda e: e.matmul(out=C.ps[0][:, 0:n2], lhsT=e0[:], rhs=RW[:, 2, 0:n2], start=True, stop=True),
             reads=[g_b, cb_], writes=[C.psb[0]])
        S.op("dve", lambda e: e.tensor_copy(out=mprev_bc[:], in_=C.ps[0][:, 0:n2]), reads=[C.psb[0]], writes=[g_b])
        S.op("dve", lambda e: e.tensor_tensor(out=mref_bc[:], in0=amax_bc[:], in1=mprev_bc[:], op=ALU.max), reads=[g_b], writes=[g_b])
        S.op("dve", lambda e: e.tensor_tensor(out=Mt[:], in0=cm_tok[:], in1=mprev_bc[:], op=ALU.max), reads=[g_b], writes=[g_b])
        S.op("dve", lambda e: e.tensor_tensor(out=tmp[:], in0=mref_bc[:], in1=Mt[:], op=ALU.subtract), reads=[g_b], writes=[g_b])
        S.op("act", lambda e: e.activation(out=r_t[:], in_=tmp[:], func=AF.Exp), reads=[g_b], writes=[g_b])
        S.op("dve", lambda e: e.tensor_tensor(out=tmp[:], in0=fl(acol), in1=mref_bc[:], op=ALU.subtract), reads=[g_b], writes=[g_b])
        S.op("act", lambda e: e.activation(out=u_t[:], in_=tmp[:], func=AF.Exp), reads=[g_b], writes=[g_b])
        S.op("dve", lambda e: e.tensor_tensor(out=tmp[:], in0=fl(bcol), in1=Mt[:], op=ALU.add), reads=[g_b], writes=[g_b])
        S.op("act", lambda e: e.activation(out=eb_t[:], in_=tmp[:], func=AF.Exp, scale=-1.0), reads=[g_b], writes=[g_b])
        S.op("dve", lambda e: e.tensor_tensor(out=tmp[:], in0=mprev_bc[:], in1=mref_bc[:], op=ALU.subtract), reads=[g_b], writes=[g_b])
        S.op("act", lambda e: e.activation(out=sc_bc[:], in_=tmp[:], func=AF.Exp), reads=[g_b], writes=[g_b])
        if mlstage < 2:
            S.barrier()
            return
        Cst = sb("Cst", [P, 65])
        Csb = sb("Csb", [P, 130], BF16)
        Kp = sb("Kp", [P, P], BF16)
        ST = [sb(f"ST{h}", [P, P], BF16) for h in range(2)]
        nd = sb("nd", [P, 2, 65])
        hid = sb("hid", [P, 2, 64])
        sm = sb("sm", [P, 8])
        stats = sb("stats", [P, 2, 6])
        mv = sb("mv", [P, 2, 2])
        yTt = sb("yTt", [P, 512], BF16)
        Cst_b, Csb_b, Kp_b, ST_b, nd_b, hid_b, sm_b, yTt_b = Buf(), Buf(), Buf(), [Buf(), Buf()], Buf(), Buf(), Buf(), Buf()
        S.op("pool", lambda e: e.memset(Cst[:], 0.0), writes=[Cst_b])
        S.op("pool", lambda e: e.memset(Csb[:], 0.0), writes=[Csb_b])
        eb3 = eb_t[:].rearrange("p (h c) -> p h c", h=2)
        for c in range(nb):
            tq = c // 4
            cs = slice(c * 128, (c + 1) * 128)
            S.op("pe", lambda e, cs=cs: e.transpose(out=C.ps_bf[:, 0:128], in_=KT[:, cs], identity=identb[:]),
                 reads=[KT_b[tq], cb_], writes=[C.psb[7]])
            for h in range(2):
                hs = slice(h * 64, (h + 1) * 64)
                ix = h * nb + c
                S.op("dve", lambda e, hs=hs, ix=ix: e.tensor_scalar(out=Kp[:, hs], in0=C.ps_bf[:, hs], scalar1=u_t[:, ix:ix + 1], scalar2=0.125,
                                                                    op0=ALU.mult, op1=ALU.mult), reads=[C.psb[7], g_b], writes=[Kp_b])
                S.op("pe", lambda e, hs=hs, h=h, cs=cs: e.matmul(out=C.ps[h][:, 0:128], lhsT=KT[hs, cs], rhs=QT[hs, cs], start=True, stop=True),
                     reads=[KT_b[tq], QT_b[tq]], writes=[C.psb[h]])
                S.op("dve", lambda e, h=h, ix=ix: e.scalar_tensor_tensor(out=ST[h][:], in0=C.ps[h][:, 0:128], scalar=u_t[:, ix:ix + 1], in1=mask8[:],
                                                                         op0=ALU.mult, op1=ALU.mult), reads=[C.psb[h], g_b, cb_], writes=[ST_b[h]])
                S.op("dve", lambda e, hs=hs, h=h, ix=ix: e.tensor_scalar(out=Csb[hs, h * 65:(h + 1) * 65], in0=Cst[hs, :], scalar1=sc_bc[hs, ix:ix + 1],
                                                                         scalar2=None, op0=ALU.mult), reads=[Cst_b, g_b], writes=[Csb_b])
            S.op("pe", lambda e, cs=cs: e.matmul(out=C.ps[2][:, 0:130], lhsT=QT[:, cs], rhs=Csb[:], start=True, stop=False, skip_group_check=True),
                 reads=[QT_b[tq], Csb_b], writes=[C.psb[2]])
            for h in range(2):
                S.op("pe", lambda e, h=h, c=c: e.matmul(out=C.ps[2][:, h * 65:(h + 1) * 65], lhsT=ST[h][:], rhs=Vext[:, c, h, 0:65], start=False, stop=(h == 1),
                                                        skip_group_check=True), reads=[ST_b[h], V_b], writes=[C.psb[2]])
            S.op("pe", lambda e, c=c: e.matmul(out=C.ps[3][:, 0:130], lhsT=Kp[:], rhs=Vext[:, c, :, 0:65], start=True, stop=True),
                 reads=[Kp_b, V_b], writes=[C.psb[3]])
            for h in range(2):
                hs = slice(h * 64, (h + 1) * 64)
                ix = h * nb + c
                S.op("dve", lambda e, hs=hs, h=h, ix=ix: e.scalar_tensor_tensor(out=Cst[hs, :], in0=Cst[hs, :], scalar=sc_bc[hs, ix:ix + 1],
                                                                                in1=C.ps[3][hs, h * 65:(h + 1) * 65], op0=ALU.mult, op1=ALU.add),
                     reads=[Cst_b, g_b, C.psb[3], Csb_b], writes=[Cst_b])
                S.op("dve", lambda e, h=h, ix=ix: e.tensor_scalar(out=nd[:, h, :], in0=C.ps[2][:, h * 65:(h + 1) * 65], scalar1=r_t[:, ix:ix + 1],
                                                                  scalar2=None, op0=ALU.mult), reads=[C.psb[2], g_b], writes=[nd_b])
            S.op("dve", lambda e: e.tensor_scalar(out=sm[:, 0:2], in0=nd[:, :, 64], scalar1=-1.0, scalar2=None, op0=ALU.mult), reads=[nd_b], writes=[sm_b])
            S.op("dve", lambda e: e.tensor_tensor(out=sm[:, 0:2], in0=sm[:, 0:2], in1=nd[:, :, 64], op=ALU.max), reads=[nd_b, sm_b], writes=[sm_b])
            S.op("dve", lambda e, c=c: e.tensor_tensor(out=sm[:, 0:2], in0=sm[:, 0:2], in1=eb3[:, :, c], op=ALU.max), reads=[sm_b, g_b], writes=[sm_b])
            S.op("dve", lambda e: e.reciprocal(out=sm[:, 0:2], in_=sm[:, 0:2]), reads=[sm_b], writes=[sm_b])
            for h in range(2):
                S.op("dve", lambda e, h=h: e.tensor_scalar(out=hid[:, h, :], in0=nd[:, h, 0:64], scalar1=sm[:, h:h + 1], scalar2=None, op0=ALU.mult),
                     reads=[nd_b, sm_b], writes=[hid_b])
                S.op("dve", lambda e, h=h: e.bn_stats(out=stats[:, h, :], in_=hid[:, h, :]), reads=[hid_b], writes=[sm_b])
                S.op("dve", lambda e, h=h: e.bn_aggr(out=mv[:, h, :], in_=stats[:, h, :]), reads=[sm_b], writes=[sm_b])
            S.op("act", lambda e: e.activation(out=sm[:, 2:4], in_=mv[:, :, 1], func=AF.Sqrt, bias=C.eps_t[:], scale=1.0), reads=[sm_b, C.ident_b], writes=[sm_b])
            S.op("dve", lambda e: e.reciprocal(out=sm[:, 2:4], in_=sm[:, 2:4]), reads=[sm_b], writes=[sm_b])
            for h in range(2):
                S.op("dve", lambda e, h=h: e.tensor_scalar(out=hid[:, h, :], in0=hid[:, h, :], scalar1=mv[:, h, 0:1], scalar2=sm[:, 2 + h:3 + h],
                                                           op0=ALU.subtract, op1=ALU.mult), reads=[hid_b, sm_b], writes=[hid_b])
            S.op("dve", lambda e, c=c: e.tensor_tensor(out=hid[:].rearrange("p h d -> p (h d)"), in0=hid[:].rearrange("p h d -> p (h d)"),
                                                        in1=gso[:, c, :], op=ALU.mult), reads=[hid_b, gso_b], writes=[hid_b])
            S.op("pe", lambda e, c=c: e.transpose(out=C.ps[4][:, (c % 4) * 128:(c % 4 + 1) * 128], in_=hid[:].rearrange("p h d -> p (h d)"),
                                                  identity=C.ident[:]), reads=[hid_b, C.ident_b], writes=[C.psb[4]])
            if c % 4 == 3:
                S.op("act", lambda e: e.copy(out=yTt[:], in_=C.ps[4][:]), reads=[C.psb[4]], writes=[yTt_b])
                S.dma("sp", yT[:, (c // 4) * 512:(c // 4 + 1) * 512], yTt[:], reads=[yTt_b], writes=[yT_b])
        S.barrier()


def make_ml_consts_np(c):
    k = np.arange(P)[:, None]
    m = np.arange(P)[None, :]
    c["ntriT"] = -(k <= m).astype(np.float32)
    c["mask8"] = (k <= m).astype(np.float32) * 0.125
    e0 = np.zeros((P, P), np.float32)
    e0[0, :] = 1.0
    c["e0ones"] = e0
    e127 = np.zeros((P, P), np.float32)
    e127[127, :] = 1.0
    c["e127ones"] = e127
    return c


def outproj_ln(C, es, tag, lhs_chunks, lhs_b, wo, wo_b, xt, x_b):
    S = C.S
    for blk in range(4):
        for hh in range(2):
            po = 4 + (blk * 2 + hh) % 2
            for k in range(8):
                S.op("pe", lambda e, k=k, blk=blk, hh=hh, po=po: e.matmul(
                    out=C.ps[po][:], lhsT=lhs_chunks[:, k, blk * 128:(blk + 1) * 128], rhs=wo[:, k, hh * 512:(hh + 1) * 512],
                    start=(k == 0), stop=(k == 7)), reads=[lhs_b, wo_b], writes=[C.psb[po]])
            S.op("dve", lambda e, blk=blk, hh=hh, po=po: e.tensor_tensor(
                out=xt[:, blk, hh * 512:(hh + 1) * 512], in0=C.ps[po][:], in1=xt[:, blk, hh * 512:(hh + 1) * 512], op=ALU.add),
                reads=[C.psb[po], x_b], writes=[x_b])


def mixout_phase(C, x_in, xin_b, ygath, yg_b, w_out, sel_dram, g_dram, b_dram, x_out, xout_b, NT, TT, tag):
    nc, S = C.nc, C.S
    ntile = NT // 512
    with contextlib.ExitStack() as es:
        wo, wo_b = load_w_bf16(C, es, f"wmo{tag}", w_out, D)
        sb = lambda name, shape, dt=F32: es.enter_context(nc.sbuf_tensor(f"mo{name}{tag}", shape, dt))
        g_bc, b_bc, sel = sb("g", [P, D]), sb("b", [P, D]), sb("sel", [P, 2])
        xts = [sb(f"xt{i}", [P, 4, D]) for i in range(2)]
        yAs = [sb(f"yA{i}", [P, 8, 512], BF16) for i in range(2)]
        yBs = [sb(f"yB{i}", [P, 8, 512], BF16) for i in range(2)]
        stats, mv, rstd = sb("stats", [P, 4, 2, 6]), sb("mv", [P, 4, 2]), sb("rstd", [P, 4, 1])
        gb_b, st_b = Buf(), Buf()
        x_bs, yA_bs, yB_bs = [Buf(), Buf()], [Buf(), Buf()], [Buf(), Buf()]
        S.dma("sp", g_bc[:], g_dram, writes=[gb_b])
        S.dma("sp", b_bc[:], b_dram, writes=[gb_b])
        S.dma("sp", sel[:], sel_dram, writes=[gb_b])
        xin_v = x_in.rearrange("(t b p) d -> t p b d", b=4, p=P)
        xout_v = x_out.rearrange("(t b p) d -> t p b d", b=4, p=P)
        yv = ygath.rearrange("(k p) t -> p k t", p=P)

        def loads(t):
            k = t % 2
            S.dma("sp", xts[k][:], xin_v[t], reads=[xin_b[t] if isinstance(xin_b, list) else xin_b], writes=[x_bs[k]])
            S.dma("sp", yAs[k][:], yv[:, :, t * 512:(t + 1) * 512], reads=[yg_b], writes=[yA_bs[k]])
            S.dma("sp", yBs[k][:], yv[:, :, NT + t * 512:NT + (t + 1) * 512], reads=[yg_b], writes=[yB_bs[k]])
        loads(0)
        for t in range(ntile):
            k = t % 2
            xt, x_b, yA, yA_b, yB, yB_b = xts[k], x_bs[k], yAs[k], yA_bs[k], yBs[k], yB_bs[k]
            if t + 1 < ntile:
                loads(t + 1)
            S.op("act", lambda e, xt=xt: e.mul(out=xt[:], in_=xt[:], mul=ALPHA), reads=[x_b], writes=[x_b])
            S.op("dve", lambda e, yA=yA: e.tensor_scalar(out=yA[:], in0=yA[:], scalar1=sel[:, 0:1], scalar2=None, op0=ALU.mult),
                 reads=[yA_b, gb_b], writes=[yA_b])
            S.op("dve", lambda e, yA=yA, yB=yB: e.scalar_tensor_tensor(out=yA[:], in0=yB[:], scalar=sel[:, 1:2], in1=yA[:], op0=ALU.mult, op1=ALU.add),
                 reads=[yA_b, yB_b, gb_b], writes=[yA_b])
            outproj_ln(C, es, tag, yA, yA_b, wo, wo_b, xt, x_b)
            layer_norm_tile(C, xt, x_b, g_bc, b_bc, gb_b, stats, mv, rstd, st_b)
            S.dma("pool", xout_v[t], xt[:], reads=[x_b], writes=[xout_b])
        S.barrier()


def xattn_phase(C, x_in, xin_b, mem, w_q, w_kv, w_o, g_dram, b_dram, x_out, xout_b, NT, tag):
    nc, S = C.nc, C.S
    ntile = NT // 512
    C.set_psum(2)
    with contextlib.ExitStack() as es:
        wq, wq_b = load_w_bf16(C, es, f"wxq{tag}", w_q, D)
        wkv, wkv_b = load_w_bf16(C, es, f"wxkv{tag}", w_kv, 2 * D)
        wo, wo_b = load_w_bf16(C, es, f"wxo{tag}", w_o, D)
        sb = lambda name, shape, dt=F32: es.enter_context(nc.sbuf_tensor(f"xa{name}{tag}", shape, dt))
        g_bc, b_bc = sb("g", [P, D]), sb("b", [P, D])
        identb = sb("idb", [P, P], BF16)
        xt = sb("xt", [P, 4, D])
        xT = sb("xT", [P, 8, 512], BF16)
        QT = sb("QT", [P, 8, 512], BF16)
        KTx = sb("KT", [P, 8, 256], BF16)
        Vx = sb("V", [P, 2, D], BF16)
        aoT = sb("aoT", [P, 8, 512], BF16)
        sc = sb("sc", [P, 256])
        pf = sb("pf", [P, 256])
        pn = sb("pn", [P, 256], BF16)
        pT = sb("pT", [P, 2, 128], BF16)
        sm = sb("sm", [P, 4])
        stats, mv, rstd = sb("stats", [P, 4, 2, 6]), sb("mv", [P, 4, 2]), sb("rstd", [P, 4, 1])
        gb_b, x_b, xT_b, QT_b, KT_b, V_b, ao_b, st_b = Buf(), Buf(), Buf(), Buf(), Buf(), Buf(), Buf(), Buf()
        sc_b, pf_b, pn_b, pT_b, sm_b = Buf(), Buf(), Buf(), Buf(), Buf()
        pf2 = [pf, sb("pfb", [P, 256])]
        pn2 = [pn, sb("pnb", [P, 256], BF16)]
        pT2 = [pT, sb("pTb", [P, 2, 128], BF16)]
        sm2 = [sm, sb("smb", [P, 4])]
        pf_b2, pn_b2, pT_b2, sm_b2 = [Buf(), Buf()], [Buf(), Buf()], [Buf(), Buf()], [Buf(), Buf()]
        pf4 = [sb(f"pf4{k}", [P, 4, 256]) for k in range(2)]
        pn4 = [sb(f"pn4{k}", [P, 4, 256], BF16) for k in range(2)]
        pT4 = [sb(f"pT4{k}", [P, 8, 128], BF16) for k in range(2)]
        sm4 = [sb(f"sm4{k}", [P, 4, 4]) for k in range(2)]
        sc_pb, bf_pb, pv_pb = [Buf(), Buf()], [Buf(), Buf()], [Buf(), Buf()]
        S.dma("sp", g_bc[:], g_dram, writes=[gb_b])
        S.dma("sp", b_bc[:], b_dram, writes=[gb_b])
        S.dma("pool", identb[:], C.ident_dram, writes=[gb_b])
        S.dma("sp", xt[:, 0:2, :], mem.rearrange("(b p) d -> p b d", p=P), writes=[x_b])
        for kc in range(8):
            for blk in range(2):
                S.op("pe", lambda e, kc=kc, blk=blk: e.transpose(out=C.ps[4][:, blk * 128:(blk + 1) * 128], in_=xt[:, blk, kc * 128:(kc + 1) * 128],
                                                                 identity=C.ident[:]), reads=[x_b, C.ident_b], writes=[C.psb[4]])
            S.op("dve", lambda e, kc=kc: e.tensor_copy(out=xT[:, kc, 0:256], in_=C.ps[4][:, 0:256]), reads=[C.psb[4]], writes=[xT_b])
        for j in range(8):
            pi = j % 2
            for kc in range(8):
                S.op("pe", lambda e, kc=kc, j=j, pi=pi: e.matmul(out=C.ps[pi][:, 0:256], lhsT=wkv[:, kc, j * 128:(j + 1) * 128], rhs=xT[:, kc, 0:256],
                                                                 start=(kc == 0), stop=(kc == 7)), reads=[wkv_b, xT_b], writes=[C.psb[pi]])
            S.op("dve", lambda e, j=j, pi=pi: e.tensor_copy(out=KTx[:, j, :], in_=C.ps[pi][:, 0:256]), reads=[C.psb[pi]], writes=[KT_b])
        for blk in range(2):
            for hh in range(2):
                pi = 2 + hh
                for kc in range(8):
                    S.op("pe", lambda e, kc=kc, blk=blk, hh=hh, pi=pi: e.matmul(
                        out=C.ps[pi][:], lhsT=xT[:, kc, blk * 128:(blk + 1) * 128], rhs=wkv[:, kc, D + hh * 512:D + (hh + 1) * 512],
                        start=(kc == 0), stop=(kc == 7)), reads=[wkv_b, xT_b], writes=[C.psb[pi]])
                S.op("act", lambda e, blk=blk, hh=hh, pi=pi: e.copy(out=Vx[:, blk, hh * 512:(hh + 1) * 512], in_=C.ps[pi][:]),
                     reads=[C.psb[pi]], writes=[V_b])
        xin_v = x_in.rearrange("(t b p) d -> t p b d", b=4, p=P)
        xout_v = x_out.rearrange("(t b p) d -> t p b d", b=4, p=P)
        xts = [xt, sb("xt2", [P, 4, D])]
        x_bs = [x_b, Buf()]
        QTs = [QT, sb("QT2", [P, 8, 512], BF16)]
        QT_bs = [QT_b, Buf()]

        def emit_load(t):
            S.dma("sp", xts[t % 2][:], xin_v[t], reads=[xin_b[t] if isinstance(xin_b, list) else xin_b], writes=[x_bs[t % 2]])

        def emit_T(t):
            load_xT(C, xts[t % 2], x_bs[t % 2], xT, xT_b, ps_ids=(0,))
            S.op("act", lambda e, t=t: e.mul(out=xts[t % 2][:], in_=xts[t % 2][:], mul=ALPHA), reads=[x_bs[t % 2]], writes=[x_bs[t % 2]])

        def emit_qproj(t, j):
            proj_fm(C, wq, wq_b, j * 128, (j + 1) * 128, xT, xT_b, 1)
            S.op("act" if j % 2 == 0 else "dve",
                 (lambda e, j=j, t=t: e.mul(out=QTs[t % 2][:, j, :], in_=C.ps[1][:], mul=0.0625)) if j % 2 == 0 else
                 (lambda e, j=j, t=t: e.tensor_scalar(out=QTs[t % 2][:, j, :], in0=C.ps[1][:], scalar1=0.0625, scalar2=None, op0=ALU.mult)),
                 reads=[C.psb[1]], writes=[QT_bs[t % 2]])

        emit_load(0)
        emit_T(0)
        for j in range(8):
            emit_qproj(0, j)
        for t in range(ntile):
            xt, x_b, QT, QT_b = xts[t % 2], x_bs[t % 2], QTs[t % 2], QT_bs[t % 2]
            if t + 1 < ntile:
                emit_load(t + 1)
            for hd in range(4):
                a = hd % 2
                if t + 1 < ntile:
                    if hd == 2:
                        emit_T(t + 1)
                    elif hd == 3:
                        for j in range(8):
                            emit_qproj(t + 1, j)
                for b in range(4):
                    ts = slice(b * 128, (b + 1) * 128)
                    for kk in range(2):
                        S.op("pe", lambda e, hd=hd, kk=kk, ts=ts, b=b: e.matmul(out=C.ps[2 + b // 2][:, (b % 2) * 256:(b % 2 + 1) * 256],
                                                                               lhsT=QT[:, 2 * hd + kk, ts], rhs=KTx[:, 2 * hd + kk, :],
                                                                               start=(kk == 0), stop=(kk == 1), skip_group_check=True),
                             reads=[QT_b, KT_b], writes=[C.psb[2 + b // 2]])
                for bank in range(2):
                    S.op("dve", lambda e, bank=bank, a=a: e.tensor_reduce(out=sm4[a][:, 0, 2 * bank:2 * bank + 2],
                                                                         in_=C.ps[2 + bank][:].rearrange("p (b k) -> p b k", b=2), axis=AX.X, op=ALU.max),
                         reads=[C.psb[2 + bank]], writes=[sm_b2[a]])
                S.op("dve", lambda e, a=a: e.tensor_scalar(out=sm4[a][:, 1, :], in0=sm4[a][:, 0, :], scalar1=-1.0, scalar2=None, op0=ALU.mult),
                     reads=[sm_b2[a]], writes=[sm_b2[a]])
                for b in range(4):
                    S.op("act", lambda e, b=b, a=a: e.activation(out=pf4[a][:, b, :], in_=C.ps[2 + b // 2][:, (b % 2) * 256:(b % 2 + 1) * 256], func=AF.Exp,
                                                                 bias=sm4[a][:, 1, b:b + 1], scale=1.0, accum_out=sm4[a][:, 2, b:b + 1]),
                         reads=[C.psb[2 + b // 2], sm_b2[a]], writes=[pf_b2[a], sm_b2[a]])
                S.op("dve", lambda e, a=a: e.reciprocal(out=sm4[a][:, 3, :], in_=sm4[a][:, 2, :]), reads=[sm_b2[a]], writes=[sm_b2[a]])
                for b in range(4):
                    if b % 2 == 0:
                        S.op("dve", lambda e, b=b, a=a: e.tensor_scalar(out=pn4[a][:, b, :], in0=pf4[a][:, b, :], scalar1=sm4[a][:, 3, b:b + 1],
                                                                        scalar2=None, op0=ALU.mult), reads=[pf_b2[a], sm_b2[a]], writes=[pn_b2[a]])
                    else:
                        S.op("act", lambda e, b=b, a=a: e.mul(out=pn4[a][:, b, :], in_=pf4[a][:, b, :], mul=sm4[a][:, 3, b:b + 1]),
                             reads=[pf_b2[a], sm_b2[a]], writes=[pn_b2[a]])
                for b in range(4):
                    for mb in range(2):
                        S.op("pe", lambda e, b=b, mb=mb, a=a: e.transpose(out=C.ps_bfs[a][:, (b * 2 + mb) * 128:(b * 2 + mb + 1) * 128],
                                                                          in_=pn4[a][:, b, mb * 128:(mb + 1) * 128], identity=identb[:]),
                             reads=[pn_b2[a], gb_b], writes=[C.psbf_b[a]])
                S.op("act", lambda e, a=a: e.copy(out=pT4[a][:, 0:4, :].rearrange("p a b -> p (a b)"), in_=C.ps_bfs[a][:, 0:512]),
                     reads=[C.psbf_b[a]], writes=[pT_b2[a]])
                S.op("dve", lambda e, a=a: e.tensor_copy(out=pT4[a][:, 4:8, :].rearrange("p a b -> p (a b)"), in_=C.ps_bfs[a][:, 512:1024]),
                     reads=[C.psbf_b[a]], writes=[pT_b2[a]])
                for cc in range(2):
                    for b in range(4):
                        for mb in range(2):
                            S.op("pe", lambda e, cc=cc, mb=mb, hd=hd, b=b, a=a: e.matmul(out=C.ps[4 + cc][:, b * 128:(b + 1) * 128],
                                                                                         lhsT=Vx[:, mb, hd * 256 + cc * 128:hd * 256 + (cc + 1) * 128],
                                                                                         rhs=pT4[a][:, b * 2 + mb, :], start=(mb == 0), stop=(mb == 1),
                                                                                         skip_group_check=True),
                                 reads=[V_b, pT_b2[a]], writes=[C.psb[4 + cc]])
                S.op("act", lambda e, hd=hd: e.copy(out=aoT[:, 2 * hd, :], in_=C.ps[4][:]), reads=[C.psb[4]], writes=[ao_b])
                S.op("dve", lambda e, hd=hd: e.tensor_copy(out=aoT[:, 2 * hd + 1, :], in_=C.ps[5][:]), reads=[C.psb[5]], writes=[ao_b])
            outproj_ln(C, es, tag, aoT, ao_b, wo, wo_b, xt, x_b)
            layer_norm_tile(C, xt, x_b, g_bc, b_bc, gb_b, stats, mv, rstd, st_b)
            S.dma("sp", xout_v[t], xt[:], reads=[x_b], writes=[xout_b])
        S.barrier()
    C.set_psum(1)


NT_OWN = 4096
TT_SEQ = 8192
DEPTH = 2
PAIRS = [[0, 1], [2, 3], [4, 5], [6, 7]]
_SPLITS = np.cumsum([0, 256, 256, 256, 512, 128, 128, 512, 256, 4, 4, 256])


def build_program():
    from concourse.bass_utils import run_bass_kernel_spmd
    nc = bass.Bass("TRN2", target_bir_lowering=False)
    din = lambda name, shape, dt=F32: nc.dram_tensor(name, shape, dt, kind="ExternalInput").ap()
    dscr = lambda name, shape, dt=F32: nc.dram_tensor(name, shape, dt, kind="Internal").ap()
    x = din("x", [NT_OWN, D])
    mem = din("mem", [256, D])
    sel = din("sel", [P, 2])
    cn = make_consts_np()
    cd = {k: din("c_" + k, list(v.shape)) for k, v in cn.items()}
    L = []
    for l in range(DEPTH):
        d = {}
        d["ffn1_in"] = din(f"l{l}_ffn1_in", [D, 2 * DFF]); d["ffn1_out"] = din(f"l{l}_ffn1_out", [DFF, D])
        d["ffn2_in"] = din(f"l{l}_ffn2_in", [D, 2 * DFF]); d["ffn2_out"] = din(f"l{l}_ffn2_out", [DFF, D])
        d["w_sb"] = din(f"l{l}_w_sb", [D, 384]); d["w_swa"] = din(f"l{l}_w_swa", [D, 448]); d["w_ml"] = din(f"l{l}_w_ml", [D, 640])
        d["cw"] = din(f"l{l}_cw", [P, 8]); d["cb"] = din(f"l{l}_cb", [P, 2]); d["ifb"] = din(f"l{l}_ifb", [P, 4]); d["ng"] = din(f"l{l}_ng", [P, P])
        d["rb"] = din(f"l{l}_rb", [P, P]); d["sink"] = din(f"l{l}_sink", [P, 4])
        d["w_mo"] = din(f"l{l}_w_mo", [D, D])
        d["xq"] = din(f"l{l}_xq", [D, D]); d["xkv"] = din(f"l{l}_xkv", [D, 2 * D]); d["xo"] = din(f"l{l}_xo", [D, D])
        d["lng"] = [din(f"l{l}_lng{i}", [P, D]) for i in range(4)]
        d["lnb"] = [din(f"l{l}_lnb{i}", [P, D]) for i in range(4)]
        L.append(d)
    out = nc.dram_tensor("out", [NT_OWN, D], F32, kind="ExternalOutput").ap()
    Xa = dscr("Xa", [NT_OWN, D]); Xb = dscr("Xb", [NT_OWN, D])
    X1g = dscr("X1g", [NT_OWN // 512, 2 * D, 512], BF16)
    XaT = dscr("XaT", [NT_OWN // 512, D, 512], BF16)
    YTo = dscr("YTo", [512, TT_SEQ], BF16)
    Yg = dscr("Yg", [1024, TT_SEQ], BF16)
    frs = dscr("frs", [4, 384])
    S = Sched(nc)
    C = Ctx(nc, S)
    C.load_consts(cd["ident"])
    x_b, Xa_b, Xb_b, X1g_b, YTo_b, Yg_b, out_b = Buf(), Buf(), Buf(), Buf(), [Buf(), Buf(), Buf()], Buf(), Buf()
    cur, cur_b = x, x_b
    import os
    kstop = int(os.environ.get("KSTOP", "99"))
    for l in range(DEPTH):
        d = L[l]
        tg = f"L{l}"
        if kstop < 99 and l > 0:
            break
        Xa_tb = [Buf() for _ in range(NT_OWN // 512)]
        XaT_tb = [Buf() for _ in range(NT_OWN // 512)]
        X1g_tb = [Buf() for _ in range(TT_SEQ // 512)]
        nch = NT_OWN // 512

        def ag1(t):
            S.custom("pool", lambda e, t=t: e.collective_compute("AllGather", ALU.bypass, replica_groups=PAIRS,
                                                                 ins=[XaT[t]], outs=[X1g[t]]), 1,
                     reads=[XaT_tb[t]], writes=[X1g_tb[t], X1g_tb[nch + t]])
        ffn_phase(C, cur, Xa, d["ffn1_in"], d["ffn1_out"], d["lng"][0], d["lnb"][0], NT_OWN, tg + "f1", cur_b, Xa_tb, after_tile=ag1,
                  xT_out=XaT, xT_out_b=XaT_tb)
        if kstop <= 2:
            break
        X1v = X1g.rearrange("j (r kc p) n -> j r p kc n", r=2, p=P)
        xtile = lambda t: X1v[t % nch, t // nch]
        sb_phase(C, xtile, X1g_tb, d["w_sb"], YTo[0:128, :], YTo_b[0], TT_SEQ, cd, tg + "sb")
        S.custom("pool", lambda e: e.collective_compute("AllGather", ALU.bypass, replica_groups=PAIRS, ins=[YTo[0:128, :]], outs=[Yg[0:256, :]]), 1,
                 reads=[YTo_b[0]], writes=[Yg_b])
        if kstop <= 3:
            break
        swa_phase(C, xtile, X1g_tb, d["w_swa"], d["rb"], d["sink"], frs, YTo[128:384, :], YTo_b[1], TT_SEQ, cd, tg + "sw")
        for k in (1, 2):
            S.custom("pool", lambda e, k=k: e.collective_compute("AllGather", ALU.bypass, replica_groups=PAIRS, ins=[YTo[k * 128:(k + 1) * 128, :]],
                                                                 outs=[Yg[k * 256:(k + 1) * 256, :]]), 1, reads=[YTo_b[1]], writes=[Yg_b])
        if kstop <= 4:
            break
        mlstm_phase(C, xtile, X1g_tb, d["w_ml"], d["cw"], d["cb"], d["ifb"], d["ng"], YTo[384:512, :], YTo_b[2], TT_SEQ, cd, tg + "ml")
        S.custom("pool", lambda e: e.collective_compute("AllGather", ALU.bypass, replica_groups=PAIRS, ins=[YTo[384:512, :]], outs=[Yg[768:1024, :]]), 1,
                 reads=[YTo_b[2]], writes=[Yg_b])
        S.barrier()
        if kstop <= 6:
            break
        mixout_phase(C, Xa, Xa_tb, Yg, Yg_b, d["w_mo"], sel, d["lng"][1], d["lnb"][1], Xb, Xb_b, NT_OWN, TT_SEQ, tg + "mo")
        if kstop <= 7:
            break
        xattn_phase(C, Xb, Xb_b, mem, d["xq"], d["xkv"], d["xo"], d["lng"][2], d["lnb"][2], Xa, Xa_b, NT_OWN, tg + "xa")
        if kstop <= 8:
            break
        dst, dst_b = (out, out_b) if l == DEPTH - 1 else (Xb, Xb_b)
        ffn_phase(C, Xa, dst, d["ffn2_in"], d["ffn2_out"], d["lng"][3], d["lnb"][3], NT_OWN, tg + "f2", Xa_b, dst_b)
        cur, cur_b = dst, dst_b
    S.finish()
    C.psum_es.close()
    return nc, cn


def make_core_inputs(c, inp, cn):
    b, r = c // 2, c % 2
    f = lambda a: np.ascontiguousarray(np.asarray(a, dtype=np.float32))
    rep = lambda v: f(np.tile(np.asarray(v, np.float32)[None, :], (P, 1)))
    m = {"x": f(inp["x"][b, r * NT_OWN:(r + 1) * NT_OWN]), "mem": f(inp["mem"][b])}
    selv = np.zeros((P, 2), np.float32)
    selv[:, r] = 1.0
    m["sel"] = selv
    for k, v in cn.items():
        m["c_" + k] = f(v)
    sp = _SPLITS
    slot = [0, 2, 1, 3]
    for l in range(DEPTH):
        w = np.asarray(inp["mix_w_in"][l], np.float32)
        seg = [w[:, sp[i]:sp[i + 1]] for i in range(11)]
        sbq, sbk, sbv, swq, swk, swv, mlqk, mlv, ig, fg, og = seg
        mlq, mlk = mlqk[:, 0:256], mlqk[:, 256:512]
        m[f"l{l}_ffn1_in"] = f(inp["ffn1_w_in"][l]); m[f"l{l}_ffn1_out"] = f(inp["ffn1_w_out"][l])
        m[f"l{l}_ffn2_in"] = f(inp["ffn2_w_in"][l]); m[f"l{l}_ffn2_out"] = f(inp["ffn2_w_out"][l])
        h2 = slice(r * 128, (r + 1) * 128)
        m[f"l{l}_w_sb"] = f(np.concatenate([sbq[:, h2], sbk[:, h2], sbv[:, h2]], 1))
        kk = swk[:, r * 64:(r + 1) * 64]
        m[f"l{l}_w_swa"] = f(np.concatenate([swq[:, r * 256:(r + 1) * 256], kk, kk, swv[:, r * 64:(r + 1) * 64]], 1))
        m[f"l{l}_w_ml"] = f(np.concatenate([mlq[:, h2], mlk[:, h2], mlv[:, h2], og[:, h2], ig[:, 2 * r:2 * r + 2], fg[:, 2 * r:2 * r + 2],
                                            np.zeros((D, 124), np.float32)], 1))
        cwl = np.asarray(inp["ml_conv_w"][l], np.float32)
        cbl = np.asarray(inp["ml_conv_b"][l], np.float32)
        m[f"l{l}_cw"] = f(np.concatenate([cwl[:, r * 128:(r + 1) * 128].T, cwl[:, 256 + r * 128:256 + (r + 1) * 128].T], 1))
        m[f"l{l}_cb"] = f(np.stack([cbl[r * 128:(r + 1) * 128], cbl[256 + r * 128:256 + (r + 1) * 128]], 1))
        ib = np.asarray(inp["ml_i_bias"][l], np.float32)[2 * r:2 * r + 2]
        fb = np.asarray(inp["ml_f_bias"][l], np.float32)[2 * r:2 * r + 2]
        m[f"l{l}_ifb"] = rep(np.concatenate([ib, fb]))
        m[f"l{l}_ng"] = rep(np.asarray(inp["ml_norm_g"][l], np.float32)[h2])
        rbl = np.asarray(inp["rel_bias"], np.float32)[:, 4 * r:4 * r + 4][:, slot]
        m[f"l{l}_rb"] = f(np.pad(rbl, ((0, 96), (0, 124))))
        m[f"l{l}_sink"] = rep(np.asarray(inp["swa_sinks"][l], np.float32)[4 * r:4 * r + 4][slot])
        wo = np.asarray(inp["mix_w_out"][l], np.float32)
        rows = []
        for k in range(4):
            for rr in range(2):
                base = [rr * 128, 256 + rr * 256, 256 + rr * 256 + 128, 768 + rr * 128][k]
                rows.append(wo[base:base + 128])
        m[f"l{l}_w_mo"] = f(np.concatenate(rows, 0))
        m[f"l{l}_xq"] = f(inp["xattn_w_q"][l]); m[f"l{l}_xkv"] = f(inp["xattn_w_kv"][l]); m[f"l{l}_xo"] = f(inp["xattn_w_o"][l])
        for i in range(4):
            m[f"l{l}_lng{i}"] = rep(inp["ln_g"][l][i]); m[f"l{l}_lnb{i}"] = rep(inp["ln_b"][l][i])
    return m


def kernel(**inputs):
    from concourse.bass_utils import run_bass_kernel_spmd
    inp = {k: np.asarray(v) for k, v in inputs.items()}
    nc, cn = build_program()
    in_maps = [make_core_inputs(c, inp, cn) for c in range(8)]
    res = run_bass_kernel_spmd(nc, in_maps, core_ids=list(range(8)))
    out = np.zeros((4, TT_SEQ, D), np.float32)
    for c in range(8):
        b, r = c // 2, c % 2
        out[b, r * NT_OWN:(r + 1) * NT_OWN] = np.asarray(res.results[c]["out"], np.float32)
    return out
```

```python
import contextlib
import numpy as np
import concourse.bass as bass
import concourse.mybir as mybir

F32 = mybir.dt.float32
BF16 = mybir.dt.bfloat16
ALU = mybir.AluOpType
AF = mybir.ActivationFunctionType
AX = mybir.AxisListType


class Buf:
    __slots__ = ("name", "w", "r")

    def __init__(self, name=""):
        self.name = name
        self.w = None
        self.r = []


class Sched:
    EPOCH = 20000

    def __init__(self, nc, n_dma_sems=24):
        self.nc = nc
        self.es = contextlib.ExitStack()
        self.eng = {"pe": nc.tensor, "act": nc.scalar, "dve": nc.vector,
                    "pool": nc.gpsimd, "sp": nc.sync}
        self.sem = {}
        self.cnt = {}
        self.nsem = 0
        for e in self.eng:
            self._new_epoch(e)
        half = n_dma_sems // 2
        self.dma_sems = [self.es.enter_context(nc.semaphore(f"dq{i}")) for i in range(n_dma_sems)]
        self.dma_cnt = [0] * n_dma_sems
        self.dma_pool = {"sp": list(range(0, half)), "pool": list(range(half, n_dma_sems)), "act": list(range(0, half))}
        self.dma_next = {"sp": 0, "pool": 0, "act": 0}
        self.waited = {e: {} for e in self.eng}
        self.last_ev = {e: None for e in self.eng}
        self.out_events = []
        self.n_ops = 0
        self.n_waits = 0

    def _new_epoch(self, e):
        self.sem[e] = self.es.enter_context(self.nc.semaphore(f"s_{e}_{self.nsem}"))
        self.nsem += 1
        self.cnt[e] = 0

    def _wait(self, e, ev):
        if ev is None:
            return
        sem, val, src = ev
        if src == "pe" and e == "pe":
            return
        key = id(sem)
        if self.waited[e].get(key, 0) >= val:
            return
        self.eng[e].wait_ge(sem, val)
        self.n_waits += 1
        self.waited[e][key] = val

    def _deps(self, e, reads, writes):
        for b in reads:
            self._wait(e, b.w)
        for b in writes:
            self._wait(e, b.w)
            for ev in b.r:
                self._wait(e, ev)

    def _commit(self, ev, reads, writes):
        for b in reads:
            b.r.append(ev)
            if len(b.r) > 12:
                latest = {}
                for x in b.r:
                    k = id(x[0])
                    if k not in latest or latest[k][1] < x[1]:
                        latest[k] = x
                b.r = list(latest.values())
        for b in writes:
            b.w = ev
            b.r = []

    def op(self, e, fn, reads=(), writes=()):
        self._deps(e, reads, writes)
        if self.cnt[e] >= self.EPOCH:
            self._new_epoch(e)
        ins = fn(self.eng[e])
        self.cnt[e] += 1
        ins.then_inc(self.sem[e], 1)
        ev = (self.sem[e], self.cnt[e], e)
        self.last_ev[e] = ev
        self._commit(ev, reads, writes)
        self.n_ops += 1
        return ev

    def dma(self, q, out, in_, reads=(), writes=(), **kw):
        self._deps(q, reads, writes)
        pl = self.dma_pool[q]
        k = pl[self.dma_next[q]]
        self.dma_next[q] = (self.dma_next[q] + 1) % len(pl)
        sem = self.dma_sems[k]
        if self.dma_cnt[k] > 0:
            self._wait(q, (sem, self.dma_cnt[k], "dma"))
        self.dma_cnt[k] += 16
        self.eng[q].dma_start(out=out, in_=in_, **kw).then_inc(sem, 16)
        ev = (sem, self.dma_cnt[k], "dma")
        self._commit(ev, reads, writes)
        self.n_ops += 1
        return ev

    def custom(self, q, fn, inc, reads=(), writes=()):
        self._deps(q, reads, writes)
        pl = self.dma_pool[q]
        k = pl[self.dma_next[q]]
        self.dma_next[q] = (self.dma_next[q] + 1) % len(pl)
        sem = self.dma_sems[k]
        if self.dma_cnt[k] > 0:
            self._wait(q, (sem, self.dma_cnt[k], "dma"))
        self.dma_cnt[k] += inc
        fn(self.eng[q]).then_inc(sem, inc)
        ev = (sem, self.dma_cnt[k], "dma")
        self._commit(ev, reads, writes)
        return ev

    def barrier(self, bufs=()):
        evs = [ev for ev in self.last_ev.values() if ev is not None]
        for k, sem in enumerate(self.dma_sems):
            if self.dma_cnt[k] > 0:
                evs.append((sem, self.dma_cnt[k], "dma"))
        for e in self.eng:
            for ev in evs:
                if ev[2] == e and e == "pe":
                    continue
                self._wait(e, ev)

    def finish(self):
        self.barrier()
        self.es.close()


D = 1024
DFF = 2816
ALPHA = 4 ** 0.25
LN_EPS = 1e-5
P = 128


class Ctx:
    def __init__(self, nc, S):
        self.nc = nc
        self.S = S
        es = S.es
        self.psum_es = None
        self.psum_gen = 0
        self.set_psum(1)
        self.ident = es.enter_context(nc.sbuf_tensor("ident_sb", [P, P], F32))
        self.ident_b = Buf("ident")

        self.eps_t = es.enter_context(nc.sbuf_tensor("eps_t", [P, 1], F32))

    def set_psum(self, n_bf=1):
        if self.psum_es is not None:
            self.psum_es.close()
        self.psum_es = contextlib.ExitStack()
        g = self.psum_gen
        self.psum_gen += 1
        nf = 8 - n_bf
        self.ps = [self.psum_es.enter_context(self.nc.psum_tensor(f"psf{g}_{i}", [P, 512], F32)) for i in range(nf)]
        self.psb = [Buf(f"ps{i}") for i in range(nf)]
        self.ps_bfs = [self.psum_es.enter_context(self.nc.psum_tensor(f"psh{g}_{i}", [P, 1024], BF16)) for i in range(n_bf)]
        self.psbf_b = [Buf(f"psbf{i}") for i in range(n_bf)]
        self.ps_bf = self.ps_bfs[0]
        while len(self.ps) < 8:
            self.ps.append(None)
            self.psb.append(self.psbf_b[0])

    def load_consts(self, ident_dram):
        self.ident_dram = ident_dram
        self.S.dma("sp", self.ident[:], ident_dram, writes=[self.ident_b])
        self.S.op("pool", lambda e: e.memset(self.eps_t[:], LN_EPS), writes=[self.ident_b])


def load_xT(C, x_tile, x_b, xT, xT_b, ps_ids, evac_engs=("act", "dve")):
    S = C.S
    for kc in range(8):
        pi = ps_ids[kc % len(ps_ids)]
        ps, pb = C.ps[pi], C.psb[pi]
        for blk in range(4):
            S.op("pe", lambda e, kc=kc, blk=blk, ps=ps: e.transpose(
                out=ps[:, blk * 128:(blk + 1) * 128], in_=x_tile[:, blk, kc * 128:(kc + 1) * 128],
                identity=C.ident[:]), reads=[x_b, C.ident_b], writes=[pb])
        eng = evac_engs[kc % len(evac_engs)]
        if eng == "act":
            S.op("act", lambda e, kc=kc, ps=ps: e.copy(out=xT[:, kc, :], in_=ps[:]), reads=[pb], writes=[xT_b])
        else:
            S.op("dve", lambda e, kc=kc, ps=ps: e.tensor_copy(out=xT[:, kc, :], in_=ps[:]), reads=[pb], writes=[xT_b])


def layer_norm_tile(C, xt, x_b, g_bc, b_bc, gb_b, stats, mv, rstd, st_b, nblk=4):
    S = C.S
    for blk in range(nblk):
        for hh in range(2):
            S.op("dve", lambda e, blk=blk, hh=hh: e.bn_stats(out=stats[:, blk, hh, :], in_=xt[:, blk, hh * 512:(hh + 1) * 512]),
                 reads=[x_b], writes=[st_b])
        S.op("dve", lambda e, blk=blk: e.bn_aggr(out=mv[:, blk, :], in_=stats[:, blk, :, :]), reads=[st_b], writes=[st_b])
        S.op("act", lambda e, blk=blk: e.activation(out=rstd[:, blk, :], in_=mv[:, blk, 1:2], func=AF.Sqrt, bias=C.eps_t[:], scale=1.0),
             reads=[st_b, C.ident_b], writes=[st_b])
        S.op("dve", lambda e, blk=blk: e.reciprocal(out=rstd[:, blk, :], in_=rstd[:, blk, :]), reads=[st_b], writes=[st_b])
        S.op("dve", lambda e, blk=blk: e.tensor_scalar(out=xt[:, blk, :], in0=xt[:, blk, :], scalar1=mv[:, blk, 0:1],
                                                       scalar2=rstd[:, blk, :], op0=ALU.subtract, op1=ALU.mult),
             reads=[st_b, x_b], writes=[x_b])
        S.op("dve", lambda e, blk=blk: e.tensor_tensor(out=xt[:, blk, :], in0=xt[:, blk, :], in1=g_bc[:], op=ALU.mult),
             reads=[x_b, gb_b], writes=[x_b])
        S.op("pool", lambda e, blk=blk: e.tensor_tensor(out=xt[:, blk, :], in0=xt[:, blk, :], in1=b_bc[:], op=ALU.add),
             reads=[x_b, gb_b], writes=[x_b])


def ffn_phase(C, x_in, x_out, w_in, w_out, g_dram, b_dram, NT, tag, xin_b, xout_b, after_tile=None, xT_out=None, xT_out_b=None):
    nc, S = C.nc, C.S
    ntile = NT // 512
    JG = [(0, 6), (6, 12), (12, 17), (17, 22)]
    with contextlib.ExitStack() as es:
        w1 = es.enter_context(nc.sbuf_tensor(f"w1{tag}", [P, 8, 2 * DFF], BF16))
        w2 = es.enter_context(nc.sbuf_tensor(f"w2{tag}", [P, 22, D], BF16))
        g_bc = es.enter_context(nc.sbuf_tensor(f"g{tag}", [P, D], F32))
        b_bc = es.enter_context(nc.sbuf_tensor(f"b{tag}", [P, D], F32))
        xtb = [es.enter_context(nc.sbuf_tensor(f"xt{tag}{i}", [P, 4, D], F32)) for i in range(2)]
        xT = es.enter_context(nc.sbuf_tensor(f"xT{tag}", [P, 8, 512], BF16))
        gT = es.enter_context(nc.sbuf_tensor(f"gT{tag}", [P, 22, 512], BF16))
        sa = es.enter_context(nc.sbuf_tensor(f"sa{tag}", [P, 512], F32))
        xo = [es.enter_context(nc.sbuf_tensor(f"xo{tag}{i}", [P, 512], BF16)) for i in range(2)]
        stats = es.enter_context(nc.sbuf_tensor(f"stats{tag}", [P, 4, 2, 6], F32))
        mv = es.enter_context(nc.sbuf_tensor(f"mv{tag}", [P, 4, 2], F32))
        rstd = es.enter_context(nc.sbuf_tensor(f"rstd{tag}", [P, 4, 1], F32))
        w1_b = [Buf() for _ in JG]
        w2_b = [Buf() for _ in range(22)]
        gb_b, xT_b, st_b, sa_b = Buf(), Buf(), Buf(), Buf()
        x_b = [Buf(), Buf()]
        xo_b = [Buf(), Buf()]
        gT_b = [Buf() for _ in range(22)]
        jgrp = {}
        for gi, (j0, j1) in enumerate(JG):
            for j in range(j0, j1):
                jgrp[j] = gi
        w_in_v = w_in.rearrange("(kc p) f -> p kc f", p=P)
        w_out_v = w_out.rearrange("(j p) d -> p j d", p=P)
        for gi, (j0, j1) in enumerate(JG):
            for half in range(2):
                c0, c1 = half * DFF + j0 * 128, half * DFF + j1 * 128
                S.dma("pool", w1[:, :, c0:c1], w_in_v[:, :, c0:c1], writes=[w1_b[gi]])
        for j in range(0, 22, 2):
            S.dma("pool", w2[:, j:j + 2, :], w_out_v[:, j:j + 2, :], writes=[w2_b[j], w2_b[j + 1]])
        S.dma("sp", g_bc[:], g_dram, writes=[gb_b])
        S.dma("sp", b_bc[:], b_dram, writes=[gb_b])
        xin_v = x_in.rearrange("(t b p) d -> t p b d", b=4, p=P)
        xout_v = x_out.rearrange("(t b p) d -> t p b d", b=4, p=P)

        def emit_load(t):
            S.dma("sp", xtb[t % 2][:], xin_v[t], reads=[xin_b[t] if isinstance(xin_b, list) else xin_b], writes=[x_b[t % 2]])

        def emit_T(t):
            load_xT(C, xtb[t % 2], x_b[t % 2], xT, xT_b, ps_ids=(6,))
            S.op("act", lambda e, t=t: e.mul(out=xtb[t % 2][:], in_=xtb[t % 2][:], mul=ALPHA), reads=[x_b[t % 2]], writes=[x_b[t % 2]])

        def emit_Tout_chunk(tt, kc):
            xt_ = xtb[tt % 2]
            for blk in range(4):
                S.op("pe", lambda e, blk=blk: e.transpose(out=C.ps[6][:, blk * 128:(blk + 1) * 128], in_=xt_[:, blk, kc * 128:(kc + 1) * 128],
                                                          identity=C.ident[:]), reads=[x_b[tt % 2], C.ident_b], writes=[C.psb[6]])
            k = kc % 2
            if k == 0:
                S.op("act", lambda e: e.copy(out=xo[k][:], in_=C.ps[6][:]), reads=[C.psb[6]], writes=[xo_b[k]])
            else:
                S.op("dve", lambda e: e.tensor_copy(out=xo[k][:], in_=C.ps[6][:]), reads=[C.psb[6]], writes=[xo_b[k]])
            S.dma("sp", xT_out[tt][kc * 128:(kc + 1) * 128, :], xo[k][:], reads=[xo_b[k]], writes=[xT_out_b[tt]])
            if kc == 7 and after_tile is not None:
                after_tile(tt)

        emit_load(0)
        emit_T(0)
        for t in range(ntile):
            xt = xtb[t % 2]
            xb = x_b[t % 2]
            for j in range(22):
                pa, pb_ = (0, 1) if j % 2 == 0 else (2, 3)
                gi = jgrp[j]
                for kc in range(8):
                    S.op("pe", lambda e, j=j, kc=kc, pa=pa: e.matmul(
                        out=C.ps[pa][:], lhsT=w1[:, kc, j * 128:(j + 1) * 128], rhs=xT[:, kc, :],
                        start=(kc == 0), stop=(kc == 7)), reads=[w1_b[gi], xT_b], writes=[C.psb[pa]])
                for kc in range(8):
                    S.op("pe", lambda e, j=j, kc=kc, pb_=pb_: e.matmul(
                        out=C.ps[pb_][:], lhsT=w1[:, kc, DFF + j * 128:DFF + (j + 1) * 128], rhs=xT[:, kc, :],
                        start=(kc == 0), stop=(kc == 7)), reads=[w1_b[gi], xT_b], writes=[C.psb[pb_]])
                S.op("act", lambda e, pa=pa: e.activation(out=sa[:], in_=C.ps[pa][:], func=AF.Silu), reads=[C.psb[pa]], writes=[sa_b])
                S.op("dve", lambda e, pb_=pb_, j=j: e.tensor_tensor(out=gT[:, j, :], in0=C.ps[pb_][:], in1=sa[:], op=ALU.mult),
                     reads=[C.psb[pb_], sa_b], writes=[gT_b[j]])
                if xT_out is not None and t >= 1 and 5 <= j < 13:
                    emit_Tout_chunk(t - 1, j - 5)
                if j == 13 and t + 1 < ntile:
                    emit_load(t + 1)
            if t + 1 < ntile:
                emit_T(t + 1)
            for blk in range(4):
                for hh in range(2):
                    po = 4 + (blk * 2 + hh) % 2
                    for j in range(22):
                        S.op("pe", lambda e, j=j, blk=blk, hh=hh, po=po: e.matmul(
                            out=C.ps[po][:], lhsT=gT[:, j, blk * 128:(blk + 1) * 128], rhs=w2[:, j, hh * 512:(hh + 1) * 512],
                            start=(j == 0), stop=(j == 21)), reads=[gT_b[j], w2_b[j]], writes=[C.psb[po]])
                    S.op("dve", lambda e, blk=blk, hh=hh, po=po, xt=xt: e.scalar_tensor_tensor(
                        out=xt[:, blk, hh * 512:(hh + 1) * 512], in0=C.ps[po][:], scalar=0.5,
                        in1=xt[:, blk, hh * 512:(hh + 1) * 512], op0=ALU.mult, op1=ALU.add),
                        reads=[C.psb[po], xb], writes=[xb])
            layer_norm_tile(C, xt, xb, g_bc, b_bc, gb_b, stats, mv, rstd, st_b)
            S.dma("sp", xout_v[t], xt[:], reads=[xb], writes=[xout_b[t] if isinstance(xout_b, list) else xout_b])
            if xT_out is None and after_tile is not None:
                after_tile(t)
        if xT_out is not None:
            for kc in range(8):
                emit_Tout_chunk(ntile - 1, kc)
        S.barrier()


def load_w_bf16(C, es, name, w_dram, ncols, q="pool"):
    w = es.enter_context(C.nc.sbuf_tensor(name, [P, 8, ncols], BF16))
    b = Buf(name)
    C.S.dma("pool", w[:], w_dram.rearrange("(kc p) f -> p kc f", p=P), writes=[b])
    return w, b


def proj_fm(C, w, w_b, c0, c1, xT, xT_b, pi):
    for kc in range(8):
        C.S.op("pe", lambda e, kc=kc: e.matmul(out=C.ps[pi][0:c1 - c0, :], lhsT=w[:, kc, c0:c1], rhs=xT[:, kc, :],
                                               start=(kc == 0), stop=(kc == 7)), reads=[w_b, xT_b], writes=[C.psb[pi]])


def proj_tm(C, w, w_b, c0, c1, xT, xT_b, blk, pi):
    for kc in range(8):
        C.S.op("pe", lambda e, kc=kc: e.matmul(out=C.ps[pi][:, 0:c1 - c0], lhsT=xT[:, kc, blk * 128:(blk + 1) * 128],
                                               rhs=w[:, kc, c0:c1], start=(kc == 0), stop=(kc == 7)),
               reads=[w_b, xT_b], writes=[C.psb[pi]])


def fetch_xT(C, xfull, xfull_b, t, xTs, xT_bs):
    k = t % 2
    C.S.dma("sp", xTs[k][:], xfull(t), reads=[xfull_b[t] if isinstance(xfull_b, list) else xfull_b], writes=[xT_bs[k]])
    return xTs[k], xT_bs[k]


def sb_phase(C, xfull, xfull_b, w_sb, yT, yT_b, TT, consts, tag):
    nc, S = C.nc, C.S
    ntile = TT // 512
    with contextlib.ExitStack() as es:
        w, w_b = load_w_bf16(C, es, f"wsb{tag}", w_sb, 384)
        QT = es.enter_context(nc.sbuf_tensor(f"sbQT{tag}", [P, TT], BF16))
        KT = es.enter_context(nc.sbuf_tensor(f"sbKT{tag}", [P, TT], BF16))
        V = es.enter_context(nc.sbuf_tensor(f"sbV{tag}", [P, TT // 128, 128], BF16))
        xTs = [es.enter_context(nc.sbuf_tensor(f"sbxT{tag}{k}", [P, 8, 512], BF16)) for k in range(2)]
        xT_bs = [Buf(), Buf()]
        negmask = es.enter_context(nc.sbuf_tensor(f"sbnm{tag}", [P, 4, 512], BF16))
        identb = es.enter_context(nc.sbuf_tensor(f"sbidb{tag}", [P, P], BF16))
        ntri = es.enter_context(nc.sbuf_tensor(f"sbntri{tag}", [P, P], BF16))
        ones = es.enter_context(nc.sbuf_tensor(f"sbones{tag}", [P, P], BF16))
        one1 = es.enter_context(nc.sbuf_tensor(f"sbone1{tag}", [P, 1], F32))
        cb = Buf()
        S.dma("pool", negmask[:], consts["sb_negmask"], writes=[cb])
        S.dma("pool", identb[:], consts["ident"], writes=[cb])
        S.dma("pool", ntri[:], consts["ntri"], writes=[cb])
        S.dma("pool", ones[:], consts["ones"], writes=[cb])
        S.op("pool", lambda e: e.memset(one1[:], 1.0), writes=[cb])
        x_b, xT_b = Buf(), Buf()
        QT_b = [Buf() for _ in range(ntile)]
        KT_b = [Buf() for _ in range(ntile)]
        V_b = [Buf() for _ in range(ntile)]
        xv = xfull if callable(xfull) else (lambda t, _v=xfull.rearrange("(t b p) d -> t p b d", b=4, p=P): _v[t])
        for t in range(ntile):
            xT, xT_b = fetch_xT(C, xfull, xfull_b, t, xTs, xT_bs)
            proj_fm(C, w, w_b, 0, 128, xT, xT_b, 0)
            S.op("act", lambda e, t=t: e.mul(out=QT[:, t * 512:(t + 1) * 512], in_=C.ps[0][:], mul=0.125),
                 reads=[C.psb[0]], writes=[QT_b[t]])
            proj_fm(C, w, w_b, 128, 256, xT, xT_b, 1)
            S.op("dve", lambda e, t=t: e.tensor_copy(out=KT[:, t * 512:(t + 1) * 512], in_=C.ps[1][:]),
                 reads=[C.psb[1]], writes=[KT_b[t]])
            for blk in range(4):
                pi = 2 + blk % 2
                proj_tm(C, w, w_b, 256, 384, xT, xT_b, blk, pi)
                S.op("act" if blk % 2 else "dve",
                     (lambda e, t=t, blk=blk, pi=pi: e.copy(out=V[:, t * 4 + blk, :], in_=C.ps[pi][:, 0:128])) if blk % 2 else
                     (lambda e, t=t, blk=blk, pi=pi: e.tensor_copy(out=V[:, t * 4 + blk, :], in_=C.ps[pi][:, 0:128])),
                     reads=[C.psb[pi]], writes=[V_b[t]])
        NCH = 4
        Ech = [es.enter_context(nc.sbuf_tensor(f"sbE2{tag}{c}", [P, 512], F32)) for c in range(NCH)]
        Lch = [es.enter_context(nc.sbuf_tensor(f"sbL2{tag}{c}", [P, 512], F32)) for c in range(NCH)]
        Sch = [[es.enter_context(nc.sbuf_tensor(f"sbS2{tag}{c}{k}", [P, 512], F32)) for k in range(2)] for c in range(NCH)]
        Wch = [es.enter_context(nc.sbuf_tensor(f"sbW2{tag}{c}", [P, 512], BF16)) for c in range(NCH)]
        Lhi = [es.enter_context(nc.sbuf_tensor(f"sbLh{tag}{c}", [P, 512], BF16)) for c in range(NCH)]
        Llo = [es.enter_context(nc.sbuf_tensor(f"sbLl{tag}{c}", [P, 512], BF16)) for c in range(NCH)]
        Shi = [es.enter_context(nc.sbuf_tensor(f"sbSh{tag}{c}", [P, 512], BF16)) for c in range(NCH)]
        Slo = [es.enter_context(nc.sbuf_tensor(f"sbSl{tag}{c}", [P, 512], BF16)) for c in range(NCH)]
        Lh_b = [Buf() for _ in range(NCH)]
        Ll_b = [Buf() for _ in range(NCH)]
        Sh_b = [Buf() for _ in range(NCH)]
        Sl_b = [Buf() for _ in range(NCH)]
        ych = [es.enter_context(nc.sbuf_tensor(f"sby2{tag}{c}", [P, 512], BF16)) for c in range(NCH)]
        E_b = [Buf() for _ in range(NCH)]
        L_b = [Buf() for _ in range(NCH)]
        S_b = [[Buf(), Buf()] for _ in range(NCH)]
        W_b = [Buf() for _ in range(NCH)]
        y_b = [Buf() for _ in range(NCH)]
        order = []
        lo, hi = 0, ntile - 1
        while lo <= hi:
            order.append(hi)
            hi -= 1
            if lo <= hi:
                order.append(lo)
                lo += 1
        queues = [[], []]
        load = [0, 0]
        for ti in sorted(range(ntile), key=lambda q: -q):
            sidx = 0 if load[0] <= load[1] else 1
            queues[sidx].append(ti)
            load[sidx] += 4 * ti + 4
        state = [None] * NCH
        qpos = [0, 0]

        def next_tile(slot):
            if qpos[slot] < len(queues[slot]):
                ti = queues[slot][qpos[slot]]
                qpos[slot] += 1
                return ti
            return None
        for slot in range(2):
            ti = next_tile(slot)
            for h in range(2):
                state[slot * 2 + h] = None if ti is None else [ti, 0]
        def mkinfo(c):
            i, n = state[c]
            nsteps = 4 * i + 4
            jb = 4 * i + 3 - n
            return dict(i=i, n=n, nsteps=nsteps, jb=jb, diag=jb >= 4 * i, r=jb - 4 * i, kt=jb // 4, h=c % 2,
                        hs=slice((c % 2) * 64, (c % 2) * 64 + 64), pz=c, py=4 + c // 2, k=n % 2)

        def H1(act, info):
            for c in act:
                f = info[c]
                if f["n"] > 0:
                    k = f["k"]
                    S.op("dve", lambda e, c=c, k=k: e.tensor_copy(out=Shi[c][:], in_=Sch[c][k][:]), reads=[S_b[c][k]], writes=[Sh_b[c]])
            for c in act:
                f = info[c]
                if f["n"] > 0:
                    k = f["k"]
                    S.op("dve", lambda e, c=c, k=k: e.tensor_tensor(out=Slo[c][:], in0=Sch[c][k][:], in1=Shi[c][:], op=ALU.subtract),
                         reads=[S_b[c][k], Sh_b[c]], writes=[Sl_b[c]])
            for c in act:
                f = info[c]
                S.op("pe", lambda e, f=f: e.matmul(out=C.ps[f["pz"]][:], lhsT=KT[f["hs"], f["jb"] * 128:(f["jb"] + 1) * 128],
                                                   rhs=QT[f["hs"], f["i"] * 512:(f["i"] + 1) * 512], start=True, stop=not f["diag"]),
                     reads=[KT_b[f["kt"]], QT_b[f["i"]]], writes=[C.psb[f["pz"]]])
                if f["diag"]:
                    S.op("pe", lambda e, f=f: e.matmul(out=C.ps[f["pz"]][:], lhsT=identb[:], rhs=negmask[:, f["r"], :], start=False, stop=True),
                         reads=[cb], writes=[C.psb[f["pz"]]])
            for c in act:
                f = info[c]
                S.op("act", lambda e, c=c, f=f: e.activation(out=Ech[c][:], in_=C.ps[f["pz"]][:], func=AF.Exp),
                     reads=[C.psb[f["pz"]]], writes=[E_b[c]])
            for c in act:
                S.op("act", lambda e, c=c: e.activation(out=Lch[c][:], in_=Ech[c][:], func=AF.Ln, bias=one1[:], scale=1.0),
                     reads=[E_b[c], cb], writes=[L_b[c]])

        def H2(act, info):
            for c in act:
                if False:
                    S.op("act", lambda e, c=c: e.copy(out=Lhi[c][:], in_=Lch[c][:]), reads=[L_b[c]], writes=[Lh_b[c]])
                else:
                    S.op("dve", lambda e, c=c: e.tensor_copy(out=Lhi[c][:], in_=Lch[c][:]), reads=[L_b[c]], writes=[Lh_b[c]])
            for c in act:
                S.op("dve", lambda e, c=c: e.tensor_tensor(out=Llo[c][:], in0=Lch[c][:], in1=Lhi[c][:], op=ALU.subtract),
                     reads=[L_b[c], Lh_b[c]], writes=[Ll_b[c]])
            for c in act:
                f = info[c]
                if f["n"] > 0:
                    S.op("pe", lambda e, c=c, f=f: e.matmul(out=C.ps[f["pz"]][:], lhsT=ones[:], rhs=Shi[c][:], start=False, stop=False,
                                                            skip_group_check=True), reads=[cb, Sh_b[c]], writes=[C.psb[f["pz"]]])
                    S.op("pe", lambda e, c=c, f=f: e.matmul(out=C.ps[f["pz"]][:], lhsT=ones[:], rhs=Slo[c][:], start=False, stop=False,
                                                            skip_group_check=True), reads=[cb, Sl_b[c]], writes=[C.psb[f["pz"]]])
                S.op("pe", lambda e, c=c, f=f: e.matmul(out=C.ps[f["pz"]][:], lhsT=ntri[:], rhs=Lhi[c][:], start=False, stop=False,
                                                        skip_group_check=True), reads=[cb, Lh_b[c]], writes=[C.psb[f["pz"]]])
                S.op("pe", lambda e, c=c, f=f: e.matmul(out=C.ps[f["pz"]][:], lhsT=ntri[:], rhs=Llo[c][:], start=False, stop=True,
                                                        skip_group_check=True), reads=[cb, Ll_b[c]], writes=[C.psb[f["pz"]]])
            for c in act:
                f = info[c]
                S.op("act", lambda e, c=c, f=f: e.activation(out=Wch[c][:], in_=C.ps[f["pz"]][:], func=AF.Exp),
                     reads=[C.psb[f["pz"]]], writes=[W_b[c]])
                if f["n"] < f["nsteps"] - 1:
                    k, k2 = f["k"], 1 - f["k"]
                    if f["n"] == 0:
                        S.op("pool", lambda e, c=c, k2=k2: e.tensor_copy(out=Sch[c][k2][:], in_=Lch[c][:]), reads=[L_b[c]], writes=[S_b[c][k2]])
                    else:
                        S.op("pool", lambda e, c=c, k=k, k2=k2: e.tensor_tensor(out=Sch[c][k2][:], in0=Sch[c][k][:], in1=Lch[c][:], op=ALU.add),
                             reads=[L_b[c], S_b[c][k]], writes=[S_b[c][k2]])
            for c in act:
                f = info[c]
                po = (c % 2) * 64
                S.op("pe", lambda e, c=c, f=f, po=po: e.matmul(out=C.ps[f["py"]][po:po + 64, :], lhsT=V[:, f["jb"], f["hs"]], rhs=Wch[c][:],
                                                               start=(f["n"] == 0), stop=(f["n"] == f["nsteps"] - 1), skip_group_check=True),
                     reads=[V_b[f["kt"]], W_b[c]], writes=[C.psb[f["py"]]])
            for c in act:
                f = info[c]
                if f["n"] == f["nsteps"] - 1:
                    po = (c % 2) * 64
                    S.op("dve", lambda e, c=c, f=f, po=po: e.tensor_copy(out=ych[c][po:po + 64, :], in_=C.ps[f["py"]][po:po + 64, :]),
                         reads=[C.psb[f["py"]]], writes=[y_b[c]])
                    S.dma("sp", yT[f["h"] * 64:(f["h"] + 1) * 64, f["i"] * 512:(f["i"] + 1) * 512], ych[c][po:po + 64, :],
                          reads=[y_b[c]], writes=[yT_b])
                    state[c] = "done"
                else:
                    state[c][1] += 1
            for slot in range(2):
                cs = [slot * 2, slot * 2 + 1]
                if all(state[c] == "done" for c in cs):
                    ti = next_tile(slot)
                    for c in cs:
                        state[c] = None if ti is None else [ti, 0]

        pending = [None, None]
        while True:
            progressed = False
            for slot in range(2):
                other = 1 - slot
                cs_ = [c for c in (2 * slot, 2 * slot + 1) if state[c] is not None]
                if cs_ and pending[slot] is None:
                    inf_ = {c: mkinfo(c) for c in cs_}
                    H1(cs_, inf_)
                    pending[slot] = (cs_, inf_)
                    progressed = True
                if pending[other] is not None:
                    H2(*pending[other])
                    pending[other] = None
                    progressed = True
            if not progressed:
                break
        S.barrier()


def make_consts_np():
    c = {}
    c["ident"] = np.eye(P, dtype=np.float32)
    j = np.arange(P)[:, None]
    s = np.arange(P)[None, :]
    c["ntri"] = -(j >= s).astype(np.float32)
    c["ones"] = -np.ones((P, P), np.float32)
    nm = np.zeros((P, 4, 512), np.float32)
    for r in range(4):
        key = 128 * r + np.arange(P)[:, None]
        col = np.arange(512)[None, :]
        nm[:, r, :] = np.where(key < col, 0.0, -30000.0)
    c["sb_negmask"] = nm
    make_swa_consts_np(c)
    make_ml_consts_np(c)
    return c


def dram_ap(t_ap, offset, pattern):
    return bass.AP(tensor=t_ap.tensor, offset=offset, ap=pattern)


def swa_phase(C, xfull, xfull_b, w_swa, rb_dram, sink_dram, frev_scr, yT, yT_b, TT, consts, tag):
    nc, S = C.nc, C.S
    ntile = TT // 512
    nblk = TT // 128
    C.set_psum(2)
    with contextlib.ExitStack() as es:
        w, w_b = load_w_bf16(C, es, f"wsw{tag}", w_swa, 448)
        QT = [es.enter_context(nc.sbuf_tensor(f"swQT{tag}{g}", [P, TT], BF16)) for g in range(2)]
        KT = es.enter_context(nc.sbuf_tensor(f"swKT{tag}", [P, TT], BF16))
        V = es.enter_context(nc.sbuf_tensor(f"swV{tag}", [P, nblk, 64], BF16))
        xTs = [es.enter_context(nc.sbuf_tensor(f"swxT{tag}{k}", [P, 8, 512], BF16)) for k in range(2)]
        xT_bs = [Buf(), Buf()]
        identb = es.enter_context(nc.sbuf_tensor(f"swidb{tag}", [P, P], BF16))
        Jm = es.enter_context(nc.sbuf_tensor(f"swJ{tag}", [P, P], F32))
        rb = es.enter_context(nc.sbuf_tensor(f"swrb{tag}", [P, P], F32))
        oh = es.enter_context(nc.sbuf_tensor(f"swoh{tag}", [P, 384], F32))
        fneg = es.enter_context(nc.sbuf_tensor(f"swfn{tag}", [4, 384], F32))
        frev = es.enter_context(nc.sbuf_tensor(f"swfr{tag}", [4, 384], F32))
        Hk = es.enter_context(nc.sbuf_tensor(f"swH{tag}", [P, 4, 256], F32))
        bias = es.enter_context(nc.sbuf_tensor(f"swbias{tag}", [P, 4, 256], F32))
        sink = es.enter_context(nc.sbuf_tensor(f"swsink{tag}", [P, 4], F32))
        Sb = es.enter_context(nc.sbuf_tensor(f"swS{tag}", [P, 4, 256], F32))
        pf = es.enter_context(nc.sbuf_tensor(f"swp{tag}", [P, 4, 256], F32))
        pn = es.enter_context(nc.sbuf_tensor(f"swpn{tag}", [P, 4, 256], BF16))
        pT = es.enter_context(nc.sbuf_tensor(f"swpT{tag}", [P, 8, 128], BF16))
        small = es.enter_context(nc.sbuf_tensor(f"swsm{tag}", [P, 6, 4], F32))
        yo = es.enter_context(nc.sbuf_tensor(f"swyo{tag}", [64, 4, 512], BF16))
        cb, x_b, xT_b = Buf(), Buf(), Buf()
        S.dma("pool", identb[:], consts["ident"], writes=[cb])
        S.dma("sp", Jm[:], consts["J"], writes=[cb])
        S.dma("sp", rb[:], rb_dram, writes=[cb])
        S.dma("sp", oh[:], consts["swa_oh"], writes=[cb])
        S.dma("sp", fneg[:], consts["swa_fneg"], writes=[cb])
        S.dma("sp", sink[:], sink_dram, writes=[cb])
        S.op("pe", lambda e: e.matmul(out=C.ps[0][:, 0:384], lhsT=rb[:], rhs=oh[:], start=True, stop=True),
             reads=[cb], writes=[C.psb[0]])
        fr_b, scr_b, H_b, bias_b = Buf(), Buf(), Buf(), Buf()
        S.op("dve", lambda e: e.tensor_tensor(out=frev[:], in0=C.ps[0][0:4, 0:384], in1=fneg[:], op=ALU.add),
             reads=[C.psb[0], cb], writes=[fr_b])
        S.dma("sp", frev_scr, frev[:], reads=[fr_b], writes=[scr_b])
        S.dma("sp", Hk[:], dram_ap(frev_scr, 0, [[1, 128], [384, 4], [1, 256]]), reads=[scr_b], writes=[H_b])
        for hh in range(2):
            S.op("pe", lambda e, hh=hh: e.matmul(out=C.ps[1 + hh][:], lhsT=Jm[:], rhs=Hk[:, 2 * hh:2 * hh + 2, :],
                                                 start=True, stop=True), reads=[cb, H_b], writes=[C.psb[1 + hh]])
            S.op("dve", lambda e, hh=hh: e.tensor_copy(out=bias[:, 2 * hh:2 * hh + 2, :], in_=C.ps[1 + hh][:]),
                 reads=[C.psb[1 + hh]], writes=[bias_b])
        QT_b = [Buf() for _ in range(ntile)]
        KT_b = [Buf() for _ in range(ntile)]
        V_b = [Buf() for _ in range(ntile)]
        xv = xfull if callable(xfull) else (lambda t, _v=xfull.rearrange("(t b p) d -> t p b d", b=4, p=P): _v[t])
        for t in range(ntile):
            xT, xT_b = fetch_xT(C, xfull, xfull_b, t, xTs, xT_bs)
            for g in range(2):
                proj_fm(C, w, w_b, g * 128, (g + 1) * 128, xT, xT_b, g)
                S.op("act", lambda e, t=t, g=g: e.mul(out=QT[g][:, t * 512:(t + 1) * 512], in_=C.ps[g][:], mul=0.125),
                     reads=[C.psb[g]], writes=[QT_b[t]])
            proj_fm(C, w, w_b, 256, 384, xT, xT_b, 2)
            S.op("dve", lambda e, t=t: e.tensor_copy(out=KT[:, t * 512:(t + 1) * 512], in_=C.ps[2][:]),
                 reads=[C.psb[2]], writes=[KT_b[t]])
            for blk in range(4):
                pi = 3 + blk % 2
                proj_tm(C, w, w_b, 384, 448, xT, xT_b, blk, pi)
                S.op("dve", lambda e, t=t, blk=blk, pi=pi: e.tensor_copy(out=V[:, t * 4 + blk, :], in_=C.ps[pi][:, 0:64]),
                     reads=[C.psb[pi]], writes=[V_b[t]])
        NS = 2
        Sb2 = [Sb] + [es.enter_context(nc.sbuf_tensor(f"swS{tag}b", [P, 4, 256], F32))]
        pf2 = [pf] + [es.enter_context(nc.sbuf_tensor(f"swp{tag}b", [P, 4, 256], F32))]
        pn2 = [pn] + [es.enter_context(nc.sbuf_tensor(f"swpn{tag}b", [P, 4, 256], BF16))]
        pT2 = [pT] + [es.enter_context(nc.sbuf_tensor(f"swpT{tag}b", [P, 8, 128], BF16))]
        sm2 = [small] + [es.enter_context(nc.sbuf_tensor(f"swsm{tag}b", [P, 6, 4], F32))]
        S_b = [Buf(), Buf()]
        p_b = [Buf(), Buf()]
        pn_b = [Buf(), Buf()]
        pT_b = [Buf(), Buf()]
        sm_b = [Buf(), Buf()]
        yo_b = Buf()
        for n0 in range(0, nblk, NS):
            blks = [(n0 + a, a) for a in range(NS) if n0 + a < nblk]
            kwd = {n: (128 if n == 0 else 256) for n, a in blks}
            for n, a in blks:
                kw = kwd[n]
                k0 = 256 - kw
                t_q = n // 4
                for h in range(4):
                    g, hs = h % 2, slice((h // 2) * 64, (h // 2) * 64 + 64)
                    bank = 2 * a + h // 2
                    col = (h % 2) * 256
                    S.op("pe", lambda e, g=g, hs=hs, bank=bank, col=col, n=n, kw=kw, k0=k0: e.matmul(
                        out=C.ps[bank][:, col + k0:col + 256], lhsT=QT[g][hs, n * 128:(n + 1) * 128],
                        rhs=KT[hs, (n + 1) * 128 - kw:(n + 1) * 128], start=True, stop=True, skip_group_check=True),
                        reads=[QT_b[t_q], KT_b[t_q], KT_b[max(0, (n - 1) // 4)]], writes=[C.psb[bank]])
            for n, a in blks:
                k0 = 256 - kwd[n]
                mx, negm, dd, esk, rs, rden = [sm2[a][:, i, :] for i in range(6)]
                for bk in range(2):
                    bank = 2 * a + bk
                    S.op("dve", lambda e, bank=bank, bk=bk, k0=k0, a=a: e.tensor_tensor(
                        out=Sb2[a][:, 2 * bk:2 * bk + 2, k0:256],
                        in0=C.ps[bank][:].rearrange("p (h k) -> p h k", h=2)[:, :, k0:256],
                        in1=bias[:, 2 * bk:2 * bk + 2, k0:256], op=ALU.add),
                        reads=[C.psb[bank], bias_b], writes=[S_b[a]])
                S.op("dve", lambda e, k0=k0, a=a, mx=mx: e.tensor_reduce(out=mx, in_=Sb2[a][:, :, k0:256], axis=AX.X, op=ALU.max),
                     reads=[S_b[a]], writes=[sm_b[a]])
                S.op("dve", lambda e, mx=mx: e.tensor_tensor(out=mx, in0=mx, in1=sink[:], op=ALU.max), reads=[sm_b[a], cb], writes=[sm_b[a]])
                S.op("dve", lambda e, mx=mx, negm=negm: e.tensor_scalar(out=negm, in0=mx, scalar1=-1.0, scalar2=None, op0=ALU.mult),
                     reads=[sm_b[a]], writes=[sm_b[a]])
                S.op("dve", lambda e, mx=mx, dd=dd: e.tensor_tensor(out=dd, in0=sink[:], in1=mx, op=ALU.subtract), reads=[sm_b[a], cb], writes=[sm_b[a]])
            for n, a in blks:
                k0 = 256 - kwd[n]
                mx, negm, dd, esk, rs, rden = [sm2[a][:, i, :] for i in range(6)]
                for h in range(4):
                    S.op("act", lambda e, h=h, k0=k0, a=a, negm=negm, rs=rs: e.activation(
                        out=pf2[a][:, h, k0:256], in_=Sb2[a][:, h, k0:256], func=AF.Exp, bias=negm[:, h:h + 1], scale=1.0, accum_out=rs[:, h:h + 1]),
                        reads=[S_b[a], sm_b[a]], writes=[p_b[a], sm_b[a]])
                S.op("act", lambda e, esk=esk, dd=dd: e.activation(out=esk, in_=dd, func=AF.Exp), reads=[sm_b[a]], writes=[sm_b[a]])
            for n, a in blks:
                k0 = 256 - kwd[n]
                mx, negm, dd, esk, rs, rden = [sm2[a][:, i, :] for i in range(6)]
                S.op("dve", lambda e, rden=rden, rs=rs, esk=esk: e.tensor_tensor(out=rden, in0=rs, in1=esk, op=ALU.add), reads=[sm_b[a]], writes=[sm_b[a]])
                S.op("dve", lambda e, rden=rden: e.reciprocal(out=rden, in_=rden), reads=[sm_b[a]], writes=[sm_b[a]])
                for h in range(4):
                    if h % 2:
                        S.op("dve", lambda e, h=h, k0=k0, a=a, rden=rden: e.tensor_scalar(
                            out=pn2[a][:, h, k0:256], in0=pf2[a][:, h, k0:256], scalar1=rden[:, h:h + 1], scalar2=None, op0=ALU.mult),
                            reads=[p_b[a], sm_b[a]], writes=[pn_b[a]])
                    else:
                        S.op("act", lambda e, h=h, k0=k0, a=a, rden=rden: e.mul(out=pn2[a][:, h, k0:256], in_=pf2[a][:, h, k0:256], mul=rden[:, h:h + 1]),
                             reads=[p_b[a], sm_b[a]], writes=[pn_b[a]])
            for n, a in blks:
                kw = kwd[n]
                k0 = 256 - kw
                nkb = kw // 128
                for h in range(4):
                    for kb in range(nkb):
                        idx = h * 2 + kb
                        S.op("pe", lambda e, h=h, kb=kb, idx=idx, k0=k0, a=a: e.transpose(
                            out=C.ps_bfs[a][:, idx * 128:(idx + 1) * 128], in_=pn2[a][:, h, k0 + kb * 128:k0 + (kb + 1) * 128],
                            identity=identb[:]), reads=[pn_b[a], cb], writes=[C.psbf_b[a]])
            for n, a in blks:
                if a == 0:
                    S.op("act", lambda e, a=a: e.copy(out=pT2[a][:].rearrange("p a b -> p (a b)"), in_=C.ps_bfs[a][:]), reads=[C.psbf_b[a]], writes=[pT_b[a]])
                else:
                    S.op("dve", lambda e, a=a: e.tensor_copy(out=pT2[a][:].rearrange("p a b -> p (a b)"), in_=C.ps_bfs[a][:]), reads=[C.psbf_b[a]], writes=[pT_b[a]])
            for n, a in blks:
                nkb = kwd[n] // 128
                po = 4 + a
                for h in range(4):
                    for kb in range(nkb):
                        idx = h * 2 + kb
                        kblk = n - (nkb - 1) + kb
                        hd = (h % 2) * 2 + h // 2
                        S.op("pe", lambda e, hd=hd, kb=kb, idx=idx, kblk=kblk, nkb=nkb, a=a, po=po: e.matmul(
                            out=C.ps[po][0:64, hd * 128:(hd + 1) * 128], lhsT=V[:, kblk, :], rhs=pT2[a][:, idx, :],
                            start=(kb == 0), stop=(kb == nkb - 1), skip_group_check=True),
                            reads=[V_b[kblk // 4], pT_b[a]], writes=[C.psb[po]])
            for n, a in blks:
                po = 4 + a
                S.op("dve" if a == 0 else "act",
                     (lambda e, n=n, po=po: e.tensor_copy(out=yo[:, :, (n % 4) * 128:(n % 4 + 1) * 128],
                                                          in_=C.ps[po][0:64, :].rearrange("p (h q) -> p h q", h=4))) if a == 0 else
                     (lambda e, n=n, po=po: e.copy(out=yo[:, :, (n % 4) * 128:(n % 4 + 1) * 128],
                                                   in_=C.ps[po][0:64, :].rearrange("p (h q) -> p h q", h=4))),
                     reads=[C.psb[po]], writes=[yo_b])
                if n % 4 == 3:
                    S.dma("sp", yT.rearrange("(h d) t -> d h t", d=64)[:, :, (n // 4) * 512:(n // 4 + 1) * 512], yo[:],
                          reads=[yo_b], writes=[yT_b])
        S.barrier()
        C.set_psum(1)


def t5_bucket_np(dist):
    max_exact = 16
    d = np.maximum(dist, 1)
    large = max_exact + (np.log(d / max_exact) / np.log(128 / max_exact) * (32 - max_exact)).astype(np.int32)
    large = np.minimum(large, 31)
    return np.where(dist < max_exact, dist, large).astype(np.int32)


def make_swa_consts_np(c):
    a = np.arange(384)
    dist = 255 - a
    valid = (dist >= 0) & (dist < 128)
    bucket = t5_bucket_np(np.clip(dist, 0, None))
    oh = np.zeros((32, 384), np.float32)
    oh[bucket[valid], a[valid]] = 1.0
    c["swa_oh"] = np.pad(oh, ((0, 96), (0, 0)))
    c["swa_fneg"] = np.tile(np.where(valid, 0.0, -30000.0).astype(np.float32)[None, :], (4, 1))
    c["J"] = np.eye(P, dtype=np.float32)[::-1].copy()
    return c


def mlstm_phase(C, xfull, xfull_b, w_ml, cw_dram, cb_dram, ifb_dram, ng_dram, yT, yT_b, TT, consts, tag):
    nc, S = C.nc, C.S
    ntile = TT // 512
    nb = TT // 128
    n2 = 2 * nb
    with contextlib.ExitStack() as es:
        w, w_b = load_w_bf16(C, es, f"wml{tag}", w_ml, 640)
        sb = lambda name, shape, dt=F32: es.enter_context(nc.sbuf_tensor(f"ml{name}{tag}", shape, dt))
        QT = sb("QT", [P, TT], BF16)
        KT = sb("KT", [P, TT], BF16)
        Vext = sb("Vext", [P, nb, 2, 66], BF16)
        gso = sb("gso", [P, nb, 128])
        G4 = sb("G4", [P, nb, 4])
        xTs = [sb(f"xT{k}", [P, 8, 512], BF16) for k in range(2)]
        xT_bs = [Buf(), Buf()]
        raws = [sb(f"raw{k}", [P, 2, 515]) for k in range(2)]
        accs = [sb(f"acc{k}", [P, 2, 512]) for k in range(2)]
        cw = sb("cw", [P, 8])
        cbt = sb("cb", [P, 2])
        ifb = sb("ifb", [P, 4])
        nfb = sb("nfb", [P, 2])
        ng = sb("ng", [P, 128])
        identb = sb("idb", [P, P], BF16)
        ntriT = sb("ntriT", [P, P])
        mask8 = sb("mask8", [P, P])
        e0 = sb("e0", [P, P])
        e127 = sb("e127", [P, P])
        one1 = sb("one1", [P, 1])
        cb_ = Buf()
        S.dma("pool", identb[:], consts["ident"], writes=[cb_])
        S.dma("sp", ntriT[:], consts["ntriT"], writes=[cb_])
        S.dma("sp", mask8[:], consts["mask8"], writes=[cb_])
        S.dma("sp", e0[:], consts["e0ones"], writes=[cb_])
        S.dma("sp", e127[:], consts["e127ones"], writes=[cb_])
        S.dma("sp", cw[:], cw_dram, writes=[cb_])
        S.dma("sp", cbt[:], cb_dram, writes=[cb_])
        S.dma("sp", ifb[:], ifb_dram, writes=[cb_])
        S.dma("sp", ng[:], ng_dram, writes=[cb_])
        S.op("pool", lambda e: e.memset(one1[:], 1.0), writes=[cb_])
        S.op("dve", lambda e: e.tensor_scalar(out=nfb[:], in0=ifb[:, 2:4], scalar1=-1.0, scalar2=None, op0=ALU.mult),
             reads=[cb_], writes=[cb_])
        x_b, xT_b, V_b, gso_b, G_b = Buf(), Buf(), Buf(), Buf(), Buf()
        raw_bs, acc_bs = [Buf(), Buf()], [Buf(), Buf()]
        QT_b = [Buf() for _ in range(ntile)]
        KT_b = [Buf() for _ in range(ntile)]
        for k in range(2):
            S.op("pool", lambda e, k=k: e.memset(raws[k][:], 0.0), writes=[raw_bs[k]])
        S.op("pool", lambda e: e.memset(Vext[:], 1.0), writes=[V_b])
        xv = xfull if callable(xfull) else (lambda t, _v=xfull.rearrange("(t b p) d -> t p b d", b=4, p=P): _v[t])
        import os
        mlstage = int(os.environ.get("ML_STAGE", "9"))
        for t in range(ntile):
            if mlstage < -1:
                break
            xT, xT_b = fetch_xT(C, xfull, xfull_b, t, xTs, xT_bs)
            raw, raw_b, acc, acc_b = raws[t % 2], raw_bs[t % 2], accs[t % 2], acc_bs[t % 2]
            if t > 0:
                S.op("pool", lambda e, raw=raw, t=t: e.tensor_copy(out=raw[:, :, 0:3], in_=raws[(t - 1) % 2][:, :, 512:515]),
                     reads=[raw_bs[(t - 1) % 2]], writes=[raw_b])
            for qk in range(2):
                proj_fm(C, w, w_b, qk * 128, (qk + 1) * 128, xT, xT_b, qk)
                S.op("act", lambda e, qk=qk, raw=raw: e.copy(out=raw[:, qk, 3:515], in_=C.ps[qk][:]), reads=[C.psb[qk]], writes=[raw_b])
            for qk in range(2):
                S.op("dve", lambda e, qk=qk, raw=raw, acc=acc: e.tensor_scalar(out=acc[:, qk, :], in0=raw[:, qk, 0:512], scalar1=cw[:, 4 * qk:4 * qk + 1],
                                                              scalar2=None, op0=ALU.mult), reads=[raw_b, cb_], writes=[acc_b])
                for j in range(1, 4):
                    S.op("dve", lambda e, qk=qk, j=j, raw=raw, acc=acc: e.scalar_tensor_tensor(
                        out=acc[:, qk, :], in0=raw[:, qk, j:j + 512], scalar=cw[:, 4 * qk + j:4 * qk + j + 1], in1=acc[:, qk, :],
                        op0=ALU.mult, op1=ALU.add), reads=[raw_b, cb_, acc_b], writes=[acc_b])
                dst, dst_b = (QT, QT_b) if qk == 0 else (KT, KT_b)
                S.op("act", lambda e, qk=qk, dst=dst, t=t, acc=acc: e.activation(out=dst[:, t * 512:(t + 1) * 512], in_=acc[:, qk, :], func=AF.Silu,
                                                                        bias=cbt[:, qk:qk + 1], scale=1.0),
                     reads=[acc_b, cb_], writes=[dst_b[t]])
            for blk in range(4):
                if mlstage < 0:
                    break
                pi = 2 + blk % 2
                bi = t * 4 + blk
                proj_tm(C, w, w_b, 256, 512, xT, xT_b, blk, pi)
                proj_tm(C, w, w_b, 512, 640, xT, xT_b, blk, 4 + blk % 2)
                sub = int(os.environ.get("ML_SUB", "9"))
                if sub < 1:
                    continue
                S.op("dve", lambda e, pi=pi, bi=bi: e.tensor_copy(out=Vext[:, bi, :, 0:64],
                                                                  in_=C.ps[pi][:, 0:128].rearrange("p (h d) -> p h d", h=2)),
                     reads=[C.psb[pi]], writes=[V_b])
                if sub < 2:
                    continue
                S.op("act", lambda e, pi=pi, bi=bi: e.activation(out=gso[:, bi, :], in_=C.ps[pi][:, 128:256], func=AF.Exp, scale=-1.0),
                     reads=[C.psb[pi]], writes=[gso_b])
                S.op("act", lambda e, bi=bi: e.add(out=gso[:, bi, :], in_=gso[:, bi, :], add=1.0), reads=[gso_b], writes=[gso_b])
                S.op("dve", lambda e, bi=bi: e.reciprocal(out=gso[:, bi, :], in_=gso[:, bi, :]), reads=[gso_b], writes=[gso_b])
                if sub < 3:
                    continue
                S.op("dve", lambda e, blk=blk, bi=bi: e.tensor_copy(out=G4[:, bi, :], in_=C.ps[4 + blk % 2][:, 0:4]),
                     reads=[C.psb[4 + blk % 2]], writes=[G_b])
                S.op("pool", lambda e, bi=bi: e.tensor_tensor(out=gso[:, bi, :], in0=gso[:, bi, :], in1=ng[:], op=ALU.mult),
                     reads=[gso_b, cb_], writes=[gso_b])
        import os
        mlstage = int(os.environ.get("ML_STAGE", "9"))
        if mlstage < 1:
            S.barrier()
            return
        icol = sb("icol", [P, 2, nb])
        lf = sb("lf", [P, 2, nb])
        bcol = sb("bcol", [P, 2, nb])
        acol = sb("acol", [P, 2, nb])
        aT = sb("aT", [P, P])
        cm_tok = sb("cm_tok", [P, n2])
        amax_bc = sb("amax_bc", [P, n2])
        cmT = sb("cmT", [P, P])
        RW = sb("RW", [P, 3, P])
        mnext = sb("mnext", [1, P])
        mprev_bc = sb("mprevbc", [P, n2])
        mref_bc = sb("mrefbc", [P, n2])
        Mt = sb("Mt", [P, n2])
        r_t = sb("r_t", [P, n2])
        u_t = sb("u_t", [P, n2])
        eb_t = sb("eb_t", [P, n2])
        sc_bc = sb("sc_bc", [P, n2])
        tmp = sb("tmpg", [P, n2])
        g_b = Buf()
        fl = lambda tl: tl[:].rearrange("p h c -> p (h c)")
        for h in range(2):
            S.op("act", lambda e, h=h: e.activation(out=icol[:, h, :], in_=G4[:, :, h], func=AF.Identity, bias=ifb[:, h:h + 1], scale=1.0),
                 reads=[G_b, cb_], writes=[g_b])
            S.op("act", lambda e, h=h: e.activation(out=lf[:, h, :], in_=G4[:, :, 2 + h], func=AF.Exp, bias=nfb[:, h:h + 1], scale=-1.0),
                 reads=[G_b, cb_], writes=[g_b])
        S.op("act", lambda e: e.activation(out=fl(lf), in_=fl(lf), func=AF.Ln, bias=one1[:], scale=1.0), reads=[g_b, cb_], writes=[g_b])
        S.op("pe", lambda e: e.matmul(out=C.ps[0][:, 0:n2], lhsT=ntriT[:], rhs=fl(lf), start=True, stop=True),
             reads=[g_b, cb_], writes=[C.psb[0]])
        S.op("dve", lambda e: e.tensor_copy(out=fl(bcol), in_=C.ps[0][:, 0:n2]), reads=[C.psb[0]], writes=[g_b])
        S.op("dve", lambda e: e.tensor_tensor(out=fl(acol), in0=fl(icol), in1=fl(bcol), op=ALU.subtract), reads=[g_b], writes=[g_b])
        S.op("pe", lambda e: e.transpose(out=C.ps[1][0:n2, 0:128], in_=fl(acol), identity=C.ident[:]), reads=[g_b, C.ident_b], writes=[C.psb[1]])
        S.op("pool", lambda e: e.memset(aT[:], 0.0), writes=[g_b])
        S.op("pool", lambda e: e.memset(RW[:], 0.0), writes=[g_b])
        S.op("dve", lambda e: e.tensor_copy(out=aT[0:n2, :], in_=C.ps[1][0:n2, 0:128]), reads=[C.psb[1]], writes=[g_b])
        S.op("dve", lambda e: e.tensor_tensor_scan(out=cmT[:], data0=aT[:], data1=aT[:], initial=-1.0e30, op0=ALU.max, op1=ALU.max),
             reads=[g_b], writes=[g_b])
        S.op("pe", lambda e: e.transpose(out=C.ps[2][:, 0:128], in_=cmT[:], identity=C.ident[:]), reads=[g_b, C.ident_b], writes=[C.psb[2]])
        S.op("dve", lambda e: e.tensor_copy(out=cm_tok[:], in_=C.ps[2][:, 0:n2]), reads=[C.psb[2]], writes=[g_b])
        S.op("pe", lambda e: e.matmul(out=C.ps[3][:, 0:n2], lhsT=e127[:], rhs=cm_tok[:], start=True, stop=True), reads=[g_b, cb_], writes=[C.psb[3]])
        S.op("pe", lambda e: e.matmul(out=C.ps[4][:, 0:n2], lhsT=e127[:], rhs=fl(bcol), start=True, stop=True), reads=[g_b, cb_], writes=[C.psb[4]])
        S.op("dve", lambda e: e.tensor_copy(out=amax_bc[:], in_=C.ps[3][:, 0:n2]), reads=[C.psb[3]], writes=[g_b])
        S.op("dve", lambda e: e.tensor_copy(out=RW[:, 1, 0:n2], in_=C.ps[4][:, 0:n2]), reads=[C.psb[4]], writes=[g_b])
        S.op("dve", lambda e: e.tensor_copy(out=RW[:, 0, 0:n2], in_=amax_bc[:]), reads=[g_b], writes=[g_b])
        for h in range(2):
            S.op("dve", lambda e, h=h: e.tensor_tensor_scan(out=mnext[0:1, h * nb:(h + 1) * nb], data0=RW[0:1, 0, h * nb:(h + 1) * nb],
                                                            data1=RW[0:1, 1, h * nb:(h + 1) * nb], initial=0.0, op0=ALU.max, op1=ALU.add),
                 reads=[g_b], writes=[g_b])
            if nb > 1:
                S.op("dve", lambda e, h=h: e.tensor_copy(out=RW[0:1, 2, h * nb + 1:(h + 1) * nb], in_=mnext[0:1, h * nb:(h + 1) * nb - 1]),
                     reads=[g_b], writes=[g_b])
        S.op("pe", lambda e: e.matmul(out=C.ps[0][:, 0:n2], lhsT=e0[:], rhs=RW[:, 2, 0:n2], start=True, stop=True),
             reads=[g_b, cb_], writes=[C.psb[0]])
        S.op("dve", lambda e: e.tensor_copy(out=mprev_bc[:], in_=C.ps[0][:, 0:n2]), reads=[C.psb[0]], writes=[g_b])
        S.op("dve", lambda e: e.tensor_tensor(out=mref_bc[:], in0=amax_bc[:], in1=mprev_bc[:], op=ALU.max), reads=[g_b], writes=[g_b])
        S.op("dve", lambda e: e.tensor_tensor(out=Mt[:], in0=cm_tok[:], in1=mprev_bc[:], op=ALU.max), reads=[g_b], writes=[g_b])
        S.op("dve", lambda e: e.tensor_tensor(out=tmp[:], in0=mref_bc[:], in1=Mt[:], op=ALU.subtract), reads=[g_b], writes=[g_b])
        S.op("act", lambda e: e.activation(out=r_t[:], in_=tmp[:], func=AF.Exp), reads=[g_b], writes=[g_b])
        S.op("dve", lambda e: e.tensor_tensor(out=tmp[:], in0=fl(acol), in1=mref_bc[:], op=ALU.subtract), reads=[g_b], writes=[g_b])
        S.op("act", lambda e: e.activation(out=u_t[:], in_=tmp[:], func=AF.Exp), reads=[g_b], writes=[g_b])
        S.op("dve", lambda e: e.tensor_tensor(out=tmp[:], in0=fl(bcol), in1=Mt[:], op=ALU.add), reads=[g_b], writes=[g_b])
        S.op("act", lambda e: e.activation(out=eb_t[:], in_=tmp[:], func=AF.Exp, scale=-1.0), reads=[g_b], writes=[g_b])
        S.op("dve", lambda e: e.tensor_tensor(out=tmp[:], in0=mprev_bc[:], in1=mref_bc[:], op=ALU.subtract), reads=[g_b], writes=[g_b])
        S.op("act", lambda e: e.activation(out=sc_bc[:], in_=tmp[:], func=AF.Exp), reads=[g_b], writes=[g_b])
        if mlstage < 2:
            S.barrier()
            return
        Cst = sb("Cst", [P, 65])
        Csb = sb("Csb", [P, 130], BF16)
        Kp = sb("Kp", [P, P], BF16)
        ST = [sb(f"ST{h}", [P, P], BF16) for h in range(2)]
        nd = sb("nd", [P, 2, 65])
        hid = sb("hid", [P, 2, 64])
        sm = sb("sm", [P, 8])
        stats = sb("stats", [P, 2, 6])
        mv = sb("mv", [P, 2, 2])
        yTt = sb("yTt", [P, 512], BF16)
        Cst_b, Csb_b, Kp_b, ST_b, nd_b, hid_b, sm_b, yTt_b = Buf(), Buf(), Buf(), [Buf(), Buf()], Buf(), Buf(), Buf(), Buf()
        S.op("pool", lambda e: e.memset(Cst[:], 0.0), writes=[Cst_b])
        S.op("pool", lambda e: e.memset(Csb[:], 0.0), writes=[Csb_b])
        eb3 = eb_t[:].rearrange("p (h c) -> p h c", h=2)
        for c in range(nb):
            tq = c // 4
            cs = slice(c * 128, (c + 1) * 128)
            S.op("pe", lambda e, cs=cs: e.transpose(out=C.ps_bf[:, 0:128], in_=KT[:, cs], identity=identb[:]),
                 reads=[KT_b[tq], cb_], writes=[C.psb[7]])
            for h in range(2):
                hs = slice(h * 64, (h + 1) * 64)
                ix = h * nb + c
                S.op("dve", lambda e, hs=hs, ix=ix: e.tensor_scalar(out=Kp[:, hs], in0=C.ps_bf[:, hs], scalar1=u_t[:, ix:ix + 1], scalar2=0.125,
                                                                    op0=ALU.mult, op1=ALU.mult), reads=[C.psb[7], g_b], writes=[Kp_b])
                S.op("pe", lambda e, hs=hs, h=h, cs=cs: e.matmul(out=C.ps[h][:, 0:128], lhsT=KT[hs, cs], rhs=QT[hs, cs], start=True, stop=True),
                     reads=[KT_b[tq], QT_b[tq]], writes=[C.psb[h]])
                S.op("dve", lambda e, h=h, ix=ix: e.scalar_tensor_tensor(out=ST[h][:], in0=C.ps[h][:, 0:128], scalar=u_t[:, ix:ix + 1], in1=mask8[:],
                                                                         op0=ALU.mult, op1=ALU.mult), reads=[C.psb[h], g_b, cb_], writes=[ST_b[h]])
                S.op("dve", lambda e, hs=hs, h=h, ix=ix: e.tensor_scalar(out=Csb[hs, h * 65:(h + 1) * 65], in0=Cst[hs, :], scalar1=sc_bc[hs, ix:ix + 1],
                                                                         scalar2=None, op0=ALU.mult), reads=[Cst_b, g_b], writes=[Csb_b])
            S.op("pe", lambda e, cs=cs: e.matmul(out=C.ps[2][:, 0:130], lhsT=QT[:, cs], rhs=Csb[:], start=True, stop=False, skip_group_check=True),
                 reads=[QT_b[tq], Csb_b], writes=[C.psb[2]])
            for h in range(2):
                S.op("pe", lambda e, h=h, c=c: e.matmul(out=C.ps[2][:, h * 65:(h + 1) * 65], lhsT=ST[h][:], rhs=Vext[:, c, h, 0:65], start=False, stop=(h == 1),
                                                        skip_group_check=True), reads=[ST_b[h], V_b], writes=[C.psb[2]])
            S.op("pe", lambda e, c=c: e.matmul(out=C.ps[3][:, 0:130], lhsT=Kp[:], rhs=Vext[:, c, :, 0:65], start=True, stop=True),
                 reads=[Kp_b, V_b], writes=[C.psb[3]])
            for h in range(2):
                hs = slice(h * 64, (h + 1) * 64)
                ix = h * nb + c
                S.op("dve", lambda e, hs=hs, h=h, ix=ix: e.scalar_tensor_tensor(out=Cst[hs, :], in0=Cst[hs, :], scalar=sc_bc[hs, ix:ix + 1],
                                                                                in1=C.ps[3][hs, h * 65:(h + 1) * 65], op0=ALU.mult, op1=ALU.add),
                     reads=[Cst_b, g_b, C.psb[3], Csb_b], writes=[Cst_b])
                S.op("dve", lambda e, h=h, ix=ix: e.tensor_scalar(out=nd[:, h, :], in0=C.ps[2][:, h * 65:(h + 1) * 65], scalar1=r_t[:, ix:ix + 1],
                                                                  scalar2=None, op0=ALU.mult), reads=[C.psb[2], g_b], writes=[nd_b])
            S.op("dve", lambda e: e.tensor_scalar(out=sm[:, 0:2], in0=nd[:, :, 64], scalar1=-1.0, scalar2=None, op0=ALU.mult), reads=[nd_b], writes=[sm_b])
            S.op("dve", lambda e: e.tensor_tensor(out=sm[:, 0:2], in0=sm[:, 0:2], in1=nd[:, :, 64], op=ALU.max), reads=[nd_b, sm_b], writes=[sm_b])
            S.op("dve", lambda e, c=c: e.tensor_tensor(out=sm[:, 0:2], in0=sm[:, 0:2], in1=eb3[:, :, c], op=ALU.max), reads=[sm_b, g_b], writes=[sm_b])
            S.op("dve", lambda e: e.reciprocal(out=sm[:, 0:2], in_=sm[:, 0:2]), reads=[sm_b], writes=[sm_b])
            for h in range(2):
                S.op("dve", lambda e, h=h: e.tensor_scalar(out=hid[:, h, :], in0=nd[:, h, 0:64], scalar1=sm[:, h:h + 1], scalar2=None, op0=ALU.mult),
                     reads=[nd_b, sm_b], writes=[hid_b])
                S.op("dve", lambda e, h=h: e.bn_stats(out=stats[:, h, :], in_=hid[:, h, :]), reads=[hid_b], writes=[sm_b])
                S.op("dve", lambda e, h=h: e.bn_aggr(out=mv[:, h, :], in_=stats[:, h, :]), reads=[sm_b], writes=[sm_b])
            S.op("act", lambda e: e.activation(out=sm[:, 2:4], in_=mv[:, :, 1], func=AF.Sqrt, bias=C.eps_t[:], scale=1.0), reads=[sm_b, C.ident_b], writes=[sm_b])
            S.op("dve", lambda e: e.reciprocal(out=sm[:, 2:4], in_=sm[:, 2:4]), reads=[sm_b], writes=[sm_b])
            for h in range(2):
                S.op("dve", lambda e, h=h: e.tensor_scalar(out=hid[:, h, :], in0=hid[:, h, :], scalar1=mv[:, h, 0:1], scalar2=sm[:, 2 + h:3 + h],
                                                           op0=ALU.subtract, op1=ALU.mult), reads=[hid_b, sm_b], writes=[hid_b])
            S.op("dve", lambda e, c=c: e.tensor_tensor(out=hid[:].rearrange("p h d -> p (h d)"), in0=hid[:].rearrange("p h d -> p (h d)"),
                                                        in1=gso[:, c, :], op=ALU.mult), reads=[hid_b, gso_b], writes=[hid_b])
            S.op("pe", lambda e, c=c: e.transpose(out=C.ps[4][:, (c % 4) * 128:(c % 4 + 1) * 128], in_=hid[:].rearrange("p h d -> p (h d)"),
                                                  identity=C.ident[:]), reads=[hid_b, C.ident_b], writes=[C.psb[4]])
            if c % 4 == 3:
                S.op("act", lambda e: e.copy(out=yTt[:], in_=C.ps[4][:]), reads=[C.psb[4]], writes=[yTt_b])
                S.dma("sp", yT[:, (c // 4) * 512:(c // 4 + 1) * 512], yTt[:], reads=[yTt_b], writes=[yT_b])
        S.barrier()


def make_ml_consts_np(c):
    k = np.arange(P)[:, None]
    m = np.arange(P)[None, :]
    c["ntriT"] = -(k <= m).astype(np.float32)
    c["mask8"] = (k <= m).astype(np.float32) * 0.125
    e0 = np.zeros((P, P), np.float32)
    e0[0, :] = 1.0
    c["e0ones"] = e0
    e127 = np.zeros((P, P), np.float32)
    e127[127, :] = 1.0
    c["e127ones"] = e127
    return c


def outproj_ln(C, es, tag, lhs_chunks, lhs_b, wo, wo_b, xt, x_b):
    S = C.S
    for blk in range(4):
        for hh in range(2):
            po = 4 + (blk * 2 + hh) % 2
            for k in range(8):
                S.op("pe", lambda e, k=k, blk=blk, hh=hh, po=po: e.matmul(
                    out=C.ps[po][:], lhsT=lhs_chunks[:, k, blk * 128:(blk + 1) * 128], rhs=wo[:, k, hh * 512:(hh + 1) * 512],
                    start=(k == 0), stop=(k == 7)), reads=[lhs_b, wo_b], writes=[C.psb[po]])
            S.op("dve", lambda e, blk=blk, hh=hh, po=po: e.tensor_tensor(
                out=xt[:, blk, hh * 512:(hh + 1) * 512], in0=C.ps[po][:], in1=xt[:, blk, hh * 512:(hh + 1) * 512], op=ALU.add),
                reads=[C.psb[po], x_b], writes=[x_b])


def mixout_phase(C, x_in, xin_b, ygath, yg_b, w_out, sel_dram, g_dram, b_dram, x_out, xout_b, NT, TT, tag):
    nc, S = C.nc, C.S
    ntile = NT // 512
    with contextlib.ExitStack() as es:
        wo, wo_b = load_w_bf16(C, es, f"wmo{tag}", w_out, D)
        sb = lambda name, shape, dt=F32: es.enter_context(nc.sbuf_tensor(f"mo{name}{tag}", shape, dt))
        g_bc, b_bc, sel = sb("g", [P, D]), sb("b", [P, D]), sb("sel", [P, 2])
        xts = [sb(f"xt{i}", [P, 4, D]) for i in range(2)]
        yAs = [sb(f"yA{i}", [P, 8, 512], BF16) for i in range(2)]
        yBs = [sb(f"yB{i}", [P, 8, 512], BF16) for i in range(2)]
        stats, mv, rstd = sb("stats", [P, 4, 2, 6]), sb("mv", [P, 4, 2]), sb("rstd", [P, 4, 1])
        gb_b, st_b = Buf(), Buf()
        x_bs, yA_bs, yB_bs = [Buf(), Buf()], [Buf(), Buf()], [Buf(), Buf()]
        S.dma("sp", g_bc[:], g_dram, writes=[gb_b])
        S.dma("sp", b_bc[:], b_dram, writes=[gb_b])
        S.dma("sp", sel[:], sel_dram, writes=[gb_b])
        xin_v = x_in.rearrange("(t b p) d -> t p b d", b=4, p=P)
        xout_v = x_out.rearrange("(t b p) d -> t p b d", b=4, p=P)
        yv = ygath.rearrange("(k p) t -> p k t", p=P)

        def loads(t):
            k = t % 2
            S.dma("sp", xts[k][:], xin_v[t], reads=[xin_b[t] if isinstance(xin_b, list) else xin_b], writes=[x_bs[k]])
            S.dma("sp", yAs[k][:], yv[:, :, t * 512:(t + 1) * 512], reads=[yg_b], writes=[yA_bs[k]])
            S.dma("sp", yBs[k][:], yv[:, :, NT + t * 512:NT + (t + 1) * 512], reads=[yg_b], writes=[yB_bs[k]])
        loads(0)
        for t in range(ntile):
            k = t % 2
            xt, x_b, yA, yA_b, yB, yB_b = xts[k], x_bs[k], yAs[k], yA_bs[k], yBs[k], yB_bs[k]
            if t + 1 < ntile:
                loads(t + 1)
            S.op("act", lambda e, xt=xt: e.mul(out=xt[:], in_=xt[:], mul=ALPHA), reads=[x_b], writes=[x_b])
            S.op("dve", lambda e, yA=yA: e.tensor_scalar(out=yA[:], in0=yA[:], scalar1=sel[:, 0:1], scalar2=None, op0=ALU.mult),
                 reads=[yA_b, gb_b], writes=[yA_b])
            S.op("dve", lambda e, yA=yA, yB=yB: e.scalar_tensor_tensor(out=yA[:], in0=yB[:], scalar=sel[:, 1:2], in1=yA[:], op0=ALU.mult, op1=ALU.add),
                 reads=[yA_b, yB_b, gb_b], writes=[yA_b])
            outproj_ln(C, es, tag, yA, yA_b, wo, wo_b, xt, x_b)
            layer_norm_tile(C, xt, x_b, g_bc, b_bc, gb_b, stats, mv, rstd, st_b)
            S.dma("pool", xout_v[t], xt[:], reads=[x_b], writes=[xout_b])
        S.barrier()


def xattn_phase(C, x_in, xin_b, mem, w_q, w_kv, w_o, g_dram, b_dram, x_out, xout_b, NT, tag):
    nc, S = C.nc, C.S
    ntile = NT // 512
    C.set_psum(2)
    with contextlib.ExitStack() as es:
        wq, wq_b = load_w_bf16(C, es, f"wxq{tag}", w_q, D)
        wkv, wkv_b = load_w_bf16(C, es, f"wxkv{tag}", w_kv, 2 * D)
        wo, wo_b = load_w_bf16(C, es, f"wxo{tag}", w_o, D)
        sb = lambda name, shape, dt=F32: es.enter_context(nc.sbuf_tensor(f"xa{name}{tag}", shape, dt))
        g_bc, b_bc = sb("g", [P, D]), sb("b", [P, D])
        identb = sb("idb", [P, P], BF16)
        xt = sb("xt", [P, 4, D])
        xT = sb("xT", [P, 8, 512], BF16)
        QT = sb("QT", [P, 8, 512], BF16)
        KTx = sb("KT", [P, 8, 256], BF16)
        Vx = sb("V", [P, 2, D], BF16)
        aoT = sb("aoT", [P, 8, 512], BF16)
        sc = sb("sc", [P, 256])
        pf = sb("pf", [P, 256])
        pn = sb("pn", [P, 256], BF16)
        pT = sb("pT", [P, 2, 128], BF16)
        sm = sb("sm", [P, 4])
        stats, mv, rstd = sb("stats", [P, 4, 2, 6]), sb("mv", [P, 4, 2]), sb("rstd", [P, 4, 1])
        gb_b, x_b, xT_b, QT_b, KT_b, V_b, ao_b, st_b = Buf(), Buf(), Buf(), Buf(), Buf(), Buf(), Buf(), Buf()
        sc_b, pf_b, pn_b, pT_b, sm_b = Buf(), Buf(), Buf(), Buf(), Buf()
        pf2 = [pf, sb("pfb", [P, 256])]
        pn2 = [pn, sb("pnb", [P, 256], BF16)]
        pT2 = [pT, sb("pTb", [P, 2, 128], BF16)]
        sm2 = [sm, sb("smb", [P, 4])]
        pf_b2, pn_b2, pT_b2, sm_b2 = [Buf(), Buf()], [Buf(), Buf()], [Buf(), Buf()], [Buf(), Buf()]
        pf4 = [sb(f"pf4{k}", [P, 4, 256]) for k in range(2)]
        pn4 = [sb(f"pn4{k}", [P, 4, 256], BF16) for k in range(2)]
        pT4 = [sb(f"pT4{k}", [P, 8, 128], BF16) for k in range(2)]
        sm4 = [sb(f"sm4{k}", [P, 4, 4]) for k in range(2)]
        sc_pb, bf_pb, pv_pb = [Buf(), Buf()], [Buf(), Buf()], [Buf(), Buf()]
        S.dma("sp", g_bc[:], g_dram, writes=[gb_b])
        S.dma("sp", b_bc[:], b_dram, writes=[gb_b])
        S.dma("pool", identb[:], C.ident_dram, writes=[gb_b])
        S.dma("sp", xt[:, 0:2, :], mem.rearrange("(b p) d -> p b d", p=P), writes=[x_b])
        for kc in range(8):
            for blk in range(2):
                S.op("pe", lambda e, kc=kc, blk=blk: e.transpose(out=C.ps[4][:, blk * 128:(blk + 1) * 128], in_=xt[:, blk, kc * 128:(kc + 1) * 128],
                                                                 identity=C.ident[:]), reads=[x_b, C.ident_b], writes=[C.psb[4]])
            S.op("dve", lambda e, kc=kc: e.tensor_copy(out=xT[:, kc, 0:256], in_=C.ps[4][:, 0:256]), reads=[C.psb[4]], writes=[xT_b])
        for j in range(8):
            pi = j % 2
            for kc in range(8):
                S.op("pe", lambda e, kc=kc, j=j, pi=pi: e.matmul(out=C.ps[pi][:, 0:256], lhsT=wkv[:, kc, j * 128:(j + 1) * 128], rhs=xT[:, kc, 0:256],
                                                                 start=(kc == 0), stop=(kc == 7)), reads=[wkv_b, xT_b], writes=[C.psb[pi]])
            S.op("dve", lambda e, j=j, pi=pi: e.tensor_copy(out=KTx[:, j, :], in_=C.ps[pi][:, 0:256]), reads=[C.psb[pi]], writes=[KT_b])
        for blk in range(2):
            for hh in range(2):
                pi = 2 + hh
                for kc in range(8):
                    S.op("pe", lambda e, kc=kc, blk=blk, hh=hh, pi=pi: e.matmul(
                        out=C.ps[pi][:], lhsT=xT[:, kc, blk * 128:(blk + 1) * 128], rhs=wkv[:, kc, D + hh * 512:D + (hh + 1) * 512],
                        start=(kc == 0), stop=(kc == 7)), reads=[wkv_b, xT_b], writes=[C.psb[pi]])
                S.op("act", lambda e, blk=blk, hh=hh, pi=pi: e.copy(out=Vx[:, blk, hh * 512:(hh + 1) * 512], in_=C.ps[pi][:]),
                     reads=[C.psb[pi]], writes=[V_b])
        xin_v = x_in.rearrange("(t b p) d -> t p b d", b=4, p=P)
        xout_v = x_out.rearrange("(t b p) d -> t p b d", b=4, p=P)
        xts = [xt, sb("xt2", [P, 4, D])]
        x_bs = [x_b, Buf()]
        QTs = [QT, sb("QT2", [P, 8, 512], BF16)]
        QT_bs = [QT_b, Buf()]

        def emit_load(t):
            S.dma("sp", xts[t % 2][:], xin_v[t], reads=[xin_b[t] if isinstance(xin_b, list) else xin_b], writes=[x_bs[t % 2]])

        def emit_T(t):
            load_xT(C, xts[t % 2], x_bs[t % 2], xT, xT_b, ps_ids=(0,))
            S.op("act", lambda e, t=t: e.mul(out=xts[t % 2][:], in_=xts[t % 2][:], mul=ALPHA), reads=[x_bs[t % 2]], writes=[x_bs[t % 2]])

        def emit_qproj(t, j):
            proj_fm(C, wq, wq_b, j * 128, (j + 1) * 128, xT, xT_b, 1)
            S.op("act" if j % 2 == 0 else "dve",
                 (lambda e, j=j, t=t: e.mul(out=QTs[t % 2][:, j, :], in_=C.ps[1][:], mul=0.0625)) if j % 2 == 0 else
                 (lambda e, j=j, t=t: e.tensor_scalar(out=QTs[t % 2][:, j, :], in0=C.ps[1][:], scalar1=0.0625, scalar2=None, op0=ALU.mult)),
                 reads=[C.psb[1]], writes=[QT_bs[t % 2]])

        emit_load(0)
        emit_T(0)
        for j in range(8):
            emit_qproj(0, j)
        for t in range(ntile):
            xt, x_b, QT, QT_b = xts[t % 2], x_bs[t % 2], QTs[t % 2], QT_bs[t % 2]
            if t + 1 < ntile:
                emit_load(t + 1)
            for hd in range(4):
                a = hd % 2
                if t + 1 < ntile:
                    if hd == 2:
                        emit_T(t + 1)
                    elif hd == 3:
                        for j in range(8):
                            emit_qproj(t + 1, j)
                for b in range(4):
                    ts = slice(b * 128, (b + 1) * 128)
                    for kk in range(2):
                        S.op("pe", lambda e, hd=hd, kk=kk, ts=ts, b=b: e.matmul(out=C.ps[2 + b // 2][:, (b % 2) * 256:(b % 2 + 1) * 256],
                                                                               lhsT=QT[:, 2 * hd + kk, ts], rhs=KTx[:, 2 * hd + kk, :],
                                                                               start=(kk == 0), stop=(kk == 1), skip_group_check=True),
                             reads=[QT_b, KT_b], writes=[C.psb[2 + b // 2]])
                for bank in range(2):
                    S.op("dve", lambda e, bank=bank, a=a: e.tensor_reduce(out=sm4[a][:, 0, 2 * bank:2 * bank + 2],
                                                                         in_=C.ps[2 + bank][:].rearrange("p (b k) -> p b k", b=2), axis=AX.X, op=ALU.max),
                         reads=[C.psb[2 + bank]], writes=[sm_b2[a]])
                S.op("dve", lambda e, a=a: e.tensor_scalar(out=sm4[a][:, 1, :], in0=sm4[a][:, 0, :], scalar1=-1.0, scalar2=None, op0=ALU.mult),
                     reads=[sm_b2[a]], writes=[sm_b2[a]])
                for b in range(4):
                    S.op("act", lambda e, b=b, a=a: e.activation(out=pf4[a][:, b, :], in_=C.ps[2 + b // 2][:, (b % 2) * 256:(b % 2 + 1) * 256], func=AF.Exp,
                                                                 bias=sm4[a][:, 1, b:b + 1], scale=1.0, accum_out=sm4[a][:, 2, b:b + 1]),
                         reads=[C.psb[2 + b // 2], sm_b2[a]], writes=[pf_b2[a], sm_b2[a]])
                S.op("dve", lambda e, a=a: e.reciprocal(out=sm4[a][:, 3, :], in_=sm4[a][:, 2, :]), reads=[sm_b2[a]], writes=[sm_b2[a]])
                for b in range(4):
                    if b % 2 == 0:
                        S.op("dve", lambda e, b=b, a=a: e.tensor_scalar(out=pn4[a][:, b, :], in0=pf4[a][:, b, :], scalar1=sm4[a][:, 3, b:b + 1],
                                                                        scalar2=None, op0=ALU.mult), reads=[pf_b2[a], sm_b2[a]], writes=[pn_b2[a]])
                    else:
                        S.op("act", lambda e, b=b, a=a: e.mul(out=pn4[a][:, b, :], in_=pf4[a][:, b, :], mul=sm4[a][:, 3, b:b + 1]),
                             reads=[pf_b2[a], sm_b2[a]], writes=[pn_b2[a]])
                for b in range(4):
                    for mb in range(2):
                        S.op("pe", lambda e, b=b, mb=mb, a=a: e.transpose(out=C.ps_bfs[a][:, (b * 2 + mb) * 128:(b * 2 + mb + 1) * 128],
                                                                          in_=pn4[a][:, b, mb * 128:(mb + 1) * 128], identity=identb[:]),
                             reads=[pn_b2[a], gb_b], writes=[C.psbf_b[a]])
                S.op("act", lambda e, a=a: e.copy(out=pT4[a][:, 0:4, :].rearrange("p a b -> p (a b)"), in_=C.ps_bfs[a][:, 0:512]),
                     reads=[C.psbf_b[a]], writes=[pT_b2[a]])
                S.op("dve", lambda e, a=a: e.tensor_copy(out=pT4[a][:, 4:8, :].rearrange("p a b -> p (a b)"), in_=C.ps_bfs[a][:, 512:1024]),
                     reads=[C.psbf_b[a]], writes=[pT_b2[a]])
                for cc in range(2):
                    for b in range(4):
                        for mb in range(2):
                            S.op("pe", lambda e, cc=cc, mb=mb, hd=hd, b=b, a=a: e.matmul(out=C.ps[4 + cc][:, b * 128:(b + 1) * 128],
                                                                                         lhsT=Vx[:, mb, hd * 256 + cc * 128:hd * 256 + (cc + 1) * 128],
                                                                                         rhs=pT4[a][:, b * 2 + mb, :], start=(mb == 0), stop=(mb == 1),
                                                                                         skip_group_check=True),
                                 reads=[V_b, pT_b2[a]], writes=[C.psb[4 + cc]])
                S.op("act", lambda e, hd=hd: e.copy(out=aoT[:, 2 * hd, :], in_=C.ps[4][:]), reads=[C.psb[4]], writes=[ao_b])
                S.op("dve", lambda e, hd=hd: e.tensor_copy(out=aoT[:, 2 * hd + 1, :], in_=C.ps[5][:]), reads=[C.psb[5]], writes=[ao_b])
            outproj_ln(C, es, tag, aoT, ao_b, wo, wo_b, xt, x_b)
            layer_norm_tile(C, xt, x_b, g_bc, b_bc, gb_b, stats, mv, rstd, st_b)
            S.dma("sp", xout_v[t], xt[:], reads=[x_b], writes=[xout_b])
        S.barrier()
    C.set_psum(1)


NT_OWN = 4096
TT_SEQ = 8192
DEPTH = 2
PAIRS = [[0, 1], [2, 3], [4, 5], [6, 7]]
_SPLITS = np.cumsum([0, 256, 256, 256, 512, 128, 128, 512, 256, 4, 4, 256])


def build_program():
    from concourse.bass_utils import run_bass_kernel_spmd
    nc = bass.Bass("TRN2", target_bir_lowering=False)
    din = lambda name, shape, dt=F32: nc.dram_tensor(name, shape, dt, kind="ExternalInput").ap()
    dscr = lambda name, shape, dt=F32: nc.dram_tensor(name, shape, dt, kind="Internal").ap()
    x = din("x", [NT_OWN, D])
    mem = din("mem", [256, D])
    sel = din("sel", [P, 2])
    cn = make_consts_np()
    cd = {k: din("c_" + k, list(v.shape)) for k, v in cn.items()}
    L = []
    for l in range(DEPTH):
        d = {}
        d["ffn1_in"] = din(f"l{l}_ffn1_in", [D, 2 * DFF]); d["ffn1_out"] = din(f"l{l}_ffn1_out", [DFF, D])
        d["ffn2_in"] = din(f"l{l}_ffn2_in", [D, 2 * DFF]); d["ffn2_out"] = din(f"l{l}_ffn2_out", [DFF, D])
        d["w_sb"] = din(f"l{l}_w_sb", [D, 384]); d["w_swa"] = din(f"l{l}_w_swa", [D, 448]); d["w_ml"] = din(f"l{l}_w_ml", [D, 640])
        d["cw"] = din(f"l{l}_cw", [P, 8]); d["cb"] = din(f"l{l}_cb", [P, 2]); d["ifb"] = din(f"l{l}_ifb", [P, 4]); d["ng"] = din(f"l{l}_ng", [P, P])
        d["rb"] = din(f"l{l}_rb", [P, P]); d["sink"] = din(f"l{l}_sink", [P, 4])
        d["w_mo"] = din(f"l{l}_w_mo", [D, D])
        d["xq"] = din(f"l{l}_xq", [D, D]); d["xkv"] = din(f"l{l}_xkv", [D, 2 * D]); d["xo"] = din(f"l{l}_xo", [D, D])
        d["lng"] = [din(f"l{l}_lng{i}", [P, D]) for i in range(4)]
        d["lnb"] = [din(f"l{l}_lnb{i}", [P, D]) for i in range(4)]
        L.append(d)
    out = nc.dram_tensor("out", [NT_OWN, D], F32, kind="ExternalOutput").ap()
    Xa = dscr("Xa", [NT_OWN, D]); Xb = dscr("Xb", [NT_OWN, D])
    X1g = dscr("X1g", [NT_OWN // 512, 2 * D, 512], BF16)
    XaT = dscr("XaT", [NT_OWN // 512, D, 512], BF16)
    YTo = dscr("YTo", [512, TT_SEQ], BF16)
    Yg = dscr("Yg", [1024, TT_SEQ], BF16)
    frs = dscr("frs", [4, 384])
    S = Sched(nc)
    C = Ctx(nc, S)
    C.load_consts(cd["ident"])
    x_b, Xa_b, Xb_b, X1g_b, YTo_b, Yg_b, out_b = Buf(), Buf(), Buf(), Buf(), [Buf(), Buf(), Buf()], Buf(), Buf()
    cur, cur_b = x, x_b
    import os
    kstop = int(os.environ.get("KSTOP", "99"))
    for l in range(DEPTH):
        d = L[l]
        tg = f"L{l}"
        if kstop < 99 and l > 0:
            break
        Xa_tb = [Buf() for _ in range(NT_OWN // 512)]
        XaT_tb = [Buf() for _ in range(NT_OWN // 512)]
        X1g_tb = [Buf() for _ in range(TT_SEQ // 512)]
        nch = NT_OWN // 512

        def ag1(t):
            S.custom("pool", lambda e, t=t: e.collective_compute("AllGather", ALU.bypass, replica_groups=PAIRS,
                                                                 ins=[XaT[t]], outs=[X1g[t]]), 1,
                     reads=[XaT_tb[t]], writes=[X1g_tb[t], X1g_tb[nch + t]])
        ffn_phase(C, cur, Xa, d["ffn1_in"], d["ffn1_out"], d["lng"][0], d["lnb"][0], NT_OWN, tg + "f1", cur_b, Xa_tb, after_tile=ag1,
                  xT_out=XaT, xT_out_b=XaT_tb)
        if kstop <= 2:
            break
        X1v = X1g.rearrange("j (r kc p) n -> j r p kc n", r=2, p=P)
        xtile = lambda t: X1v[t % nch, t // nch]
        sb_phase(C, xtile, X1g_tb, d["w_sb"], YTo[0:128, :], YTo_b[0], TT_SEQ, cd, tg + "sb")
        S.custom("pool", lambda e: e.collective_compute("AllGather", ALU.bypass, replica_groups=PAIRS, ins=[YTo[0:128, :]], outs=[Yg[0:256, :]]), 1,
                 reads=[YTo_b[0]], writes=[Yg_b])
        if kstop <= 3:
            break
        swa_phase(C, xtile, X1g_tb, d["w_swa"], d["rb"], d["sink"], frs, YTo[128:384, :], YTo_b[1], TT_SEQ, cd, tg + "sw")
        for k in (1, 2):
            S.custom("pool", lambda e, k=k: e.collective_compute("AllGather", ALU.bypass, replica_groups=PAIRS, ins=[YTo[k * 128:(k + 1) * 128, :]],
                                                                 outs=[Yg[k * 256:(k + 1) * 256, :]]), 1, reads=[YTo_b[1]], writes=[Yg_b])
        if kstop <= 4:
            break
        mlstm_phase(C, xtile, X1g_tb, d["w_ml"], d["cw"], d["cb"], d["ifb"], d["ng"], YTo[384:512, :], YTo_b[2], TT_SEQ, cd, tg + "ml")
        S.custom("pool", lambda e: e.collective_compute("AllGather", ALU.bypass, replica_groups=PAIRS, ins=[YTo[384:512, :]], outs=[Yg[768:1024, :]]), 1,
                 reads=[YTo_b[2]], writes=[Yg_b])
        S.barrier()
        if kstop <= 6:
            break
        mixout_phase(C, Xa, Xa_tb, Yg, Yg_b, d["w_mo"], sel, d["lng"][1], d["lnb"][1], Xb, Xb_b, NT_OWN, TT_SEQ, tg + "mo")
        if kstop <= 7:
            break
        xattn_phase(C, Xb, Xb_b, mem, d["xq"], d["xkv"], d["xo"], d["lng"][2], d["lnb"][2], Xa, Xa_b, NT_OWN, tg + "xa")
        if kstop <= 8:
            break
        dst, dst_b = (out, out_b) if l == DEPTH - 1 else (Xb, Xb_b)
        ffn_phase(C, Xa, dst, d["ffn2_in"], d["ffn2_out"], d["lng"][3], d["lnb"][3], NT_OWN, tg + "f2", Xa_b, dst_b)
        cur, cur_b = dst, dst_b
    S.finish()
    C.psum_es.close()
    return nc, cn


def make_core_inputs(c, inp, cn):
    b, r = c // 2, c % 2
    f = lambda a: np.ascontiguousarray(np.asarray(a, dtype=np.float32))
    rep = lambda v: f(np.tile(np.asarray(v, np.float32)[None, :], (P, 1)))
    m = {"x": f(inp["x"][b, r * NT_OWN:(r + 1) * NT_OWN]), "mem": f(inp["mem"][b])}
    selv = np.zeros((P, 2), np.float32)
    selv[:, r] = 1.0
    m["sel"] = selv
    for k, v in cn.items():
        m["c_" + k] = f(v)
    sp = _SPLITS
    slot = [0, 2, 1, 3]
    for l in range(DEPTH):
        w = np.asarray(inp["mix_w_in"][l], np.float32)
        seg = [w[:, sp[i]:sp[i + 1]] for i in range(11)]
        sbq, sbk, sbv, swq, swk, swv, mlqk, mlv, ig, fg, og = seg
        mlq, mlk = mlqk[:, 0:256], mlqk[:, 256:512]
        m[f"l{l}_ffn1_in"] = f(inp["ffn1_w_in"][l]); m[f"l{l}_ffn1_out"] = f(inp["ffn1_w_out"][l])
        m[f"l{l}_ffn2_in"] = f(inp["ffn2_w_in"][l]); m[f"l{l}_ffn2_out"] = f(inp["ffn2_w_out"][l])
        h2 = slice(r * 128, (r + 1) * 128)
        m[f"l{l}_w_sb"] = f(np.concatenate([sbq[:, h2], sbk[:, h2], sbv[:, h2]], 1))
        kk = swk[:, r * 64:(r + 1) * 64]
        m[f"l{l}_w_swa"] = f(np.concatenate([swq[:, r * 256:(r + 1) * 256], kk, kk, swv[:, r * 64:(r + 1) * 64]], 1))
        m[f"l{l}_w_ml"] = f(np.concatenate([mlq[:, h2], mlk[:, h2], mlv[:, h2], og[:, h2], ig[:, 2 * r:2 * r + 2], fg[:, 2 * r:2 * r + 2],
                                            np.zeros((D, 124), np.float32)], 1))
        cwl = np.asarray(inp["ml_conv_w"][l], np.float32)
        cbl = np.asarray(inp["ml_conv_b"][l], np.float32)
        m[f"l{l}_cw"] = f(np.concatenate([cwl[:, r * 128:(r + 1) * 128].T, cwl[:, 256 + r * 128:256 + (r + 1) * 128].T], 1))
        m[f"l{l}_cb"] = f(np.stack([cbl[r * 128:(r + 1) * 128], cbl[256 + r * 128:256 + (r + 1) * 128]], 1))
        ib = np.asarray(inp["ml_i_bias"][l], np.float32)[2 * r:2 * r + 2]
        fb = np.asarray(inp["ml_f_bias"][l], np.float32)[2 * r:2 * r + 2]
        m[f"l{l}_ifb"] = rep(np.concatenate([ib, fb]))
        m[f"l{l}_ng"] = rep(np.asarray(inp["ml_norm_g"][l], np.float32)[h2])
        rbl = np.asarray(inp["rel_bias"], np.float32)[:, 4 * r:4 * r + 4][:, slot]
        m[f"l{l}_rb"] = f(np.pad(rbl, ((0, 96), (0, 124))))
        m[f"l{l}_sink"] = rep(np.asarray(inp["swa_sinks"][l], np.float32)[4 * r:4 * r + 4][slot])
        wo = np.asarray(inp["mix_w_out"][l], np.float32)
        rows = []
        for k in range(4):
            for rr in range(2):
                base = [rr * 128, 256 + rr * 256, 256 + rr * 256 + 128, 768 + rr * 128][k]
                rows.append(wo[base:base + 128])
        m[f"l{l}_w_mo"] = f(np.concatenate(rows, 0))
        m[f"l{l}_xq"] = f(inp["xattn_w_q"][l]); m[f"l{l}_xkv"] = f(inp["xattn_w_kv"][l]); m[f"l{l}_xo"] = f(inp["xattn_w_o"][l])
        for i in range(4):
            m[f"l{l}_lng{i}"] = rep(inp["ln_g"][l][i]); m[f"l{l}_lnb{i}"] = rep(inp["ln_b"][l][i])
    return m


def kernel(**inputs):
    from concourse.bass_utils import run_bass_kernel_spmd
    inp = {k: np.asarray(v) for k, v in inputs.items()}
    nc, cn = build_program()
    in_maps = [make_core_inputs(c, inp, cn) for c in range(8)]
    res = run_bass_kernel_spmd(nc, in_maps, core_ids=list(range(8)))
    out = np.zeros((4, TT_SEQ, D), np.float32)
    for c in range(8):
        b, r = c // 2, c % 2
        out[b, r * NT_OWN:(r + 1) * NT_OWN] = np.asarray(res.results[c]["out"], np.float32)
    return out
```

```python
import contextlib
import numpy as np
import concourse.bass as bass
import concourse.mybir as mybir

F32 = mybir.dt.float32
BF16 = mybir.dt.bfloat16
ALU = mybir.AluOpType
AF = mybir.ActivationFunctionType
AX = mybir.AxisListType


class Buf:
    __slots__ = ("name", "w", "r")

    def __init__(self, name=""):
        self.name = name
        self.w = None
        self.r = []


class Sched:
    EPOCH = 20000

    def __init__(self, nc, n_dma_sems=64):
        self.nc = nc
        self.es = contextlib.ExitStack()
        self.eng = {"pe": nc.tensor, "act": nc.scalar, "dve": nc.vector,
                    "pool": nc.gpsimd, "sp": nc.sync}
        self.sem = {}
        self.cnt = {}
        self.nsem = 0
        for e in self.eng:
            self._new_epoch(e)
        half = n_dma_sems // 2
        self.dma_sems = [self.es.enter_context(nc.semaphore(f"dq{i}")) for i in range(n_dma_sems)]
        self.dma_cnt = [0] * n_dma_sems
        self.dma_pool = {"sp": list(range(0, half)), "pool": list(range(half, n_dma_sems)), "act": list(range(0, half))}
        self.dma_next = {"sp": 0, "pool": 0, "act": 0}
        self.waited = {e: {} for e in self.eng}
        self.last_ev = {e: None for e in self.eng}
        self.out_events = []
        self.n_ops = 0
        self.n_waits = 0

    def _new_epoch(self, e):
        self.sem[e] = self.es.enter_context(self.nc.semaphore(f"s_{e}_{self.nsem}"))
        self.nsem += 1
        self.cnt[e] = 0

    def _wait(self, e, ev):
        if ev is None:
            return
        sem, val, src = ev
        if src == "pe" and e == "pe":
            return
        key = id(sem)
        if self.waited[e].get(key, 0) >= val:
            return
        self.eng[e].wait_ge(sem, val)
        self.n_waits += 1
        self.waited[e][key] = val

    def _deps(self, e, reads, writes):
        for b in reads:
            self._wait(e, b.w)
        for b in writes:
            self._wait(e, b.w)
            for ev in b.r:
                self._wait(e, ev)

    def _commit(self, ev, reads, writes):
        for b in reads:
            b.r.append(ev)
            if len(b.r) > 12:
                latest = {}
                for x in b.r:
                    k = id(x[0])
                    if k not in latest or latest[k][1] < x[1]:
                        latest[k] = x
                b.r = list(latest.values())
        for b in writes:
            b.w = ev
            b.r = []

    def op(self, e, fn, reads=(), writes=()):
        self._deps(e, reads, writes)
        if self.cnt[e] >= self.EPOCH:
            self._new_epoch(e)
        ins = fn(self.eng[e])
        self.cnt[e] += 1
        ins.then_inc(self.sem[e], 1)
        ev = (self.sem[e], self.cnt[e], e)
        self.last_ev[e] = ev
        self._commit(ev, reads, writes)
        self.n_ops += 1
        return ev

    def dma(self, q, out, in_, reads=(), writes=(), **kw):
        self._deps(q, reads, writes)
        pl = self.dma_pool[q]
        k = pl[self.dma_next[q]]
        self.dma_next[q] = (self.dma_next[q] + 1) % len(pl)
        sem = self.dma_sems[k]
        if self.dma_cnt[k] > 0:
            self._wait(q, (sem, self.dma_cnt[k], "dma"))
        self.dma_cnt[k] += 16
        self.eng[q].dma_start(out=out, in_=in_, **kw).then_inc(sem, 16)
        ev = (sem, self.dma_cnt[k], "dma")
        self._commit(ev, reads, writes)
        self.n_ops += 1
        return ev

    def custom(self, q, fn, inc, reads=(), writes=()):
        self._deps(q, reads, writes)
        pl = self.dma_pool[q]
        k = pl[self.dma_next[q]]
        self.dma_next[q] = (self.dma_next[q] + 1) % len(pl)
        sem = self.dma_sems[k]
        if self.dma_cnt[k] > 0:
            self._wait(q, (sem, self.dma_cnt[k], "dma"))
        self.dma_cnt[k] += inc
        fn(self.eng[q]).then_inc(sem, inc)
        ev = (sem, self.dma_cnt[k], "dma")
        self._commit(ev, reads, writes)
        return ev

    def barrier(self, bufs=()):
        evs = [ev for ev in self.last_ev.values() if ev is not None]
        for k, sem in enumerate(self.dma_sems):
            if self.dma_cnt[k] > 0:
                evs.append((sem, self.dma_cnt[k], "dma"))
        for e in self.eng:
            for ev in evs:
                if ev[2] == e and e == "pe":
                    continue
                self._wait(e, ev)

    def finish(self):
        self.barrier()
        self.es.close()


D = 1024
DFF = 2816
ALPHA = 4 ** 0.25
LN_EPS = 1e-5
P = 128


class Ctx:
    def __init__(self, nc, S):
        self.nc = nc
        self.S = S
        es = S.es
        self.psum_es = None
        self.psum_gen = 0
        self.set_psum(1)
        self.ident = es.enter_context(nc.sbuf_tensor("ident_sb", [P, P], F32))
        self.ident_b = Buf("ident")

        self.eps_t = es.enter_context(nc.sbuf_tensor("eps_t", [P, 1], F32))

    def set_psum(self, n_bf=1):
        if self.psum_es is not None:
            self.psum_es.close()
        self.psum_es = contextlib.ExitStack()
        g = self.psum_gen
        self.psum_gen += 1
        nf = 8 - n_bf
        self.ps = [self.psum_es.enter_context(self.nc.psum_tensor(f"psf{g}_{i}", [P, 512], F32)) for i in range(nf)]
        self.psb = [Buf(f"ps{i}") for i in range(nf)]
        self.ps_bfs = [self.psum_es.enter_context(self.nc.psum_tensor(f"psh{g}_{i}", [P, 1024], BF16)) for i in range(n_bf)]
        self.psbf_b = [Buf(f"psbf{i}") for i in range(n_bf)]
        self.ps_bf = self.ps_bfs[0]
        while len(self.ps) < 8:
            self.ps.append(None)
            self.psb.append(self.psbf_b[0])

    def load_consts(self, ident_dram):
        self.ident_dram = ident_dram
        self.S.dma("sp", self.ident[:], ident_dram, writes=[self.ident_b])
        self.S.op("pool", lambda e: e.memset(self.eps_t[:], LN_EPS), writes=[self.ident_b])


def load_xT(C, x_tile, x_b, xT, xT_b, ps_ids, evac_engs=("act", "dve")):
    S = C.S
    for kc in range(8):
        pi = ps_ids[kc % len(ps_ids)]
        ps, pb = C.ps[pi], C.psb[pi]
        for blk in range(4):
            S.op("pe", lambda e, kc=kc, blk=blk, ps=ps: e.transpose(
                out=ps[:, blk * 128:(blk + 1) * 128], in_=x_tile[:, blk, kc * 128:(kc + 1) * 128],
                identity=C.ident[:]), reads=[x_b, C.ident_b], writes=[pb])
        eng = evac_engs[kc % len(evac_engs)]
        if eng == "act":
            S.op("act", lambda e, kc=kc, ps=ps: e.copy(out=xT[:, kc, :], in_=ps[:]), reads=[pb], writes=[xT_b])
        else:
            S.op("dve", lambda e, kc=kc, ps=ps: e.tensor_copy(out=xT[:, kc, :], in_=ps[:]), reads=[pb], writes=[xT_b])


def layer_norm_tile(C, xt, x_b, g_bc, b_bc, gb_b, stats, mv, rstd, st_b, nblk=4):
    S = C.S
    for blk in range(nblk):
        for hh in range(2):
            S.op("dve", lambda e, blk=blk, hh=hh: e.bn_stats(out=stats[:, blk, hh, :], in_=xt[:, blk, hh * 512:(hh + 1) * 512]),
                 reads=[x_b], writes=[st_b])
        S.op("dve", lambda e, blk=blk: e.bn_aggr(out=mv[:, blk, :], in_=stats[:, blk, :, :]), reads=[st_b], writes=[st_b])
        S.op("act", lambda e, blk=blk: e.activation(out=rstd[:, blk, :], in_=mv[:, blk, 1:2], func=AF.Sqrt, bias=C.eps_t[:], scale=1.0),
             reads=[st_b, C.ident_b], writes=[st_b])
        S.op("dve", lambda e, blk=blk: e.reciprocal(out=rstd[:, blk, :], in_=rstd[:, blk, :]), reads=[st_b], writes=[st_b])
        S.op("dve", lambda e, blk=blk: e.tensor_scalar(out=xt[:, blk, :], in0=xt[:, blk, :], scalar1=mv[:, blk, 0:1],
                                                       scalar2=rstd[:, blk, :], op0=ALU.subtract, op1=ALU.mult),
             reads=[st_b, x_b], writes=[x_b])
        S.op("dve", lambda e, blk=blk: e.tensor_tensor(out=xt[:, blk, :], in0=xt[:, blk, :], in1=g_bc[:], op=ALU.mult),
             reads=[x_b, gb_b], writes=[x_b])
        S.op("pool", lambda e, blk=blk: e.tensor_tensor(out=xt[:, blk, :], in0=xt[:, blk, :], in1=b_bc[:], op=ALU.add),
             reads=[x_b, gb_b], writes=[x_b])


def ffn_phase(C, x_in, x_out, w_in, w_out, g_dram, b_dram, NT, tag, xin_b, xout_b, after_tile=None, xT_out=None, xT_out_b=None):
    nc, S = C.nc, C.S
    ntile = NT // 512
    JG = [(0, 6), (6, 12), (12, 17), (17, 22)]
    with contextlib.ExitStack() as es:
        w1 = es.enter_context(nc.sbuf_tensor(f"w1{tag}", [P, 8, 2 * DFF], BF16))
        w2 = es.enter_context(nc.sbuf_tensor(f"w2{tag}", [P, 22, D], BF16))
        g_bc = es.enter_context(nc.sbuf_tensor(f"g{tag}", [P, D], F32))
        b_bc = es.enter_context(nc.sbuf_tensor(f"b{tag}", [P, D], F32))
        xtb = [es.enter_context(nc.sbuf_tensor(f"xt{tag}{i}", [P, 4, D], F32)) for i in range(2)]
        xT = es.enter_context(nc.sbuf_tensor(f"xT{tag}", [P, 8, 512], BF16))
        gT = es.enter_context(nc.sbuf_tensor(f"gT{tag}", [P, 22, 512], BF16))
        sa = es.enter_context(nc.sbuf_tensor(f"sa{tag}", [P, 512], F32))
        xo = [es.enter_context(nc.sbuf_tensor(f"xo{tag}{i}", [P, 512], BF16)) for i in range(2)]
        stats = es.enter_context(nc.sbuf_tensor(f"stats{tag}", [P, 4, 2, 6], F32))
        mv = es.enter_context(nc.sbuf_tensor(f"mv{tag}", [P, 4, 2], F32))
        rstd = es.enter_context(nc.sbuf_tensor(f"rstd{tag}", [P, 4, 1], F32))
        w1_b = [Buf() for _ in JG]
        w2_b = [Buf() for _ in range(22)]
        gb_b, xT_b, st_b, sa_b = Buf(), Buf(), Buf(), Buf()
        x_b = [Buf(), Buf()]
        xo_b = [Buf(), Buf()]
        gT_b = [Buf() for _ in range(22)]
        jgrp = {}
        for gi, (j0, j1) in enumerate(JG):
            for j in range(j0, j1):
                jgrp[j] = gi
        w_in_v = w_in.rearrange("(kc p) f -> p kc f", p=P)
        w_out_v = w_out.rearrange("(j p) d -> p j d", p=P)
        for gi, (j0, j1) in enumerate(JG):
            for half in range(2):
                c0, c1 = half * DFF + j0 * 128, half * DFF + j1 * 128
                S.dma("pool", w1[:, :, c0:c1], w_in_v[:, :, c0:c1], writes=[w1_b[gi]])
        for j in range(0, 22, 2):
            S.dma("pool", w2[:, j:j + 2, :], w_out_v[:, j:j + 2, :], writes=[w2_b[j], w2_b[j + 1]])
        S.dma("sp", g_bc[:], g_dram, writes=[gb_b])
        S.dma("sp", b_bc[:], b_dram, writes=[gb_b])
        xin_v = x_in.rearrange("(t b p) d -> t p b d", b=4, p=P)
        xout_v = x_out.rearrange("(t b p) d -> t p b d", b=4, p=P)

        def emit_load(t):
            S.dma("sp", xtb[t % 2][:], xin_v[t], reads=[xin_b[t] if isinstance(xin_b, list) else xin_b], writes=[x_b[t % 2]])

        def emit_T(t):
            load_xT(C, xtb[t % 2], x_b[t % 2], xT, xT_b, ps_ids=(6,))
            S.op("act", lambda e, t=t: e.mul(out=xtb[t % 2][:], in_=xtb[t % 2][:], mul=ALPHA), reads=[x_b[t % 2]], writes=[x_b[t % 2]])

        def emit_Tout_chunk(tt, kc):
            xt_ = xtb[tt % 2]
            for blk in range(4):
                S.op("pe", lambda e, blk=blk: e.transpose(out=C.ps[6][:, blk * 128:(blk + 1) * 128], in_=xt_[:, blk, kc * 128:(kc + 1) * 128],
                                                          identity=C.ident[:]), reads=[x_b[tt % 2], C.ident_b], writes=[C.psb[6]])
            k = kc % 2
            if k == 0:
                S.op("act", lambda e: e.copy(out=xo[k][:], in_=C.ps[6][:]), reads=[C.psb[6]], writes=[xo_b[k]])
            else:
                S.op("dve", lambda e: e.tensor_copy(out=xo[k][:], in_=C.ps[6][:]), reads=[C.psb[6]], writes=[xo_b[k]])
            S.dma("sp", xT_out[tt][kc * 128:(kc + 1) * 128, :], xo[k][:], reads=[xo_b[k]], writes=[xT_out_b[tt]])
            if kc == 7 and after_tile is not None:
                after_tile(tt)

        emit_load(0)
        emit_T(0)
        for t in range(ntile):
            xt = xtb[t % 2]
            xb = x_b[t % 2]
            for j in range(22):
                pa, pb_ = (0, 1) if j % 2 == 0 else (2, 3)
                gi = jgrp[j]
                for kc in range(8):
                    S.op("pe", lambda e, j=j, kc=kc, pa=pa: e.matmul(
                        out=C.ps[pa][:], lhsT=w1[:, kc, j * 128:(j + 1) * 128], rhs=xT[:, kc, :],
                        start=(kc == 0), stop=(kc == 7)), reads=[w1_b[gi], xT_b], writes=[C.psb[pa]])
                for kc in range(8):
                    S.op("pe", lambda e, j=j, kc=kc, pb_=pb_: e.matmul(
                        out=C.ps[pb_][:], lhsT=w1[:, kc, DFF + j * 128:DFF + (j + 1) * 128], rhs=xT[:, kc, :],
                        start=(kc == 0), stop=(kc == 7)), reads=[w1_b[gi], xT_b], writes=[C.psb[pb_]])
                S.op("act", lambda e, pa=pa: e.activation(out=sa[:], in_=C.ps[pa][:], func=AF.Silu), reads=[C.psb[pa]], writes=[sa_b])
                S.op("dve", lambda e, pb_=pb_, j=j: e.tensor_tensor(out=gT[:, j, :], in0=C.ps[pb_][:], in1=sa[:], op=ALU.mult),
                     reads=[C.psb[pb_], sa_b], writes=[gT_b[j]])
                if xT_out is not None and t >= 1 and 5 <= j < 13:
                    emit_Tout_chunk(t - 1, j - 5)
                if j == 13 and t + 1 < ntile:
                    emit_load(t + 1)
            if t + 1 < ntile:
                emit_T(t + 1)
            for blk in range(4):
                for hh in range(2):
                    po = 4 + (blk * 2 + hh) % 2
                    for j in range(22):
                        S.op("pe", lambda e, j=j, blk=blk, hh=hh, po=po: e.matmul(
                            out=C.ps[po][:], lhsT=gT[:, j, blk * 128:(blk + 1) * 128], rhs=w2[:, j, hh * 512:(hh + 1) * 512],
                            start=(j == 0), stop=(j == 21)), reads=[gT_b[j], w2_b[j]], writes=[C.psb[po]])
                    S.op("dve", lambda e, blk=blk, hh=hh, po=po, xt=xt: e.scalar_tensor_tensor(
                        out=xt[:, blk, hh * 512:(hh + 1) * 512], in0=C.ps[po][:], scalar=0.5,
                        in1=xt[:, blk, hh * 512:(hh + 1) * 512], op0=ALU.mult, op1=ALU.add),
                        reads=[C.psb[po], xb], writes=[xb])
            layer_norm_tile(C, xt, xb, g_bc, b_bc, gb_b, stats, mv, rstd, st_b)
            S.dma("sp", xout_v[t], xt[:], reads=[xb], writes=[xout_b[t] if isinstance(xout_b, list) else xout_b])
            if xT_out is None and after_tile is not None:
                after_tile(t)
        if xT_out is not None:
            for kc in range(8):
                emit_Tout_chunk(ntile - 1, kc)
        S.barrier()


def load_w_bf16(C, es, name, w_dram, ncols, q="pool"):
    w = es.enter_context(C.nc.sbuf_tensor(name, [P, 8, ncols], BF16))
    b = Buf(name)
    C.S.dma("pool", w[:], w_dram.rearrange("(kc p) f -> p kc f", p=P), writes=[b])
    return w, b


def proj_fm(C, w, w_b, c0, c1, xT, xT_b, pi):
    for kc in range(8):
        C.S.op("pe", lambda e, kc=kc: e.matmul(out=C.ps[pi][0:c1 - c0, :], lhsT=w[:, kc, c0:c1], rhs=xT[:, kc, :],
                                               start=(kc == 0), stop=(kc == 7)), reads=[w_b, xT_b], writes=[C.psb[pi]])


def proj_tm(C, w, w_b, c0, c1, xT, xT_b, blk, pi):
    for kc in range(8):
        C.S.op("pe", lambda e, kc=kc: e.matmul(out=C.ps[pi][:, 0:c1 - c0], lhsT=xT[:, kc, blk * 128:(blk + 1) * 128],
                                               rhs=w[:, kc, c0:c1], start=(kc == 0), stop=(kc == 7)),
               reads=[w_b, xT_b], writes=[C.psb[pi]])


def fetch_xT(C, xfull, xfull_b, t, xTs, xT_bs):
    k = t % 2
    C.S.dma("sp", xTs[k][:], xfull(t), reads=[xfull_b[t] if isinstance(xfull_b, list) else xfull_b], writes=[xT_bs[k]])
    return xTs[k], xT_bs[k]


def sb_phase(C, xfull, xfull_b, w_sb, yT, yT_b, TT, consts, tag):
    nc, S = C.nc, C.S
    ntile = TT // 512
    with contextlib.ExitStack() as es:
        w, w_b = load_w_bf16(C, es, f"wsb{tag}", w_sb, 384)
        QT = es.enter_context(nc.sbuf_tensor(f"sbQT{tag}", [P, TT], BF16))
        KT = es.enter_context(nc.sbuf_tensor(f"sbKT{tag}", [P, TT], BF16))
        V = es.enter_context(nc.sbuf_tensor(f"sbV{tag}", [P, TT // 128, 128], BF16))
        xTs = [es.enter_context(nc.sbuf_tensor(f"sbxT{tag}{k}", [P, 8, 512], BF16)) for k in range(2)]
        xT_bs = [Buf(), Buf()]
        negmask = es.enter_context(nc.sbuf_tensor(f"sbnm{tag}", [P, 4, 512], BF16))
        identb = es.enter_context(nc.sbuf_tensor(f"sbidb{tag}", [P, P], BF16))
        ntri = es.enter_context(nc.sbuf_tensor(f"sbntri{tag}", [P, P], BF16))
        ones = es.enter_context(nc.sbuf_tensor(f"sbones{tag}", [P, P], BF16))
        one1 = es.enter_context(nc.sbuf_tensor(f"sbone1{tag}", [P, 1], F32))
        cb = Buf()
        S.dma("pool", negmask[:], consts["sb_negmask"], writes=[cb])
        S.dma("pool", identb[:], consts["ident"], writes=[cb])
        S.dma("pool", ntri[:], consts["ntri"], writes=[cb])
        S.dma("pool", ones[:], consts["ones"], writes=[cb])
        S.op("pool", lambda e: e.memset(one1[:], 1.0), writes=[cb])
        x_b, xT_b = Buf(), Buf()
        QT_b = [Buf() for _ in range(ntile)]
        KT_b = [Buf() for _ in range(ntile)]
        V_b = [Buf() for _ in range(ntile)]
        xv = xfull if callable(xfull) else (lambda t, _v=xfull.rearrange("(t b p) d -> t p b d", b=4, p=P): _v[t])
        for t in range(ntile):
            xT, xT_b = fetch_xT(C, xfull, xfull_b, t, xTs, xT_bs)
            proj_fm(C, w, w_b, 0, 128, xT, xT_b, 0)
            S.op("act", lambda e, t=t: e.mul(out=QT[:, t * 512:(t + 1) * 512], in_=C.ps[0][:], mul=0.125),
                 reads=[C.psb[0]], writes=[QT_b[t]])
            proj_fm(C, w, w_b, 128, 256, xT, xT_b, 1)
            S.op("dve", lambda e, t=t: e.tensor_copy(out=KT[:, t * 512:(t + 1) * 512], in_=C.ps[1][:]),
                 reads=[C.psb[1]], writes=[KT_b[t]])
            for blk in range(4):
                pi = 2 + blk % 2
                proj_tm(C, w, w_b, 256, 384, xT, xT_b, blk, pi)
                S.op("act" if blk % 2 else "dve",
                     (lambda e, t=t, blk=blk, pi=pi: e.copy(out=V[:, t * 4 + blk, :], in_=C.ps[pi][:, 0:128])) if blk % 2 else
                     (lambda e, t=t, blk=blk, pi=pi: e.tensor_copy(out=V[:, t * 4 + blk, :], in_=C.ps[pi][:, 0:128])),
                     reads=[C.psb[pi]], writes=[V_b[t]])
        NCH = 4
        Ech = [es.enter_context(nc.sbuf_tensor(f"sbE2{tag}{c}", [P, 512], F32)) for c in range(NCH)]
        Lch = [es.enter_context(nc.sbuf_tensor(f"sbL2{tag}{c}", [P, 512], F32)) for c in range(NCH)]
        Sch = [[es.enter_context(nc.sbuf_tensor(f"sbS2{tag}{c}{k}", [P, 512], F32)) for k in range(2)] for c in range(NCH)]
        Wch = [es.enter_context(nc.sbuf_tensor(f"sbW2{tag}{c}", [P, 512], BF16)) for c in range(NCH)]
        Lhi = [es.enter_context(nc.sbuf_tensor(f"sbLh{tag}{c}", [P, 512], BF16)) for c in range(NCH)]
        Llo = [es.enter_context(nc.sbuf_tensor(f"sbLl{tag}{c}", [P, 512], BF16)) for c in range(NCH)]
        Shi = [es.enter_context(nc.sbuf_tensor(f"sbSh{tag}{c}", [P, 512], BF16)) for c in range(NCH)]
        Slo = [es.enter_context(nc.sbuf_tensor(f"sbSl{tag}{c}", [P, 512], BF16)) for c in range(NCH)]
        Lh_b = [Buf() for _ in range(NCH)]
        Ll_b = [Buf() for _ in range(NCH)]
        Sh_b = [Buf() for _ in range(NCH)]
        Sl_b = [Buf() for _ in range(NCH)]
        ych = [es.enter_context(nc.sbuf_tensor(f"sby2{tag}{c}", [P, 512], BF16)) for c in range(NCH)]
        E_b = [Buf() for _ in range(NCH)]
        L_b = [Buf() for _ in range(NCH)]
        S_b = [[Buf(), Buf()] for _ in range(NCH)]
        W_b = [Buf() for _ in range(NCH)]
        y_b = [Buf() for _ in range(NCH)]
        order = []
        lo, hi = 0, ntile - 1
        while lo <= hi:
            order.append(hi)
            hi -= 1
            if lo <= hi:
                order.append(lo)
                lo += 1
        queues = [[], []]
        load = [0, 0]
        for ti in sorted(range(ntile), key=lambda q: -q):
            sidx = 0 if load[0] <= load[1] else 1
            queues[sidx].append(ti)
            load[sidx] += 4 * ti + 4
        state = [None] * NCH
        qpos = [0, 0]

        def next_tile(slot):
            if qpos[slot] < len(queues[slot]):
                ti = queues[slot][qpos[slot]]
                qpos[slot] += 1
                return ti
            return None
        for slot in range(2):
            ti = next_tile(slot)
            for h in range(2):
                state[slot * 2 + h] = None if ti is None else [ti, 0]
        def mkinfo(c):
            i, n = state[c]
            nsteps = 4 * i + 4
            jb = 4 * i + 3 - n
            return dict(i=i, n=n, nsteps=nsteps, jb=jb, diag=jb >= 4 * i, r=jb - 4 * i, kt=jb // 4, h=c % 2,
                        hs=slice((c % 2) * 64, (c % 2) * 64 + 64), pz=c, py=4 + c // 2, k=n % 2)

        def H1(act, info):
            for c in act:
                f = info[c]
                if f["n"] > 0:
                    k = f["k"]
                    S.op("dve", lambda e, c=c, k=k: e.tensor_copy(out=Shi[c][:], in_=Sch[c][k][:]), reads=[S_b[c][k]], writes=[Sh_b[c]])
            for c in act:
                f = info[c]
                if f["n"] > 0:
                    k = f["k"]
                    S.op("dve", lambda e, c=c, k=k: e.tensor_tensor(out=Slo[c][:], in0=Sch[c][k][:], in1=Shi[c][:], op=ALU.subtract),
                         reads=[S_b[c][k], Sh_b[c]], writes=[Sl_b[c]])
            for c in act:
                f = info[c]
                S.op("pe", lambda e, f=f: e.matmul(out=C.ps[f["pz"]][:], lhsT=KT[f["hs"], f["jb"] * 128:(f["jb"] + 1) * 128],
                                                   rhs=QT[f["hs"], f["i"] * 512:(f["i"] + 1) * 512], start=True, stop=not f["diag"]),
                     reads=[KT_b[f["kt"]], QT_b[f["i"]]], writes=[C.psb[f["pz"]]])
                if f["diag"]:
                    S.op("pe", lambda e, f=f: e.matmul(out=C.ps[f["pz"]][:], lhsT=identb[:], rhs=negmask[:, f["r"], :], start=False, stop=True),
                         reads=[cb], writes=[C.psb[f["pz"]]])
            for c in act:
                f = info[c]
                S.op("act", lambda e, c=c, f=f: e.activation(out=Ech[c][:], in_=C.ps[f["pz"]][:], func=AF.Exp),
                     reads=[C.psb[f["pz"]]], writes=[E_b[c]])
            for c in act:
                S.op("act", lambda e, c=c: e.activation(out=Lch[c][:], in_=Ech[c][:], func=AF.Ln, bias=one1[:], scale=1.0),
                     reads=[E_b[c], cb], writes=[L_b[c]])

        def H2(act, info):
            for c in act:
                if False:
                    S.op("act", lambda e, c=c: e.copy(out=Lhi[c][:], in_=Lch[c][:]), reads=[L_b[c]], writes=[Lh_b[c]])
                else:
                    S.op("dve", lambda e, c=c: e.tensor_copy(out=Lhi[c][:], in_=Lch[c][:]), reads=[L_b[c]], writes=[Lh_b[c]])
            for c in act:
                S.op("dve", lambda e, c=c: e.tensor_tensor(out=Llo[c][:], in0=Lch[c][:], in1=Lhi[c][:], op=ALU.subtract),
                     reads=[L_b[c], Lh_b[c]], writes=[Ll_b[c]])
            for c in act:
                f = info[c]
                if f["n"] > 0:
                    S.op("pe", lambda e, c=c, f=f: e.matmul(out=C.ps[f["pz"]][:], lhsT=ones[:], rhs=Shi[c][:], start=False, stop=False,
                                                            skip_group_check=True), reads=[cb, Sh_b[c]], writes=[C.psb[f["pz"]]])
                    S.op("pe", lambda e, c=c, f=f: e.matmul(out=C.ps[f["pz"]][:], lhsT=ones[:], rhs=Slo[c][:], start=False, stop=False,
                                                            skip_group_check=True), reads=[cb, Sl_b[c]], writes=[C.psb[f["pz"]]])
                S.op("pe", lambda e, c=c, f=f: e.matmul(out=C.ps[f["pz"]][:], lhsT=ntri[:], rhs=Lhi[c][:], start=False, stop=False,
                                                        skip_group_check=True), reads=[cb, Lh_b[c]], writes=[C.psb[f["pz"]]])
                S.op("pe", lambda e, c=c, f=f: e.matmul(out=C.ps[f["pz"]][:], lhsT=ntri[:], rhs=Llo[c][:], start=False, stop=True,
                                                        skip_group_check=True), reads=[cb, Ll_b[c]], writes=[C.psb[f["pz"]]])
            for c in act:
                f = info[c]
                S.op("act", lambda e, c=c, f=f: e.activation(out=Wch[c][:], in_=C.ps[f["pz"]][:], func=AF.Exp),
                     reads=[C.psb[f["pz"]]], writes=[W_b[c]])
                if f["n"] < f["nsteps"] - 1:
                    k, k2 = f["k"], 1 - f["k"]
                    if f["n"] == 0:
                        S.op("pool", lambda e, c=c, k2=k2: e.tensor_copy(out=Sch[c][k2][:], in_=Lch[c][:]), reads=[L_b[c]], writes=[S_b[c][k2]])
                    else:
                        S.op("pool", lambda e, c=c, k=k, k2=k2: e.tensor_tensor(out=Sch[c][k2][:], in0=Sch[c][k][:], in1=Lch[c][:], op=ALU.add),
                             reads=[L_b[c], S_b[c][k]], writes=[S_b[c][k2]])
            for c in act:
                f = info[c]
                po = (c % 2) * 64
                S.op("pe", lambda e, c=c, f=f, po=po: e.matmul(out=C.ps[f["py"]][po:po + 64, :], lhsT=V[:, f["jb"], f["hs"]], rhs=Wch[c][:],
                                                               start=(f["n"] == 0), stop=(f["n"] == f["nsteps"] - 1), skip_group_check=True),
                     reads=[V_b[f["kt"]], W_b[c]], writes=[C.psb[f["py"]]])
            for c in act:
                f = info[c]
                if f["n"] == f["nsteps"] - 1:
                    po = (c % 2) * 64
                    S.op("dve", lambda e, c=c, f=f, po=po: e.tensor_copy(out=ych[c][po:po + 64, :], in_=C.ps[f["py"]][po:po + 64, :]),
                         reads=[C.psb[f["py"]]], writes=[y_b[c]])
                    S.dma("sp", yT[f["h"] * 64:(f["h"] + 1) * 64, f["i"] * 512:(f["i"] + 1) * 512], ych[c][po:po + 64, :],
                          reads=[y_b[c]], writes=[yT_b])
                    state[c] = "done"
                else:
                    state[c][1] += 1
            for slot in range(2):
                cs = [slot * 2, slot * 2 + 1]
                if all(state[c] == "done" for c in cs):
                    ti = next_tile(slot)
                    for c in cs:
                        state[c] = None if ti is None else [ti, 0]

        pending = [None, None]
        while True:
            progressed = False
            for slot in range(2):
                other = 1 - slot
                cs_ = [c for c in (2 * slot, 2 * slot + 1) if state[c] is not None]
                if cs_ and pending[slot] is None:
                    inf_ = {c: mkinfo(c) for c in cs_}
                    H1(cs_, inf_)
                    pending[slot] = (cs_, inf_)
                    progressed = True
                if pending[other] is not None:
                    H2(*pending[other])
                    pending[other] = None
                    progressed = True
            if not progressed:
                break
        S.barrier()


def make_consts_np():
    c = {}
    c["ident"] = np.eye(P, dtype=np.float32)
    j = np.arange(P)[:, None]
    s = np.arange(P)[None, :]
    c["ntri"] = -(j >= s).astype(np.float32)
    c["ones"] = -np.ones((P, P), np.float32)
    nm = np.zeros((P, 4, 512), np.float32)
    for r in range(4):
        key = 128 * r + np.arange(P)[:, None]
        col = np.arange(512)[None, :]
        nm[:, r, :] = np.where(key < col, 0.0, -30000.0)
    c["sb_negmask"] = nm
    make_swa_consts_np(c)
    make_ml_consts_np(c)
    return c


def dram_ap(t_ap, offset, pattern):
    return bass.AP(tensor=t_ap.tensor, offset=offset, ap=pattern)


def swa_phase(C, xfull, xfull_b, w_swa, rb_dram, sink_dram, frev_scr, yT, yT_b, TT, consts, tag):
    nc, S = C.nc, C.S
    ntile = TT // 512
    nblk = TT // 128
    C.set_psum(2)
    with contextlib.ExitStack() as es:
        w, w_b = load_w_bf16(C, es, f"wsw{tag}", w_swa, 448)
        QT = [es.enter_context(nc.sbuf_tensor(f"swQT{tag}{g}", [P, TT], BF16)) for g in range(2)]
        KT = es.enter_context(nc.sbuf_tensor(f"swKT{tag}", [P, TT], BF16))
        V = es.enter_context(nc.sbuf_tensor(f"swV{tag}", [P, nblk, 64], BF16))
        xTs = [es.enter_context(nc.sbuf_tensor(f"swxT{tag}{k}", [P, 8, 512], BF16)) for k in range(2)]
        xT_bs = [Buf(), Buf()]
        identb = es.enter_context(nc.sbuf_tensor(f"swidb{tag}", [P, P], BF16))
        Jm = es.enter_context(nc.sbuf_tensor(f"swJ{tag}", [P, P], F32))
        rb = es.enter_context(nc.sbuf_tensor(f"swrb{tag}", [P, P], F32))
        oh = es.enter_context(nc.sbuf_tensor(f"swoh{tag}", [P, 384], F32))
        fneg = es.enter_context(nc.sbuf_tensor(f"swfn{tag}", [4, 384], F32))
        frev = es.enter_context(nc.sbuf_tensor(f"swfr{tag}", [4, 384], F32))
        Hk = es.enter_context(nc.sbuf_tensor(f"swH{tag}", [P, 4, 256], F32))
        bias = es.enter_context(nc.sbuf_tensor(f"swbias{tag}", [P, 4, 256], F32))
        sink = es.enter_context(nc.sbuf_tensor(f"swsink{tag}", [P, 4], F32))
        Sb = es.enter_context(nc.sbuf_tensor(f"swS{tag}", [P, 4, 256], F32))
        pf = es.enter_context(nc.sbuf_tensor(f"swp{tag}", [P, 4, 256], F32))
        pn = es.enter_context(nc.sbuf_tensor(f"swpn{tag}", [P, 4, 256], BF16))
        pT = es.enter_context(nc.sbuf_tensor(f"swpT{tag}", [P, 8, 128], BF16))
        small = es.enter_context(nc.sbuf_tensor(f"swsm{tag}", [P, 6, 4], F32))
        yo = es.enter_context(nc.sbuf_tensor(f"swyo{tag}", [64, 4, 512], BF16))
        cb, x_b, xT_b = Buf(), Buf(), Buf()
        S.dma("pool", identb[:], consts["ident"], writes=[cb])
        S.dma("sp", Jm[:], consts["J"], writes=[cb])
        S.dma("sp", rb[:], rb_dram, writes=[cb])
        S.dma("sp", oh[:], consts["swa_oh"], writes=[cb])
        S.dma("sp", fneg[:], consts["swa_fneg"], writes=[cb])
        S.dma("sp", sink[:], sink_dram, writes=[cb])
        S.op("pe", lambda e: e.matmul(out=C.ps[0][:, 0:384], lhsT=rb[:], rhs=oh[:], start=True, stop=True),
             reads=[cb], writes=[C.psb[0]])
        fr_b, scr_b, H_b, bias_b = Buf(), Buf(), Buf(), Buf()
        S.op("dve", lambda e: e.tensor_tensor(out=frev[:], in0=C.ps[0][0:4, 0:384], in1=fneg[:], op=ALU.add),
             reads=[C.psb[0], cb], writes=[fr_b])
        S.dma("sp", frev_scr, frev[:], reads=[fr_b], writes=[scr_b])
        S.dma("sp", Hk[:], dram_ap(frev_scr, 0, [[1, 128], [384, 4], [1, 256]]), reads=[scr_b], writes=[H_b])
        for hh in range(2):
            S.op("pe", lambda e, hh=hh: e.matmul(out=C.ps[1 + hh][:], lhsT=Jm[:], rhs=Hk[:, 2 * hh:2 * hh + 2, :],
                                                 start=True, stop=True), reads=[cb, H_b], writes=[C.psb[1 + hh]])
            S.op("dve", lambda e, hh=hh: e.tensor_copy(out=bias[:, 2 * hh:2 * hh + 2, :], in_=C.ps[1 + hh][:]),
                 reads=[C.psb[1 + hh]], writes=[bias_b])
        QT_b = [Buf() for _ in range(ntile)]
        KT_b = [Buf() for _ in range(ntile)]
        V_b = [Buf() for _ in range(ntile)]
        xv = xfull if callable(xfull) else (lambda t, _v=xfull.rearrange("(t b p) d -> t p b d", b=4, p=P): _v[t])
        for t in range(ntile):
            xT, xT_b = fetch_xT(C, xfull, xfull_b, t, xTs, xT_bs)
            for g in range(2):
                proj_fm(C, w, w_b, g * 128, (g + 1) * 128, xT, xT_b, g)
                S.op("act", lambda e, t=t, g=g: e.mul(out=QT[g][:, t * 512:(t + 1) * 512], in_=C.ps[g][:], mul=0.125),
                     reads=[C.psb[g]], writes=[QT_b[t]])
            proj_fm(C, w, w_b, 256, 384, xT, xT_b, 2)
            S.op("dve", lambda e, t=t: e.tensor_copy(out=KT[:, t * 512:(t + 1) * 512], in_=C.ps[2][:]),
                 reads=[C.psb[2]], writes=[KT_b[t]])
            for blk in range(4):
                pi = 3 + blk % 2
                proj_tm(C, w, w_b, 384, 448, xT, xT_b, blk, pi)
                S.op("dve", lambda e, t=t, blk=blk, pi=pi: e.tensor_copy(out=V[:, t * 4 + blk, :], in_=C.ps[pi][:, 0:64]),
                     reads=[C.psb[pi]], writes=[V_b[t]])
        NS = 2
        Sb2 = [Sb] + [es.enter_context(nc.sbuf_tensor(f"swS{tag}b", [P, 4, 256], F32))]
        pf2 = [pf] + [es.enter_context(nc.sbuf_tensor(f"swp{tag}b", [P, 4, 256], F32))]
        pn2 = [pn] + [es.enter_context(nc.sbuf_tensor(f"swpn{tag}b", [P, 4, 256], BF16))]
        pT2 = [pT] + [es.enter_context(nc.sbuf_tensor(f"swpT{tag}b", [P, 8, 128], BF16))]
        sm2 = [small] + [es.enter_context(nc.sbuf_tensor(f"swsm{tag}b", [P, 6, 4], F32))]
        S_b = [Buf(), Buf()]
        p_b = [Buf(), Buf()]
        pn_b = [Buf(), Buf()]
        pT_b = [Buf(), Buf()]
        sm_b = [Buf(), Buf()]
        yo_b = Buf()
        for n0 in range(0, nblk, NS):
            blks = [(n0 + a, a) for a in range(NS) if n0 + a < nblk]
            kwd = {n: (128 if n == 0 else 256) for n, a in blks}
            for n, a in blks:
                kw = kwd[n]
                k0 = 256 - kw
                t_q = n // 4
                for h in range(4):
                    g, hs = h % 2, slice((h // 2) * 64, (h // 2) * 64 + 64)
                    bank = 2 * a + h // 2
                    col = (h % 2) * 256
                    S.op("pe", lambda e, g=g, hs=hs, bank=bank, col=col, n=n, kw=kw, k0=k0: e.matmul(
                        out=C.ps[bank][:, col + k0:col + 256], lhsT=QT[g][hs, n * 128:(n + 1) * 128],
                        rhs=KT[hs, (n + 1) * 128 - kw:(n + 1) * 128], start=True, stop=True, skip_group_check=True),
                        reads=[QT_b[t_q], KT_b[t_q], KT_b[max(0, (n - 1) // 4)]], writes=[C.psb[bank]])
            for n, a in blks:
                k0 = 256 - kwd[n]
                mx, negm, dd, esk, rs, rden = [sm2[a][:, i, :] for i in range(6)]
                for bk in range(2):
                    bank = 2 * a + bk
                    S.op("dve", lambda e, bank=bank, bk=bk, k0=k0, a=a: e.tensor_tensor(
                        out=Sb2[a][:, 2 * bk:2 * bk + 2, k0:256],
                        in0=C.ps[bank][:].rearrange("p (h k) -> p h k", h=2)[:, :, k0:256],
                        in1=bias[:, 2 * bk:2 * bk + 2, k0:256], op=ALU.add),
                        reads=[C.psb[bank], bias_b], writes=[S_b[a]])
                S.op("dve", lambda e, k0=k0, a=a, mx=mx: e.tensor_reduce(out=mx, in_=Sb2[a][:, :, k0:256], axis=AX.X, op=ALU.max),
                     reads=[S_b[a]], writes=[sm_b[a]])
                S.op("dve", lambda e, mx=mx: e.tensor_tensor(out=mx, in0=mx, in1=sink[:], op=ALU.max), reads=[sm_b[a], cb], writes=[sm_b[a]])
                S.op("dve", lambda e, mx=mx, negm=negm: e.tensor_scalar(out=negm, in0=mx, scalar1=-1.0, scalar2=None, op0=ALU.mult),
                     reads=[sm_b[a]], writes=[sm_b[a]])
                S.op("dve", lambda e, mx=mx, dd=dd: e.tensor_tensor(out=dd, in0=sink[:], in1=mx, op=ALU.subtract), reads=[sm_b[a], cb], writes=[sm_b[a]])
            for n, a in blks:
                k0 = 256 - kwd[n]
                mx, negm, dd, esk, rs, rden = [sm2[a][:, i, :] for i in range(6)]
                for h in range(4):
                    S.op("act", lambda e, h=h, k0=k0, a=a, negm=negm, rs=rs: e.activation(
                        out=pf2[a][:, h, k0:256], in_=Sb2[a][:, h, k0:256], func=AF.Exp, bias=negm[:, h:h + 1], scale=1.0, accum_out=rs[:, h:h + 1]),
                        reads=[S_b[a], sm_b[a]], writes=[p_b[a], sm_b[a]])
                S.op("act", lambda e, esk=esk, dd=dd: e.activation(out=esk, in_=dd, func=AF.Exp), reads=[sm_b[a]], writes=[sm_b[a]])
            for n, a in blks:
                k0 = 256 - kwd[n]
                mx, negm, dd, esk, rs, rden = [sm2[a][:, i, :] for i in range(6)]
                S.op("dve", lambda e, rden=rden, rs=rs, esk=esk: e.tensor_tensor(out=rden, in0=rs, in1=esk, op=ALU.add), reads=[sm_b[a]], writes=[sm_b[a]])
                S.op("dve", lambda e, rden=rden: e.reciprocal(out=rden, in_=rden), reads=[sm_b[a]], writes=[sm_b[a]])
                for h in range(4):
                    if h % 2:
                        S.op("dve", lambda e, h=h, k0=k0, a=a, rden=rden: e.tensor_scalar(
                            out=pn2[a][:, h, k0:256], in0=pf2[a][:, h, k0:256], scalar1=rden[:, h:h + 1], scalar2=None, op0=ALU.mult),
                            reads=[p_b[a], sm_b[a]], writes=[pn_b[a]])
                    else:
                        S.op("act", lambda e, h=h, k0=k0, a=a, rden=rden: e.mul(out=pn2[a][:, h, k0:256], in_=pf2[a][:, h, k0:256], mul=rden[:, h:h + 1]),
                             reads=[p_b[a], sm_b[a]], writes=[pn_b[a]])
            for n, a in blks:
                kw = kwd[n]
                k0 = 256 - kw
                nkb = kw // 128
                for h in range(4):
                    for kb in range(nkb):
                        idx = h * 2 + kb
                        S.op("pe", lambda e, h=h, kb=kb, idx=idx, k0=k0, a=a: e.transpose(
                            out=C.ps_bfs[a][:, idx * 128:(idx + 1) * 128], in_=pn2[a][:, h, k0 + kb * 128:k0 + (kb + 1) * 128],
                            identity=identb[:]), reads=[pn_b[a], cb], writes=[C.psbf_b[a]])
            for n, a in blks:
                if a == 0:
                    S.op("act", lambda e, a=a: e.copy(out=pT2[a][:].rearrange("p a b -> p (a b)"), in_=C.ps_bfs[a][:]), reads=[C.psbf_b[a]], writes=[pT_b[a]])
                else:
                    S.op("dve", lambda e, a=a: e.tensor_copy(out=pT2[a][:].rearrange("p a b -> p (a b)"), in_=C.ps_bfs[a][:]), reads=[C.psbf_b[a]], writes=[pT_b[a]])
            for n, a in blks:
                nkb = kwd[n] // 128
                po = 4 + a
                for h in range(4):
                    for kb in range(nkb):
                        idx = h * 2 + kb
                        kblk = n - (nkb - 1) + kb
                        hd = (h % 2) * 2 + h // 2
                        S.op("pe", lambda e, hd=hd, kb=kb, idx=idx, kblk=kblk, nkb=nkb, a=a, po=po: e.matmul(
                            out=C.ps[po][0:64, hd * 128:(hd + 1) * 128], lhsT=V[:, kblk, :], rhs=pT2[a][:, idx, :],
                            start=(kb == 0), stop=(kb == nkb - 1), skip_group_check=True),
                            reads=[V_b[kblk // 4], pT_b[a]], writes=[C.psb[po]])
            for n, a in blks:
                po = 4 + a
                S.op("dve" if a == 0 else "act",
                     (lambda e, n=n, po=po: e.tensor_copy(out=yo[:, :, (n % 4) * 128:(n % 4 + 1) * 128],
                                                          in_=C.ps[po][0:64, :].rearrange("p (h q) -> p h q", h=4))) if a == 0 else
                     (lambda e, n=n, po=po: e.copy(out=yo[:, :, (n % 4) * 128:(n % 4 + 1) * 128],
                                                   in_=C.ps[po][0:64, :].rearrange("p (h q) -> p h q", h=4))),
                     reads=[C.psb[po]], writes=[yo_b])
                if n % 4 == 3:
                    S.dma("sp", yT.rearrange("(h d) t -> d h t", d=64)[:, :, (n // 4) * 512:(n // 4 + 1) * 512], yo[:],
                          reads=[yo_b], writes=[yT_b])
        S.barrier()
        C.set_psum(1)


def t5_bucket_np(dist):
    max_exact = 16
    d = np.maximum(dist, 1)
    large = max_exact + (np.log(d / max_exact) / np.log(128 / max_exact) * (32 - max_exact)).astype(np.int32)
    large = np.minimum(large, 31)
    return np.where(dist < max_exact, dist, large).astype(np.int32)


def make_swa_consts_np(c):
    a = np.arange(384)
    dist = 255 - a
    valid = (dist >= 0) & (dist < 128)
    bucket = t5_bucket_np(np.clip(dist, 0, None))
    oh = np.zeros((32, 384), np.float32)
    oh[bucket[valid], a[valid]] = 1.0
    c["swa_oh"] = np.pad(oh, ((0, 96), (0, 0)))
    c["swa_fneg"] = np.tile(np.where(valid, 0.0, -30000.0).astype(np.float32)[None, :], (4, 1))
    c["J"] = np.eye(P, dtype=np.float32)[::-1].copy()
    return c


def mlstm_phase(C, xfull, xfull_b, w_ml, cw_dram, cb_dram, ifb_dram, ng_dram, yT, yT_b, TT, consts, tag):
    nc, S = C.nc, C.S
    ntile = TT // 512
    nb = TT // 128
    n2 = 2 * nb
    with contextlib.ExitStack() as es:
        w, w_b = load_w_bf16(C, es, f"wml{tag}", w_ml, 640)
        sb = lambda name, shape, dt=F32: es.enter_context(nc.sbuf_tensor(f"ml{name}{tag}", shape, dt))
        QT = sb("QT", [P, TT], BF16)
        KT = sb("KT", [P, TT], BF16)
        Vext = sb("Vext", [P, nb, 2, 66], BF16)
        gso = sb("gso", [P, nb, 128])
        G4 = sb("G4", [P, nb, 4])
        xTs = [sb(f"xT{k}", [P, 8, 512], BF16) for k in range(2)]
        xT_bs = [Buf(), Buf()]
        raws = [sb(f"raw{k}", [P, 2, 515]) for k in range(2)]
        accs = [sb(f"acc{k}", [P, 2, 512]) for k in range(2)]
        cw = sb("cw", [P, 8])
        cbt = sb("cb", [P, 2])
        ifb = sb("ifb", [P, 4])
        nfb = sb("nfb", [P, 2])
        ng = sb("ng", [P, 128])
        identb = sb("idb", [P, P], BF16)
        ntriT = sb("ntriT", [P, P])
        mask8 = sb("mask8", [P, P])
        e0 = sb("e0", [P, P])
        e127 = sb("e127", [P, P])
        one1 = sb("one1", [P, 1])
        cb_ = Buf()
        S.dma("pool", identb[:], consts["ident"], writes=[cb_])
        S.dma("sp", ntriT[:], consts["ntriT"], writes=[cb_])
        S.dma("sp", mask8[:], consts["mask8"], writes=[cb_])
        S.dma("sp", e0[:], consts["e0ones"], writes=[cb_])
        S.dma("sp", e127[:], consts["e127ones"], writes=[cb_])
        S.dma("sp", cw[:], cw_dram, writes=[cb_])
        S.dma("sp", cbt[:], cb_dram, writes=[cb_])
        S.dma("sp", ifb[:], ifb_dram, writes=[cb_])
        S.dma("sp", ng[:], ng_dram, writes=[cb_])
        S.op("pool", lambda e: e.memset(one1[:], 1.0), writes=[cb_])
        S.op("dve", lambda e: e.tensor_scalar(out=nfb[:], in0=ifb[:, 2:4], scalar1=-1.0, scalar2=None, op0=ALU.mult),
             reads=[cb_], writes=[cb_])
        x_b, xT_b, V_b, gso_b, G_b = Buf(), Buf(), Buf(), Buf(), Buf()
        raw_bs, acc_bs = [Buf(), Buf()], [Buf(), Buf()]
        QT_b = [Buf() for _ in range(ntile)]
        KT_b = [Buf() for _ in range(ntile)]
        for k in range(2):
            S.op("pool", lambda e, k=k: e.memset(raws[k][:], 0.0), writes=[raw_bs[k]])
        S.op("pool", lambda e: e.memset(Vext[:], 1.0), writes=[V_b])
        xv = xfull if callable(xfull) else (lambda t, _v=xfull.rearrange("(t b p) d -> t p b d", b=4, p=P): _v[t])
        import os
        mlstage = int(os.environ.get("ML_STAGE", "9"))
        for t in range(ntile):
            if mlstage < -1:
                break
            xT, xT_b = fetch_xT(C, xfull, xfull_b, t, xTs, xT_bs)
            raw, raw_b, acc, acc_b = raws[t % 2], raw_bs[t % 2], accs[t % 2], acc_bs[t % 2]
            if t > 0:
                S.op("pool", lambda e, raw=raw, t=t: e.tensor_copy(out=raw[:, :, 0:3], in_=raws[(t - 1) % 2][:, :, 512:515]),
                     reads=[raw_bs[(t - 1) % 2]], writes=[raw_b])
            for qk in range(2):
                proj_fm(C, w, w_b, qk * 128, (qk + 1) * 128, xT, xT_b, qk)
                S.op("act", lambda e, qk=qk, raw=raw: e.copy(out=raw[:, qk, 3:515], in_=C.ps[qk][:]), reads=[C.psb[qk]], writes=[raw_b])
            for qk in range(2):
                S.op("dve", lambda e, qk=qk, raw=raw, acc=acc: e.tensor_scalar(out=acc[:, qk, :], in0=raw[:, qk, 0:512], scalar1=cw[:, 4 * qk:4 * qk + 1],
                                                              scalar2=None, op0=ALU.mult), reads=[raw_b, cb_], writes=[acc_b])
                for j in range(1, 4):
                    S.op("dve", lambda e, qk=qk, j=j, raw=raw, acc=acc: e.scalar_tensor_tensor(
                        out=acc[:, qk, :], in0=raw[:, qk, j:j + 512], scalar=cw[:, 4 * qk + j:4 * qk + j + 1], in1=acc[:, qk, :],
                        op0=ALU.mult, op1=ALU.add), reads=[raw_b, cb_, acc_b], writes=[acc_b])
                dst, dst_b = (QT, QT_b) if qk == 0 else (KT, KT_b)
                S.op("act", lambda e, qk=qk, dst=dst, t=t, acc=acc: e.activation(out=dst[:, t * 512:(t + 1) * 512], in_=acc[:, qk, :], func=AF.Silu,
                                                                        bias=cbt[:, qk:qk + 1], scale=1.0),
                     reads=[acc_b, cb_], writes=[dst_b[t]])
            for blk in range(4):
                if mlstage < 0:
                    break
                pi = 2 + blk % 2
                bi = t * 4 + blk
                proj_tm(C, w, w_b, 256, 512, xT, xT_b, blk, pi)
                proj_tm(C, w, w_b, 512, 640, xT, xT_b, blk, 4 + blk % 2)
                sub = int(os.environ.get("ML_SUB", "9"))
                if sub < 1:
                    continue
                S.op("dve", lambda e, pi=pi, bi=bi: e.tensor_copy(out=Vext[:, bi, :, 0:64],
                                                                  in_=C.ps[pi][:, 0:128].rearrange("p (h d) -> p h d", h=2)),
                     reads=[C.psb[pi]], writes=[V_b])
                if sub < 2:
                    continue
                S.op("act", lambda e, pi=pi, bi=bi: e.activation(out=gso[:, bi, :], in_=C.ps[pi][:, 128:256], func=AF.Exp, scale=-1.0),
                     reads=[C.psb[pi]], writes=[gso_b])
                S.op("act", lambda e, bi=bi: e.add(out=gso[:, bi, :], in_=gso[:, bi, :], add=1.0), reads=[gso_b], writes=[gso_b])
                S.op("dve", lambda e, bi=bi: e.reciprocal(out=gso[:, bi, :], in_=gso[:, bi, :]), reads=[gso_b], writes=[gso_b])
                if sub < 3:
                    continue
                S.op("dve", lambda e, blk=blk, bi=bi: e.tensor_copy(out=G4[:, bi, :], in_=C.ps[4 + blk % 2][:, 0:4]),
                     reads=[C.psb[4 + blk % 2]], writes=[G_b])
                S.op("pool", lambda e, bi=bi: e.tensor_tensor(out=gso[:, bi, :], in0=gso[:, bi, :], in1=ng[:], op=ALU.mult),
                     reads=[gso_b, cb_], writes=[gso_b])
        import os
        mlstage = int(os.environ.get("ML_STAGE", "9"))
        if mlstage < 1:
            S.barrier()
            return
        icol = sb("icol", [P, 2, nb])
        lf = sb("lf", [P, 2, nb])
        bcol = sb("bcol", [P, 2, nb])
        acol = sb("acol", [P, 2, nb])
        aT = sb("aT", [P, P])
        cm_tok = sb("cm_tok", [P, n2])
        amax_bc = sb("amax_bc", [P, n2])
        cmT = sb("cmT", [P, P])
        RW = sb("RW", [P, 3, P])
        mnext = sb("mnext", [1, P])
        mprev_bc = sb("mprevbc", [P, n2])
        mref_bc = sb("mrefbc", [P, n2])
        Mt = sb("Mt", [P, n2])
        r_t = sb("r_t", [P, n2])
        u_t = sb("u_t", [P, n2])
        eb_t = sb("eb_t", [P, n2])
        sc_bc = sb("sc_bc", [P, n2])
        tmp = sb("tmpg", [P, n2])
        g_b = Buf()
        fl = lambda tl: tl[:].rearrange("p h c -> p (h c)")
        for h in range(2):
            S.op("act", lambda e, h=h: e.activation(out=icol[:, h, :], in_=G4[:, :, h], func=AF.Identity, bias=ifb[:, h:h + 1], scale=1.0),
                 reads=[G_b, cb_], writes=[g_b])
            S.op("act", lambda e, h=h: e.activation(out=lf[:, h, :], in_=G4[:, :, 2 + h], func=AF.Exp, bias=nfb[:, h:h + 1], scale=-1.0),
                 reads=[G_b, cb_], writes=[g_b])
        S.op("act", lambda e: e.activation(out=fl(lf), in_=fl(lf), func=AF.Ln, bias=one1[:], scale=1.0), reads=[g_b, cb_], writes=[g_b])
        S.op("pe", lambda e: e.matmul(out=C.ps[0][:, 0:n2], lhsT=ntriT[:], rhs=fl(lf), start=True, stop=True),
             reads=[g_b, cb_], writes=[C.psb[0]])
        S.op("dve", lambda e: e.tensor_copy(out=fl(bcol), in_=C.ps[0][:, 0:n2]), reads=[C.psb[0]], writes=[g_b])
        S.op("dve", lambda e: e.tensor_tensor(out=fl(acol), in0=fl(icol), in1=fl(bcol), op=ALU.subtract), reads=[g_b], writes=[g_b])
        S.op("pe", lambda e: e.transpose(out=C.ps[1][0:n2, 0:128], in_=fl(acol), identity=C.ident[:]), reads=[g_b, C.ident_b], writes=[C.psb[1]])
        S.op("pool", lambda e: e.memset(aT[:], 0.0), writes=[g_b])
        S.op("pool", lambda e: e.memset(RW[:], 0.0), writes=[g_b])
        S.op("dve", lambda e: e.tensor_copy(out=aT[0:n2, :], in_=C.ps[1][0:n2, 0:128]), reads=[C.psb[1]], writes=[g_b])
        S.op("dve", lambda e: e.tensor_tensor_scan(out=cmT[:], data0=aT[:], data1=aT[:], initial=-1.0e30, op0=ALU.max, op1=ALU.max),
             reads=[g_b], writes=[g_b])
        S.op("pe", lambda e: e.transpose(out=C.ps[2][:, 0:128], in_=cmT[:], identity=C.ident[:]), reads=[g_b, C.ident_b], writes=[C.psb[2]])
        S.op("dve", lambda e: e.tensor_copy(out=cm_tok[:], in_=C.ps[2][:, 0:n2]), reads=[C.psb[2]], writes=[g_b])
        S.op("pe", lambda e: e.matmul(out=C.ps[3][:, 0:n2], lhsT=e127[:], rhs=cm_tok[:], start=True, stop=True), reads=[g_b, cb_], writes=[C.psb[3]])
        S.op("pe", lambda e: e.matmul(out=C.ps[4][:, 0:n2], lhsT=e127[:], rhs=fl(bcol), start=True, stop=True), reads=[g_b, cb_], writes=[C.psb[4]])
        S.op("dve", lambda e: e.tensor_copy(out=amax_bc[:], in_=C.ps[3][:, 0:n2]), reads=[C.psb[3]], writes=[g_b])
        S.op("dve", lambda e: e.tensor_copy(out=RW[:, 1, 0:n2], in_=C.ps[4][:, 0:n2]), reads=[C.psb[4]], writes=[g_b])
        S.op("dve", lambda e: e.tensor_copy(out=RW[:, 0, 0:n2], in_=amax_bc[:]), reads=[g_b], writes=[g_b])
        for h in range(2):
            S.op("dve", lambda e, h=h: e.tensor_tensor_scan(out=mnext[0:1, h * nb:(h + 1) * nb], data0=RW[0:1, 0, h * nb:(h + 1) * nb],
                                                            data1=RW[0:1, 1, h * nb:(h + 1) * nb], initial=0.0, op0=ALU.max, op1=ALU.add),
                 reads=[g_b], writes=[g_b])
            if nb > 1:
                S.op("dve", lambda e, h=h: e.tensor_copy(out=RW[0:1, 2, h * nb + 1:(h + 1) * nb], in_=mnext[0:1, h * nb:(h + 1) * nb - 1]),
                     reads=[g_b], writes=[g_b])
        S.op("pe", lambda e: e.matmul(out=C.ps[0][:, 0:n2], lhsT=e0[:], rhs=RW[:, 2, 0:n2], start=True, stop=True),
             reads=[g_b, cb_], writes=[C.psb[0]])
        S.op("dve", lambda e: e.tensor_copy(out=mprev_bc[:], in_=C.ps[0][:, 0:n2]), reads=[C.psb[0]], writes=[g_b])
        S.op("dve", lambda e: e.tensor_tensor(out=mref_bc[:], in0=amax_bc[:], in1=mprev_bc[:], op=ALU.max), reads=[g_b], writes=[g_b])
        S.op("dve", lambda e: e.tensor_tensor(out=Mt[:], in0=cm_tok[:], in1=mprev_bc[:], op=ALU.max), reads=[g_b], writes=[g_b])
        S.op("dve", lambda e: e.tensor_tensor(out=tmp[:], in0=mref_bc[:], in1=Mt[:], op=ALU.subtract), reads=[g_b], writes=[g_b])
        S.op("act", lambda e: e.activation(out=r_t[:], in_=tmp[:], func=AF.Exp), reads=[g_b], writes=[g_b])
        S.op("dve", lambda e: e.tensor_tensor(out=tmp[:], in0=fl(acol), in1=mref_bc[:], op=ALU.subtract), reads=[g_b], writes=[g_b])
        S.op("act", lambda e: e.activation(out=u_t[:], in_=tmp[:], func=AF.Exp), reads=[g_b], writes=[g_b])
        S.op("dve", lambda e: e.tensor_tensor(out=tmp[:], in0=fl(bcol), in1=Mt[:], op=ALU.add), reads=[g_b], writes=[g_b])
        S.op("act", lambda e: e.activation(out=eb_t[:], in_=tmp[:], func=AF.Exp, scale=-1.0), reads=[g_b], writes=[g_b])
        S.op("dve", lambda e: e.tensor_tensor(out=tmp[:], in0=mprev_bc[:], in1=mref_bc[:], op=ALU.subtract), reads=[g_b], writes=[g_b])
        S.op("act", lambda e: e.activation(out=sc_bc[:], in_=tmp[:], func=AF.Exp), reads=[g_b], writes=[g_b])
        if mlstage < 2:
            S.barrier()
            return
        Cst = sb("Cst", [P, 65])
        Csb = sb("Csb", [P, 130], BF16)
        Kp = sb("Kp", [P, P], BF16)
        ST = [sb(f"ST{h}", [P, P], BF16) for h in range(2)]
        nd = sb("nd", [P, 2, 65])
        hid = sb("hid", [P, 2, 64])
        sm = sb("sm", [P, 8])
        stats = sb("stats", [P, 2, 6])
        mv = sb("mv", [P, 2, 2])
        yTt = sb("yTt", [P, 512], BF16)
        Cst_b, Csb_b, Kp_b, ST_b, nd_b, hid_b, sm_b, yTt_b = Buf(), Buf(), Buf(), [Buf(), Buf()], Buf(), Buf(), Buf(), Buf()
        S.op("pool", lambda e: e.memset(Cst[:], 0.0), writes=[Cst_b])
        S.op("pool", lambda e: e.memset(Csb[:], 0.0), writes=[Csb_b])
        eb3 = eb_t[:].rearrange("p (h c) -> p h c", h=2)
        for c in range(nb):
            tq = c // 4
            cs = slice(c * 128, (c + 1) * 128)
            S.op("pe", lambda e, cs=cs: e.transpose(out=C.ps_bf[:, 0:128], in_=KT[:, cs], identity=identb[:]),
                 reads=[KT_b[tq], cb_], writes=[C.psb[7]])
            for h in range(2):
                hs = slice(h * 64, (h + 1) * 64)
                ix = h * nb + c
                S.op("dve", lambda e, hs=hs, ix=ix: e.tensor_scalar(out=Kp[:, hs], in0=C.ps_bf[:, hs], scalar1=u_t[:, ix:ix + 1], scalar2=0.125,
                                                                    op0=ALU.mult, op1=ALU.mult), reads=[C.psb[7], g_b], writes=[Kp_b])
                S.op("pe", lambda e, hs=hs, h=h, cs=cs: e.matmul(out=C.ps[h][:, 0:128], lhsT=KT[hs, cs], rhs=QT[hs, cs], start=True, stop=True),
                     reads=[KT_b[tq], QT_b[tq]], writes=[C.psb[h]])
                S.op("dve", lambda e, h=h, ix=ix: e.scalar_tensor_tensor(out=ST[h][:], in0=C.ps[h][:, 0:128], scalar=u_t[:, ix:ix + 1], in1=mask8[:],
                                                                         op0=ALU.mult, op1=ALU.mult), reads=[C.psb[h], g_b, cb_], writes=[ST_b[h]])
                S.op("dve", lambda e, hs=hs, h=h, ix=ix: e.tensor_scalar(out=Csb[hs, h * 65:(h + 1) * 65], in0=Cst[hs, :], scalar1=sc_bc[hs, ix:ix + 1],
                                                                         scalar2=None, op0=ALU.mult), reads=[Cst_b, g_b], writes=[Csb_b])
            S.op("pe", lambda e, cs=cs: e.matmul(out=C.ps[2][:, 0:130], lhsT=QT[:, cs], rhs=Csb[:], start=True, stop=False, skip_group_check=True),
                 reads=[QT_b[tq], Csb_b], writes=[C.psb[2]])
            for h in range(2):
                S.op("pe", lambda e, h=h, c=c: e.matmul(out=C.ps[2][:, h * 65:(h + 1) * 65], lhsT=ST[h][:], rhs=Vext[:, c, h, 0:65], start=False, stop=(h == 1),
                                                        skip_group_check=True), reads=[ST_b[h], V_b], writes=[C.psb[2]])
            S.op("pe", lambda e, c=c: e.matmul(out=C.ps[3][:, 0:130], lhsT=Kp[:], rhs=Vext[:, c, :, 0:65], start=True, stop=True),
                 reads=[Kp_b, V_b], writes=[C.psb[3]])
            for h in range(2):
                hs = slice(h * 64, (h + 1) * 64)
                ix = h * nb + c
                S.op("dve", lambda e, hs=hs, h=h, ix=ix: e.scalar_tensor_tensor(out=Cst[hs, :], in0=Cst[hs, :], scalar=sc_bc[hs, ix:ix + 1],
                                                                                in1=C.ps[3][hs, h * 65:(h + 1) * 65], op0=ALU.mult, op1=ALU.add),
                     reads=[Cst_b, g_b, C.psb[3], Csb_b], writes=[Cst_b])
                S.op("dve", lambda e, h=h, ix=ix: e.tensor_scalar(out=nd[:, h, :], in0=C.ps[2][:, h * 65:(h + 1) * 65], scalar1=r_t[:, ix:ix + 1],
                                                                  scalar2=None, op0=ALU.mult), reads=[C.psb[2], g_b], writes=[nd_b])
            S.op("dve", lambda e: e.tensor_scalar(out=sm[:, 0:2], in0=nd[:, :, 64], scalar1=-1.0, scalar2=None, op0=ALU.mult), reads=[nd_b], writes=[sm_b])
            S.op("dve", lambda e: e.tensor_tensor(out=sm[:, 0:2], in0=sm[:, 0:2], in1=nd[:, :, 64], op=ALU.max), reads=[nd_b, sm_b], writes=[sm_b])
            S.op("dve", lambda e, c=c: e.tensor_tensor(out=sm[:, 0:2], in0=sm[:, 0:2], in1=eb3[:, :, c], op=ALU.max), reads=[sm_b, g_b], writes=[sm_b])
            S.op("dve", lambda e: e.reciprocal(out=sm[:, 0:2], in_=sm[:, 0:2]), reads=[sm_b], writes=[sm_b])
            for h in range(2):
                S.op("dve", lambda e, h=h: e.tensor_scalar(out=hid[:, h, :], in0=nd[:, h, 0:64], scalar1=sm[:, h:h + 1], scalar2=None, op0=ALU.mult),
                     reads=[nd_b, sm_b], writes=[hid_b])
                S.op("dve", lambda e, h=h: e.bn_stats(out=stats[:, h, :], in_=hid[:, h, :]), reads=[hid_b], writes=[sm_b])
                S.op("dve", lambda e, h=h: e.bn_aggr(out=mv[:, h, :], in_=stats[:, h, :]), reads=[sm_b], writes=[sm_b])
            S.op("act", lambda e: e.activation(out=sm[:, 2:4], in_=mv[:, :, 1], func=AF.Sqrt, bias=C.eps_t[:], scale=1.0), reads=[sm_b, C.ident_b], writes=[sm_b])
            S.op("dve", lambda e: e.reciprocal(out=sm[:, 2:4], in_=sm[:, 2:4]), reads=[sm_b], writes=[sm_b])
            for h in range(2):
                S.op("dve", lambda e, h=h: e.tensor_scalar(out=hid[:, h, :], in0=hid[:, h, :], scalar1=mv[:, h, 0:1], scalar2=sm[:, 2 + h:3 + h],
                                                           op0=ALU.subtract, op1=ALU.mult), reads=[hid_b, sm_b], writes=[hid_b])
            S.op("dve", lambda e, c=c: e.tensor_tensor(out=hid[:].rearrange("p h d -> p (h d)"), in0=hid[:].rearrange("p h d -> p (h d)"),
                                                        in1=gso[:, c, :], op=ALU.mult), reads=[hid_b, gso_b], writes=[hid_b])
            S.op("pe", lambda e, c=c: e.transpose(out=C.ps[4][:, (c % 4) * 128:(c % 4 + 1) * 128], in_=hid[:].rearrange("p h d -> p (h d)"),
                                                  identity=C.ident[:]), reads=[hid_b, C.ident_b], writes=[C.psb[4]])
            if c % 4 == 3:
                S.op("act", lambda e: e.copy(out=yTt[:], in_=C.ps[4][:]), reads=[C.psb[4]], writes=[yTt_b])
                S.dma("sp", yT[:, (c // 4) * 512:(c // 4 + 1) * 512], yTt[:], reads=[yTt_b], writes=[yT_b])
        S.barrier()


def make_ml_consts_np(c):
    k = np.arange(P)[:, None]
    m = np.arange(P)[None, :]
    c["ntriT"] = -(k <= m).astype(np.float32)
    c["mask8"] = (k <= m).astype(np.float32) * 0.125
    e0 = np.zeros((P, P), np.float32)
    e0[0, :] = 1.0
    c["e0ones"] = e0
    e127 = np.zeros((P, P), np.float32)
    e127[127, :] = 1.0
    c["e127ones"] = e127
    return c


def outproj_ln(C, es, tag, lhs_chunks, lhs_b, wo, wo_b, xt, x_b):
    S = C.S
    for blk in range(4):
        for hh in range(2):
            po = 4 + (blk * 2 + hh) % 2
            for k in range(8):
                S.op("pe", lambda e, k=k, blk=blk, hh=hh, po=po: e.matmul(
                    out=C.ps[po][:], lhsT=lhs_chunks[:, k, blk * 128:(blk + 1) * 128], rhs=wo[:, k, hh * 512:(hh + 1) * 512],
                    start=(k == 0), stop=(k == 7)), reads=[lhs_b, wo_b], writes=[C.psb[po]])
            S.op("dve", lambda e, blk=blk, hh=hh, po=po: e.tensor_tensor(
                out=xt[:, blk, hh * 512:(hh + 1) * 512], in0=C.ps[po][:], in1=xt[:, blk, hh * 512:(hh + 1) * 512], op=ALU.add),
                reads=[C.psb[po], x_b], writes=[x_b])


def mixout_phase(C, x_in, xin_b, ygath, yg_b, w_out, sel_dram, g_dram, b_dram, x_out, xout_b, NT, TT, tag):
    nc, S = C.nc, C.S
    ntile = NT // 512
    with contextlib.ExitStack() as es:
        wo, wo_b = load_w_bf16(C, es, f"wmo{tag}", w_out, D)
        sb = lambda name, shape, dt=F32: es.enter_context(nc.sbuf_tensor(f"mo{name}{tag}", shape, dt))
        g_bc, b_bc, sel = sb("g", [P, D]), sb("b", [P, D]), sb("sel", [P, 2])
        xts = [sb(f"xt{i}", [P, 4, D]) for i in range(2)]
        yAs = [sb(f"yA{i}", [P, 8, 512], BF16) for i in range(2)]
        yBs = [sb(f"yB{i}", [P, 8, 512], BF16) for i in range(2)]
        stats, mv, rstd = sb("stats", [P, 4, 2, 6]), sb("mv", [P, 4, 2]), sb("rstd", [P, 4, 1])
        gb_b, st_b = Buf(), Buf()
        x_bs, yA_bs, yB_bs = [Buf(), Buf()], [Buf(), Buf()], [Buf(), Buf()]
        S.dma("sp", g_bc[:], g_dram, writes=[gb_b])
        S.dma("sp", b_bc[:], b_dram, writes=[gb_b])
        S.dma("sp", sel[:], sel_dram, writes=[gb_b])
        xin_v = x_in.rearrange("(t b p) d -> t p b d", b=4, p=P)
        xout_v = x_out.rearrange("(t b p) d -> t p b d", b=4, p=P)
        yv = ygath.rearrange("(k p) t -> p k t", p=P)

        def loads(t):
            k = t % 2
            S.dma("sp", xts[k][:], xin_v[t], reads=[xin_b[t] if isinstance(xin_b, list) else xin_b], writes=[x_bs[k]])
            S.dma("sp", yAs[k][:], yv[:, :, t * 512:(t + 1) * 512], reads=[yg_b], writes=[yA_bs[k]])
            S.dma("sp", yBs[k][:], yv[:, :, NT + t * 512:NT + (t + 1) * 512], reads=[yg_b], writes=[yB_bs[k]])
        loads(0)
        for t in range(ntile):
            k = t % 2
            xt, x_b, yA, yA_b, yB, yB_b = xts[k], x_bs[k], yAs[k], yA_bs[k], yBs[k], yB_bs[k]
            if t + 1 < ntile:
                loads(t + 1)
            S.op("act", lambda e, xt=xt: e.mul(out=xt[:], in_=xt[:], mul=ALPHA), reads=[x_b], writes=[x_b])
            S.op("dve", lambda e, yA=yA: e.tensor_scalar(out=yA[:], in0=yA[:], scalar1=sel[:, 0:1], scalar2=None, op0=ALU.mult),
                 reads=[yA_b, gb_b], writes=[yA_b])
            S.op("dve", lambda e, yA=yA, yB=yB: e.scalar_tensor_tensor(out=yA[:], in0=yB[:], scalar=sel[:, 1:2], in1=yA[:], op0=ALU.mult, op1=ALU.add),
                 reads=[yA_b, yB_b, gb_b], writes=[yA_b])
            outproj_ln(C, es, tag, yA, yA_b, wo, wo_b, xt, x_b)
            layer_norm_tile(C, xt, x_b, g_bc, b_bc, gb_b, stats, mv, rstd, st_b)
            S.dma("pool", xout_v[t], xt[:], reads=[x_b], writes=[xout_b])
        S.barrier()


def xattn_phase(C, x_in, xin_b, mem, w_q, w_kv, w_o, g_dram, b_dram, x_out, xout_b, NT, tag):
    nc, S = C.nc, C.S
    ntile = NT // 512
    C.set_psum(2)
    with contextlib.ExitStack() as es:
        wq, wq_b = load_w_bf16(C, es, f"wxq{tag}", w_q, D)
        wkv, wkv_b = load_w_bf16(C, es, f"wxkv{tag}", w_kv, 2 * D)
        wo, wo_b = load_w_bf16(C, es, f"wxo{tag}", w_o, D)
        sb = lambda name, shape, dt=F32: es.enter_context(nc.sbuf_tensor(f"xa{name}{tag}", shape, dt))
        g_bc, b_bc = sb("g", [P, D]), sb("b", [P, D])
        identb = sb("idb", [P, P], BF16)
        xt = sb("xt", [P, 4, D])
        xT = sb("xT", [P, 8, 512], BF16)
        QT = sb("QT", [P, 8, 512], BF16)
        KTx = sb("KT", [P, 8, 256], BF16)
        Vx = sb("V", [P, 2, D], BF16)
        aoT = sb("aoT", [P, 8, 512], BF16)
        sc = sb("sc", [P, 256])
        pf = sb("pf", [P, 256])
        pn = sb("pn", [P, 256], BF16)
        pT = sb("pT", [P, 2, 128], BF16)
        sm = sb("sm", [P, 4])
        stats, mv, rstd = sb("stats", [P, 4, 2, 6]), sb("mv", [P, 4, 2]), sb("rstd", [P, 4, 1])
        gb_b, x_b, xT_b, QT_b, KT_b, V_b, ao_b, st_b = Buf(), Buf(), Buf(), Buf(), Buf(), Buf(), Buf(), Buf()
        sc_b, pf_b, pn_b, pT_b, sm_b = Buf(), Buf(), Buf(), Buf(), Buf()
        pf2 = [pf, sb("pfb", [P, 256])]
        pn2 = [pn, sb("pnb", [P, 256], BF16)]
        pT2 = [pT, sb("pTb", [P, 2, 128], BF16)]
        sm2 = [sm, sb("smb", [P, 4])]
        pf_b2, pn_b2, pT_b2, sm_b2 = [Buf(), Buf()], [Buf(), Buf()], [Buf(), Buf()], [Buf(), Buf()]
        pf4 = [sb(f"pf4{k}", [P, 4, 256]) for k in range(2)]
        pn4 = [sb(f"pn4{k}", [P, 4, 256], BF16) for k in range(2)]
        pT4 = [sb(f"pT4{k}", [P, 8, 128], BF16) for k in range(2)]
        sm4 = [sb(f"sm4{k}", [P, 4, 4]) for k in range(2)]
        sc_pb, bf_pb, pv_pb = [Buf(), Buf()], [Buf(), Buf()], [Buf(), Buf()]
        S.dma("sp", g_bc[:], g_dram, writes=[gb_b])
        S.dma("sp", b_bc[:], b_dram, writes=[gb_b])
        S.dma("pool", identb[:], C.ident_dram, writes=[gb_b])
        S.dma("sp", xt[:, 0:2, :], mem.rearrange("(b p) d -> p b d", p=P), writes=[x_b])
        for kc in range(8):
            for blk in range(2):
                S.op("pe", lambda e, kc=kc, blk=blk: e.transpose(out=C.ps[4][:, blk * 128:(blk + 1) * 128], in_=xt[:, blk, kc * 128:(kc + 1) * 128],
                                                                 identity=C.ident[:]), reads=[x_b, C.ident_b], writes=[C.psb[4]])
            S.op("dve", lambda e, kc=kc: e.tensor_copy(out=xT[:, kc, 0:256], in_=C.ps[4][:, 0:256]), reads=[C.psb[4]], writes=[xT_b])
        for j in range(8):
            pi = j % 2
            for kc in range(8):
                S.op("pe", lambda e, kc=kc, j=j, pi=pi: e.matmul(out=C.ps[pi][:, 0:256], lhsT=wkv[:, kc, j * 128:(j + 1) * 128], rhs=xT[:, kc, 0:256],
                                                                 start=(kc == 0), stop=(kc == 7)), reads=[wkv_b, xT_b], writes=[C.psb[pi]])
            S.op("dve", lambda e, j=j, pi=pi: e.tensor_copy(out=KTx[:, j, :], in_=C.ps[pi][:, 0:256]), reads=[C.psb[pi]], writes=[KT_b])
        for blk in range(2):
            for hh in range(2):
                pi = 2 + hh
                for kc in range(8):
                    S.op("pe", lambda e, kc=kc, blk=blk, hh=hh, pi=pi: e.matmul(
                        out=C.ps[pi][:], lhsT=xT[:, kc, blk * 128:(blk + 1) * 128], rhs=wkv[:, kc, D + hh * 512:D + (hh + 1) * 512],
                        start=(kc == 0), stop=(kc == 7)), reads=[wkv_b, xT_b], writes=[C.psb[pi]])
                S.op("act", lambda e, blk=blk, hh=hh, pi=pi: e.copy(out=Vx[:, blk, hh * 512:(hh + 1) * 512], in_=C.ps[pi][:]),
                     reads=[C.psb[pi]], writes=[V_b])
        xin_v = x_in.rearrange("(t b p) d -> t p b d", b=4, p=P)
        xout_v = x_out.rearrange("(t b p) d -> t p b d", b=4, p=P)
        xts = [xt, sb("xt2", [P, 4, D])]
        x_bs = [x_b, Buf()]
        QTs = [QT, sb("QT2", [P, 8, 512], BF16)]
        QT_bs = [QT_b, Buf()]

        def emit_load(t):
            S.dma("sp", xts[t % 2][:], xin_v[t], reads=[xin_b[t] if isinstance(xin_b, list) else xin_b], writes=[x_bs[t % 2]])

        def emit_T(t):
            load_xT(C, xts[t % 2], x_bs[t % 2], xT, xT_b, ps_ids=(0,))
            S.op("act", lambda e, t=t: e.mul(out=xts[t % 2][:], in_=xts[t % 2][:], mul=ALPHA), reads=[x_bs[t % 2]], writes=[x_bs[t % 2]])

        def emit_qproj(t, j):
            proj_fm(C, wq, wq_b, j * 128, (j + 1) * 128, xT, xT_b, 1)
            S.op("act" if j % 2 == 0 else "dve",
                 (lambda e, j=j, t=t: e.mul(out=QTs[t % 2][:, j, :], in_=C.ps[1][:], mul=0.0625)) if j % 2 == 0 else
                 (lambda e, j=j, t=t: e.tensor_scalar(out=QTs[t % 2][:, j, :], in0=C.ps[1][:], scalar1=0.0625, scalar2=None, op0=ALU.mult)),
                 reads=[C.psb[1]], writes=[QT_bs[t % 2]])

        emit_load(0)
        emit_T(0)
        for j in range(8):
            emit_qproj(0, j)
        for t in range(ntile):
            xt, x_b, QT, QT_b = xts[t % 2], x_bs[t % 2], QTs[t % 2], QT_bs[t % 2]
            if t + 1 < ntile:
                emit_load(t + 1)
            for hd in range(4):
                a = hd % 2
                if t + 1 < ntile:
                    if hd == 2:
                        emit_T(t + 1)
                    elif hd == 3:
                        for j in range(8):
                            emit_qproj(t + 1, j)
                for b in range(4):
                    ts = slice(b * 128, (b + 1) * 128)
                    for kk in range(2):
                        S.op("pe", lambda e, hd=hd, kk=kk, ts=ts, b=b: e.matmul(out=C.ps[2 + b // 2][:, (b % 2) * 256:(b % 2 + 1) * 256],
                                                                               lhsT=QT[:, 2 * hd + kk, ts], rhs=KTx[:, 2 * hd + kk, :],
                                                                               start=(kk == 0), stop=(kk == 1), skip_group_check=True),
                             reads=[QT_b, KT_b], writes=[C.psb[2 + b // 2]])
                for bank in range(2):
                    S.op("dve", lambda e, bank=bank, a=a: e.tensor_reduce(out=sm4[a][:, 0, 2 * bank:2 * bank + 2],
                                                                         in_=C.ps[2 + bank][:].rearrange("p (b k) -> p b k", b=2), axis=AX.X, op=ALU.max),
                         reads=[C.psb[2 + bank]], writes=[sm_b2[a]])
                S.op("dve", lambda e, a=a: e.tensor_scalar(out=sm4[a][:, 1, :], in0=sm4[a][:, 0, :], scalar1=-1.0, scalar2=None, op0=ALU.mult),
                     reads=[sm_b2[a]], writes=[sm_b2[a]])
                for b in range(4):
                    S.op("act", lambda e, b=b, a=a: e.activation(out=pf4[a][:, b, :], in_=C.ps[2 + b // 2][:, (b % 2) * 256:(b % 2 + 1) * 256], func=AF.Exp,
                                                                 bias=sm4[a][:, 1, b:b + 1], scale=1.0, accum_out=sm4[a][:, 2, b:b + 1]),
                         reads=[C.psb[2 + b // 2], sm_b2[a]], writes=[pf_b2[a], sm_b2[a]])
                S.op("dve", lambda e, a=a: e.reciprocal(out=sm4[a][:, 3, :], in_=sm4[a][:, 2, :]), reads=[sm_b2[a]], writes=[sm_b2[a]])
                for b in range(4):
                    if b % 2 == 0:
                        S.op("dve", lambda e, b=b, a=a: e.tensor_scalar(out=pn4[a][:, b, :], in0=pf4[a][:, b, :], scalar1=sm4[a][:, 3, b:b + 1],
                                                                        scalar2=None, op0=ALU.mult), reads=[pf_b2[a], sm_b2[a]], writes=[pn_b2[a]])
                    else:
                        S.op("act", lambda e, b=b, a=a: e.mul(out=pn4[a][:, b, :], in_=pf4[a][:, b, :], mul=sm4[a][:, 3, b:b + 1]),
                             reads=[pf_b2[a], sm_b2[a]], writes=[pn_b2[a]])
                for b in range(4):
                    for mb in range(2):
                        S.op("pe", lambda e, b=b, mb=mb, a=a: e.transpose(out=C.ps_bfs[a][:, (b * 2 + mb) * 128:(b * 2 + mb + 1) * 128],
                                                                          in_=pn4[a][:, b, mb * 128:(mb + 1) * 128], identity=identb[:]),
                             reads=[pn_b2[a], gb_b], writes=[C.psbf_b[a]])
                S.op("act", lambda e, a=a: e.copy(out=pT4[a][:, 0:4, :].rearrange("p a b -> p (a b)"), in_=C.ps_bfs[a][:, 0:512]),
                     reads=[C.psbf_b[a]], writes=[pT_b2[a]])
                S.op("dve", lambda e, a=a: e.tensor_copy(out=pT4[a][:, 4:8, :].rearrange("p a b -> p (a b)"), in_=C.ps_bfs[a][:, 512:1024]),
                     reads=[C.psbf_b[a]], writes=[pT_b2[a]])
                for cc in range(2):
                    for b in range(4):
                        for mb in range(2):
                            S.op("pe", lambda e, cc=cc, mb=mb, hd=hd, b=b, a=a: e.matmul(out=C.ps[4 + cc][:, b * 128:(b + 1) * 128],
                                                                                         lhsT=Vx[:, mb, hd * 256 + cc * 128:hd * 256 + (cc + 1) * 128],
                                                                                         rhs=pT4[a][:, b * 2 + mb, :], start=(mb == 0), stop=(mb == 1),
                                                                                         skip_group_check=True),
                                 reads=[V_b, pT_b2[a]], writes=[C.psb[4 + cc]])
                S.op("act", lambda e, hd=hd: e.copy(out=aoT[:, 2 * hd, :], in_=C.ps[4][:]), reads=[C.psb[4]], writes=[ao_b])
                S.op("dve", lambda e, hd=hd: e.tensor_copy(out=aoT[:, 2 * hd + 1, :], in_=C.ps[5][:]), reads=[C.psb[5]], writes=[ao_b])
            outproj_ln(C, es, tag, aoT, ao_b, wo, wo_b, xt, x_b)
            layer_norm_tile(C, xt, x_b, g_bc, b_bc, gb_b, stats, mv, rstd, st_b)
            S.dma("sp", xout_v[t], xt[:], reads=[x_b], writes=[xout_b])
        S.barrier()
    C.set_psum(1)


NT_OWN = 4096
TT_SEQ = 8192
DEPTH = 2
PAIRS = [[0, 1], [2, 3], [4, 5], [6, 7]]
_SPLITS = np.cumsum([0, 256, 256, 256, 512, 128, 128, 512, 256, 4, 4, 256])


def build_program():
    from concourse.bass_utils import run_bass_kernel_spmd
    nc = bass.Bass("TRN2", target_bir_lowering=False)
    din = lambda name, shape, dt=F32: nc.dram_tensor(name, shape, dt, kind="ExternalInput").ap()
    dscr = lambda name, shape, dt=F32: nc.dram_tensor(name, shape, dt, kind="Internal").ap()
    x = din("x", [NT_OWN, D])
    mem = din("mem", [256, D])
    sel = din("sel", [P, 2])
    cn = make_consts_np()
    cd = {k: din("c_" + k, list(v.shape)) for k, v in cn.items()}
    L = []
    for l in range(DEPTH):
        d = {}
        d["ffn1_in"] = din(f"l{l}_ffn1_in", [D, 2 * DFF]); d["ffn1_out"] = din(f"l{l}_ffn1_out", [DFF, D])
        d["ffn2_in"] = din(f"l{l}_ffn2_in", [D, 2 * DFF]); d["ffn2_out"] = din(f"l{l}_ffn2_out", [DFF, D])
        d["w_sb"] = din(f"l{l}_w_sb", [D, 384]); d["w_swa"] = din(f"l{l}_w_swa", [D, 448]); d["w_ml"] = din(f"l{l}_w_ml", [D, 640])
        d["cw"] = din(f"l{l}_cw", [P, 8]); d["cb"] = din(f"l{l}_cb", [P, 2]); d["ifb"] = din(f"l{l}_ifb", [P, 4]); d["ng"] = din(f"l{l}_ng", [P, P])
        d["rb"] = din(f"l{l}_rb", [P, P]); d["sink"] = din(f"l{l}_sink", [P, 4])
        d["w_mo"] = din(f"l{l}_w_mo", [D, D])
        d["xq"] = din(f"l{l}_xq", [D, D]); d["xkv"] = din(f"l{l}_xkv", [D, 2 * D]); d["xo"] = din(f"l{l}_xo", [D, D])
        d["lng"] = [din(f"l{l}_lng{i}", [P, D]) for i in range(4)]
        d["lnb"] = [din(f"l{l}_lnb{i}", [P, D]) for i in range(4)]
        L.append(d)
    out = nc.dram_tensor("out", [NT_OWN, D], F32, kind="ExternalOutput").ap()
    Xa = dscr("Xa", [NT_OWN, D]); Xb = dscr("Xb", [NT_OWN, D])
    X1g = dscr("X1g", [NT_OWN // 512, 2 * D, 512], BF16)
    XaT = dscr("XaT", [NT_OWN // 512, D, 512], BF16)
    YTo = dscr("YTo", [512, TT_SEQ], BF16)
    Yg = dscr("Yg", [1024, TT_SEQ], BF16)
    frs = dscr("frs", [4, 384])
    S = Sched(nc)
    C = Ctx(nc, S)
    C.load_consts(cd["ident"])
    x_b, Xa_b, Xb_b, X1g_b, YTo_b, Yg_b, out_b = Buf(), Buf(), Buf(), Buf(), [Buf(), Buf(), Buf()], Buf(), Buf()
    cur, cur_b = x, x_b
    import os
    kstop = int(os.environ.get("KSTOP", "99"))
    for l in range(DEPTH):
        d = L[l]
        tg = f"L{l}"
        if kstop < 99 and l > 0:
            break
        Xa_tb = [Buf() for _ in range(NT_OWN // 512)]
        XaT_tb = [Buf() for _ in range(NT_OWN // 512)]
        X1g_tb = [Buf() for _ in range(TT_SEQ // 512)]
        nch = NT_OWN // 512

        def ag1(t):
            S.custom("pool", lambda e, t=t: e.collective_compute("AllGather", ALU.bypass, replica_groups=PAIRS,
                                                                 ins=[XaT[t]], outs=[X1g[t]]), 1,
                     reads=[XaT_tb[t]], writes=[X1g_tb[t], X1g_tb[nch + t]])
        ffn_phase(C, cur, Xa, d["ffn1_in"], d["ffn1_out"], d["lng"][0], d["lnb"][0], NT_OWN, tg + "f1", cur_b, Xa_tb, after_tile=ag1,
                  xT_out=XaT, xT_out_b=XaT_tb)
        if kstop <= 2:
            break
        X1v = X1g.rearrange("j (r kc p) n -> j r p kc n", r=2, p=P)
        xtile = lambda t: X1v[t % nch, t // nch]
        sb_phase(C, xtile, X1g_tb, d["w_sb"], YTo[0:128, :], YTo_b[0], TT_SEQ, cd, tg + "sb")
        S.custom("pool", lambda e: e.collective_compute("AllGather", ALU.bypass, replica_groups=PAIRS, ins=[YTo[0:128, :]], outs=[Yg[0:256, :]]), 1,
                 reads=[YTo_b[0]], writes=[Yg_b])
        if kstop <= 3:
            break
        swa_phase(C, xtile, X1g_tb, d["w_swa"], d["rb"], d["sink"], frs, YTo[128:384, :], YTo_b[1], TT_SEQ, cd, tg + "sw")
        for k in (1, 2):
            S.custom("pool", lambda e, k=k: e.collective_compute("AllGather", ALU.bypass, replica_groups=PAIRS, ins=[YTo[k * 128:(k + 1) * 128, :]],
                                                                 outs=[Yg[k * 256:(k + 1) * 256, :]]), 1, reads=[YTo_b[1]], writes=[Yg_b])
        if kstop <= 4:
            break
        mlstm_phase(C, xtile, X1g_tb, d["w_ml"], d["cw"], d["cb"], d["ifb"], d["ng"], YTo[384:512, :], YTo_b[2], TT_SEQ, cd, tg + "ml")
        S.custom("pool", lambda e: e.collective_compute("AllGather", ALU.bypass, replica_groups=PAIRS, ins=[YTo[384:512, :]], outs=[Yg[768:1024, :]]), 1,
                 reads=[YTo_b[2]], writes=[Yg_b])
        S.barrier()
        if kstop <= 6:
            break
        mixout_phase(C, Xa, Xa_tb, Yg, Yg_b, d["w_mo"], sel, d["lng"][1], d["lnb"][1], Xb, Xb_b, NT_OWN, TT_SEQ, tg + "mo")
        if kstop <= 7:
            break
        xattn_phase(C, Xb, Xb_b, mem, d["xq"], d["xkv"], d["xo"], d["lng"][2], d["lnb"][2], Xa, Xa_b, NT_OWN, tg + "xa")
        if kstop <= 8:
            break
        dst, dst_b = (out, out_b) if l == DEPTH - 1 else (Xb, Xb_b)
        ffn_phase(C, Xa, dst, d["ffn2_in"], d["ffn2_out"], d["lng"][3], d["lnb"][3], NT_OWN, tg + "f2", Xa_b, dst_b)
        cur, cur_b = dst, dst_b
    S.finish()
    C.psum_es.close()
    return nc, cn


def make_core_inputs(c, inp, cn):
    b, r = c // 2, c % 2
    f = lambda a: np.ascontiguousarray(np.asarray(a, dtype=np.float32))
    rep = lambda v: f(np.tile(np.asarray(v, np.float32)[None, :], (P, 1)))
    m = {"x": f(inp["x"][b, r * NT_OWN:(r + 1) * NT_OWN]), "mem": f(inp["mem"][b])}
    selv = np.zeros((P, 2), np.float32)
    selv[:, r] = 1.0
    m["sel"] = selv
    for k, v in cn.items():
        m["c_" + k] = f(v)
    sp = _SPLITS
    slot = [0, 2, 1, 3]
    for l in range(DEPTH):
        w = np.asarray(inp["mix_w_in"][l], np.float32)
        seg = [w[:, sp[i]:sp[i + 1]] for i in range(11)]
        sbq, sbk, sbv, swq, swk, swv, mlqk, mlv, ig, fg, og = seg
        mlq, mlk = mlqk[:, 0:256], mlqk[:, 256:512]
        m[f"l{l}_ffn1_in"] = f(inp["ffn1_w_in"][l]); m[f"l{l}_ffn1_out"] = f(inp["ffn1_w_out"][l])
        m[f"l{l}_ffn2_in"] = f(inp["ffn2_w_in"][l]); m[f"l{l}_ffn2_out"] = f(inp["ffn2_w_out"][l])
        h2 = slice(r * 128, (r + 1) * 128)
        m[f"l{l}_w_sb"] = f(np.concatenate([sbq[:, h2], sbk[:, h2], sbv[:, h2]], 1))
        kk = swk[:, r * 64:(r + 1) * 64]
        m[f"l{l}_w_swa"] = f(np.concatenate([swq[:, r * 256:(r + 1) * 256], kk, kk, swv[:, r * 64:(r + 1) * 64]], 1))
        m[f"l{l}_w_ml"] = f(np.concatenate([mlq[:, h2], mlk[:, h2], mlv[:, h2], og[:, h2], ig[:, 2 * r:2 * r + 2], fg[:, 2 * r:2 * r + 2],
                                            np.zeros((D, 124), np.float32)], 1))
        cwl = np.asarray(inp["ml_conv_w"][l], np.float32)
        cbl = np.asarray(inp["ml_conv_b"][l], np.float32)
        m[f"l{l}_cw"] = f(np.concatenate([cwl[:, r * 128:(r + 1) * 128].T, cwl[:, 256 + r * 128:256 + (r + 1) * 128].T], 1))
        m[f"l{l}_cb"] = f(np.stack([cbl[r * 128:(r + 1) * 128], cbl[256 + r * 128:256 + (r + 1) * 128]], 1))
        ib = np.asarray(inp["ml_i_bias"][l], np.float32)[2 * r:2 * r + 2]
        fb = np.asarray(inp["ml_f_bias"][l], np.float32)[2 * r:2 * r + 2]
        m[f"l{l}_ifb"] = rep(np.concatenate([ib, fb]))
        m[f"l{l}_ng"] = rep(np.asarray(inp["ml_norm_g"][l], np.float32)[h2])
        rbl = np.asarray(inp["rel_bias"], np.float32)[:, 4 * r:4 * r + 4][:, slot]
        m[f"l{l}_rb"] = f(np.pad(rbl, ((0, 96), (0, 124))))
        m[f"l{l}_sink"] = rep(np.asarray(inp["swa_sinks"][l], np.float32)[4 * r:4 * r + 4][slot])
        wo = np.asarray(inp["mix_w_out"][l], np.float32)
        rows = []
        for k in range(4):
            for rr in range(2):
                base = [rr * 128, 256 + rr * 256, 256 + rr * 256 + 128, 768 + rr * 128][k]
                rows.append(wo[base:base + 128])
        m[f"l{l}_w_mo"] = f(np.concatenate(rows, 0))
        m[f"l{l}_xq"] = f(inp["xattn_w_q"][l]); m[f"l{l}_xkv"] = f(inp["xattn_w_kv"][l]); m[f"l{l}_xo"] = f(inp["xattn_w_o"][l])
        for i in range(4):
            m[f"l{l}_lng{i}"] = rep(inp["ln_g"][l][i]); m[f"l{l}_lnb{i}"] = rep(inp["ln_b"][l][i])
    return m


def kernel(**inputs):
    from concourse.bass_utils import run_bass_kernel_spmd
    inp = {k: np.asarray(v) for k, v in inputs.items()}
    nc, cn = build_program()
    in_maps = [make_core_inputs(c, inp, cn) for c in range(8)]
    res = run_bass_kernel_spmd(nc, in_maps, core_ids=list(range(8)))
    out = np.zeros((4, TT_SEQ, D), np.float32)
    for c in range(8):
        b, r = c // 2, c % 2
        out[b, r * NT_OWN:(r + 1) * NT_OWN] = np.asarray(res.results[c]["out"], np.float32)
    return out
```
